# Optimizing a Trainium2 kernel written in Bass

```python
import math
import jax, jax.numpy as jnp
from jax import lax
import numpy as np

D_MODEL = 2048
BATCH = 4
SEQ = 2048
DEPTH = 2
DEC_BATCH = 128
DEC_SEQ = 8
PAST_LEN = 16384
PAGE_SIZE = 128

N_EVEN = (DEPTH + 1) // 2
N_ODD = DEPTH // 2
M_HEADS = 4
M_HEAD_DIM = D_MODEL // 8
M_WIDTH = M_HEADS * M_HEAD_DIM
M_CONV = 4
G_HEADS = 8
G_HEAD_DIM = D_MODEL // 16
G_WIDTH = G_HEADS * G_HEAD_DIM
MIX_WIDTH = M_WIDTH + G_WIDTH
IN_SIZES = (M_WIDTH,) * 4 + (M_HEADS,) * 2 + (G_WIDTH,) * 4
IN_WIDTH = 4 * M_WIDTH + 2 * M_HEADS + 4 * G_WIDTH
R_HEAD_DIM = 64
R_HEADS = D_MODEL // R_HEAD_DIM
R_DECAY_LORA = 96
R_AAA_LORA = 96
R_GATE_LORA = 256
D_FF = 4 * D_MODEL
N_MOD = 6
CHUNK = 64
RMS_EPS = 1e-6
LN_X_EPS = 64e-5
F32 = jnp.float32

kernel_name = 'xlstm_hgrn2_rwkv7_hybrid_step'


def rmsnorm(x, g):
    x32 = x.astype(F32)
    y = x32 * lax.rsqrt(jnp.mean(x32 * x32, axis=-1, keepdims=True) + RMS_EPS)
    return (y * g.astype(F32)).astype(x.dtype)


def head_rms(y):
    return y * lax.rsqrt(jnp.mean(y * y, axis=-1, keepdims=True) + RMS_EPS)


def head_ln(y, eps):
    d = y - jnp.mean(y, axis=-1, keepdims=True)
    return d * lax.rsqrt(jnp.mean(d * d, axis=-1, keepdims=True) + eps)


def causal_conv(u, buf, w):
    T = u.shape[1]
    K = w.shape[0]
    ext = jnp.concatenate([buf.astype(u.dtype), u], axis=1)
    out = ext[:, K - 1:K - 1 + T] * w[K - 1]
    for j in range(K - 1):
        out = out + ext[:, j:j + T] * w[j]
    return out, ext[:, T:]


def to_chunks(a, L):
    B, T = a.shape[:2]
    return jnp.moveaxis(a.reshape((B, T // L, L) + a.shape[2:]), 1, 0)


def from_chunks(a):
    NC, B, L = a.shape[:3]
    return jnp.moveaxis(a, 0, 1).reshape((B, NC * L) + a.shape[3:])


def mlstm_chunkwise(q, k, v, ig, lf, C0, n0, m0):
    T = q.shape[1]
    L = math.gcd(T, CHUNK)
    mask = jnp.tril(jnp.ones((L, L), dtype=bool))[None, :, :, None]

    def step(carry, xs):
        C, n, m = carry
        qc, kc, vc, ic, fc = xs
        b = jnp.cumsum(fc, axis=1)
        dmat = b[:, :, None, :] - b[:, None, :, :] + ic[:, None, :, :]
        dmat = jnp.where(mask, dmat, -jnp.inf)
        inter = b + m[:, None, :]
        m_t = jnp.maximum(inter, jnp.max(dmat, axis=2))
        w_intra = jnp.exp(dmat - m_t[:, :, None, :])
        w_inter = jnp.exp(inter - m_t)
        s = jnp.einsum('bthd,bshd->btsh', qc, kc) * w_intra
        num = jnp.einsum('btsh,bshe->bthe', s, vc) + w_inter[..., None] * jnp.einsum('bthd,bhde->bthe', qc, C)
        den = jnp.sum(s, axis=2) + w_inter * jnp.einsum('bthd,bhd->bth', qc, n)
        h = num / jnp.maximum(jnp.abs(den), jnp.exp(-m_t))[..., None]
        m_new = m_t[:, -1]
        w_end = jnp.exp(b[:, -1:, :] - b + ic - m_new[:, None, :])
        decay = jnp.exp(b[:, -1] + m - m_new)
        C = decay[..., None, None] * C + jnp.einsum('bsh,bshd,bshe->bhde', w_end, kc, vc)
        n = decay[..., None] * n + jnp.einsum('bsh,bshd->bhd', w_end, kc)
        return (C, n, m_new), h

    xs = tuple(to_chunks(t, L) for t in (q, k, v, ig, lf))
    (C, n, m), hs = lax.scan(step, (C0, n0, m0), xs)
    return from_chunks(hs), C, n, m


def hgrn2_chunkwise(q, k, v, lg, S0):
    T = q.shape[1]
    L = math.gcd(T, CHUNK)
    mask = jnp.tril(jnp.ones((L, L), dtype=bool))[None, :, :, None, None]

    def step(S, xs):
        qc, kc, vc, gc = xs
        bc = jnp.cumsum(gc, axis=1)
        diff = jnp.where(mask, bc[:, :, None] - bc[:, None, :], -jnp.inf)
        A = jnp.einsum('btshk,bshk->btsh', qc[:, :, None] * jnp.exp(diff), kc)
        o = jnp.einsum('btsh,bshv->bthv', A, vc) + jnp.einsum('bthk,bhkv->bthv', qc * jnp.exp(bc), S)
        bl = bc[:, -1]
        S = jnp.exp(bl)[..., None] * S + jnp.einsum('bshk,bshv->bhkv', kc * jnp.exp(bl[:, None] - bc), vc)
        return S, o

    xs = tuple(to_chunks(t, L) for t in (q, k, v, lg))
    S, os_ = lax.scan(step, S0, xs)
    return from_chunks(os_), S


def rwkv7_recurrence(r, w, k, v, a, b, S0):
    def step(S, xs):
        rt, wt, kt, vt, at, bt = xs
        sa = jnp.einsum('bhij,bhj->bhi', S, at)
        S = S * wt[:, :, None, :] + sa[..., None] * bt[:, :, None, :] + vt[..., None] * kt[:, :, None, :]
        return S, jnp.einsum('bhij,bhj->bhi', S, rt)

    xs = tuple(jnp.moveaxis(t, 1, 0) for t in (r, w, k, v, a, b))
    S, ys = lax.scan(step, S0, xs)
    return jnp.moveaxis(ys, 0, 1), S


def ab_mixer(h, W_in, gate_b, conv_w, m_gain, lb, g_gain, W_out, C0, n0, m0, conv0, S0):
    B, T, _ = h.shape
    z = h @ W_in
    idx = np.cumsum(IN_SIZES)[:-1].tolist()
    mq, mk, mv, mo, mi, mf, gq, gf, gi, gg = jnp.split(z, idx, axis=-1)
    qk, conv_new = causal_conv(jnp.concatenate([mq, mk], axis=-1), conv0, conv_w)
    qk = jax.nn.silu(qk.astype(F32))
    mhd = lambda t: t.reshape(B, T, M_HEADS, M_HEAD_DIM)
    q = mhd(qk[..., :M_WIDTH])
    k = mhd(qk[..., M_WIDTH:]) * (M_HEAD_DIM ** -0.5)
    v = mhd(mv.astype(F32))
    ig = mi.astype(F32) + gate_b[:M_HEADS].astype(F32)
    lf = jax.nn.log_sigmoid(mf.astype(F32) + gate_b[M_HEADS:].astype(F32))
    hm, C, n, m = mlstm_chunkwise(q, k, v, ig, lf, C0.astype(F32), n0.astype(F32), m0.astype(F32))
    hm = jax.nn.sigmoid(mhd(mo.astype(F32))) * head_ln(hm, RMS_EPS) * m_gain.astype(F32).reshape(M_HEADS, M_HEAD_DIM)
    ghd = lambda t: t.reshape(B, T, G_HEADS, G_HEAD_DIM)
    gq_ = ghd(jax.nn.silu(gq.astype(F32))) * (G_HEAD_DIM ** -0.5)
    f = lb + (1.0 - lb) * jax.nn.sigmoid(gf.astype(F32))
    og, S = hgrn2_chunkwise(gq_, ghd(1.0 - f), ghd(gi.astype(F32)), ghd(jnp.log(f)), S0.astype(F32))
    og = head_rms(og) * g_gain.astype(F32).reshape(G_HEADS, G_HEAD_DIM) * ghd(jax.nn.silu(gg.astype(F32)))
    mixed = jnp.concatenate([hm.reshape(B, T, M_WIDTH), og.reshape(B, T, G_WIDTH)], axis=-1).astype(h.dtype)
    return mixed @ W_out, C, n, m, conv_new, S


def rwkv7_mixer(h, shift0, S0, mu, w0, w1, w2, a0, a1, a2, g1, g2, k_k, k_a, r_k, W_r, W_k, W_v, W_o, ln_w, ln_b):
    B, T, _ = h.shape
    prev = jnp.concatenate([shift0[:, None, :].astype(h.dtype), h[:, :-1]], axis=1)
    xx = prev - h
    xr, xw, xk, xv, xa, xg = (h + xx * mu[i] for i in range(6))
    r = (xr @ W_r).astype(F32)
    k = (xk @ W_k).astype(F32)
    v = (xv @ W_v).astype(F32)
    w = -jax.nn.softplus(-(w0.astype(F32) + (jnp.tanh(xw @ w1) @ w2).astype(F32))) - 0.5
    a = jax.nn.sigmoid(a0.astype(F32) + ((xa @ a1) @ a2).astype(F32))
    g = jax.nn.sigmoid(xg @ g1) @ g2
    hd = lambda t: t.reshape(B, T, R_HEADS, R_HEAD_DIM)
    kk = hd(k * k_k.astype(F32))
    kk = kk / jnp.maximum(jnp.linalg.norm(kk, axis=-1, keepdims=True), 1e-12)
    k = k * (1.0 + (a - 1.0) * k_a.astype(F32))
    decay = jnp.exp(-jnp.exp(w))
    r4, k4, v4, a4 = hd(r), hd(k), hd(v), hd(a)
    y, S = rwkv7_recurrence(r4, hd(decay), k4, v4, -kk, kk * a4, S0.astype(F32))
    y = head_ln(y, LN_X_EPS).reshape(B, T, D_MODEL) * ln_w.astype(F32) + ln_b.astype(F32)
    bonus = jnp.sum(r4 * k4 * r_k.astype(F32), axis=-1, keepdims=True) * v4
    y = y + bonus.reshape(B, T, D_MODEL)
    out = (y.astype(h.dtype) * g) @ W_o
    return out, S, h[:, -1]


def trunk(x, c, m_C, m_n, m_m, m_conv, g_S, r_S, r_shift, p):
    B = x.shape[0]
    sc = jax.nn.silu(c)
    lbs = jnp.cumsum(jax.nn.softmax(p['g_lb'].astype(F32), axis=0), axis=0)
    outC, outn, outm, outconv, outS, outrS, outsh = [], [], [], [], [], [], []
    for l in range(DEPTH):
        mod = (sc @ p['mod_w'][l] + p['mod_b'][l]).reshape(B, N_MOD, D_MODEL)
        sh1, sc1, gt1, sh2, sc2, gt2 = (mod[:, i, None, :] for i in range(N_MOD))
        h = rmsnorm(x, p['norm_mix'][l]) * (1.0 + sc1) + sh1
        j = l // 2
        if l % 2 == 0:
            out, C, n, m, conv, S = ab_mixer(h, p['ab_w_in'][j], p['ab_gate_b'][j], p['m_conv_w'][j], p['m_norm'][j],
                                             lbs[j], p['g_norm'][j], p['ab_w_out'][j],
                                             m_C[j], m_n[j], m_m[j], m_conv[j], g_S[j])
            outC.append(C); outn.append(n); outm.append(m); outconv.append(conv); outS.append(S)
        else:
            out, S, sh = rwkv7_mixer(h, r_shift[j], r_S[j], p['r_mu'][j], p['r_w0'][j], p['r_w1'][j], p['r_w2'][j],
                                     p['r_a0'][j], p['r_a1'][j], p['r_a2'][j], p['r_g1'][j], p['r_g2'][j],
                                     p['r_kk'][j], p['r_ka'][j], p['r_rk'][j], p['r_wr'][j], p['r_wk'][j],
                                     p['r_wv'][j], p['r_wo'][j], p['r_lnw'][j], p['r_lnb'][j])
            outrS.append(S); outsh.append(sh)
        x = x + (gt1 * out).astype(x.dtype)
        h = rmsnorm(x, p['norm_ffn'][l]) * (1.0 + sc2) + sh2
        ff = jnp.square(jax.nn.relu(h @ p['ffn_w1'][l])) @ p['ffn_w2'][l]
        x = x + (gt2 * ff).astype(x.dtype)
    y = rmsnorm(x, p['final_norm'])
    return y, (jnp.stack(outC), jnp.stack(outn), jnp.stack(outm), jnp.stack(outconv),
               jnp.stack(outS), jnp.stack(outrS), jnp.stack(outsh))


def setup_inputs(seed: int = 0) -> dict:
    key = jax.random.key(seed)
    ks = iter(jax.random.split(key, 64))
    D = D_MODEL

    def nrm(shape, scale):
        return jax.random.normal(next(ks), shape, F32) * scale

    def uni(shape, lo, hi):
        return jax.random.uniform(next(ks), shape, F32, lo, hi)

    return {
        'x_prompt': nrm((BATCH, SEQ, D), 1.0),
        'x_sample': nrm((DEC_BATCH, DEC_SEQ, D), 1.0),
        'c_prompt': nrm((BATCH, D), 1.0),
        'c_sample': nrm((DEC_BATCH, D), 1.0),
        'state_mlstm_C': nrm((N_EVEN, DEC_BATCH, M_HEADS, M_HEAD_DIM, M_HEAD_DIM), 0.05),
        'state_mlstm_n': nrm((N_EVEN, DEC_BATCH, M_HEADS, M_HEAD_DIM), 0.5),
        'state_mlstm_m': uni((N_EVEN, DEC_BATCH, M_HEADS), 0.0, 3.0),
        'state_mlstm_conv': nrm((N_EVEN, DEC_BATCH, M_CONV - 1, 2 * M_WIDTH), 1.0),
        'state_hgrn_S': nrm((N_EVEN, DEC_BATCH, G_HEADS, G_HEAD_DIM, G_HEAD_DIM), 0.3),
        'state_rwkv_S': nrm((N_ODD, DEC_BATCH, R_HEADS, R_HEAD_DIM, R_HEAD_DIM), 0.3),
        'state_rwkv_shift': nrm((N_ODD, DEC_BATCH, D), 1.0),
        'mod_w': nrm((DEPTH, D, N_MOD * D), 0.5 * D ** -0.5),
        'mod_b': nrm((DEPTH, N_MOD * D), 0.02),
        'norm_mix': 1.0 + nrm((DEPTH, D), 0.02),
        'norm_ffn': 1.0 + nrm((DEPTH, D), 0.02),
        'ffn_w1': nrm((DEPTH, D, D_FF), D ** -0.5),
        'ffn_w2': nrm((DEPTH, D_FF, D), D_FF ** -0.5),
        'final_norm': 1.0 + nrm((D,), 0.02),
        'ab_w_in': nrm((N_EVEN, D, IN_WIDTH), D ** -0.5),
        'ab_gate_b': jnp.concatenate([uni((N_EVEN, M_HEADS), -3.0, -1.0), uni((N_EVEN, M_HEADS), 3.0, 6.0)], axis=-1),
        'm_conv_w': nrm((N_EVEN, M_CONV, 2 * M_WIDTH), M_CONV ** -0.5),
        'm_norm': 1.0 + nrm((N_EVEN, M_WIDTH), 0.02),
        'g_lb': nrm((N_EVEN + 1, G_WIDTH), 0.1) + jnp.arange(N_EVEN + 1, dtype=F32)[:, None],
        'g_norm': 1.0 + nrm((N_EVEN, G_WIDTH), 0.02),
        'ab_w_out': nrm((N_EVEN, MIX_WIDTH, D), MIX_WIDTH ** -0.5),
        'r_mu': uni((N_ODD, 6, D), 0.0, 1.0),
        'r_w0': uni((N_ODD, D), -5.0, 1.0),
        'r_w1': nrm((N_ODD, D, R_DECAY_LORA), D ** -0.5),
        'r_w2': nrm((N_ODD, R_DECAY_LORA, D), 0.5 * R_DECAY_LORA ** -0.5),
        'r_a0': nrm((N_ODD, D), 0.5),
        'r_a1': nrm((N_ODD, D, R_AAA_LORA), D ** -0.5),
        'r_a2': nrm((N_ODD, R_AAA_LORA, D), R_AAA_LORA ** -0.5),
        'r_g1': nrm((N_ODD, D, R_GATE_LORA), D ** -0.5),
        'r_g2': nrm((N_ODD, R_GATE_LORA, D), R_GATE_LORA ** -0.5),
        'r_kk': 0.85 + nrm((N_ODD, D), 0.02),
        'r_ka': 1.0 + nrm((N_ODD, D), 0.02),
        'r_rk': nrm((N_ODD, R_HEADS, R_HEAD_DIM), 0.1),
        'r_wr': nrm((N_ODD, D, D), D ** -0.5),
        'r_wk': nrm((N_ODD, D, D), D ** -0.5),
        'r_wv': nrm((N_ODD, D, D), D ** -0.5),
        'r_wo': nrm((N_ODD, D, D), D ** -0.5),
        'r_lnw': 1.0 + nrm((N_ODD, D), 0.02),
        'r_lnb': nrm((N_ODD, D), 0.02),
    }


def reference(x_prompt, x_sample, c_prompt, c_sample, state_mlstm_C, state_mlstm_n, state_mlstm_m,
              state_mlstm_conv, state_hgrn_S, state_rwkv_S, state_rwkv_shift, mod_w, mod_b, norm_mix,
              norm_ffn, ffn_w1, ffn_w2, final_norm, ab_w_in, ab_gate_b, m_conv_w, m_norm, g_lb, g_norm,
              ab_w_out, r_mu, r_w0, r_w1, r_w2, r_a0, r_a1, r_a2, r_g1, r_g2, r_kk, r_ka, r_rk, r_wr,
              r_wk, r_wv, r_wo, r_lnw, r_lnb):
    p = dict(mod_w=mod_w, mod_b=mod_b, norm_mix=norm_mix, norm_ffn=norm_ffn, ffn_w1=ffn_w1, ffn_w2=ffn_w2,
             final_norm=final_norm, ab_w_in=ab_w_in, ab_gate_b=ab_gate_b, m_conv_w=m_conv_w, m_norm=m_norm,
             g_lb=g_lb, g_norm=g_norm, ab_w_out=ab_w_out, r_mu=r_mu, r_w0=r_w0, r_w1=r_w1, r_w2=r_w2,
             r_a0=r_a0, r_a1=r_a1, r_a2=r_a2, r_g1=r_g1, r_g2=r_g2, r_kk=r_kk, r_ka=r_ka, r_rk=r_rk,
             r_wr=r_wr, r_wk=r_wk, r_wv=r_wv, r_wo=r_wo, r_lnw=r_lnw, r_lnb=r_lnb)
    B = x_prompt.shape[0]
    z_C = jnp.zeros((N_EVEN, B, M_HEADS, M_HEAD_DIM, M_HEAD_DIM), F32)
    z_n = jnp.zeros((N_EVEN, B, M_HEADS, M_HEAD_DIM), F32)
    z_m = jnp.zeros((N_EVEN, B, M_HEADS), F32)
    z_conv = jnp.zeros((N_EVEN, B, M_CONV - 1, 2 * M_WIDTH), x_prompt.dtype)
    z_S = jnp.zeros((N_EVEN, B, G_HEADS, G_HEAD_DIM, G_HEAD_DIM), F32)
    z_rS = jnp.zeros((N_ODD, B, R_HEADS, R_HEAD_DIM, R_HEAD_DIM), F32)
    z_sh = jnp.zeros((N_ODD, B, D_MODEL), x_prompt.dtype)
    y_prompt, (pC, pn, pm, pconv, pS, prS, psh) = trunk(x_prompt, c_prompt, z_C, z_n, z_m, z_conv, z_S, z_rS, z_sh, p)
    y_sample, (sC, sn, sm, sconv, sS, srS, ssh) = trunk(x_sample, c_sample, state_mlstm_C, state_mlstm_n,
                                                        state_mlstm_m, state_mlstm_conv, state_hgrn_S,
                                                        state_rwkv_S, state_rwkv_shift, p)
    return (y_prompt, y_sample, pC, pn, pm, pconv, pS, prS, psh, sC, sn, sm, sconv, sS, srS, ssh)
```

```python
import contextlib
import numpy as np
import concourse.bass as bass
import concourse.mybir as mybir
from concourse.bass_utils import run_bass_kernel_spmd

F32 = mybir.dt.float32
BF16 = mybir.dt.bfloat16
AF = mybir.ActivationFunctionType
ALU = mybir.AluOpType
AX = mybir.AxisListType

D = 2048
NB = 17
RMS_EPS = 1e-6
LN_X_EPS = 64e-5


class Res:
    __slots__ = ("name", "lw", "rd", "dsem")

    def __init__(self, name):
        self.name = name
        self.lw = None
        self.rd = {}
        self.dsem = {}


class Ctx:
    def __init__(self, nc):
        self.nc = nc
        self.eng = {"pe": nc.tensor, "act": nc.scalar, "dve": nc.vector, "pool": nc.gpsimd, "sp": nc.sync}
        self.sems = {}
        self.tot = {}
        self.isdma = {}
        self.seen = {e: {} for e in self.eng}
        for e in ("pe", "act", "dve", "pool"):
            self._newsem("E_" + e, False)
        self.free_dma = {"hw": [], "sw": []}
        self.ndma = 0

    def _newsem(self, key, isdma):
        self.sems[key] = self.nc.alloc_semaphore(name=key)
        self.tot[key] = 0
        self.isdma[key] = isdma
        return key

    def _dma_sem_for(self, res, q):
        kind = "sw" if q == "pool" else "hw"
        if kind not in res.dsem:
            if self.free_dma[kind]:
                res.dsem[kind] = self.free_dma[kind].pop()
            else:
                self.ndma += 1
                res.dsem[kind] = self._newsem("D%s%d" % (kind, self.ndma), True)
        return res.dsem[kind]

    def release(self, bufs):
        for b in bufs:
            r = b.r
            for kind, key in r.dsem.items():
                self.free_dma[kind].append(key)
            r.dsem = {}
            r.lw = None
            r.rd = {}

    def _need(self, e, deps):
        eng = self.eng[e]
        seen = self.seen[e]
        for key, val in deps:
            if self.isdma[key]:
                val = self.tot[key]
            elif key == "E_pe" and e == "pe":
                continue
            if seen.get(key, 0) >= val:
                continue
            eng.wait_ge(self.sems[key], val)
            seen[key] = val

    @staticmethod
    def _deps(reads, writes):
        deps = []
        for r in reads:
            if r.lw is not None:
                deps.append(r.lw)
        for w in writes:
            if w.lw is not None:
                deps.append(w.lw)
            deps.extend(w.rd.items())
        return deps

    @staticmethod
    def _commit(key, val, reads, writes):
        for w in writes:
            w.lw = (key, val)
            w.rd = {}
        for r in reads:
            if r in writes:
                continue
            if r.rd.get(key, 0) < val:
                r.rd[key] = val

    def op(self, e, fn, reads=(), writes=()):
        reads = [b.r for b in reads]
        writes = [b.r for b in writes]
        self._need(e, self._deps(reads, writes))
        inst = fn(self.eng[e])
        key = "E_" + e
        self.tot[key] += 1
        inst.then_inc(self.sems[key], 1)
        self._commit(key, self.tot[key], reads, writes)
        return inst

    def dma(self, q, out, in_, reads=(), writes=(), owner=None, **kw):
        reads = [b.r for b in reads]
        writes = [b.r for b in writes]
        self._need(q, self._deps(reads, writes))
        key = self._dma_sem_for(owner.r, q)
        inst = self.eng[q].dma_start(out=out, in_=in_, **kw)
        self.tot[key] += 16
        inst.then_inc(self.sems[key], 16)
        self._commit(key, self.tot[key], reads, writes)
        return inst

    def barrier(self):
        for e in self.eng:
            self._need(e, [(k, v) for k, v in self.tot.items() if v > 0])


class Buf:
    def __init__(self, t, name, shape):
        self.t = t
        self.r = Res(name)
        self.shape = list(shape)
        self.ps = int(np.prod(shape[1:]))

    def __getitem__(self, idx):
        return self.t[idx]

    def v(self, off, dims, p0=0, np_=128):
        return bass.AP(self.t, p0 * self.ps + off, [[self.ps, np_]] + [list(d) for d in dims])


class View:
    def __init__(self, parent, c0, n):
        self.parent = parent
        self.c0 = c0
        self.n = n
        self.r = parent.r

    def __getitem__(self, idx):
        if isinstance(idx, slice):
            assert idx == slice(None)
            return self.parent[:, self.c0:self.c0 + self.n]
        p, c = idx
        a = 0 if c.start is None else c.start
        b = self.n if c.stop is None else c.stop
        return self.parent[p, self.c0 + a:self.c0 + b]


def build_nc(NTP):
    NT = NTP + 1
    NP = NTP * 128
    NTOK = NT * 128
    nc = bass.Bass("TRN2", target_bir_lowering=False)
    cx = Ctx(nc)
    DT = {}

    def din(name, shape, dt=F32):
        b = Buf(nc.dram_tensor(name, list(shape), dt, kind="ExternalInput"), name, shape)
        DT[name] = b
        return b

    def dout(name, shape):
        b = Buf(nc.dram_tensor(name, list(shape), F32, kind="ExternalOutput"), name, shape)
        DT[name] = b
        return b

    def dscr(name, shape, dt=F32):
        return Buf(nc.dram_tensor(name, list(shape), dt), name, shape)

    xin = din("xin", [NTOK, D]); cin = din("cin", [NB, D])
    mC = din("mC", [NB, 4, 256, 256]); mn = din("mn", [NB, 4, 256]); mm = din("mm", [NB, 4])
    mconv = din("mconv", [NB, 3, D]); gS = din("gS", [NB, 8, 128, 128]); rS = din("rS", [NB, 32, 64, 64])
    rsh = din("rsh", [NB, D])
    W = {}
    for name, shape in [("mod_w", [2, D, 6 * D]), ("mod_b", [2, 6 * D]), ("norm_mix", [2, D]), ("norm_ffn", [2, D]),
                        ("ffn_w1", [2, D, 4 * D]), ("ffn_w2", [2, 4 * D, D]), ("final_norm", [D]),
                        ("ab_w_in", [1, D, 8200]), ("ab_gate_b", [1, 8]), ("m_conv_w", [1, 4, D]), ("m_norm", [1, 1024]),
                        ("g_lb", [2, 1024]), ("g_norm", [1, 1024]), ("ab_w_out", [1, D, D]),
                        ("r_mu", [1, 6, D]), ("r_w0", [1, D]), ("r_w1", [1, D, 96]), ("r_w2", [1, 96, D]),
                        ("r_a0", [1, D]), ("r_a1", [1, D, 96]), ("r_a2", [1, 96, D]), ("r_g1", [1, D, 256]),
                        ("r_g2", [1, 256, D]), ("r_kk", [1, D]), ("r_ka", [1, D]), ("r_rk", [1, D]),
                        ("r_wr", [1, D, D]), ("r_wk", [1, D, D]), ("r_wv", [1, D, D]), ("r_wo", [1, D, D]),
                        ("r_lnw", [1, D]), ("r_lnb", [1, D])]:
        W[name] = din(name, shape)
    CN = {}
    for name, shape in [("ident", [128, 128]), ("ones", [128, 128]), ("mle_p", [128, 128]), ("mle_s", [128, 128]),
                        ("mlt_p", [128, 128]), ("mlt_s", [128, 128]), ("mgt_p", [128, 128]), ("mgt_s", [128, 128]),
                        ("neg_p", [128, 128]), ("neg_s", [128, 128]), ("trir_p", [128, 128]),
                        ("ref_p", [128, 34]), ("ref_s", [128, 34]), ("blkt_p", [NB, 128]), ("blkt_s", [NB, 128]),
                        ("selend_p", [128, NB]), ("selend_s", [128, NB]), ("blkrow", [128, 16 * 128]),
                        ("rle_p", [128, 128]), ("rlt_p", [128, 128]), ("rgt_p", [128, 128]), ("rblk_p", [128, NB]),
                        ("rblkrow_p", [128, 4 * 128])]:
        CN[name] = din("c_" + name, shape)
    y = dout("y", [NTOK, D])
    oC = dout("oC", [NB, 4, 256, 256]); on = dout("on", [NB, 4, 256]); om = dout("om", [NB, 4])
    oconv = dout("oconv", [NB, 3, D]); oS = dout("oS", [NB, 8, 128, 128]); orS = dout("orS", [NB, 32, 64, 64])
    osh = dout("osh", [NB, D])
    mod_d = dscr("mod_d", [2, NB, 6 * D])
    ext_p = dscr("ext_p", [NP + 3, D]); ext_s = dscr("ext_s", [16, 11, D])
    z_d = dscr("z_d", [NTOK, 6152])
    x1_d = dscr("x1_d", [NTOK, D]); x2_d = dscr("x2_d", [NTOK, D]); x3_d = dscr("x3_d", [NTOK, D]); x4_d = dscr("x4_d", [NTOK, D])
    h_d = dscr("h_d", [NTOK, D])
    rz_d = dscr("rz_d", [NTOK, 6 * D])
    dummy = Buf(None, "dummy", [1, 1])

    stacks = [contextlib.ExitStack()]

    uid = [0]

    stage_bufs = [[]]

    def sb(name, shape, dt=F32):
        uid[0] += 1
        name = "%s_%d" % (name, uid[0])
        b = Buf(stacks[-1].enter_context(nc.sbuf_tensor(name, list(shape), dt)), name, shape)
        stage_bufs[-1].append(b)
        return b

    def staged(fn):
        def wrapper(*a, **k):
            stacks.append(contextlib.ExitStack())
            stage_bufs.append([])
            try:
                return fn(*a, **k)
            finally:
                cx.barrier()
                cx.release(stage_bufs.pop())
                stacks.pop().close()
                npsmod[0] = 8
        return wrapper

    ident = sb("ident", [128, 128]); ones = sb("ones", [128, 128])
    identb = sb("identb", [128, 128], BF16)
    PS = [Buf(nc.alloc_psum_tensor("ps%d" % i, [128, 512], F32), "ps%d" % i, [128, 512]) for i in range(8)]
    psi = [0]

    def nps():
        p = PS[psi[0] % npsmod[0]]
        psi[0] += 1
        return p

    npsmod = [8]

    acci = [0]

    def accps():
        p = PS[6 + acci[0] % 2]
        acci[0] += 1
        return p

    cx.dma("sp", ident[:], CN["ident"][:], writes=[ident], owner=ident)
    cx.dma("sp", ones[:], CN["ones"][:], writes=[ones], owner=ones)
    cx.op("dve", lambda e: e.tensor_copy(identb[:], ident[:]), reads=[ident], writes=[identb])

    evq = [0]

    def evac(out_ap, in_ap, reads, writes, scale=None):
        evq[0] += 1
        if evq[0] % 2 == 0:
            if scale is None:
                cx.op("act", lambda e: e.activation(out_ap, in_ap, AF.Copy), reads=reads, writes=writes)
            else:
                cx.op("act", lambda e: e.activation(out_ap, in_ap, AF.Copy, scale=float(scale)), reads=reads, writes=writes)
        else:
            if scale is None:
                cx.op("dve", lambda e: e.tensor_copy(out_ap, in_ap), reads=reads, writes=writes)
            else:
                cx.op("dve", lambda e: e.tensor_scalar(out_ap, in_ap, float(scale), None, ALU.mult), reads=reads, writes=writes)

    def transpose_to(dst, dst_ap_fn, src, src_ap_fn, nchunks, scale_fn=None, npart=128):
        for c0 in range(0, nchunks, 4):
            n = min(4, nchunks - c0)
            p = nps()
            for j in range(n):
                cx.op("pe", lambda e, j=j: e.transpose(p.v(j * 128, [[1, npart]]),
                                                      src_ap_fn(c0 + j), ident[0:npart, 0:npart]),
                      reads=[src, ident], writes=[p])
            for j in range(n):
                sc = None if scale_fn is None else scale_fn(c0 + j)
                evac(dst_ap_fn(c0 + j), p.v(j * 128, [[1, npart]]), [p], [dst], scale=sc)

    def rows_bcast(dst, src_dram_rows_fn, tile_is_sample, width, col0=0, q="sp"):
        if not tile_is_sample:
            cx.dma(q, dst[:, col0:col0 + width], src_dram_rows_fn(0).partition_broadcast(128), writes=[dst], owner=dst)
        else:
            for b in range(16):
                cx.dma(q, dst[8 * b:8 * b + 8, col0:col0 + width], src_dram_rows_fn(b + 1).partition_broadcast(8), writes=[dst], owner=dst)

    @staged
    def stage_mod():
        csb = sb("csb", [NB, D]); scT = sb("scT", [128, 16, NB], BF16)
        modsb = sb("modsb", [NB, 6 * D]); biasb = sb("biasb", [NB, 6 * D])
        wb = [sb("modw%d" % i, [128, 16, 512], BF16) for i in range(2)]
        cx.dma("sp", csb[:], cin[:], writes=[csb], owner=csb)
        cx.op("act", lambda e: e.activation(csb[:], csb[:], AF.Silu), reads=[csb], writes=[csb])
        for c0 in range(0, 16, 4):
            p = nps()
            for j in range(4):
                c = c0 + j
                cx.op("pe", lambda e, j=j, c=c: e.transpose(p.v(j * 32, [[1, NB]]), csb[:, c * 128:(c + 1) * 128], ident[0:NB, 0:NB]),
                      reads=[csb, ident], writes=[p])
            for j in range(4):
                evac(scT[:, c0 + j, :], p.v(j * 32, [[1, NB]]), [p], [scT])
        for l in range(2):
            cx.dma("sp", biasb[:], W["mod_b"][l].partition_broadcast(NB), writes=[biasb], owner=biasb)
            for cb in range(24):
                w = wb[cb % 2]
                cx.dma("pool", w[:], W["mod_w"][l, :, cb * 512:(cb + 1) * 512].rearrange("(kc p) n -> p kc n", p=128),
                       writes=[w], owner=w)
                p = nps()
                for kc in range(16):
                    cx.op("pe", lambda e, kc=kc: e.matmul(p[0:NB, :], scT[:, kc, :], w[:, kc, :], start=(kc == 0), stop=(kc == 15)),
                          reads=[scT, w], writes=[p])
                cx.op("dve", lambda e: e.tensor_tensor(modsb[:, cb * 512:(cb + 1) * 512], p[0:NB, :], biasb[:, cb * 512:(cb + 1) * 512], ALU.add),
                      reads=[p, biasb], writes=[modsb])
            cx.dma("sp", mod_d[l], modsb[:], reads=[modsb], writes=[mod_d], owner=modsb)
        cx.barrier()
        cx.release([csb, scT, modsb, biasb] + wb)
        return [csb, scT, modsb, biasb] + wb

    @staged
    def stage_norm(x_d, l, normw_ap, ish, isc, hT, h_dram=None, shift_out=None):
        G = [sb("nG%d" % i, [128, D]) for i in range(2)]; SH = [sb("nSH%d" % i, [128, D]) for i in range(2)]
        nw = sb("nnw", [128, D])
        xt = [sb("nxt%d" % i, [128, D]) for i in range(2)]; ht = [sb("nht%d" % i, [128, D]) for i in range(2)]
        junk = sb("njunk", [128, D]); ss = sb("nss", [128, 2])
        cx.dma("sp", nw[:], normw_ap.partition_broadcast(128), writes=[nw], owner=nw)
        for s in range(2):
            rows_bcast(G[s], lambda b: mod_d[l, b, isc * D:(isc + 1) * D], s == 1, D)
            rows_bcast(SH[s], lambda b: mod_d[l, b, ish * D:(ish + 1) * D], s == 1, D)
            cx.op("dve", lambda e, s=s: e.scalar_tensor_tensor(G[s][:], G[s][:], 1.0, nw[:], ALU.add, ALU.mult), reads=[G[s], nw], writes=[G[s]])
        for i in range(NT):
            s = 1 if i == NTP else 0
            x = xt[i % 2]; h = ht[i % 2]
            cx.dma("sp", x[:], x_d[i * 128:(i + 1) * 128, :], reads=[x_d], writes=[x], owner=x)
            cx.op("act", lambda e: e.activation(junk[:], x[:], AF.Square, accum_out=ss[:, 0:1]), reads=[x], writes=[junk, ss])
            cx.op("dve", lambda e: e.tensor_scalar(ss[:, 1:2], ss[:, 0:1], 1.0 / D, RMS_EPS, ALU.mult, ALU.add), reads=[ss], writes=[ss])
            cx.op("act", lambda e: e.activation(ss[:, 1:2], ss[:, 1:2], AF.Sqrt), reads=[ss], writes=[ss])
            cx.op("dve", lambda e: e.reciprocal(ss[:, 1:2], ss[:, 1:2]), reads=[ss], writes=[ss])
            cx.op("dve", lambda e: e.scalar_tensor_tensor(h[:], x[:], ss[:, 1:2], G[s][:], ALU.mult, ALU.mult), reads=[x, ss, G[s]], writes=[h])
            if SH is not None:
                cx.op("dve", lambda e: e.tensor_tensor(h[:], h[:], SH[s][:], ALU.add), reads=[h, SH[s]], writes=[h])
            if h_dram is not None:
                cx.dma("sp", h_dram[i * 128:(i + 1) * 128, :], h[:], reads=[h], writes=[h_dram], owner=h)
            if shift_out is not None:
                if s == 0 and i == NTP - 1:
                    cx.dma("sp", shift_out[0:1, :], h[127:128, :], reads=[h], writes=[shift_out], owner=h)
                if s == 1:
                    for b in range(16):
                        cx.dma("sp", shift_out[b + 1:b + 2, :], h[8 * b + 7:8 * b + 8, :], reads=[h], writes=[shift_out], owner=h)
            transpose_to(hT, lambda c: hT[:, c, i * 128:(i + 1) * 128], h, lambda c: h[:, c * 128:(c + 1) * 128], 16)
        cx.barrier()
        tmp = G + SH + [nw, junk, ss] + xt + ht
        cx.release(tmp)

    @staged
    def proj_tok(aT, blocks, K=16, kpart=128, bias_fn=None, sigmoid=False):
        wb = [sb("pw%d" % i, [128, K, 512], BF16) for i in range(2)]
        ob = [sb("po%d" % i, [128, 512]) for i in range(3)]
        bb_ = [sb("pb%d" % i, [128, 512]) for i in range(2)] if bias_fn is not None else None
        oi = 0
        for bi, (w_ap, dst_fn) in enumerate(blocks):
            n = w_ap.shape[-1]
            w = wb[bi % 2]
            wv = w_ap.rearrange("(kc p) n -> p kc n", p=kpart)
            for k0 in range(0, K, 4):
                k1 = min(K, k0 + 4)
                cx.dma("pool", w[0:kpart, k0:k1, 0:n], wv[:, k0:k1, :], writes=[w], owner=w)
            if bias_fn is not None:
                bt = bb_[bi % 2]
                cx.dma("sp", bt[:, 0:n], bias_fn(bi).partition_broadcast(128), writes=[bt], owner=bt)
            for i in range(NT):
                p = nps()
                for kc in range(K):
                    cx.op("pe", lambda e, kc=kc: e.matmul(p[:, 0:n], aT[0:kpart, kc, i * 128:(i + 1) * 128], w[0:kpart, kc, 0:n],
                                                          start=(kc == 0), stop=(kc == K - 1)), reads=[aT, w], writes=[p])
                o = ob[oi % 3]; oi += 1
                if bias_fn is None:
                    cx.op("dve", lambda e: e.tensor_copy(o[:, 0:n], p[:, 0:n]), reads=[p], writes=[o])
                else:
                    cx.op("dve", lambda e: e.tensor_tensor(o[:, 0:n], p[:, 0:n], bt[:, 0:n], ALU.add), reads=[p, bt], writes=[o])
                    if sigmoid:
                        cx.op("act", lambda e: e.activation(o[:, 0:n], o[:, 0:n], AF.Sigmoid), reads=[o], writes=[o])
                for (dbuf, dap, p0, p1) in dst_fn(i):
                    if p0 == "3d":
                        cx.dma("act", dap, o[:, 0:n], reads=[o], writes=[dbuf], owner=o)
                    else:
                        cx.dma("act", dap, o[p0:p1, 0:n], reads=[o], writes=[dbuf], owner=o)

    @staged
    def proj_resid(aT, w_fn, x_old, x_new, l, igate, K=16):
        wb = [sb("rw%d" % i, [128, K, 512], BF16) for i in range(2)]
        xb = [sb("rx%d" % i, [128, 512]) for i in range(3)]
        GT = [sb("rgt%d" % i, [128, D]) for i in range(2)]
        for s in range(2):
            rows_bcast(GT[s], lambda b: mod_d[l, b, igate * D:(igate + 1) * D], s == 1, D)
        oi = 0
        for cb in range(4):
            w = wb[cb % 2]
            cx.dma("pool", w[:], w_fn(cb * 512, 512).rearrange("(kc p) n -> p kc n", p=128), writes=[w], owner=w)
            for i in range(NT):
                s = 1 if i == NTP else 0
                xo = xb[oi % 3]; oi += 1
                cx.dma("sp", xo[:], x_old[i * 128:(i + 1) * 128, cb * 512:(cb + 1) * 512], reads=[x_old], writes=[xo], owner=xo)
                p = nps()
                for kc in range(K):
                    cx.op("pe", lambda e, kc=kc: e.matmul(p[:], aT[:, kc, i * 128:(i + 1) * 128], w[:, kc, :], start=(kc == 0), stop=(kc == K - 1)),
                          reads=[aT, w], writes=[p])
                t = sbtmp512[oi % 2]
                cx.op("dve", lambda e: e.tensor_tensor(t[:], p[:], GT[s][:, cb * 512:(cb + 1) * 512], ALU.mult), reads=[p, GT[s]], writes=[t])
                cx.op("dve", lambda e: e.tensor_tensor(xo[:], xo[:], t[:], ALU.add), reads=[xo, t], writes=[xo])
                cx.dma("act", x_new[i * 128:(i + 1) * 128, cb * 512:(cb + 1) * 512], xo[:], reads=[xo], writes=[x_new], owner=xo)
        cx.barrier()
        cx.release(wb + xb + GT)

    sbtmp512 = [sb("tmp512_%d" % i, [128, 512]) for i in range(2)]

    @staged
    def stage_ffn(hT, l, x_old, x_new):
        FB = 2048
        nfb = 4 * D // FB
        KC = FB // 128
        hidT = sb("hidT", [128, KC, NTOK], BF16)
        w1b = [sb("fw1_%d" % i, [128, 16, 256], BF16) for i in range(2)]
        w2b = [sb("fw2_%d" % i, [128, KC, 512], BF16) for i in range(2)]
        xb = [sb("fx%d" % i, [128, 512]) for i in range(4)]
        GTs = [[sb("fgt%d_%d" % (i, j), [128, 512]) for j in range(2)] for i in range(2)]
        groups = [(g, min(512, NTOK - g)) for g in range(0, NTOK, 512)]
        w1i = 0; w2i = 0; oi = 0
        for fb in range(nfb):
            for blk in range(FB // 256):
                w = w1b[w1i % 2]; w1i += 1
                c0 = fb * FB + blk * 256
                wv = W["ffn_w1"][l, :, c0:c0 + 256].rearrange("(kc p) n -> p kc n", p=128)
                for k0 in range(0, 16, 8):
                    cx.dma("pool", w[:, k0:k0 + 8, :], wv[:, k0:k0 + 8, :], writes=[w], owner=w)
                for oc in range(2):
                    for (g0, gn) in groups:
                        p = nps()
                        for kc in range(16):
                            cx.op("pe", lambda e, kc=kc: e.matmul(p[:, 0:gn], w[:, kc, oc * 128:(oc + 1) * 128], hT[:, kc, g0:g0 + gn],
                                                                  start=(kc == 0), stop=(kc == 15)), reads=[hT, w], writes=[p])
                        t = sbtmp512[oi % 2]; oi += 1
                        cx.op("act", lambda e: e.activation(t[:, 0:gn], p[:, 0:gn], AF.Relu), reads=[p], writes=[t])
                        cx.op("dve", lambda e: e.tensor_tensor(hidT[:, blk * 2 + oc, g0:g0 + gn], t[:, 0:gn], t[:, 0:gn], ALU.mult),
                              reads=[t], writes=[hidT])
            for cb in range(4):
                w = w2b[w2i % 2]; w2i += 1
                wv = W["ffn_w2"][l, fb * FB:(fb + 1) * FB, cb * 512:(cb + 1) * 512].rearrange("(kc p) n -> p kc n", p=128)
                for k0 in range(0, KC, 4):
                    cx.dma("pool", w[:, k0:k0 + 4, :], wv[:, k0:k0 + 4, :], writes=[w], owner=w)
                GT = GTs[(fb * 4 + cb) % 2]
                for s_ in range(2):
                    rows_bcast(GT[s_], lambda b: mod_d[l, b, 5 * D + cb * 512:5 * D + (cb + 1) * 512], s_ == 1, 512)
                for i in range(NT):
                    s_ = 1 if i == NTP else 0
                    xo = xb[oi % 4]; oi += 1
                    src = x_old if fb == 0 else x_new
                    cx.dma("sp", xo[:], src[i * 128:(i + 1) * 128, cb * 512:(cb + 1) * 512], reads=[src], writes=[xo], owner=xo)
                    p = nps()
                    for kc in range(KC):
                        cx.op("pe", lambda e, kc=kc: e.matmul(p[:], hidT[:, kc, i * 128:(i + 1) * 128], w[:, kc, :], start=(kc == 0), stop=(kc == KC - 1)),
                              reads=[hidT, w], writes=[p])
                    t = sbtmp512[oi % 2]
                    cx.op("dve", lambda e: e.tensor_tensor(t[:], p[:], GT[s_][:], ALU.mult), reads=[p, GT[s_]], writes=[t])
                    cx.op("dve", lambda e: e.tensor_tensor(xo[:], xo[:], t[:], ALU.add), reads=[xo, t], writes=[xo])
                    cx.dma("act", x_new[i * 128:(i + 1) * 128, cb * 512:(cb + 1) * 512], xo[:], reads=[xo], writes=[x_new], owner=xo)

    @staged
    def stage_final(x_d):
        nw = sb("fnw", [128, D]); xt = [sb("fnx%d" % i, [128, D]) for i in range(2)]
        junk = sb("fnj", [128, D]); ss = sb("fns", [128, 2])
        cx.dma("sp", nw[:], W["final_norm"][:].partition_broadcast(128), writes=[nw], owner=nw)
        for i in range(NT):
            x = xt[i % 2]
            cx.dma("sp", x[:], x_d[i * 128:(i + 1) * 128, :], reads=[x_d], writes=[x], owner=x)
            cx.op("act", lambda e: e.activation(junk[:], x[:], AF.Square, accum_out=ss[:, 0:1]), reads=[x], writes=[junk, ss])
            cx.op("dve", lambda e: e.tensor_scalar(ss[:, 1:2], ss[:, 0:1], 1.0 / D, RMS_EPS, ALU.mult, ALU.add), reads=[ss], writes=[ss])
            cx.op("act", lambda e: e.activation(ss[:, 1:2], ss[:, 1:2], AF.Sqrt), reads=[ss], writes=[ss])
            cx.op("dve", lambda e: e.reciprocal(ss[:, 1:2], ss[:, 1:2]), reads=[ss], writes=[ss])
            cx.op("dve", lambda e: e.scalar_tensor_tensor(x[:], x[:], ss[:, 1:2], nw[:], ALU.mult, ALU.mult), reads=[x, ss, nw], writes=[x])
            cx.dma("sp", y[i * 128:(i + 1) * 128, :], x[:], reads=[x], writes=[y], owner=x)
        cx.barrier()
        cx.release([nw, junk, ss] + xt)

    def stage_l0_proj(hT):
        Wi = W["ab_w_in"][0]
        cx.dma("sp", ext_p[0:3, :], mconv[0], reads=[mconv], writes=[ext_p], owner=ext_p)
        cx.dma("sp", ext_s[:, 0:3, :], mconv[1:17], reads=[mconv], writes=[ext_s], owner=ext_s)
        blocks = []

        def dst_ext(c0, n):
            def f(i):
                if i < NTP:
                    return [(ext_p, ext_p[3 + i * 128:3 + (i + 1) * 128, c0:c0 + n], 0, 128)]
                return [(ext_s, ext_s[:, 3:11, c0:c0 + n], "3d", n)]
            return f

        def dst_z(c0, n):
            return lambda i: [(z_d, z_d[i * 128:(i + 1) * 128, c0:c0 + n], 0, 128)]
        for c0 in range(0, 2048, 512):
            blocks.append((Wi[:, c0:c0 + 512], dst_ext(c0, 512)))
        for c0 in range(0, 2048, 512):
            blocks.append((Wi[:, 2048 + c0:2048 + c0 + 512], dst_z(c0, 512)))
        blocks.append((Wi[:, 4096:4104], dst_z(2048, 8)))
        for c0 in range(0, 4096, 512):
            blocks.append((Wi[:, 4104 + c0:4104 + c0 + 512], dst_z(2056 + c0, 512)))
        proj_tok(hT, blocks)

    def mmg(p_ap, pairs, reads, pbuf):
        n = len(pairs)
        for idx, (l_ap, r_ap) in enumerate(pairs):
            cx.op("pe", lambda e, l_ap=l_ap, r_ap=r_ap, idx=idx: e.matmul(p_ap, l_ap, r_ap, start=(idx == 0), stop=(idx == n - 1)),
                  reads=reads, writes=[pbuf])

    def rstd_col(dst, col, src_ap, inv_n, eps):
        cx.op("dve", lambda e: e.tensor_scalar(dst[:, col:col + 1], src_ap, float(inv_n), float(eps), ALU.mult, ALU.add), reads=[dst], writes=[dst])
        cx.op("act", lambda e: e.activation(dst[:, col:col + 1], dst[:, col:col + 1], AF.Sqrt), reads=[dst], writes=[dst])
        cx.op("dve", lambda e: e.reciprocal(dst[:, col:col + 1], dst[:, col:col + 1]), reads=[dst], writes=[dst])

    @staged
    def stage_l0_rec(mixT):
        npsmod[0] = 4
        dv = lambda fn, r, w: cx.op("dve", fn, reads=r, writes=w)
        ac = lambda fn, r, w: cx.op("act", fn, reads=r, writes=w)
        mle = [sb("mle%d" % i, [128, 128]) for i in range(2)]; neg = [sb("neg%d" % i, [128, 128]) for i in range(2)]
        trirp = sb("trirp", [128, 128]); ref = [sb("ref%d" % i, [128, 34]) for i in range(2)]
        blkt = [sb("blkt%d" % i, [NB, 128]) for i in range(2)]; selend = [sb("selend%d" % i, [128, NB]) for i in range(2)]
        blkrow = sb("blkrow", [128, 16, 128], BF16)
        for i, sfx in enumerate(["p", "s"]):
            cx.dma("sp", mle[i][:], CN["mle_" + sfx][:], writes=[mle[i]], owner=mle[i])
            cx.dma("sp", neg[i][:], CN["neg_" + sfx][:], writes=[neg[i]], owner=neg[i])
            cx.dma("sp", ref[i][:], CN["ref_" + sfx][:], writes=[ref[i]], owner=ref[i])
            cx.dma("sp", blkt[i][:], CN["blkt_" + sfx][:], writes=[blkt[i]], owner=blkt[i])
            cx.dma("sp", selend[i][:], CN["selend_" + sfx][:], writes=[selend[i]], owner=selend[i])
        cx.dma("sp", trirp[:], CN["trir_p"][:], writes=[trirp], owner=trirp)
        cx.dma("pool", blkrow[:], CN["blkrow"][:].rearrange("p (b t) -> p b t", b=16), writes=[blkrow], owner=blkrow)
        trir = [trirp, mle[1]]
        gb = sb("gb", [128, 8]); mgain = sb("mgain", [128, 1024]); ggain = sb("ggain", [128, 1024])
        LB = sb("LB", [128, 1024]); OMLB = sb("OMLB", [128, 1024])
        cx.dma("sp", gb[:], W["ab_gate_b"][0].partition_broadcast(128), writes=[gb], owner=gb)
        cx.dma("sp", mgain[:], W["m_norm"][0].partition_broadcast(128), writes=[mgain], owner=mgain)
        cx.dma("sp", ggain[:], W["g_norm"][0].partition_broadcast(128), writes=[ggain], owner=ggain)
        cx.dma("sp", LB[:], W["g_lb"][0].partition_broadcast(128), writes=[LB], owner=LB)
        cx.dma("sp", OMLB[:], W["g_lb"][1].partition_broadcast(128), writes=[OMLB], owner=OMLB)
        dv(lambda e: e.tensor_tensor(LB[:], LB[:], OMLB[:], ALU.subtract), [LB, OMLB], [LB])
        ac(lambda e: e.activation(LB[:], LB[:], AF.Sigmoid), [LB], [LB])
        dv(lambda e: e.tensor_scalar(OMLB[:], LB[:], -1.0, 1.0, ALU.mult, ALU.add), [LB], [OMLB])
        mst = sb("mst", [NB, 4]); mend = sb("mend", [NB, 4])
        cx.dma("sp", mst[:], mm[:], writes=[mst], owner=mst)
        Cst = sb("Cst", [128, 4, 2, 257]); Cb = sb("Cb", [128, 4, 2, 257], BF16)
        Sst = sb("Sst", [128, 8, 128]); Smid = sb("Smid", [128, 8, 128]); Smidb = sb("Smidb", [128, 8, 128], BF16)
        for h in range(4):
            for c in range(2):
                cx.dma("sp", Cst[:, h, c, 0:256], mC[0, h, c * 128:(c + 1) * 128, :], writes=[Cst], owner=Cst)
                cx.dma("sp", Cst[:, h, c, 256:257], mn[0, h, c * 128:(c + 1) * 128].rearrange("(p o) -> p o", o=1), writes=[Cst], owner=Cst)
        cx.dma("sp", Sst[:], gS[0].rearrange("h k v -> k h v"), writes=[Sst], owner=Sst)
        dv(lambda e: e.tensor_copy(Cb[:], Cst[:]), [Cst], [Cb])
        NSB = 4
        Cs = [sb("Cs%d" % i, [128, 2, 257]) for i in range(NSB)]; Csb = [sb("Csb%d" % i, [128, 2, 257], BF16) for i in range(NSB)]
        Ss = [sb("Ss%d" % i, [128, 128]) for i in range(NSB)]; Ssb = [sb("Ssb%d" % i, [128, 128], BF16) for i in range(NSB)]
        gsm = sb("gsm", [128, 64]); diag4 = sb("diag4", [128, 4, 128]); dtmp = sb("dtmp", [128, 4, 128])
        Rt = sb("Rt", [128, NB, 4]); bend = sb("bend", [128, NB * 4])
        CQ = 256
        taps = [sb("tap%d" % j, [128, CQ]) for j in range(4)]; cw = [sb("cw%d" % j, [128, CQ]) for j in range(4)]
        qk = sb("qk", [128, 2048]); qT = sb("qT", [128, 8, 128], BF16); kT = sb("kT", [128, 8, 128], BF16)
        khat = sb("khat", [128, 1024], BF16); khm = sb("khm", [128, 1024], BF16)
        zmv = sb("zmv", [128, 2048]); Vp = sb("Vp", [128, 4, 257], BF16); MG = sb("MG", [128, 1024])
        sTs = sb("sTs", [128, 128], BF16); qTm = sb("qTm", [128, 1, 128], BF16)
        hm = sb("hm", [128, 256]); hj = sb("hj", [128, 256]); st = sb("st", [128, 8])
        A = [View(qk, 0, 1024), View(qk, 1024, 1024)] + [sb("A%d" % i, [128, 1024]) for i in range(2, 6)]
        ktb = sb("ktb", [128, 1024], BF16); vb = sb("vb", [128, 1024], BF16)
        qtT = sb("qtT", [128, 8, 128], BF16); ktT = sb("ktT", [128, 8, 128], BF16)
        dec = sb("dec", [128, 8, 34]); ATs = sb("ATs", [128, 128], BF16)
        mixed = zmv
        cx.op("pool", lambda e: e.memset(Vp[:], 1.0), writes=[Vp])
        IG, LF, BB, GG_, CMX, MP, CM, AL, BE, MT, FL, T1, T2, AL16 = [slice(4 * k, 4 * k + 4) for k in range(14)]
        ssi = [0]

        for i in range(NT):
            s = 1 if i == NTP else 0
            r0 = i * 128
            cx.dma("sp", gsm[:, 0:8], z_d[r0:r0 + 128, 2048:2056], reads=[z_d], writes=[gsm], owner=gsm)
            dv(lambda e: e.tensor_tensor(gsm[:, 0:8], gsm[:, 0:8], gb[:], ALU.add), [gsm, gb], [gsm])
            ac(lambda e: e.activation(gsm[:, T1], gsm[:, LF], AF.Exp, scale=-1.0), [gsm], [gsm])
            ac(lambda e: e.activation(gsm[:, T1], gsm[:, T1], AF.Ln, bias=1.0), [gsm], [gsm])
            dv(lambda e: e.tensor_scalar_mul(gsm[:, LF], gsm[:, T1], -1.0), [gsm], [gsm])
            p = nps()
            mmg(p[:, 0:4], [(trir[1][:] if s else mle[0][:], gsm[:, LF])], [mle[s], gsm], p)
            dv(lambda e: e.tensor_copy(gsm[:, BB], p[:, 0:4]), [p], [gsm])
            dv(lambda e: e.tensor_tensor(gsm[:, GG_], gsm[:, IG], gsm[:, BB], ALU.subtract), [gsm], [gsm])
            for h in range(4):
                dv(lambda e, h=h: e.tensor_scalar_mul(diag4[:, h, :], ident[:], gsm[:, 12 + h:13 + h]), [ident, gsm], [diag4])
            p = nps()
            for h in range(4):
                mmg(p[:, h * 128:(h + 1) * 128], [(ones[:], diag4[:, h, :])], [ones, diag4], p)
            dv(lambda e: e.tensor_tensor(dtmp[:], p[:].rearrange("p (a b) -> p a b", a=4), neg[s].v(0, [[0, 4], [1, 128]]), ALU.add), [p, neg[s]], [dtmp])
            dv(lambda e: e.tensor_reduce(gsm[:, CMX], dtmp[:], AX.X, ALU.max), [dtmp], [gsm])
            p = nps()
            mmg(p[:, 0:4], [(blkt[s][:], mst[:])], [blkt[s], mst], p)
            dv(lambda e: e.tensor_copy(gsm[:, MP], p[:, 0:4]), [p], [gsm])
            dv(lambda e: e.tensor_tensor(gsm[:, CM], gsm[:, CMX], gsm[:, MP], ALU.max), [gsm], [gsm])
            dv(lambda e: e.tensor_tensor(gsm[:, T1], gsm[:, GG_], gsm[:, MP], ALU.subtract), [gsm], [gsm])
            ac(lambda e: e.activation(gsm[:, AL], gsm[:, T1], AF.Exp), [gsm], [gsm])
            dv(lambda e: e.tensor_tensor(gsm[:, T1], gsm[:, MP], gsm[:, CM], ALU.subtract), [gsm], [gsm])
            ac(lambda e: e.activation(gsm[:, BE], gsm[:, T1], AF.Exp), [gsm], [gsm])
            dv(lambda e: e.tensor_tensor(gsm[:, MT], gsm[:, BB], gsm[:, CM], ALU.add), [gsm], [gsm])
            ac(lambda e: e.activation(gsm[:, FL], gsm[:, MT], AF.Exp, scale=-1.0), [gsm], [gsm])
            dv(lambda e: e.tensor_scalar_mul(gsm[:, AL16], gsm[:, AL], 0.0625), [gsm], [gsm])
            p = nps()
            mmg(p[0:NB, 0:4], [(selend[s][:], gsm[:, MT])], [selend[s], gsm], p)
            dv(lambda e: e.tensor_copy(mend[:], p[0:NB, 0:4]), [p], [mend])
            if s == 0:
                dv(lambda e: e.tensor_copy(mst[0:1, :], mend[0:1, :]), [mend], [mst])
                if i == NTP - 1:
                    cx.dma("act", om[0:1, :], mend[0:1, :], reads=[mend], writes=[om], owner=mend)
            else:
                cx.dma("act", om[1:NB, :], mend[1:NB, :], reads=[mend], writes=[om], owner=mend)
            dv(lambda e: e.tensor_tensor(Rt[:], selend[s].v(0, [[1, NB], [0, 4]]), gsm.v(32, [[0, NB], [1, 4]]), ALU.mult), [selend[s], gsm], [Rt])
            p = nps()
            mmg(p[:, 0:NB * 4], [(ones[:], Rt[:].rearrange("p a b -> p (a b)"))], [ones, Rt], p)
            dv(lambda e: e.tensor_copy(bend[:], p[:, 0:NB * 4]), [p], [bend])
            for qd in range(2048 // CQ):
                c0 = qd * CQ
                for j in range(4):
                    if s == 0:
                        cx.dma("sp", taps[j][:], ext_p[r0 + j:r0 + j + 128, c0:c0 + CQ], reads=[ext_p], writes=[taps[j]], owner=taps[j])
                    else:
                        cx.dma("sp", taps[j][:], ext_s[:, j:j + 8, c0:c0 + CQ], reads=[ext_s], writes=[taps[j]], owner=taps[j])
                    cx.dma("sp", cw[j][:], W["m_conv_w"][0, j, c0:c0 + CQ].partition_broadcast(128), writes=[cw[j]], owner=cw[j])
                for j in range(4):
                    cx.op("pool" if j % 2 else "dve", lambda e, j=j: e.tensor_tensor(taps[j][:], taps[j][:], cw[j][:], ALU.mult), reads=[taps[j], cw[j]], writes=[taps[j]])
                dv(lambda e: e.tensor_tensor(taps[0][:], taps[0][:], taps[1][:], ALU.add), [taps[0], taps[1]], [taps[0]])
                cx.op("pool", lambda e: e.tensor_tensor(taps[2][:], taps[2][:], taps[3][:], ALU.add), reads=[taps[2], taps[3]], writes=[taps[2]])
                dv(lambda e: e.tensor_tensor(taps[0][:], taps[0][:], taps[2][:], ALU.add), [taps[0], taps[2]], [taps[0]])
                ac(lambda e, c0=c0: e.activation(qk[:, c0:c0 + CQ], taps[0][:], AF.Silu), [taps[0]], [qk])
            transpose_to(qT, lambda c: qT[:, c, :], qk, lambda c: qk[:, c * 128:(c + 1) * 128], 8)
            transpose_to(kT, lambda c: kT[:, c, :], qk, lambda c: qk[:, 1024 + c * 128:1024 + (c + 1) * 128], 8, scale_fn=lambda c: 0.0625)
            for h in range(4):
                dv(lambda e, h=h: e.tensor_scalar_mul(khat[:, h * 256:(h + 1) * 256], qk[:, 1024 + h * 256:1024 + (h + 1) * 256], gsm[:, 52 + h:53 + h]),
                   [qk, gsm], [khat])
            cx.dma("sp", zmv[:], z_d[r0:r0 + 128, 0:2048], reads=[z_d], writes=[zmv], owner=zmv)
            dv(lambda e: e.tensor_copy(Vp.v(0, [[257, 4], [1, 256]]), zmv.v(0, [[256, 4], [1, 256]])), [zmv], [Vp])
            ac(lambda e: e.activation(MG[:], zmv[:, 1024:2048], AF.Sigmoid), [zmv], [MG])
            dv(lambda e: e.tensor_tensor(MG[:], MG[:], mgain[:], ALU.mult), [MG, mgain], [MG])
            for h in range(4):
                p = nps()
                mmg(p[:, 0:128], [(kT[:, 2 * h + c, :], qT[:, 2 * h + c, :]) for c in range(2)], [kT, qT], p)
                dv(lambda e, h=h, p=p: e.scalar_tensor_tensor(sTs[:], p[:, 0:128], gsm[:, 28 + h:29 + h], mle[s][:], ALU.mult, ALU.mult), [p, gsm, mle[s]], [sTs])
                nd = accps()
                if s == 0:
                    pairs = [(sTs[:], Vp[:, h, :])] + [(qT[:, 2 * h + c, :], Cb[:, h, c, :]) for c in range(2)]
                    mmg(nd[:, 0:257], pairs, [sTs, Vp, qT, Cb], nd)
                    for c in range(2):
                        pu = nps()
                        mmg(pu[:, 0:257], [(khat[:, h * 256 + c * 128:h * 256 + (c + 1) * 128], Vp[:, h, :])], [khat, Vp], pu)
                        dv(lambda e, h=h, c=c, pu=pu: e.tensor_tensor(Cst[:, h, c, :], pu[:, 0:257], Cst[:, h, c, :], ALU.add), [pu, Cst], [Cst])
                        dv(lambda e, h=h, c=c: e.tensor_scalar_mul(Cst[:, h, c, :], Cst[:, h, c, :], bend[:, h:h + 1]), [Cst, bend], [Cst])
                else:
                    cx.op("pe", lambda e, h=h: e.matmul(nd[:, 0:257], sTs[:], Vp[:, h, :], start=True, stop=False), reads=[sTs, Vp], writes=[nd])
                    for c in range(2):
                        pass
                    for b in range(16):
                        Cq = Cs[ssi[0] % NSB]; Cqb = Csb[ssi[0] % NSB]; ssi[0] += 1
                        cx.dma("sp", Cq[:, :, 0:256], mC[b + 1, h].rearrange("(c p) e -> p c e", p=128), reads=[mC], writes=[Cq], owner=Cq)
                        cx.dma("sp", Cq[:, :, 256:257], mn[b + 1, h].rearrange("(c p o) -> p c o", p=128, o=1), reads=[mn], writes=[Cq], owner=Cq, allow_slow_non_contiguous=True)
                        cx.op("pool", lambda e, Cq=Cq, Cqb=Cqb: e.tensor_copy(Cqb[:], Cq[:]), reads=[Cq], writes=[Cqb])
                        dv(lambda e, b=b: e.tensor_scalar_mul(khm[:, h * 256:(h + 1) * 256], khat[:, h * 256:(h + 1) * 256], ref[1][:, 18 + b:19 + b]), [khat, ref[1]], [khm])
                        for c in range(2):
                            dv(lambda e, c=c, b=b: e.tensor_tensor(qTm[:, 0, :], qT[:, 2 * h + c, :], blkrow[:, b, :], ALU.mult), [qT, blkrow], [qTm])
                            cx.op("pe", lambda e, c=c, b=b, Cqb=Cqb: e.matmul(nd[:, 0:257], qTm[:, 0, :], Cqb[:, c, :], start=False, stop=(b == 15 and c == 1)),
                                  reads=[qTm, Cqb], writes=[nd])
                        for c in range(2):
                            pu = nps()
                            mmg(pu[:, 0:257], [(khm[:, h * 256 + c * 128:h * 256 + (c + 1) * 128], Vp[:, h, :])], [khm, Vp], pu)
                            dv(lambda e, c=c, pu=pu, Cq=Cq: e.tensor_tensor(Cq[:, c, :], pu[:, 0:257], Cq[:, c, :], ALU.add), [pu, Cq], [Cq])
                            dv(lambda e, c=c, b=b, Cq=Cq: e.tensor_scalar_mul(Cq[:, c, :], Cq[:, c, :], bend[:, (b + 1) * 4 + h:(b + 1) * 4 + h + 1]), [Cq, bend], [Cq])
                            if c == 1:
                                cx.dma("act", oC[b + 1, h].rearrange("(c p) e -> p c e", p=128), Cq[:, :, 0:256], reads=[Cq], writes=[oC], owner=Cq)
                                cx.dma("act", on[b + 1, h].rearrange("(c p o) -> p c o", p=128, o=1), Cq[:, :, 256:257], reads=[Cq], writes=[on], owner=Cq, allow_slow_non_contiguous=True)
                dv(lambda e, nd=nd: e.tensor_copy(st[:, 0:1], nd[:, 256:257]), [nd], [st])
                dv(lambda e: e.tensor_scalar_mul(st[:, 6:7], st[:, 0:1], -1.0), [st], [st])
                dv(lambda e: e.tensor_tensor(st[:, 0:1], st[:, 0:1], st[:, 6:7], ALU.max), [st], [st])
                dv(lambda e, h=h: e.tensor_tensor(st[:, 0:1], st[:, 0:1], gsm[:, 32 + h:33 + h], ALU.mult), [st, gsm], [st])
                dv(lambda e, h=h: e.tensor_tensor(st[:, 0:1], st[:, 0:1], gsm[:, 40 + h:41 + h], ALU.max), [st, gsm], [st])
                dv(lambda e: e.reciprocal(st[:, 0:1], st[:, 0:1]), [st], [st])
                dv(lambda e, h=h: e.tensor_tensor(st[:, 0:1], st[:, 0:1], gsm[:, 32 + h:33 + h], ALU.mult), [st, gsm], [st])
                dv(lambda e, nd=nd: e.tensor_scalar_mul(hm[:], nd[:, 0:256], st[:, 0:1]), [nd, st], [hm])
                dv(lambda e: e.tensor_reduce(st[:, 1:2], hm[:], AX.X, ALU.add), [hm], [st])
                dv(lambda e: e.tensor_scalar_mul(st[:, 1:2], st[:, 1:2], 1.0 / 256), [st], [st])
                dv(lambda e: e.tensor_scalar(hm[:], hm[:], st[:, 1:2], None, ALU.subtract), [hm, st], [hm]) if False else \
                    dv(lambda e: e.tensor_scalar_sub(hm[:], hm[:], st[:, 1:2]), [hm, st], [hm])
                dv(lambda e: e.tensor_tensor(hj[:], hm[:], hm[:], ALU.mult), [hm], [hj])
                dv(lambda e: e.tensor_reduce(st[:, 2:3], hj[:], AX.X, ALU.add), [hj], [st])
                rstd_col(st, 3, st[:, 2:3], 1.0 / 256, RMS_EPS)
                dv(lambda e, h=h: e.scalar_tensor_tensor(mixed[:, h * 256:(h + 1) * 256], hm[:], st[:, 3:4], MG[:, h * 256:(h + 1) * 256], ALU.mult, ALU.mult),
                   [hm, st, MG], [mixed])
            if s == 0:
                dv(lambda e: e.tensor_copy(Cb[:], Cst[:]), [Cst], [Cb])
                if i == NTP - 1:
                    for h in range(4):
                        for c in range(2):
                            cx.dma("act", oC[0, h, c * 128:(c + 1) * 128, :], Cst[:, h, c, 0:256], reads=[Cst], writes=[oC], owner=Cst)
                            cx.dma("act", on[0, h, c * 128:(c + 1) * 128].rearrange("(p o) -> p o", o=1), Cst[:, h, c, 256:257], reads=[Cst], writes=[on], owner=Cst)
            zc = 2056
            for k in range(4):
                cx.dma("sp", A[k][:], z_d[r0:r0 + 128, zc + k * 1024:zc + (k + 1) * 1024], reads=[z_d], writes=[A[k]], owner=A[k])
            ac(lambda e: e.activation(A[1][:], A[1][:], AF.Sigmoid), [A[1]], [A[1]])
            dv(lambda e: e.tensor_tensor(A[1][:], A[1][:], OMLB[:], ALU.mult), [A[1], OMLB], [A[1]])
            dv(lambda e: e.tensor_tensor(A[1][:], A[1][:], LB[:], ALU.add), [A[1], LB], [A[1]])
            ac(lambda e: e.activation(A[4][:], A[1][:], AF.Ln), [A[1]], [A[4]])
            pb = [nps(), nps()]
            for hf in range(2):
                mmg(pb[hf][:], [(trir[s][:], A[4][:, hf * 512:(hf + 1) * 512])], [trir[s], A[4]], pb[hf])
            pd = nps()
            for h in range(8):
                mmg(pd[:, h * 34:(h + 1) * 34], [(A[4][:, h * 128:(h + 1) * 128], ref[s][:])], [A[4], ref[s]], pd)
            ac(lambda e: e.activation(dec[:].rearrange("p a b -> p (a b)"), pd[:, 0:272], AF.Exp), [pd], [dec])
            for hf in range(2):
                ac(lambda e, hf=hf: e.activation(A[5][:, hf * 512:(hf + 1) * 512], pb[hf][:], AF.Exp), [pb[hf]], [A[5]])
                ac(lambda e, hf=hf: e.activation(A[4][:, hf * 512:(hf + 1) * 512], pb[hf][:], AF.Exp, scale=-1.0), [pb[hf]], [A[4]])
            ac(lambda e: e.activation(A[0][:], A[0][:], AF.Silu), [A[0]], [A[0]])
            dv(lambda e: e.scalar_tensor_tensor(A[0][:], A[0][:], float(128 ** -0.5), A[5][:], ALU.mult, ALU.mult), [A[0], A[5]], [A[0]])
            dv(lambda e: e.tensor_scalar(A[1][:], A[1][:], -1.0, 1.0, ALU.mult, ALU.add), [A[1]], [A[1]])
            dv(lambda e: e.tensor_tensor(A[1][:], A[1][:], A[4][:], ALU.mult), [A[1], A[4]], [A[1]])
            cx.op("pool", lambda e: e.tensor_copy(ktb[:], A[1][:]), reads=[A[1]], writes=[ktb])
            cx.op("pool", lambda e: e.tensor_copy(vb[:], A[2][:]), reads=[A[2]], writes=[vb])
            ac(lambda e: e.activation(A[3][:], A[3][:], AF.Silu), [A[3]], [A[3]])
            dv(lambda e: e.tensor_tensor(A[3][:], A[3][:], ggain[:], ALU.mult), [A[3], ggain], [A[3]])
            transpose_to(qtT, lambda c: qtT[:, c, :], A[0], lambda c: A[0][:, c * 128:(c + 1) * 128], 8)
            transpose_to(ktT, lambda c: ktT[:, c, :], A[1], lambda c: A[1][:, c * 128:(c + 1) * 128], 8)
            if s == 0:
                dv(lambda e: e.tensor_tensor(Smid[:], Sst[:], dec.v(0, [[34, 8], [0, 128]]), ALU.mult), [Sst, dec], [Smid])
                cx.op("pool", lambda e: e.tensor_copy(Smidb[:], Smid[:]), reads=[Smid], writes=[Smidb])
            for h in range(8):
                hc = slice(h * 128, (h + 1) * 128)
                p = nps()
                if s == 0:
                    mmg(p[0:64, 0:64], [(ktT[:, h, 0:64], qtT[:, h, 0:64])], [ktT, qtT], p)
                    mmg(p[:, 64:128], [(ktT[:, h, :], qtT[:, h, 64:128])], [ktT, qtT], p)
                    dv(lambda e, p=p: e.tensor_tensor(ATs[0:64, 0:64], p[0:64, 0:64], mle[s][0:64, 0:64], ALU.mult), [p, mle[s]], [ATs])
                    dv(lambda e, p=p: e.tensor_tensor(ATs[:, 64:128], p[:, 64:128], mle[s][:, 64:128], ALU.mult), [p, mle[s]], [ATs])
                    dv(lambda e: e.memset(ATs[64:128, 0:64], 0.0), [], [ATs])
                else:
                    mmg(p[:, 0:128], [(ktT[:, h, :], qtT[:, h, :])], [ktT, qtT], p)
                    dv(lambda e, p=p: e.tensor_tensor(ATs[:], p[:, 0:128], mle[s][:], ALU.mult), [p, mle[s]], [ATs])
                po = accps()
                if s == 0:
                    mmg(po[:, 0:128], [(ATs[:], vb[:, hc]), (qtT[:, h, :], Smidb[:, h, :])], [ATs, vb, qtT, Smidb], po)
                    pu = nps()
                    mmg(pu[:, 0:128], [(ktb[:, hc], vb[:, hc])], [ktb, vb], pu)
                    dv(lambda e, h=h, pu=pu: e.tensor_tensor(Sst[:, h, :], pu[:, 0:128], Smid[:, h, :], ALU.add), [pu, Smid], [Sst])
                    dv(lambda e, h=h: e.tensor_scalar_mul(Sst[:, h, :], Sst[:, h, :], dec[:, h, 17:18]), [Sst, dec], [Sst])
                else:
                    cx.op("pe", lambda e, hc=hc: e.matmul(po[:, 0:128], ATs[:], vb[:, hc], start=True, stop=False), reads=[ATs, vb], writes=[po])
                    for b in range(16):
                        Sq = Ss[ssi[0] % NSB]; Sqb = Ssb[ssi[0] % NSB]; ssi[0] += 1
                        cx.dma("sp", Sq[:], gS[b + 1, h], reads=[gS], writes=[Sq], owner=Sq)
                        cx.op("pool", lambda e, Sq=Sq, Sqb=Sqb: e.tensor_copy(Sqb[:], Sq[:]), reads=[Sq], writes=[Sqb])
                        dv(lambda e, b=b, h=h: e.tensor_tensor(qTm[:, 0, :], qtT[:, h, :], blkrow[:, b, :], ALU.mult), [qtT, blkrow], [qTm])
                        cx.op("pe", lambda e, b=b, Sqb=Sqb: e.matmul(po[:, 0:128], qTm[:, 0, :], Sqb[:], start=False, stop=(b == 15)), reads=[qTm, Sqb], writes=[po])
                        dv(lambda e, b=b, hc=hc: e.tensor_scalar_mul(khm[:, 0:128], ktb[:, hc], ref[1][:, 18 + b:19 + b]), [ktb, ref[1]], [khm])
                        pu = nps()
                        mmg(pu[:, 0:128], [(khm[:, 0:128], vb[:, hc])], [khm, vb], pu)
                        dv(lambda e, pu=pu, Sq=Sq: e.tensor_tensor(Sq[:], pu[:, 0:128], Sq[:], ALU.add), [pu, Sq], [Sq])
                        dv(lambda e, b=b, h=h, Sq=Sq: e.tensor_scalar_mul(Sq[:], Sq[:], dec[:, h, 17 + b + 1:17 + b + 2]), [Sq, dec], [Sq])
                        cx.dma("act", oS[b + 1, h], Sq[:], reads=[Sq], writes=[oS], owner=Sq)
                dv(lambda e, po=po: e.tensor_tensor(hj[:, 0:128], po[:, 0:128], po[:, 0:128], ALU.mult) if False else e.tensor_copy(hm[:, 0:128], po[:, 0:128]), [po], [hm])
                dv(lambda e: e.tensor_tensor(hj[:, 0:128], hm[:, 0:128], hm[:, 0:128], ALU.mult), [hm], [hj])
                dv(lambda e: e.tensor_reduce(st[:, 4:5], hj[:, 0:128], AX.X, ALU.add), [hj], [st])
                rstd_col(st, 5, st[:, 4:5], 1.0 / 128, RMS_EPS)
                dv(lambda e, hc=hc: e.scalar_tensor_tensor(mixed[:, 1024 + hc.start:1024 + hc.stop], hm[:, 0:128], st[:, 5:6], A[3][:, hc], ALU.mult, ALU.mult),
                   [hm, st, A[3]], [mixed])
            if s == 0 and i == NTP - 1:
                cx.dma("act", oS[0].rearrange("h k v -> k h v"), Sst[:], reads=[Sst], writes=[oS], owner=Sst)
            transpose_to(mixT, lambda c: mixT[:, c, r0:r0 + 128], mixed, lambda c: mixed[:, c * 128:(c + 1) * 128], 16)
        cx.dma("sp", oconv[0], ext_p[NP:NP + 3, :], reads=[ext_p], writes=[oconv], owner=oconv)
        cx.dma("sp", oconv[1:17], ext_s[:, 8:11, :], reads=[ext_s], writes=[oconv], owner=oconv)

    mix_d = dscr("mix_d", [6, 128, 16 * NTOK], BF16)

    @staged
    def stage_l1_mix(hT):
        shs = sb("shs", [NB, D]); shT = sb("shT", [128, 16, NB], BF16); muT = sb("muT", [128, 6, 16])
        xx = [sb("xx%d" % i, [128, NTOK], BF16) for i in range(2)]
        mo_ = [sb("mo%d" % i, [128, NTOK], BF16) for i in range(4)]
        cx.dma("sp", shs[:], rsh[:], writes=[shs], owner=shs)
        for c0 in range(0, 16, 4):
            p = nps()
            for j in range(4):
                c = c0 + j
                cx.op("pe", lambda e, j=j, c=c: e.transpose(p.v(j * 32, [[1, NB]]), shs[:, c * 128:(c + 1) * 128], ident[0:NB, 0:NB]),
                      reads=[shs, ident], writes=[p])
            for j in range(4):
                evac(shT[:, c0 + j, :], p.v(j * 32, [[1, NB]]), [p], [shT])
        for i in range(6):
            cx.dma("sp", muT[:, i, :], W["r_mu"][0, i].rearrange("(c p) -> p c", p=128), writes=[muT], owner=muT, allow_slow_non_contiguous=True)
        oi = 0
        for c in range(16):
            x = xx[c % 2]
            cx.op("dve", lambda e: e.tensor_tensor(x[:, 1:NTOK], hT[:, c, 0:NTOK - 1], hT[:, c, 1:NTOK], ALU.subtract), reads=[hT], writes=[x])
            cx.op("dve", lambda e: e.tensor_tensor(x[:, 0:1], shT[:, c, 0:1], hT[:, c, 0:1], ALU.subtract), reads=[hT, shT], writes=[x])
            cx.op("dve", lambda e: e.tensor_tensor(x.v(NP, [[8, 16]]), shT[:, c, 1:NB], hT.v(c * NTOK + NP, [[8, 16]]), ALU.subtract), reads=[hT, shT], writes=[x])
            for i in range(6):
                o = mo_[oi % 4]; oi += 1
                cx.op("dve", lambda e, i=i, o=o: e.scalar_tensor_tensor(o[:], x[:], muT[:, i, c:c + 1], hT[:, c, :], ALU.mult, ALU.add),
                      reads=[x, muT, hT], writes=[o])
                cx.dma("sp", mix_d[i, :, c * NTOK:(c + 1) * NTOK], o[:], reads=[o], writes=[mix_d], owner=o)

    def load_mix(i, hT):
        cx.dma("sp", hT[:].rearrange("p a b -> p (a b)"), mix_d[i], reads=[mix_d], writes=[hT], owner=hT)
        cx.barrier()

    @staged
    def lora1(aT, w1_ap, R, func, tT):
        w1 = sb("l1w", [128, 16, R], BF16)
        cx.dma("pool", w1[:], w1_ap.rearrange("(kc p) n -> p kc n", p=128), writes=[w1], owner=w1)
        for oc in range((R + 127) // 128):
            rc = min(128, R - oc * 128)
            for g0 in range(0, NTOK, 512):
                gn = min(512, NTOK - g0)
                p = nps()
                for kc in range(16):
                    cx.op("pe", lambda e, kc=kc: e.matmul(p[0:rc, 0:gn], w1[:, kc, oc * 128:oc * 128 + rc], aT[:, kc, g0:g0 + gn], start=(kc == 0), stop=(kc == 15)),
                          reads=[w1, aT], writes=[p])
                cx.op("act", lambda e: e.activation(tT[0:rc, oc, g0:g0 + gn], p[0:rc, 0:gn], func), reads=[p], writes=[tT])

    def stage_l1_proj(hT):
        def dst_rz(base):
            return lambda c0: (lambda i: [(rz_d, rz_d[i * 128:(i + 1) * 128, base + c0:base + c0 + 512], 0, 128)])
        def blocks_for(w, base):
            return [(w[:, c0:c0 + 512], dst_rz(base)(c0)) for c0 in range(0, D, 512)]
        stacks.append(contextlib.ExitStack()); stage_bufs.append([])
        tT = sb("tT", [128, 2, NTOK], BF16)
        load_mix(0, hT); proj_tok(hT, blocks_for(W["r_wr"][0], 0))
        load_mix(2, hT); proj_tok(hT, blocks_for(W["r_wk"][0], D))
        load_mix(3, hT); proj_tok(hT, blocks_for(W["r_wv"][0], 2 * D))
        load_mix(1, hT); lora1(hT, W["r_w1"][0], 96, AF.Tanh, tT)
        proj_tok(tT, blocks_for(W["r_w2"][0], 3 * D), K=1, kpart=96, bias_fn=lambda bi: W["r_w0"][0, bi * 512:(bi + 1) * 512], sigmoid=True)
        load_mix(4, hT); lora1(hT, W["r_a1"][0], 96, AF.Copy, tT)
        proj_tok(tT, blocks_for(W["r_a2"][0], 4 * D), K=1, kpart=96, bias_fn=lambda bi: W["r_a0"][0, bi * 512:(bi + 1) * 512], sigmoid=True)
        load_mix(5, hT); lora1(hT, W["r_g1"][0], 256, AF.Sigmoid, tT)
        proj_tok(tT, blocks_for(W["r_g2"][0], 5 * D), K=2, kpart=128)
        cx.barrier(); cx.release(stage_bufs.pop()); stacks.pop().close()

    @staged
    def stage_l1_rec(outT):
        npsmod[0] = 4
        dv = lambda fn, r, w: cx.op("dve", fn, reads=r, writes=w)
        ac = lambda fn, r, w: cx.op("act", fn, reads=r, writes=w)
        po_ = lambda fn, r, w: cx.op("pool", fn, reads=r, writes=w)
        mle = [sb("mle%d" % i, [128, 128]) for i in range(2)]; mlt = [sb("mlt%d" % i, [128, 128]) for i in range(2)]
        mgt = [sb("mgt%d" % i, [128, 128]) for i in range(2)]; refs = sb("refs", [128, 34])
        blkrow = sb("blkrow", [128, 16, 128], BF16)
        NBP = 4
        for i, (a_, b_, c_) in enumerate([("rle_p", "rlt_p", "rgt_p"), ("mle_s", "mlt_s", "mgt_s")]):
            cx.dma("sp", mle[i][:], CN[a_][:], writes=[mle[i]], owner=mle[i])
            cx.dma("sp", mlt[i][:], CN[b_][:], writes=[mlt[i]], owner=mlt[i])
            cx.dma("sp", mgt[i][:], CN[c_][:], writes=[mgt[i]], owner=mgt[i])
        cx.dma("sp", refs[:], CN["ref_s"][:], writes=[refs], owner=refs)
        rblk = sb("rblk", [128, NB]); rblkrow = sb("rblkrow", [128, NBP, 128], BF16)
        cx.dma("sp", rblk[:], CN["rblk_p"][:], writes=[rblk], owner=rblk)
        cx.dma("pool", rblkrow[:], CN["rblkrow_p"][:].rearrange("p (b t) -> p b t", b=NBP), writes=[rblkrow], owner=rblkrow)
        Hrd = [sb("Hrd%d" % i, [128, NBP, 64], BF16) for i in range(2)]
        rmk = sb("rmk", [128, NBP, 128], BF16)
        cx.dma("pool", blkrow[:], CN["blkrow"][:].rearrange("p (b t) -> p b t", b=16), writes=[blkrow], owner=blkrow)
        HW = 1024
        PRMH = [{k: sb("prm_%s%d" % (k, hf), [128, HW]) for k in ["r_kk", "r_ka", "r_rk", "r_lnw", "r_lnb"]} for hf in range(2)]
        for hf in range(2):
            for k_, b_ in PRMH[hf].items():
                cx.dma("sp", b_[:], W[k_][0, hf * HW:(hf + 1) * HW].partition_broadcast(128), writes=[b_], owner=b_)
        Rb, Kb, Vb_, SW, Aa, Cc, KK, TMP, E1, E2 = [sb("rw%d" % i, [128, HW]) for i in range(10)]
        Gg = E2
        aTt, bTt, kTt, rTt = [sb("rt%d" % i, [128, 8, 128], BF16) for i in range(4)]
        btk, ktk, vtk = [sb("rk%d" % i, [128, HW], BF16) for i in range(3)]
        Hst = sb("Hst", [128, 16, 64]); Hbm = [sb("Hbm%d" % i, [128, 16, 64], BF16) for i in range(2)]
        bTm = [sb("bTm%d" % i, [128, 8, 128], BF16) for i in range(2)]; kTm = [sb("kTm%d" % i, [128, 8, 128], BF16) for i in range(2)]
        Hsbm = [[sb("Hsbm%d_%d" % (i, j), [128, 64], BF16) for j in range(2)] for i in range(3)]
        cx.op("dve", lambda e: e.memset(Hst[:], 0.0), writes=[Hst])
        for zb in Hbm + bTm + kTm + Hsbm[0] + Hsbm[1] + Hsbm[2] + Hrd:
            cx.op("dve", lambda e, zb=zb: e.memset(zb[:], 0.0), writes=[zb])
        Nb = [sb("Nb%d" % i, [128, 2, 128], BF16) for i in range(2)]; Mb = [sb("Mb%d" % i, [128, 2, 128], BF16) for i in range(2)]
        TtS = [sb("TtS%d" % i, [128, 2, 128], BF16) for i in range(2)]; AakS = [sb("AakS%d" % i, [128, 2, 128], BF16) for i in range(2)]
        ArbS = [sb("ArbS%d" % i, [128, 2, 128], BF16) for i in range(2)]; ArkS = [sb("ArkS%d" % i, [128, 2, 128], BF16) for i in range(2)]
        Tt, AakT, ArbT, ArkT = TtS[0], AakS[0], ArbS[0], ArkS[0]
        Xb = sb("Xb", [128, 128], BF16); Ub = sb("Ub", [128, 128], BF16); Ubf = Ub
        am = sb("am", [128, 128], BF16); rm = sb("rm", [128, 128], BF16); bm = sb("bm", [128, 128], BF16); km = sb("km", [128, 128], BF16)
        dL = sb("dL", [128, 8, NB]); stt_ = sb("stt", [128, 4, 16])
        NSB = 3
        Sin = [sb("Sin%d" % i, [64, 2, 64]) for i in range(NSB)]; Hs = [sb("Hs%d" % i, [128, 64]) for i in range(NSB)]
        Sout = [sb("Sout%d" % i, [64, 2, 64]) for i in range(NSB)]
        ssi = [0]
        h3 = lambda b_: b_[:].rearrange("p (h j) -> p h j", j=64)

        def prompt_pairs(half, ncp, Y):
            s = 0
            nit = 4

            def genA(cp, st):
                def pairmm(p, lT, rT_):
                    for hh in range(2):
                        lb = lT[hh] if isinstance(lT, list) else lT
                        rb = rT_[hh] if isinstance(rT_, list) else rT_
                        mmg(p[:, hh * 128:(hh + 1) * 128], [(lb[:, cp, :], rb[:, cp, :])], [lb, rb], p)

                def pairev(dst, p, mask):
                    dv(lambda e: e.tensor_tensor(dst[:], p[:, 0:256].rearrange("p (a b) -> p a b", a=2), mask.v(0, [[0, 2], [1, 128]]), ALU.mult), [p, mask], [dst])
                Tt_ = TtS[st]
                p = nps(); pairmm(p, aTt, bTm); pairev(Nb[0], p, mgt[s]); yield
                p = nps(); pairmm(p, bTm, aTt); pairev(Mb[0], p, mlt[s]); yield
                p = nps(); pairmm(p, kTm, aTt); pairev(AakS[st], p, mlt[s]); yield
                p = nps(); pairmm(p, bTm, rTt); pairev(ArbS[st], p, mle[s]); yield
                p = nps(); pairmm(p, kTm, rTt); pairev(ArkS[st], p, mle[s]); yield
                dv(lambda e: e.tensor_tensor(Tt_[:], Mb[0][:], identb.v(0, [[0, 2], [1, 128]]), ALU.add), [Mb[0], identb], [Tt_])
                cur = 0
                for it in range(nit):
                    nx = 1 - cur
                    p1 = nps(); p2 = nps()
                    for hh in range(2):
                        if it < nit - 1:
                            mmg(p1[:, hh * 128:(hh + 1) * 128], [(Nb[cur][:, hh, :], Mb[cur][:, hh, :])], [Nb[cur], Mb[cur]], p1)
                        mmg(p2[:, hh * 128:(hh + 1) * 128], [(Mb[cur][:, hh, :], Nb[cur][:, hh, :])], [Nb[cur], Mb[cur]], p2)
                    yield
                    if it < nit - 1:
                        ac(lambda e: e.activation(Mb[nx][:].rearrange("p a b -> p (a b)"), p1[:, 0:256], AF.Copy), [p1], [Mb[nx]])
                    dv(lambda e: e.tensor_copy(Nb[nx][:].rearrange("p a b -> p (a b)"), p2[:, 0:256]), [p2], [Nb[nx]])
                    p3 = nps()
                    for hh in range(2):
                        mmg(p3[:, hh * 128:(hh + 1) * 128], [(Nb[nx][:, hh, :], Tt_[:, hh, :])], [Nb[nx], Tt_], p3)
                    yield
                    dv(lambda e: e.tensor_tensor(Tt_[:].rearrange("p a b -> p (a b)"), p3[:, 0:256], Tt_[:].rearrange("p a b -> p (a b)"), ALU.add), [p3, Tt_], [Tt_])
                    cur = nx
                    yield

            def genB(cp, st):
                gc = half * 8 + cp
                cs = slice(cp * 128, (cp + 1) * 128)
                Tt_, AakT_, ArbT_, ArkT_ = TtS[st], AakS[st], ArbS[st], ArkS[st]
                px, pu, ph, py = PS[4], PS[5], PS[6], PS[7]
                for k in range(NBP):
                    dv(lambda e: e.tensor_tensor(am[:], aTt[:, cp, :], rblkrow[:, k, :], ALU.mult), [aTt, rblkrow], [am])
                    po_(lambda e: e.tensor_tensor(rmk[:, k, :], rTt[:, cp, :], rblkrow[:, k, :], ALU.mult), [rTt, rblkrow], [rmk])
                    Hsrc = [(Hbm[hh][:, gc, :], Hbm[hh]) if k == 0 else (Hrd[hh][:, k, :], Hrd[hh]) for hh in range(2)]
                    for hh in range(2):
                        mmg(px[:, hh * 64:(hh + 1) * 64], [(AakT_[:, hh, :], vtk[:, cp * 128 + hh * 64:cp * 128 + (hh + 1) * 64]),
                                                           (am[:], Hsrc[hh][0])], [AakT_, vtk, am, Hsrc[hh][1]], px)
                    yield
                    ac(lambda e: e.activation(Xb[:], px[:, 0:128], AF.Copy), [px], [Xb])
                    for hh in range(2):
                        mmg(pu[:, hh * 64:(hh + 1) * 64], [(Tt_[:, hh, :], Xb[:, hh * 64:(hh + 1) * 64])], [Tt_, Xb], pu)
                    yield
                    if k == 0:
                        dv(lambda e: e.tensor_scalar_mul(Ubf[:], pu[:, 0:128], rblk[:, k:k + 1]), [pu, rblk], [Ubf])
                    else:
                        dv(lambda e: e.scalar_tensor_tensor(Ubf[:], pu[:, 0:128], rblk[:, k:k + 1], Ubf[:], ALU.mult, ALU.add), [pu, rblk, Ubf], [Ubf])
                    ac(lambda e: e.activation(bm[:], btk[:, cs], AF.Copy, scale=rblk[:, k:k + 1]), [btk, rblk], [bm])
                    ac(lambda e: e.activation(km[:], ktk[:, cs], AF.Copy, scale=rblk[:, k:k + 1]), [ktk, rblk], [km])
                    mmg(ph[:, 0:128], [(bm[:], Ubf[:]), (km[:], vtk[:, cs])], [bm, Ubf, km, vtk], ph)
                    yield
                    for hh in range(2):
                        pb = 64 * hh
                        dv(lambda e: e.tensor_tensor(Hst[pb:pb + 64, gc, :], ph[pb:pb + 64, pb:pb + 64], Hst[pb:pb + 64, gc, :], ALU.add), [ph, Hst], [Hst])
                    ac(lambda e: e.activation(Hst[:, gc, :], Hst[:, gc, :], AF.Copy, scale=dL[:, cp, k:k + 1]), [Hst, dL], [Hst])
                    if k < NBP - 1:
                        for hh in range(2):
                            pb = 64 * hh
                            po_(lambda e: e.tensor_copy(Hrd[hh][pb:pb + 64, k + 1, :], Hst[pb:pb + 64, gc, :]), [Hst], [Hrd[hh]])
                    yield
                for hh in range(2):
                    vh = vtk[:, cp * 128 + hh * 64:cp * 128 + (hh + 1) * 64]
                    pairs = [(ArbT_[:, hh, :], Ubf[:, hh * 64:(hh + 1) * 64]), (ArkT_[:, hh, :], vh), (rmk[:, 0, :], Hbm[hh][:, gc, :])]
                    pairs += [(rmk[:, k, :], Hrd[hh][:, k, :]) for k in range(1, NBP)]
                    mmg(py[:, hh * 64:(hh + 1) * 64], pairs, [ArbT_, Ubf, ArkT_, vtk, rmk, Hbm[hh], Hrd[hh]], py)
                yield
                ac(lambda e: e.activation(Y[:, cs], py[:, 0:128], AF.Copy), [py], [Y])
                for hh in range(2):
                    pb = 64 * hh
                    po_(lambda e: e.tensor_copy(Hbm[hh][pb:pb + 64, gc, :], Hst[pb:pb + 64, gc, :]), [Hst], [Hbm[hh]])
                yield

            npsmod[0] = 3
            gb = iter(())
            for cp in range(ncp + 1):
                ga = genA(cp, cp % 2) if cp < ncp else iter(())
                alive = [ga, gb]
                while alive:
                    for g in list(alive):
                        try:
                            next(g)
                        except StopIteration:
                            alive.remove(g)
                gb = genB(cp, cp % 2) if cp < ncp else iter(())
            npsmod[0] = 4

        def bc16(b_, col):
            return stt_.v(col * 16, [[1, 16], [0, 64]])

        import os
        for i in range(NT):
            s = 1 if i == NTP else 0
            if os.environ.get("K_REC") == "p" and s == 1:
                continue
            if os.environ.get("K_REC") == "s" and s == 0:
                continue
            r0 = i * 128
            nit = 2 if s else 4
            for half in range(2):
                f0 = half * HW
                PRM = PRMH[half]
                for idx, b_ in enumerate([Rb, Kb, Vb_, SW, Aa]):
                    cx.dma("sp", b_[:], rz_d[r0:r0 + 128, idx * D + f0:idx * D + f0 + HW], reads=[rz_d], writes=[b_], owner=b_)
                dv(lambda e: e.tensor_scalar_mul(SW[:], SW[:], -0.6065306597126334), [SW], [SW])
                pc = [nps(), nps()]
                for hf in range(2):
                    mmg(pc[hf][:], [(mle[s][:], SW[:, hf * 512:(hf + 1) * 512])], [mle[s], SW], pc[hf])
                for hf in range(2):
                    evac(Cc[:, hf * 512:(hf + 1) * 512], pc[hf][:], [pc[hf]], [Cc])
                pd = nps()
                for cp in range(8):
                    mmg(pd[:, cp * NB:(cp + 1) * NB], [(SW[:, cp * 128:(cp + 1) * 128], refs[:, 17:34] if s else rblk[:])], [SW, refs, rblk], pd)
                ac(lambda e: e.activation(dL[:].rearrange("p a b -> p (a b)"), pd[:, 0:8 * NB], AF.Exp), [pd], [dL])
                dv(lambda e: e.tensor_tensor(KK[:], Kb[:], PRM["r_kk"][:], ALU.mult), [Kb, PRM["r_kk"]], [KK])
                ac(lambda e: e.activation(TMP[:], KK[:], AF.Square), [KK], [TMP])
                dv(lambda e: e.tensor_reduce(stt_[:, 0, :], h3(TMP), AX.X, ALU.add), [TMP], [stt_])
                dv(lambda e: e.tensor_scalar_max(stt_[:, 0, :], stt_[:, 0, :], 1e-24), [stt_], [stt_])
                ac(lambda e: e.activation(stt_[:, 0, :], stt_[:, 0, :], AF.Sqrt), [stt_], [stt_])
                dv(lambda e: e.reciprocal(stt_[:, 0, :], stt_[:, 0, :]), [stt_], [stt_])
                dv(lambda e: e.tensor_tensor(h3(KK), h3(KK), bc16(stt_, 0), ALU.mult), [KK, stt_], [KK])
                dv(lambda e: e.scalar_tensor_tensor(TMP[:], Aa[:], -1.0, PRM["r_ka"][:], ALU.add, ALU.mult), [Aa, PRM["r_ka"]], [TMP])
                dv(lambda e: e.scalar_tensor_tensor(Kb[:], TMP[:], 1.0, Kb[:], ALU.add, ALU.mult), [TMP, Kb], [Kb])
                po_(lambda e: e.tensor_tensor(TMP[:], Rb[:], Kb[:], ALU.mult), [Rb, Kb], [TMP])
                po_(lambda e: e.tensor_tensor(TMP[:], TMP[:], PRM["r_rk"][:], ALU.mult), [TMP, PRM["r_rk"]], [TMP])
                dv(lambda e: e.tensor_reduce(stt_[:, 1, :], h3(TMP), AX.X, ALU.add), [TMP], [stt_])
                dv(lambda e: e.tensor_tensor(Aa[:], KK[:], Aa[:], ALU.mult), [KK, Aa], [Aa])
                ac(lambda e: e.activation(E1[:], Cc[:], AF.Exp), [Cc], [E1])
                ac(lambda e: e.activation(E2[:], Cc[:], AF.Exp, scale=-1.0), [Cc], [E2])
                dv(lambda e: e.tensor_tensor(TMP[:], Cc[:], SW[:], ALU.subtract), [Cc, SW], [TMP])
                ac(lambda e: e.activation(TMP[:], TMP[:], AF.Exp), [TMP], [TMP])
                dv(lambda e: e.tensor_tensor(Rb[:], Rb[:], E1[:], ALU.mult), [Rb, E1], [Rb])
                dv(lambda e: e.scalar_tensor_tensor(KK[:], KK[:], -1.0, TMP[:], ALU.mult, ALU.mult), [KK, TMP], [KK])
                po_(lambda e: e.tensor_tensor(Aa[:], Aa[:], E2[:], ALU.mult), [Aa, E2], [Aa])
                dv(lambda e: e.tensor_tensor(Kb[:], Kb[:], E2[:], ALU.mult), [Kb, E2], [Kb])
                cx.dma("sp", Gg[:], rz_d[r0:r0 + 128, 5 * D + f0:5 * D + f0 + HW], reads=[rz_d], writes=[Gg], owner=Gg)
                ac(lambda e: e.activation(btk[:], Aa[:], AF.Copy), [Aa], [btk])
                ac(lambda e: e.activation(ktk[:], Kb[:], AF.Copy), [Kb], [ktk])
                ac(lambda e: e.activation(vtk[:], Vb_[:], AF.Copy), [Vb_], [vtk])
                transpose_to(aTt, lambda c: aTt[:, c, :], KK, lambda c: KK[:, c * 128:(c + 1) * 128], 8)
                transpose_to(bTt, lambda c: bTt[:, c, :], Aa, lambda c: Aa[:, c * 128:(c + 1) * 128], 8)
                transpose_to(kTt, lambda c: kTt[:, c, :], Kb, lambda c: Kb[:, c * 128:(c + 1) * 128], 8)
                transpose_to(rTt, lambda c: rTt[:, c, :], Rb, lambda c: Rb[:, c * 128:(c + 1) * 128], 8)
                for hh in range(2):
                    pb = 64 * hh
                    ac(lambda e, hh=hh, pb=pb: e.activation(bTm[hh][pb:pb + 64, :, :], bTt[pb:pb + 64, :, :], AF.Copy), [bTt], [bTm[hh]])
                    po_(lambda e, hh=hh, pb=pb: e.tensor_copy(kTm[hh][pb:pb + 64, :, :], kTt[pb:pb + 64, :, :]), [kTt], [kTm[hh]])
                Y = E1
                ncp = int(os.environ.get("K_NCP", "8"))
                if s == 0:
                    prompt_pairs(half, ncp, Y)
                for cp in range(ncp if s == 1 else 0):
                    gc = half * 8 + cp
                    cs = slice(cp * 128, (cp + 1) * 128)

                    def pairmm(p, lT, rT_):
                        for hh in range(2):
                            lb = lT[hh] if isinstance(lT, list) else lT
                            rb = rT_[hh] if isinstance(rT_, list) else rT_
                            mmg(p[:, hh * 128:(hh + 1) * 128], [(lb[:, cp, :], rb[:, cp, :])], [lb, rb], p)

                    def pairev(dst, p, mask):
                        dv(lambda e: e.tensor_tensor(dst[:], p[:, 0:256].rearrange("p (a b) -> p a b", a=2), mask.v(0, [[0, 2], [1, 128]]), ALU.mult), [p, mask], [dst])
                    p = nps(); pairmm(p, aTt, bTm); pairev(Nb[0], p, mgt[s])
                    p = nps(); pairmm(p, bTm, aTt); pairev(Mb[0], p, mlt[s])
                    p = nps(); pairmm(p, kTm, aTt); pairev(AakT, p, mlt[s])
                    p = nps(); pairmm(p, bTm, rTt); pairev(ArbT, p, mle[s])
                    p = nps(); pairmm(p, kTm, rTt); pairev(ArkT, p, mle[s])
                    dv(lambda e: e.tensor_tensor(Tt[:], Mb[0][:], identb.v(0, [[0, 2], [1, 128]]), ALU.add), [Mb[0], identb], [Tt])
                    cur = 0
                    for it in range(nit):
                        nx = 1 - cur
                        p1 = nps(); p2 = nps()
                        for hh in range(2):
                            if it < nit - 1:
                                mmg(p1[:, hh * 128:(hh + 1) * 128], [(Nb[cur][:, hh, :], Mb[cur][:, hh, :])], [Nb[cur], Mb[cur]], p1)
                            mmg(p2[:, hh * 128:(hh + 1) * 128], [(Mb[cur][:, hh, :], Nb[cur][:, hh, :])], [Nb[cur], Mb[cur]], p2)
                        if it < nit - 1:
                            ac(lambda e, p1=p1, nx=nx: e.activation(Mb[nx][:].rearrange("p a b -> p (a b)"), p1[:, 0:256], AF.Copy), [p1], [Mb[nx]])
                        dv(lambda e, p2=p2, nx=nx: e.tensor_copy(Nb[nx][:].rearrange("p a b -> p (a b)"), p2[:, 0:256]), [p2], [Nb[nx]])
                        p3 = nps()
                        for hh in range(2):
                            mmg(p3[:, hh * 128:(hh + 1) * 128], [(Nb[nx][:, hh, :], Tt[:, hh, :])], [Nb[nx], Tt], p3)
                        dv(lambda e, p3=p3: e.tensor_tensor(Tt[:].rearrange("p a b -> p (a b)"), p3[:, 0:256], Tt[:].rearrange("p a b -> p (a b)"), ALU.add), [p3, Tt], [Tt])
                        cur = nx
                    if s == 1:
                        pxs = [PS[4], PS[5]]; pys = [PS[6], PS[7]]
                        for hh in range(2):
                            cx.op("pe", lambda e, hh=hh: e.matmul(pxs[hh][:, 0:64], AakT[:, hh, :], vtk[:, cp * 128 + hh * 64:cp * 128 + (hh + 1) * 64], start=True, stop=False),
                                  reads=[AakT, vtk], writes=[pxs[hh]])
                    if s == 0:
                        for hh in range(2):
                            for k in range(1, NBP):
                                pass
                        for k in range(NBP):
                            dv(lambda e, k=k: e.tensor_tensor(am[:], aTt[:, cp, :], rblkrow[:, k, :], ALU.mult), [aTt, rblkrow], [am])
                            po_(lambda e, k=k: e.tensor_tensor(rmk[:, k, :], rTt[:, cp, :], rblkrow[:, k, :], ALU.mult), [rTt, rblkrow], [rmk])
                            Hsrc = [(Hbm[hh][:, gc, :], Hbm[hh]) if k == 0 else (Hrd[hh][:, k, :], Hrd[hh]) for hh in range(2)]
                            px = accps()
                            for hh in range(2):
                                mmg(px[:, hh * 64:(hh + 1) * 64], [(AakT[:, hh, :], vtk[:, cp * 128 + hh * 64:cp * 128 + (hh + 1) * 64]),
                                                                   (am[:], Hsrc[hh][0])], [AakT, vtk, am, Hsrc[hh][1]], px)
                            dv(lambda e, px=px: e.tensor_copy(Xb[:], px[:, 0:128]), [px], [Xb])
                            pu = nps()
                            for hh in range(2):
                                mmg(pu[:, hh * 64:(hh + 1) * 64], [(Tt[:, hh, :], Xb[:, hh * 64:(hh + 1) * 64])], [Tt, Xb], pu)
                            if k == 0:
                                dv(lambda e, pu=pu, k=k: e.tensor_scalar_mul(Ubf[:], pu[:, 0:128], rblk[:, k:k + 1]), [pu, rblk], [Ubf])
                            else:
                                dv(lambda e, pu=pu, k=k: e.scalar_tensor_tensor(Ubf[:], pu[:, 0:128], rblk[:, k:k + 1], Ubf[:], ALU.mult, ALU.add), [pu, rblk, Ubf], [Ubf])
                            dv(lambda e, k=k, cs=cs: e.tensor_scalar_mul(bm[:], btk[:, cs], rblk[:, k:k + 1]), [btk, rblk], [bm])
                            po_(lambda e, k=k, cs=cs: e.tensor_scalar_mul(km[:], ktk[:, cs], rblk[:, k:k + 1]), [ktk, rblk], [km])
                            ph = nps()
                            mmg(ph[:, 0:128], [(bm[:], Ubf[:]), (km[:], vtk[:, cs])], [bm, Ubf, km, vtk], ph)
                            for hh in range(2):
                                pb = 64 * hh
                                dv(lambda e, pb=pb, ph=ph: e.tensor_tensor(Hst[pb:pb + 64, gc, :], ph[pb:pb + 64, pb:pb + 64], Hst[pb:pb + 64, gc, :], ALU.add), [ph, Hst], [Hst])
                            dv(lambda e, k=k: e.tensor_scalar_mul(Hst[:, gc, :], Hst[:, gc, :], dL[:, cp, k:k + 1]), [Hst, dL], [Hst])
                            if k < NBP - 1:
                                for hh in range(2):
                                    pb = 64 * hh
                                    po_(lambda e, hh=hh, pb=pb, k=k: e.tensor_copy(Hrd[hh][pb:pb + 64, k + 1, :], Hst[pb:pb + 64, gc, :]), [Hst], [Hrd[hh]])
                        py = accps()
                        for hh in range(2):
                            vh = vtk[:, cp * 128 + hh * 64:cp * 128 + (hh + 1) * 64]
                            pairs = [(ArbT[:, hh, :], Ubf[:, hh * 64:(hh + 1) * 64]), (ArkT[:, hh, :], vh), (rmk[:, 0, :], Hbm[hh][:, gc, :])]
                            pairs += [(rmk[:, k, :], Hrd[hh][:, k, :]) for k in range(1, NBP)]
                            mmg(py[:, hh * 64:(hh + 1) * 64], pairs, [ArbT, Ubf, ArkT, vtk, rmk, Hbm[hh], Hrd[hh]], py)
                        dv(lambda e, py=py, cs=cs: e.tensor_copy(Y[:, cs], py[:, 0:128]), [py], [Y])
                        for hh in range(2):
                            pb = 64 * hh
                            po_(lambda e, hh=hh, pb=pb: e.tensor_copy(Hbm[hh][pb:pb + 64, gc, :], Hst[pb:pb + 64, gc, :]), [Hst], [Hbm[hh]])
                    else:
                        hs_list = []
                        for b in range(16):
                            k2 = ssi[0] % 3; ssi[0] += 1
                            cx.dma("sp", Sin[k2][:], rS[b + 1, 2 * gc:2 * gc + 2].rearrange("h i j -> i h j"), reads=[rS], writes=[Sin[k2]], owner=Sin[k2])
                            pt = nps()
                            cx.op("pe", lambda e, pt=pt, k2=k2: e.transpose(pt[:, 0:64], Sin[k2][:].rearrange("p a b -> p (a b)"), ident[0:64, 0:64]), reads=[Sin[k2], ident], writes=[pt])
                            for hh in range(2):
                                pb = 64 * hh
                                dv(lambda e, pt=pt, k2=k2, hh=hh, pb=pb: e.tensor_copy(Hsbm[k2][hh][pb:pb + 64, :], pt[pb:pb + 64, 0:64]), [pt], [Hsbm[k2][hh]])
                            dv(lambda e, b=b: e.tensor_tensor(am[:], aTt[:, cp, :], blkrow[:, b, :], ALU.mult), [aTt, blkrow], [am])
                            for hh in range(2):
                                cx.op("pe", lambda e, hh=hh, k2=k2, b=b: e.matmul(pxs[hh][:, 0:64], am[:], Hsbm[k2][hh][:], start=False, stop=(b == 15)),
                                      reads=[am, Hsbm[k2][hh]], writes=[pxs[hh]])
                        for hh in range(2):
                            dv(lambda e, hh=hh: e.tensor_copy(Xb[:, hh * 64:(hh + 1) * 64], pxs[hh][:, 0:64]), [pxs[hh]], [Xb])
                        pu = nps()
                        for hh in range(2):
                            mmg(pu[:, hh * 64:(hh + 1) * 64], [(Tt[:, hh, :], Xb[:, hh * 64:(hh + 1) * 64])], [Tt, Xb], pu)
                        ac(lambda e, pu=pu: e.activation(Ub[:], pu[:, 0:128], AF.Copy), [pu], [Ub])
                        for hh in range(2):
                            vh = vtk[:, cp * 128 + hh * 64:cp * 128 + (hh + 1) * 64]
                            cx.op("pe", lambda e, hh=hh: e.matmul(pys[hh][:, 0:64], ArbT[:, hh, :], Ub[:, hh * 64:(hh + 1) * 64], start=True, stop=False), reads=[ArbT, Ub], writes=[pys[hh]])
                            cx.op("pe", lambda e, hh=hh, vh=vh: e.matmul(pys[hh][:, 0:64], ArkT[:, hh, :], vh, start=False, stop=False), reads=[ArkT, vtk], writes=[pys[hh]])
                        for b in range(16):
                            k2 = ssi[0] % 3; ssi[0] += 1
                            cx.dma("sp", Sin[k2][:], rS[b + 1, 2 * gc:2 * gc + 2].rearrange("h i j -> i h j"), reads=[rS], writes=[Sin[k2]], owner=Sin[k2])
                            pt = nps()
                            cx.op("pe", lambda e, pt=pt, k2=k2: e.transpose(pt[:, 0:64], Sin[k2][:].rearrange("p a b -> p (a b)"), ident[0:64, 0:64]), reads=[Sin[k2], ident], writes=[pt])
                            for hh in range(2):
                                pb = 64 * hh
                                dv(lambda e, pt=pt, k2=k2, hh=hh, pb=pb: e.tensor_copy(Hsbm[k2][hh][pb:pb + 64, :], pt[pb:pb + 64, 0:64]), [pt], [Hsbm[k2][hh]])
                            ac(lambda e, pt=pt, k2=k2: e.activation(Hs[k2][:], pt[:, 0:64], AF.Copy), [pt], [Hs[k2]])
                            dv(lambda e, b=b: e.tensor_tensor(rm[:], rTt[:, cp, :], blkrow[:, b, :], ALU.mult), [rTt, blkrow], [rm])
                            for hh in range(2):
                                cx.op("pe", lambda e, hh=hh, k2=k2, b=b: e.matmul(pys[hh][:, 0:64], rm[:], Hsbm[k2][hh][:], start=False, stop=(b == 15)),
                                      reads=[rm, Hsbm[k2][hh]], writes=[pys[hh]])
                            dv(lambda e, b=b, cs=cs: e.tensor_scalar_mul(bm[:], btk[:, cs], refs[:, 18 + b:19 + b]), [btk, refs], [bm])
                            po_(lambda e, b=b, cs=cs: e.tensor_scalar_mul(km[:], ktk[:, cs], refs[:, 18 + b:19 + b]), [ktk, refs], [km])
                            ph = nps()
                            mmg(ph[:, 0:128], [(bm[:], Ub[:]), (km[:], vtk[:, cs])], [bm, Ub, km, vtk], ph)
                            for hh in range(2):
                                pb = 64 * hh
                                dv(lambda e, pb=pb, ph=ph, k2=k2: e.tensor_tensor(Hs[k2][pb:pb + 64, :], ph[pb:pb + 64, pb:pb + 64], Hs[k2][pb:pb + 64, :], ALU.add), [ph, Hs[k2]], [Hs[k2]])
                            dv(lambda e, k2=k2, b=b: e.tensor_scalar_mul(Hs[k2][:], Hs[k2][:], dL[:, cp, b + 1:b + 2]), [Hs[k2], dL], [Hs[k2]])
                            pt2 = nps()
                            cx.op("pe", lambda e, pt2=pt2, k2=k2: e.transpose(pt2[0:64, 0:128], Hs[k2][:], ident[:]), reads=[Hs[k2], ident], writes=[pt2])
                            ac(lambda e, pt2=pt2, k2=k2: e.activation(Sout[k2][:].rearrange("p a b -> p (a b)"), pt2[0:64, 0:128], AF.Copy), [pt2], [Sout[k2]])
                            cx.dma("act", orS[b + 1, 2 * gc:2 * gc + 2].rearrange("h i j -> i h j"), Sout[k2][:], reads=[Sout[k2]], writes=[orS], owner=Sout[k2])
                        for hh in range(2):
                            dv(lambda e, hh=hh, cp=cp: e.tensor_copy(Y[:, cp * 128 + hh * 64:cp * 128 + (hh + 1) * 64], pys[hh][:, 0:64]), [pys[hh]], [Y])
                dv(lambda e: e.tensor_reduce(stt_[:, 2, :], h3(Y), AX.X, ALU.add), [Y], [stt_])
                dv(lambda e: e.tensor_scalar_mul(stt_[:, 2, :], stt_[:, 2, :], 1.0 / 64), [stt_], [stt_])
                dv(lambda e: e.tensor_tensor(h3(Y), h3(Y), bc16(stt_, 2), ALU.subtract), [Y, stt_], [Y])
                ac(lambda e: e.activation(TMP[:], Y[:], AF.Square), [Y], [TMP])
                dv(lambda e: e.tensor_reduce(stt_[:, 3, :], h3(TMP), AX.X, ALU.add), [TMP], [stt_])
                dv(lambda e: e.tensor_scalar(stt_[:, 3, :], stt_[:, 3, :], 1.0 / 64, LN_X_EPS, ALU.mult, ALU.add), [stt_], [stt_])
                ac(lambda e: e.activation(stt_[:, 3, :], stt_[:, 3, :], AF.Sqrt), [stt_], [stt_])
                dv(lambda e: e.reciprocal(stt_[:, 3, :], stt_[:, 3, :]), [stt_], [stt_])
                dv(lambda e: e.tensor_tensor(h3(Y), h3(Y), bc16(stt_, 3), ALU.mult), [Y, stt_], [Y])
                dv(lambda e: e.tensor_tensor(Y[:], Y[:], PRM["r_lnw"][:], ALU.mult), [Y, PRM["r_lnw"]], [Y])
                po_(lambda e: e.tensor_tensor(Y[:], Y[:], PRM["r_lnb"][:], ALU.add), [Y, PRM["r_lnb"]], [Y])
                dv(lambda e: e.tensor_tensor(h3(TMP), h3(Vb_), bc16(stt_, 1), ALU.mult), [Vb_, stt_], [TMP])
                po_(lambda e: e.tensor_tensor(Y[:], Y[:], TMP[:], ALU.add), [Y, TMP], [Y])
                dv(lambda e: e.tensor_tensor(Y[:], Y[:], Gg[:], ALU.mult), [Y, Gg], [Y])
                transpose_to(outT, lambda c: outT[:, half * 8 + c, r0:r0 + 128], Y, lambda c: Y[:, c * 128:(c + 1) * 128], 8)
            if s == 0 and i == NTP - 1:
                for gc in range(16):
                    k2 = ssi[0] % 3; ssi[0] += 1
                    pt2 = nps()
                    cx.op("pe", lambda e, pt2=pt2, gc=gc: e.transpose(pt2[0:64, 0:128], Hst[:, gc, :], ident[:]), reads=[Hst, ident], writes=[pt2])
                    ac(lambda e, pt2=pt2, k2=k2: e.activation(Sout[k2][:].rearrange("p a b -> p (a b)"), pt2[0:64, 0:128], AF.Copy), [pt2], [Sout[k2]])
                    cx.dma("act", orS[0, 2 * gc:2 * gc + 2].rearrange("h i j -> i h j"), Sout[k2][:], reads=[Sout[k2]], writes=[orS], owner=Sout[k2])

    stage_mod()
    hT = sb("hT", [128, 16, NTOK], BF16)
    stage_norm(xin, 0, W["norm_mix"][0], 0, 1, hT)
    stage_l0_proj(hT)
    stage_l0_rec(hT)
    proj_resid(hT, lambda c0, n: W["ab_w_out"][0, :, c0:c0 + n], xin, x1_d, 0, 2)
    stage_norm(x1_d, 0, W["norm_ffn"][0], 3, 4, hT)
    stage_ffn(hT, 0, x1_d, x2_d)
    stage_norm(x2_d, 1, W["norm_mix"][1], 0, 1, hT, h_dram=h_d, shift_out=osh)
    import os
    stage_l1_mix(hT)
    if os.environ.get("K_SKIP") != "proj":
        stage_l1_proj(hT)
    if os.environ.get("K_SKIP") not in ("rec", "proj"):
        stage_l1_rec(hT)
    proj_resid(hT, lambda c0, n: W["r_wo"][0, :, c0:c0 + n], x2_d, x3_d, 1, 2)
    stage_norm(x3_d, 1, W["norm_ffn"][1], 3, 4, hT)
    stage_ffn(hT, 1, x3_d, x4_d)
    stage_final(x4_d)
    cx.barrier()
    return nc


def LAYER0(L):
    cx, xin, x1_d, NTOK = L["cx"], L["xin"], L["x1_d"], L["NTOK"]
    cx.dma("sp", x1_d[:], xin[:], reads=[xin], writes=[x1_d], owner=x1_d)
    cx.barrier()


def LAYER1(L):
    cx, x2_d, x3_d = L["cx"], L["x2_d"], L["x3_d"]
    cx.dma("sp", x3_d[:], x2_d[:], reads=[x2_d], writes=[x3_d], owner=x3_d)
    cx.barrier()


def make_consts(NTP):
    c = {}
    s = np.arange(128)
    blk = s // 8
    le_p = (s[:, None] <= s[None, :]).astype(np.float32)
    same = (blk[:, None] == blk[None, :])
    le_s = (le_p * same).astype(np.float32)
    lt_p = (s[:, None] < s[None, :]).astype(np.float32)
    lt_s = (lt_p * same).astype(np.float32)
    c["ident"] = np.eye(128, dtype=np.float32)
    c["ones"] = np.ones((128, 128), np.float32)
    c["mle_p"], c["mle_s"], c["mlt_p"], c["mlt_s"] = le_p, le_s, lt_p, lt_s
    c["mgt_p"], c["mgt_s"] = lt_p.T.copy(), lt_s.T.copy()
    c["neg_p"] = ((le_p.T - 1.0) * 1e30).astype(np.float32)
    c["neg_s"] = ((le_s.T - 1.0) * 1e30).astype(np.float32)
    c["trir_p"] = (le_p - (s[:, None] <= 63).astype(np.float32)).astype(np.float32)
    ref_p = np.zeros((128, 34), np.float32); ref_p[:, 0] = (s <= 63); ref_p[:, 17] = (s > 63)
    ref_s = np.zeros((128, 34), np.float32)
    for b in range(16):
        ref_s[8 * b:8 * b + 8, 17 + b + 1] = 1.0
    c["ref_p"], c["ref_s"] = ref_p, ref_s
    blkt_p = np.zeros((NB, 128), np.float32); blkt_p[0] = 1.0
    blkt_s = np.zeros((NB, 128), np.float32)
    sel_p = np.zeros((128, NB), np.float32); sel_p[127, 0] = 1.0
    sel_s = np.zeros((128, NB), np.float32)
    for b in range(16):
        blkt_s[b + 1, 8 * b:8 * b + 8] = 1.0
        sel_s[8 * b + 7, b + 1] = 1.0
    c["blkt_p"], c["blkt_s"], c["selend_p"], c["selend_s"] = blkt_p, blkt_s, sel_p, sel_s
    NBP = 4; LC = 128 // NBP
    pb_ = s // LC
    samep = (pb_[:, None] == pb_[None, :])
    c["rle_p"] = (le_p * samep).astype(np.float32); c["rlt_p"] = (lt_p * samep).astype(np.float32)
    c["rgt_p"] = c["rlt_p"].T.copy()
    rb = np.zeros((128, NB), np.float32)
    rbr = np.zeros((128, NBP, 128), np.float32)
    for k in range(NBP):
        rb[k * LC:(k + 1) * LC, k] = 1.0
        rbr[:, k, k * LC:(k + 1) * LC] = 1.0
    c["rblk_p"] = rb; c["rblkrow_p"] = rbr.reshape(128, NBP * 128)
    br = np.zeros((128, 16, 128), np.float32)
    for b in range(16):
        br[:, b, 8 * b:8 * b + 8] = 1.0
    c["blkrow"] = br.reshape(128, 16 * 128)
    return {"c_" + k: np.ascontiguousarray(v) for k, v in c.items()}


WNAMES = ["mod_w", "mod_b", "norm_mix", "norm_ffn", "ffn_w1", "ffn_w2", "final_norm", "ab_w_in", "ab_gate_b", "m_conv_w",
          "m_norm", "g_lb", "g_norm", "ab_w_out", "r_mu", "r_w0", "r_w1", "r_w2", "r_a0", "r_a1", "r_a2", "r_g1", "r_g2",
          "r_kk", "r_ka", "r_rk", "r_wr", "r_wk", "r_wv", "r_wo", "r_lnw", "r_lnb"]
_NC_CACHE = {}


def core_inputs(inp, core, NTP, consts):
    b = core // 2
    T = NTP * 128
    f = lambda a: np.ascontiguousarray(np.asarray(a, dtype=np.float32))
    sl = slice(16 * core, 16 * core + 16)
    m = {}
    m["xin"] = f(np.concatenate([inp["x_prompt"][b, :T], inp["x_sample"][sl].reshape(128, D)], axis=0))
    m["cin"] = f(np.concatenate([inp["c_prompt"][b:b + 1], inp["c_sample"][sl]], axis=0))

    def st(a):
        a = np.asarray(a)[0, sl]
        return f(np.concatenate([np.zeros((1,) + a.shape[1:], np.float32), a], axis=0))
    m["mC"] = st(inp["state_mlstm_C"]); m["mn"] = st(inp["state_mlstm_n"]); m["mm"] = st(inp["state_mlstm_m"])
    m["mconv"] = st(inp["state_mlstm_conv"]); m["gS"] = st(inp["state_hgrn_S"]); m["rS"] = st(inp["state_rwkv_S"])
    m["rsh"] = st(inp["state_rwkv_shift"])
    for n in WNAMES:
        a = f(inp[n])
        if n == "r_rk":
            a = a.reshape(1, D)
        m[n] = a
    m.update(consts)
    return m


def kernel(**inp):
    NTP = 16
    n = 8
    if NTP not in _NC_CACHE:
        _NC_CACHE[NTP] = build_nc(NTP)
    nc = _NC_CACHE[NTP]
    consts = make_consts(NTP)
    in_maps = [core_inputs(inp, c, NTP, consts) for c in range(n)]
    res = run_bass_kernel_spmd(nc, in_maps, core_ids=list(range(n)))
    R = res.results
    T = NTP * 128
    y_prompt = np.stack([R[2 * b]["y"][:T] for b in range(4)], axis=0)
    y_sample = np.concatenate([R[c]["y"][T:].reshape(16, 8, D) for c in range(n)], axis=0)

    def pst(k):
        return np.stack([R[2 * b][k][0] for b in range(4)], axis=0)[None]

    def sst(k):
        return np.concatenate([R[c][k][1:] for c in range(n)], axis=0)[None]
    keys = ["oC", "on", "om", "oconv", "oS", "orS", "osh"]
    outs = [y_prompt, y_sample] + [pst(k) for k in keys] + [sst(k) for k in keys]
    return tuple(np.ascontiguousarray(o, dtype=np.float32) for o in outs)
```

```python
import contextlib
import numpy as np
import concourse.bass as bass
import concourse.mybir as mybir
from concourse.bass_utils import run_bass_kernel_spmd

F32 = mybir.dt.float32
BF16 = mybir.dt.bfloat16
AF = mybir.ActivationFunctionType
ALU = mybir.AluOpType
AX = mybir.AxisListType

D = 2048
NB = 17
RMS_EPS = 1e-6
LN_X_EPS = 64e-5


class Res:
    __slots__ = ("name", "lw", "rd", "dsem")

    def __init__(self, name):
        self.name = name
        self.lw = None
        self.rd = {}
        self.dsem = {}


class Ctx:
    def __init__(self, nc):
        self.nc = nc
        self.eng = {"pe": nc.tensor, "act": nc.scalar, "dve": nc.vector, "pool": nc.gpsimd, "sp": nc.sync}
        self.sems = {}
        self.tot = {}
        self.isdma = {}
        self.seen = {e: {} for e in self.eng}
        for e in ("pe", "act", "dve", "pool"):
            self._newsem("E_" + e, False)
        self.free_dma = {"hw": [], "sw": []}
        self.ndma = 0

    def _newsem(self, key, isdma):
        self.sems[key] = self.nc.alloc_semaphore(name=key)
        self.tot[key] = 0
        self.isdma[key] = isdma
        return key

    def _dma_sem_for(self, res, q):
        kind = "sw" if q == "pool" else "hw"
        if kind not in res.dsem:
            if self.free_dma[kind]:
                res.dsem[kind] = self.free_dma[kind].pop()
            else:
                self.ndma += 1
                res.dsem[kind] = self._newsem("D%s%d" % (kind, self.ndma), True)
        return res.dsem[kind]

    def release(self, bufs):
        for b in bufs:
            r = b.r
            for kind, key in r.dsem.items():
                self.free_dma[kind].append(key)
            r.dsem = {}
            r.lw = None
            r.rd = {}

    def _need(self, e, deps):
        eng = self.eng[e]
        seen = self.seen[e]
        for key, val in deps:
            if self.isdma[key]:
                val = self.tot[key]
            elif key == "E_pe" and e == "pe":
                continue
            if seen.get(key, 0) >= val:
                continue
            eng.wait_ge(self.sems[key], val)
            seen[key] = val

    @staticmethod
    def _deps(reads, writes):
        deps = []
        for r in reads:
            if r.lw is not None:
                deps.append(r.lw)
        for w in writes:
            if w.lw is not None:
                deps.append(w.lw)
            deps.extend(w.rd.items())
        return deps

    @staticmethod
    def _commit(key, val, reads, writes):
        for w in writes:
            w.lw = (key, val)
            w.rd = {}
        for r in reads:
            if r in writes:
                continue
            if r.rd.get(key, 0) < val:
                r.rd[key] = val

    def op(self, e, fn, reads=(), writes=()):
        reads = [b.r for b in reads]
        writes = [b.r for b in writes]
        self._need(e, self._deps(reads, writes))
        inst = fn(self.eng[e])
        key = "E_" + e
        self.tot[key] += 1
        inst.then_inc(self.sems[key], 1)
        self._commit(key, self.tot[key], reads, writes)
        return inst

    def dma(self, q, out, in_, reads=(), writes=(), owner=None, **kw):
        reads = [b.r for b in reads]
        writes = [b.r for b in writes]
        self._need(q, self._deps(reads, writes))
        key = self._dma_sem_for(owner.r, q)
        inst = self.eng[q].dma_start(out=out, in_=in_, **kw)
        self.tot[key] += 16
        inst.then_inc(self.sems[key], 16)
        self._commit(key, self.tot[key], reads, writes)
        return inst

    def barrier(self):
        for e in self.eng:
            self._need(e, [(k, v) for k, v in self.tot.items() if v > 0])


class Buf:
    def __init__(self, t, name, shape):
        self.t = t
        self.r = Res(name)
        self.shape = list(shape)
        self.ps = int(np.prod(shape[1:]))

    def __getitem__(self, idx):
        return self.t[idx]

    def v(self, off, dims, p0=0, np_=128):
        return bass.AP(self.t, p0 * self.ps + off, [[self.ps, np_]] + [list(d) for d in dims])


class View:
    def __init__(self, parent, c0, n):
        self.parent = parent
        self.c0 = c0
        self.n = n
        self.r = parent.r

    def __getitem__(self, idx):
        if isinstance(idx, slice):
            assert idx == slice(None)
            return self.parent[:, self.c0:self.c0 + self.n]
        p, c = idx
        a = 0 if c.start is None else c.start
        b = self.n if c.stop is None else c.stop
        return self.parent[p, self.c0 + a:self.c0 + b]


def build_nc(NTP):
    NT = NTP + 1
    NP = NTP * 128
    NTOK = NT * 128
    nc = bass.Bass("TRN2", target_bir_lowering=False)
    cx = Ctx(nc)
    DT = {}

    def din(name, shape, dt=F32):
        b = Buf(nc.dram_tensor(name, list(shape), dt, kind="ExternalInput"), name, shape)
        DT[name] = b
        return b

    def dout(name, shape):
        b = Buf(nc.dram_tensor(name, list(shape), F32, kind="ExternalOutput"), name, shape)
        DT[name] = b
        return b

    def dscr(name, shape, dt=F32):
        return Buf(nc.dram_tensor(name, list(shape), dt), name, shape)

    xin = din("xin", [NTOK, D]); cin = din("cin", [NB, D])
    mC = din("mC", [NB, 4, 256, 256]); mn = din("mn", [NB, 4, 256]); mm = din("mm", [NB, 4])
    mconv = din("mconv", [NB, 3, D]); gS = din("gS", [NB, 8, 128, 128]); rS = din("rS", [NB, 32, 64, 64])
    rsh = din("rsh", [NB, D])
    W = {}
    for name, shape in [("mod_w", [2, D, 6 * D]), ("mod_b", [2, 6 * D]), ("norm_mix", [2, D]), ("norm_ffn", [2, D]),
                        ("ffn_w1", [2, D, 4 * D]), ("ffn_w2", [2, 4 * D, D]), ("final_norm", [D]),
                        ("ab_w_in", [1, D, 8200]), ("ab_gate_b", [1, 8]), ("m_conv_w", [1, 4, D]), ("m_norm", [1, 1024]),
                        ("g_lb", [2, 1024]), ("g_norm", [1, 1024]), ("ab_w_out", [1, D, D]),
                        ("r_mu", [1, 6, D]), ("r_w0", [1, D]), ("r_w1", [1, D, 96]), ("r_w2", [1, 96, D]),
                        ("r_a0", [1, D]), ("r_a1", [1, D, 96]), ("r_a2", [1, 96, D]), ("r_g1", [1, D, 256]),
                        ("r_g2", [1, 256, D]), ("r_kk", [1, D]), ("r_ka", [1, D]), ("r_rk", [1, D]),
                        ("r_wr", [1, D, D]), ("r_wk", [1, D, D]), ("r_wv", [1, D, D]), ("r_wo", [1, D, D]),
                        ("r_lnw", [1, D]), ("r_lnb", [1, D])]:
        W[name] = din(name, shape)
    CN = {}
    for name, shape in [("ident", [128, 128]), ("ones", [128, 128]), ("mle_p", [128, 128]), ("mle_s", [128, 128]),
                        ("mlt_p", [128, 128]), ("mlt_s", [128, 128]), ("mgt_p", [128, 128]), ("mgt_s", [128, 128]),
                        ("neg_p", [128, 128]), ("neg_s", [128, 128]), ("trir_p", [128, 128]),
                        ("ref_p", [128, 34]), ("ref_s", [128, 34]), ("blkt_p", [NB, 128]), ("blkt_s", [NB, 128]),
                        ("selend_p", [128, NB]), ("selend_s", [128, NB]), ("blkrow", [128, 16 * 128]),
                        ("rle_p", [128, 128]), ("rlt_p", [128, 128]), ("rgt_p", [128, 128]), ("rblk_p", [128, NB]),
                        ("rblkrow_p", [128, 4 * 128])]:
        CN[name] = din("c_" + name, shape)
    y = dout("y", [NTOK, D])
    oC = dout("oC", [NB, 4, 256, 256]); on = dout("on", [NB, 4, 256]); om = dout("om", [NB, 4])
    oconv = dout("oconv", [NB, 3, D]); oS = dout("oS", [NB, 8, 128, 128]); orS = dout("orS", [NB, 32, 64, 64])
    osh = dout("osh", [NB, D])
    mod_d = dscr("mod_d", [2, NB, 6 * D])
    ext_p = dscr("ext_p", [NP + 3, D]); ext_s = dscr("ext_s", [16, 11, D])
    z_d = dscr("z_d", [NTOK, 6152])
    x1_d = dscr("x1_d", [NTOK, D]); x2_d = dscr("x2_d", [NTOK, D]); x3_d = dscr("x3_d", [NTOK, D]); x4_d = dscr("x4_d", [NTOK, D])
    h_d = dscr("h_d", [NTOK, D])
    rz_d = dscr("rz_d", [NTOK, 6 * D])
    dummy = Buf(None, "dummy", [1, 1])

    stacks = [contextlib.ExitStack()]

    uid = [0]

    stage_bufs = [[]]

    def sb(name, shape, dt=F32):
        uid[0] += 1
        name = "%s_%d" % (name, uid[0])
        b = Buf(stacks[-1].enter_context(nc.sbuf_tensor(name, list(shape), dt)), name, shape)
        stage_bufs[-1].append(b)
        return b

    def staged(fn):
        def wrapper(*a, **k):
            stacks.append(contextlib.ExitStack())
            stage_bufs.append([])
            try:
                return fn(*a, **k)
            finally:
                cx.barrier()
                cx.release(stage_bufs.pop())
                stacks.pop().close()
                npsmod[0] = 8
        return wrapper

    ident = sb("ident", [128, 128]); ones = sb("ones", [128, 128])
    identb = sb("identb", [128, 128], BF16)
    PS = [Buf(nc.alloc_psum_tensor("ps%d" % i, [128, 512], F32), "ps%d" % i, [128, 512]) for i in range(8)]
    psi = [0]

    def nps():
        p = PS[psi[0] % npsmod[0]]
        psi[0] += 1
        return p

    npsmod = [8]

    acci = [0]

    def accps():
        p = PS[6 + acci[0] % 2]
        acci[0] += 1
        return p

    cx.dma("sp", ident[:], CN["ident"][:], writes=[ident], owner=ident)
    cx.dma("sp", ones[:], CN["ones"][:], writes=[ones], owner=ones)
    cx.op("dve", lambda e: e.tensor_copy(identb[:], ident[:]), reads=[ident], writes=[identb])

    evq = [0]

    def evac(out_ap, in_ap, reads, writes, scale=None):
        evq[0] += 1
        if evq[0] % 2 == 0:
            if scale is None:
                cx.op("act", lambda e: e.activation(out_ap, in_ap, AF.Copy), reads=reads, writes=writes)
            else:
                cx.op("act", lambda e: e.activation(out_ap, in_ap, AF.Copy, scale=float(scale)), reads=reads, writes=writes)
        else:
            if scale is None:
                cx.op("dve", lambda e: e.tensor_copy(out_ap, in_ap), reads=reads, writes=writes)
            else:
                cx.op("dve", lambda e: e.tensor_scalar(out_ap, in_ap, float(scale), None, ALU.mult), reads=reads, writes=writes)

    def transpose_to(dst, dst_ap_fn, src, src_ap_fn, nchunks, scale_fn=None, npart=128):
        for c0 in range(0, nchunks, 4):
            n = min(4, nchunks - c0)
            p = nps()
            for j in range(n):
                cx.op("pe", lambda e, j=j: e.transpose(p.v(j * 128, [[1, npart]]),
                                                      src_ap_fn(c0 + j), ident[0:npart, 0:npart]),
                      reads=[src, ident], writes=[p])
            for j in range(n):
                sc = None if scale_fn is None else scale_fn(c0 + j)
                evac(dst_ap_fn(c0 + j), p.v(j * 128, [[1, npart]]), [p], [dst], scale=sc)

    def rows_bcast(dst, src_dram_rows_fn, tile_is_sample, width, col0=0, q="sp"):
        if not tile_is_sample:
            cx.dma(q, dst[:, col0:col0 + width], src_dram_rows_fn(0).partition_broadcast(128), writes=[dst], owner=dst)
        else:
            for b in range(16):
                cx.dma(q, dst[8 * b:8 * b + 8, col0:col0 + width], src_dram_rows_fn(b + 1).partition_broadcast(8), writes=[dst], owner=dst)

    @staged
    def stage_mod():
        csb = sb("csb", [NB, D]); scT = sb("scT", [128, 16, NB], BF16)
        modsb = sb("modsb", [NB, 6 * D]); biasb = sb("biasb", [NB, 6 * D])
        wb = [sb("modw%d" % i, [128, 16, 512], BF16) for i in range(2)]
        cx.dma("sp", csb[:], cin[:], writes=[csb], owner=csb)
        cx.op("act", lambda e: e.activation(csb[:], csb[:], AF.Silu), reads=[csb], writes=[csb])
        for c0 in range(0, 16, 4):
            p = nps()
            for j in range(4):
                c = c0 + j
                cx.op("pe", lambda e, j=j, c=c: e.transpose(p.v(j * 32, [[1, NB]]), csb[:, c * 128:(c + 1) * 128], ident[0:NB, 0:NB]),
                      reads=[csb, ident], writes=[p])
            for j in range(4):
                evac(scT[:, c0 + j, :], p.v(j * 32, [[1, NB]]), [p], [scT])
        for l in range(2):
            cx.dma("sp", biasb[:], W["mod_b"][l].partition_broadcast(NB), writes=[biasb], owner=biasb)
            for cb in range(24):
                w = wb[cb % 2]
                cx.dma("pool", w[:], W["mod_w"][l, :, cb * 512:(cb + 1) * 512].rearrange("(kc p) n -> p kc n", p=128),
                       writes=[w], owner=w)
                p = nps()
                for kc in range(16):
                    cx.op("pe", lambda e, kc=kc: e.matmul(p[0:NB, :], scT[:, kc, :], w[:, kc, :], start=(kc == 0), stop=(kc == 15)),
                          reads=[scT, w], writes=[p])
                cx.op("dve", lambda e: e.tensor_tensor(modsb[:, cb * 512:(cb + 1) * 512], p[0:NB, :], biasb[:, cb * 512:(cb + 1) * 512], ALU.add),
                      reads=[p, biasb], writes=[modsb])
            cx.dma("sp", mod_d[l], modsb[:], reads=[modsb], writes=[mod_d], owner=modsb)
        cx.barrier()
        cx.release([csb, scT, modsb, biasb] + wb)
        return [csb, scT, modsb, biasb] + wb

    @staged
    def stage_norm(x_d, l, normw_ap, ish, isc, hT, h_dram=None, shift_out=None):
        G = [sb("nG%d" % i, [128, D]) for i in range(2)]; SH = [sb("nSH%d" % i, [128, D]) for i in range(2)]
        nw = sb("nnw", [128, D])
        xt = [sb("nxt%d" % i, [128, D]) for i in range(2)]; ht = [sb("nht%d" % i, [128, D]) for i in range(2)]
        junk = sb("njunk", [128, D]); ss = sb("nss", [128, 2])
        cx.dma("sp", nw[:], normw_ap.partition_broadcast(128), writes=[nw], owner=nw)
        for s in range(2):
            rows_bcast(G[s], lambda b: mod_d[l, b, isc * D:(isc + 1) * D], s == 1, D)
            rows_bcast(SH[s], lambda b: mod_d[l, b, ish * D:(ish + 1) * D], s == 1, D)
            cx.op("dve", lambda e, s=s: e.scalar_tensor_tensor(G[s][:], G[s][:], 1.0, nw[:], ALU.add, ALU.mult), reads=[G[s], nw], writes=[G[s]])
        for i in range(NT):
            s = 1 if i == NTP else 0
            x = xt[i % 2]; h = ht[i % 2]
            cx.dma("sp", x[:], x_d[i * 128:(i + 1) * 128, :], reads=[x_d], writes=[x], owner=x)
            cx.op("act", lambda e: e.activation(junk[:], x[:], AF.Square, accum_out=ss[:, 0:1]), reads=[x], writes=[junk, ss])
            cx.op("dve", lambda e: e.tensor_scalar(ss[:, 1:2], ss[:, 0:1], 1.0 / D, RMS_EPS, ALU.mult, ALU.add), reads=[ss], writes=[ss])
            cx.op("act", lambda e: e.activation(ss[:, 1:2], ss[:, 1:2], AF.Sqrt), reads=[ss], writes=[ss])
            cx.op("dve", lambda e: e.reciprocal(ss[:, 1:2], ss[:, 1:2]), reads=[ss], writes=[ss])
            cx.op("dve", lambda e: e.scalar_tensor_tensor(h[:], x[:], ss[:, 1:2], G[s][:], ALU.mult, ALU.mult), reads=[x, ss, G[s]], writes=[h])
            if SH is not None:
                cx.op("dve", lambda e: e.tensor_tensor(h[:], h[:], SH[s][:], ALU.add), reads=[h, SH[s]], writes=[h])
            if h_dram is not None:
                cx.dma("sp", h_dram[i * 128:(i + 1) * 128, :], h[:], reads=[h], writes=[h_dram], owner=h)
            if shift_out is not None:
                if s == 0 and i == NTP - 1:
                    cx.dma("sp", shift_out[0:1, :], h[127:128, :], reads=[h], writes=[shift_out], owner=h)
                if s == 1:
                    for b in range(16):
                        cx.dma("sp", shift_out[b + 1:b + 2, :], h[8 * b + 7:8 * b + 8, :], reads=[h], writes=[shift_out], owner=h)
            transpose_to(hT, lambda c: hT[:, c, i * 128:(i + 1) * 128], h, lambda c: h[:, c * 128:(c + 1) * 128], 16)
        cx.barrier()
        tmp = G + SH + [nw, junk, ss] + xt + ht
        cx.release(tmp)

    @staged
    def proj_tok(aT, blocks, K=16, kpart=128, bias_fn=None, sigmoid=False):
        wb = [sb("pw%d" % i, [128, K, 512], BF16) for i in range(2)]
        ob = [sb("po%d" % i, [128, 512]) for i in range(3)]
        bb_ = [sb("pb%d" % i, [128, 512]) for i in range(2)] if bias_fn is not None else None
        oi = 0
        for bi, (w_ap, dst_fn) in enumerate(blocks):
            n = w_ap.shape[-1]
            w = wb[bi % 2]
            wv = w_ap.rearrange("(kc p) n -> p kc n", p=kpart)
            for k0 in range(0, K, 4):
                k1 = min(K, k0 + 4)
                cx.dma("pool", w[0:kpart, k0:k1, 0:n], wv[:, k0:k1, :], writes=[w], owner=w)
            if bias_fn is not None:
                bt = bb_[bi % 2]
                cx.dma("sp", bt[:, 0:n], bias_fn(bi).partition_broadcast(128), writes=[bt], owner=bt)
            for i in range(NT):
                p = nps()
                for kc in range(K):
                    cx.op("pe", lambda e, kc=kc: e.matmul(p[:, 0:n], aT[0:kpart, kc, i * 128:(i + 1) * 128], w[0:kpart, kc, 0:n],
                                                          start=(kc == 0), stop=(kc == K - 1)), reads=[aT, w], writes=[p])
                o = ob[oi % 3]; oi += 1
                if bias_fn is None:
                    cx.op("dve", lambda e: e.tensor_copy(o[:, 0:n], p[:, 0:n]), reads=[p], writes=[o])
                else:
                    cx.op("dve", lambda e: e.tensor_tensor(o[:, 0:n], p[:, 0:n], bt[:, 0:n], ALU.add), reads=[p, bt], writes=[o])
                    if sigmoid:
                        cx.op("act", lambda e: e.activation(o[:, 0:n], o[:, 0:n], AF.Sigmoid), reads=[o], writes=[o])
                for (dbuf, dap, p0, p1) in dst_fn(i):
                    if p0 == "3d":
                        cx.dma("act", dap, o[:, 0:n], reads=[o], writes=[dbuf], owner=o)
                    else:
                        cx.dma("act", dap, o[p0:p1, 0:n], reads=[o], writes=[dbuf], owner=o)

    @staged
    def proj_resid(aT, w_fn, x_old, x_new, l, igate, K=16):
        wb = [sb("rw%d" % i, [128, K, 512], BF16) for i in range(2)]
        xb = [sb("rx%d" % i, [128, 512]) for i in range(3)]
        GT = [sb("rgt%d" % i, [128, D]) for i in range(2)]
        for s in range(2):
            rows_bcast(GT[s], lambda b: mod_d[l, b, igate * D:(igate + 1) * D], s == 1, D)
        oi = 0
        for cb in range(4):
            w = wb[cb % 2]
            cx.dma("pool", w[:], w_fn(cb * 512, 512).rearrange("(kc p) n -> p kc n", p=128), writes=[w], owner=w)
            for i in range(NT):
                s = 1 if i == NTP else 0
                xo = xb[oi % 3]; oi += 1
                cx.dma("sp", xo[:], x_old[i * 128:(i + 1) * 128, cb * 512:(cb + 1) * 512], reads=[x_old], writes=[xo], owner=xo)
                p = nps()
                for kc in range(K):
                    cx.op("pe", lambda e, kc=kc: e.matmul(p[:], aT[:, kc, i * 128:(i + 1) * 128], w[:, kc, :], start=(kc == 0), stop=(kc == K - 1)),
                          reads=[aT, w], writes=[p])
                t = sbtmp512[oi % 2]
                cx.op("dve", lambda e: e.tensor_tensor(t[:], p[:], GT[s][:, cb * 512:(cb + 1) * 512], ALU.mult), reads=[p, GT[s]], writes=[t])
                cx.op("dve", lambda e: e.tensor_tensor(xo[:], xo[:], t[:], ALU.add), reads=[xo, t], writes=[xo])
                cx.dma("act", x_new[i * 128:(i + 1) * 128, cb * 512:(cb + 1) * 512], xo[:], reads=[xo], writes=[x_new], owner=xo)
        cx.barrier()
        cx.release(wb + xb + GT)

    sbtmp512 = [sb("tmp512_%d" % i, [128, 512]) for i in range(2)]

    @staged
    def stage_ffn(hT, l, x_old, x_new):
        FB = 2048
        nfb = 4 * D // FB
        KC = FB // 128
        hidT = sb("hidT", [128, KC, NTOK], BF16)
        w1b = [sb("fw1_%d" % i, [128, 16, 256], BF16) for i in range(2)]
        w2b = [sb("fw2_%d" % i, [128, KC, 512], BF16) for i in range(2)]
        xb = [sb("fx%d" % i, [128, 512]) for i in range(4)]
        GTs = [[sb("fgt%d_%d" % (i, j), [128, 512]) for j in range(2)] for i in range(2)]
        groups = [(g, min(512, NTOK - g)) for g in range(0, NTOK, 512)]
        w1i = 0; w2i = 0; oi = 0
        for fb in range(nfb):
            for blk in range(FB // 256):
                w = w1b[w1i % 2]; w1i += 1
                c0 = fb * FB + blk * 256
                wv = W["ffn_w1"][l, :, c0:c0 + 256].rearrange("(kc p) n -> p kc n", p=128)
                for k0 in range(0, 16, 8):
                    cx.dma("pool", w[:, k0:k0 + 8, :], wv[:, k0:k0 + 8, :], writes=[w], owner=w)
                for oc in range(2):
                    for (g0, gn) in groups:
                        p = nps()
                        for kc in range(16):
                            cx.op("pe", lambda e, kc=kc: e.matmul(p[:, 0:gn], w[:, kc, oc * 128:(oc + 1) * 128], hT[:, kc, g0:g0 + gn],
                                                                  start=(kc == 0), stop=(kc == 15)), reads=[hT, w], writes=[p])
                        t = sbtmp512[oi % 2]; oi += 1
                        cx.op("act", lambda e: e.activation(t[:, 0:gn], p[:, 0:gn], AF.Relu), reads=[p], writes=[t])
                        cx.op("dve", lambda e: e.tensor_tensor(hidT[:, blk * 2 + oc, g0:g0 + gn], t[:, 0:gn], t[:, 0:gn], ALU.mult),
                              reads=[t], writes=[hidT])
            for cb in range(4):
                w = w2b[w2i % 2]; w2i += 1
                wv = W["ffn_w2"][l, fb * FB:(fb + 1) * FB, cb * 512:(cb + 1) * 512].rearrange("(kc p) n -> p kc n", p=128)
                for k0 in range(0, KC, 4):
                    cx.dma("pool", w[:, k0:k0 + 4, :], wv[:, k0:k0 + 4, :], writes=[w], owner=w)
                GT = GTs[(fb * 4 + cb) % 2]
                for s_ in range(2):
                    rows_bcast(GT[s_], lambda b: mod_d[l, b, 5 * D + cb * 512:5 * D + (cb + 1) * 512], s_ == 1, 512)
                for i in range(NT):
                    s_ = 1 if i == NTP else 0
                    xo = xb[oi % 4]; oi += 1
                    src = x_old if fb == 0 else x_new
                    cx.dma("sp", xo[:], src[i * 128:(i + 1) * 128, cb * 512:(cb + 1) * 512], reads=[src], writes=[xo], owner=xo)
                    p = nps()
                    for kc in range(KC):
                        cx.op("pe", lambda e, kc=kc: e.matmul(p[:], hidT[:, kc, i * 128:(i + 1) * 128], w[:, kc, :], start=(kc == 0), stop=(kc == KC - 1)),
                              reads=[hidT, w], writes=[p])
                    t = sbtmp512[oi % 2]
                    cx.op("dve", lambda e: e.tensor_tensor(t[:], p[:], GT[s_][:], ALU.mult), reads=[p, GT[s_]], writes=[t])
                    cx.op("dve", lambda e: e.tensor_tensor(xo[:], xo[:], t[:], ALU.add), reads=[xo, t], writes=[xo])
                    cx.dma("act", x_new[i * 128:(i + 1) * 128, cb * 512:(cb + 1) * 512], xo[:], reads=[xo], writes=[x_new], owner=xo)

    @staged
    def stage_final(x_d):
        nw = sb("fnw", [128, D]); xt = [sb("fnx%d" % i, [128, D]) for i in range(2)]
        junk = sb("fnj", [128, D]); ss = sb("fns", [128, 2])
        cx.dma("sp", nw[:], W["final_norm"][:].partition_broadcast(128), writes=[nw], owner=nw)
        for i in range(NT):
            x = xt[i % 2]
            cx.dma("sp", x[:], x_d[i * 128:(i + 1) * 128, :], reads=[x_d], writes=[x], owner=x)
            cx.op("act", lambda e: e.activation(junk[:], x[:], AF.Square, accum_out=ss[:, 0:1]), reads=[x], writes=[junk, ss])
            cx.op("dve", lambda e: e.tensor_scalar(ss[:, 1:2], ss[:, 0:1], 1.0 / D, RMS_EPS, ALU.mult, ALU.add), reads=[ss], writes=[ss])
            cx.op("act", lambda e: e.activation(ss[:, 1:2], ss[:, 1:2], AF.Sqrt), reads=[ss], writes=[ss])
            cx.op("dve", lambda e: e.reciprocal(ss[:, 1:2], ss[:, 1:2]), reads=[ss], writes=[ss])
            cx.op("dve", lambda e: e.scalar_tensor_tensor(x[:], x[:], ss[:, 1:2], nw[:], ALU.mult, ALU.mult), reads=[x, ss, nw], writes=[x])
            cx.dma("sp", y[i * 128:(i + 1) * 128, :], x[:], reads=[x], writes=[y], owner=x)
        cx.barrier()
        cx.release([nw, junk, ss] + xt)

    def stage_l0_proj(hT):
        Wi = W["ab_w_in"][0]
        cx.dma("sp", ext_p[0:3, :], mconv[0], reads=[mconv], writes=[ext_p], owner=ext_p)
        cx.dma("sp", ext_s[:, 0:3, :], mconv[1:17], reads=[mconv], writes=[ext_s], owner=ext_s)
        blocks = []

        def dst_ext(c0, n):
            def f(i):
                if i < NTP:
                    return [(ext_p, ext_p[3 + i * 128:3 + (i + 1) * 128, c0:c0 + n], 0, 128)]
                return [(ext_s, ext_s[:, 3:11, c0:c0 + n], "3d", n)]
            return f

        def dst_z(c0, n):
            return lambda i: [(z_d, z_d[i * 128:(i + 1) * 128, c0:c0 + n], 0, 128)]
        for c0 in range(0, 2048, 512):
            blocks.append((Wi[:, c0:c0 + 512], dst_ext(c0, 512)))
        for c0 in range(0, 2048, 512):
            blocks.append((Wi[:, 2048 + c0:2048 + c0 + 512], dst_z(c0, 512)))
        blocks.append((Wi[:, 4096:4104], dst_z(2048, 8)))
        for c0 in range(0, 4096, 512):
            blocks.append((Wi[:, 4104 + c0:4104 + c0 + 512], dst_z(2056 + c0, 512)))
        proj_tok(hT, blocks)

    def mmg(p_ap, pairs, reads, pbuf):
        n = len(pairs)
        for idx, (l_ap, r_ap) in enumerate(pairs):
            cx.op("pe", lambda e, l_ap=l_ap, r_ap=r_ap, idx=idx: e.matmul(p_ap, l_ap, r_ap, start=(idx == 0), stop=(idx == n - 1)),
                  reads=reads, writes=[pbuf])

    def rstd_col(dst, col, src_ap, inv_n, eps):
        cx.op("dve", lambda e: e.tensor_scalar(dst[:, col:col + 1], src_ap, float(inv_n), float(eps), ALU.mult, ALU.add), reads=[dst], writes=[dst])
        cx.op("act", lambda e: e.activation(dst[:, col:col + 1], dst[:, col:col + 1], AF.Sqrt), reads=[dst], writes=[dst])
        cx.op("dve", lambda e: e.reciprocal(dst[:, col:col + 1], dst[:, col:col + 1]), reads=[dst], writes=[dst])

    @staged
    def stage_l0_rec(mixT):
        npsmod[0] = 4
        dv = lambda fn, r, w: cx.op("dve", fn, reads=r, writes=w)
        ac = lambda fn, r, w: cx.op("act", fn, reads=r, writes=w)
        mle = [sb("mle%d" % i, [128, 128]) for i in range(2)]; neg = [sb("neg%d" % i, [128, 128]) for i in range(2)]
        trirp = sb("trirp", [128, 128]); ref = [sb("ref%d" % i, [128, 34]) for i in range(2)]
        blkt = [sb("blkt%d" % i, [NB, 128]) for i in range(2)]; selend = [sb("selend%d" % i, [128, NB]) for i in range(2)]
        blkrow = sb("blkrow", [128, 16, 128], BF16)
        for i, sfx in enumerate(["p", "s"]):
            cx.dma("sp", mle[i][:], CN["mle_" + sfx][:], writes=[mle[i]], owner=mle[i])
            cx.dma("sp", neg[i][:], CN["neg_" + sfx][:], writes=[neg[i]], owner=neg[i])
            cx.dma("sp", ref[i][:], CN["ref_" + sfx][:], writes=[ref[i]], owner=ref[i])
            cx.dma("sp", blkt[i][:], CN["blkt_" + sfx][:], writes=[blkt[i]], owner=blkt[i])
            cx.dma("sp", selend[i][:], CN["selend_" + sfx][:], writes=[selend[i]], owner=selend[i])
        cx.dma("sp", trirp[:], CN["trir_p"][:], writes=[trirp], owner=trirp)
        cx.dma("pool", blkrow[:], CN["blkrow"][:].rearrange("p (b t) -> p b t", b=16), writes=[blkrow], owner=blkrow)
        trir = [trirp, mle[1]]
        gb = sb("gb", [128, 8]); mgain = sb("mgain", [128, 1024]); ggain = sb("ggain", [128, 1024])
        LB = sb("LB", [128, 1024]); OMLB = sb("OMLB", [128, 1024])
        cx.dma("sp", gb[:], W["ab_gate_b"][0].partition_broadcast(128), writes=[gb], owner=gb)
        cx.dma("sp", mgain[:], W["m_norm"][0].partition_broadcast(128), writes=[mgain], owner=mgain)
        cx.dma("sp", ggain[:], W["g_norm"][0].partition_broadcast(128), writes=[ggain], owner=ggain)
        cx.dma("sp", LB[:], W["g_lb"][0].partition_broadcast(128), writes=[LB], owner=LB)
        cx.dma("sp", OMLB[:], W["g_lb"][1].partition_broadcast(128), writes=[OMLB], owner=OMLB)
        dv(lambda e: e.tensor_tensor(LB[:], LB[:], OMLB[:], ALU.subtract), [LB, OMLB], [LB])
        ac(lambda e: e.activation(LB[:], LB[:], AF.Sigmoid), [LB], [LB])
        dv(lambda e: e.tensor_scalar(OMLB[:], LB[:], -1.0, 1.0, ALU.mult, ALU.add), [LB], [OMLB])
        mst = sb("mst", [NB, 4]); mend = sb("mend", [NB, 4])
        cx.dma("sp", mst[:], mm[:], writes=[mst], owner=mst)
        Cst = sb("Cst", [128, 4, 2, 257]); Cb = sb("Cb", [128, 4, 2, 257], BF16)
        Sst = sb("Sst", [128, 8, 128]); Smid = sb("Smid", [128, 8, 128]); Smidb = sb("Smidb", [128, 8, 128], BF16)
        for h in range(4):
            for c in range(2):
                cx.dma("sp", Cst[:, h, c, 0:256], mC[0, h, c * 128:(c + 1) * 128, :], writes=[Cst], owner=Cst)
                cx.dma("sp", Cst[:, h, c, 256:257], mn[0, h, c * 128:(c + 1) * 128].rearrange("(p o) -> p o", o=1), writes=[Cst], owner=Cst)
        cx.dma("sp", Sst[:], gS[0].rearrange("h k v -> k h v"), writes=[Sst], owner=Sst)
        dv(lambda e: e.tensor_copy(Cb[:], Cst[:]), [Cst], [Cb])
        NSB = 4
        Cs = [sb("Cs%d" % i, [128, 2, 257]) for i in range(NSB)]; Csb = [sb("Csb%d" % i, [128, 2, 257], BF16) for i in range(NSB)]
        Ss = [sb("Ss%d" % i, [128, 128]) for i in range(NSB)]; Ssb = [sb("Ssb%d" % i, [128, 128], BF16) for i in range(NSB)]
        gsm = sb("gsm", [128, 64]); diag4 = sb("diag4", [128, 4, 128]); dtmp = sb("dtmp", [128, 4, 128])
        Rt = sb("Rt", [128, NB, 4]); bend = sb("bend", [128, NB * 4])
        CQ = 256
        taps = [sb("tap%d" % j, [128, CQ]) for j in range(4)]; cw = [sb("cw%d" % j, [128, CQ]) for j in range(4)]
        qk = sb("qk", [128, 2048]); qT = sb("qT", [128, 8, 128], BF16); kT = sb("kT", [128, 8, 128], BF16)
        khat = sb("khat", [128, 1024], BF16); khm = sb("khm", [128, 1024], BF16)
        zmv = sb("zmv", [128, 2048]); Vp = sb("Vp", [128, 4, 257], BF16); MG = sb("MG", [128, 1024])
        sTs = sb("sTs", [128, 128], BF16); qTm = sb("qTm", [128, 1, 128], BF16)
        hm = sb("hm", [128, 256]); hj = sb("hj", [128, 256]); st = sb("st", [128, 8])
        A = [View(qk, 0, 1024), View(qk, 1024, 1024)] + [sb("A%d" % i, [128, 1024]) for i in range(2, 6)]
        ktb = sb("ktb", [128, 1024], BF16); vb = sb("vb", [128, 1024], BF16)
        qtT = sb("qtT", [128, 8, 128], BF16); ktT = sb("ktT", [128, 8, 128], BF16)
        dec = sb("dec", [128, 8, 34]); ATs = sb("ATs", [128, 128], BF16)
        mixed = zmv
        cx.op("pool", lambda e: e.memset(Vp[:], 1.0), writes=[Vp])
        IG, LF, BB, GG_, CMX, MP, CM, AL, BE, MT, FL, T1, T2, AL16 = [slice(4 * k, 4 * k + 4) for k in range(14)]
        ssi = [0]

        for i in range(NT):
            s = 1 if i == NTP else 0
            r0 = i * 128
            cx.dma("sp", gsm[:, 0:8], z_d[r0:r0 + 128, 2048:2056], reads=[z_d], writes=[gsm], owner=gsm)
            dv(lambda e: e.tensor_tensor(gsm[:, 0:8], gsm[:, 0:8], gb[:], ALU.add), [gsm, gb], [gsm])
            ac(lambda e: e.activation(gsm[:, T1], gsm[:, LF], AF.Exp, scale=-1.0), [gsm], [gsm])
            ac(lambda e: e.activation(gsm[:, T1], gsm[:, T1], AF.Ln, bias=1.0), [gsm], [gsm])
            dv(lambda e: e.tensor_scalar_mul(gsm[:, LF], gsm[:, T1], -1.0), [gsm], [gsm])
            p = nps()
            mmg(p[:, 0:4], [(trir[1][:] if s else mle[0][:], gsm[:, LF])], [mle[s], gsm], p)
            dv(lambda e: e.tensor_copy(gsm[:, BB], p[:, 0:4]), [p], [gsm])
            dv(lambda e: e.tensor_tensor(gsm[:, GG_], gsm[:, IG], gsm[:, BB], ALU.subtract), [gsm], [gsm])
            for h in range(4):
                dv(lambda e, h=h: e.tensor_scalar_mul(diag4[:, h, :], ident[:], gsm[:, 12 + h:13 + h]), [ident, gsm], [diag4])
            p = nps()
            for h in range(4):
                mmg(p[:, h * 128:(h + 1) * 128], [(ones[:], diag4[:, h, :])], [ones, diag4], p)
            dv(lambda e: e.tensor_tensor(dtmp[:], p[:].rearrange("p (a b) -> p a b", a=4), neg[s].v(0, [[0, 4], [1, 128]]), ALU.add), [p, neg[s]], [dtmp])
            dv(lambda e: e.tensor_reduce(gsm[:, CMX], dtmp[:], AX.X, ALU.max), [dtmp], [gsm])
            p = nps()
            mmg(p[:, 0:4], [(blkt[s][:], mst[:])], [blkt[s], mst], p)
            dv(lambda e: e.tensor_copy(gsm[:, MP], p[:, 0:4]), [p], [gsm])
            dv(lambda e: e.tensor_tensor(gsm[:, CM], gsm[:, CMX], gsm[:, MP], ALU.max), [gsm], [gsm])
            dv(lambda e: e.tensor_tensor(gsm[:, T1], gsm[:, GG_], gsm[:, MP], ALU.subtract), [gsm], [gsm])
            ac(lambda e: e.activation(gsm[:, AL], gsm[:, T1], AF.Exp), [gsm], [gsm])
            dv(lambda e: e.tensor_tensor(gsm[:, T1], gsm[:, MP], gsm[:, CM], ALU.subtract), [gsm], [gsm])
            ac(lambda e: e.activation(gsm[:, BE], gsm[:, T1], AF.Exp), [gsm], [gsm])
            dv(lambda e: e.tensor_tensor(gsm[:, MT], gsm[:, BB], gsm[:, CM], ALU.add), [gsm], [gsm])
            ac(lambda e: e.activation(gsm[:, FL], gsm[:, MT], AF.Exp, scale=-1.0), [gsm], [gsm])
            dv(lambda e: e.tensor_scalar_mul(gsm[:, AL16], gsm[:, AL], 0.0625), [gsm], [gsm])
            p = nps()
            mmg(p[0:NB, 0:4], [(selend[s][:], gsm[:, MT])], [selend[s], gsm], p)
            dv(lambda e: e.tensor_copy(mend[:], p[0:NB, 0:4]), [p], [mend])
            if s == 0:
                dv(lambda e: e.tensor_copy(mst[0:1, :], mend[0:1, :]), [mend], [mst])
                if i == NTP - 1:
                    cx.dma("act", om[0:1, :], mend[0:1, :], reads=[mend], writes=[om], owner=mend)
            else:
                cx.dma("act", om[1:NB, :], mend[1:NB, :], reads=[mend], writes=[om], owner=mend)
            dv(lambda e: e.tensor_tensor(Rt[:], selend[s].v(0, [[1, NB], [0, 4]]), gsm.v(32, [[0, NB], [1, 4]]), ALU.mult), [selend[s], gsm], [Rt])
            p = nps()
            mmg(p[:, 0:NB * 4], [(ones[:], Rt[:].rearrange("p a b -> p (a b)"))], [ones, Rt], p)
            dv(lambda e: e.tensor_copy(bend[:], p[:, 0:NB * 4]), [p], [bend])
            for qd in range(2048 // CQ):
                c0 = qd * CQ
                for j in range(4):
                    if s == 0:
                        cx.dma("sp", taps[j][:], ext_p[r0 + j:r0 + j + 128, c0:c0 + CQ], reads=[ext_p], writes=[taps[j]], owner=taps[j])
                    else:
                        cx.dma("sp", taps[j][:], ext_s[:, j:j + 8, c0:c0 + CQ], reads=[ext_s], writes=[taps[j]], owner=taps[j])
                    cx.dma("sp", cw[j][:], W["m_conv_w"][0, j, c0:c0 + CQ].partition_broadcast(128), writes=[cw[j]], owner=cw[j])
                for j in range(4):
                    cx.op("pool" if j % 2 else "dve", lambda e, j=j: e.tensor_tensor(taps[j][:], taps[j][:], cw[j][:], ALU.mult), reads=[taps[j], cw[j]], writes=[taps[j]])
                dv(lambda e: e.tensor_tensor(taps[0][:], taps[0][:], taps[1][:], ALU.add), [taps[0], taps[1]], [taps[0]])
                cx.op("pool", lambda e: e.tensor_tensor(taps[2][:], taps[2][:], taps[3][:], ALU.add), reads=[taps[2], taps[3]], writes=[taps[2]])
                dv(lambda e: e.tensor_tensor(taps[0][:], taps[0][:], taps[2][:], ALU.add), [taps[0], taps[2]], [taps[0]])
                ac(lambda e, c0=c0: e.activation(qk[:, c0:c0 + CQ], taps[0][:], AF.Silu), [taps[0]], [qk])
            transpose_to(qT, lambda c: qT[:, c, :], qk, lambda c: qk[:, c * 128:(c + 1) * 128], 8)
            transpose_to(kT, lambda c: kT[:, c, :], qk, lambda c: qk[:, 1024 + c * 128:1024 + (c + 1) * 128], 8, scale_fn=lambda c: 0.0625)
            for h in range(4):
                dv(lambda e, h=h: e.tensor_scalar_mul(khat[:, h * 256:(h + 1) * 256], qk[:, 1024 + h * 256:1024 + (h + 1) * 256], gsm[:, 52 + h:53 + h]),
                   [qk, gsm], [khat])
            cx.dma("sp", zmv[:], z_d[r0:r0 + 128, 0:2048], reads=[z_d], writes=[zmv], owner=zmv)
            dv(lambda e: e.tensor_copy(Vp.v(0, [[257, 4], [1, 256]]), zmv.v(0, [[256, 4], [1, 256]])), [zmv], [Vp])
            ac(lambda e: e.activation(MG[:], zmv[:, 1024:2048], AF.Sigmoid), [zmv], [MG])
            dv(lambda e: e.tensor_tensor(MG[:], MG[:], mgain[:], ALU.mult), [MG, mgain], [MG])
            for h in range(4):
                p = nps()
                mmg(p[:, 0:128], [(kT[:, 2 * h + c, :], qT[:, 2 * h + c, :]) for c in range(2)], [kT, qT], p)
                dv(lambda e, h=h, p=p: e.scalar_tensor_tensor(sTs[:], p[:, 0:128], gsm[:, 28 + h:29 + h], mle[s][:], ALU.mult, ALU.mult), [p, gsm, mle[s]], [sTs])
                nd = accps()
                if s == 0:
                    pairs = [(sTs[:], Vp[:, h, :])] + [(qT[:, 2 * h + c, :], Cb[:, h, c, :]) for c in range(2)]
                    mmg(nd[:, 0:257], pairs, [sTs, Vp, qT, Cb], nd)
                    for c in range(2):
                        pu = nps()
                        mmg(pu[:, 0:257], [(khat[:, h * 256 + c * 128:h * 256 + (c + 1) * 128], Vp[:, h, :])], [khat, Vp], pu)
                        dv(lambda e, h=h, c=c, pu=pu: e.tensor_tensor(Cst[:, h, c, :], pu[:, 0:257], Cst[:, h, c, :], ALU.add), [pu, Cst], [Cst])
                        dv(lambda e, h=h, c=c: e.tensor_scalar_mul(Cst[:, h, c, :], Cst[:, h, c, :], bend[:, h:h + 1]), [Cst, bend], [Cst])
                else:
                    cx.op("pe", lambda e, h=h: e.matmul(nd[:, 0:257], sTs[:], Vp[:, h, :], start=True, stop=False), reads=[sTs, Vp], writes=[nd])
                    for c in range(2):
                        pass
                    for b in range(16):
                        Cq = Cs[ssi[0] % NSB]; Cqb = Csb[ssi[0] % NSB]; ssi[0] += 1
                        cx.dma("sp", Cq[:, :, 0:256], mC[b + 1, h].rearrange("(c p) e -> p c e", p=128), reads=[mC], writes=[Cq], owner=Cq)
                        cx.dma("sp", Cq[:, :, 256:257], mn[b + 1, h].rearrange("(c p o) -> p c o", p=128, o=1), reads=[mn], writes=[Cq], owner=Cq, allow_slow_non_contiguous=True)
                        cx.op("pool", lambda e, Cq=Cq, Cqb=Cqb: e.tensor_copy(Cqb[:], Cq[:]), reads=[Cq], writes=[Cqb])
                        dv(lambda e, b=b: e.tensor_scalar_mul(khm[:, h * 256:(h + 1) * 256], khat[:, h * 256:(h + 1) * 256], ref[1][:, 18 + b:19 + b]), [khat, ref[1]], [khm])
                        for c in range(2):
                            dv(lambda e, c=c, b=b: e.tensor_tensor(qTm[:, 0, :], qT[:, 2 * h + c, :], blkrow[:, b, :], ALU.mult), [qT, blkrow], [qTm])
                            cx.op("pe", lambda e, c=c, b=b, Cqb=Cqb: e.matmul(nd[:, 0:257], qTm[:, 0, :], Cqb[:, c, :], start=False, stop=(b == 15 and c == 1)),
                                  reads=[qTm, Cqb], writes=[nd])
                        for c in range(2):
                            pu = nps()
                            mmg(pu[:, 0:257], [(khm[:, h * 256 + c * 128:h * 256 + (c + 1) * 128], Vp[:, h, :])], [khm, Vp], pu)
                            dv(lambda e, c=c, pu=pu, Cq=Cq: e.tensor_tensor(Cq[:, c, :], pu[:, 0:257], Cq[:, c, :], ALU.add), [pu, Cq], [Cq])
                            dv(lambda e, c=c, b=b, Cq=Cq: e.tensor_scalar_mul(Cq[:, c, :], Cq[:, c, :], bend[:, (b + 1) * 4 + h:(b + 1) * 4 + h + 1]), [Cq, bend], [Cq])
                            if c == 1:
                                cx.dma("act", oC[b + 1, h].rearrange("(c p) e -> p c e", p=128), Cq[:, :, 0:256], reads=[Cq], writes=[oC], owner=Cq)
                                cx.dma("act", on[b + 1, h].rearrange("(c p o) -> p c o", p=128, o=1), Cq[:, :, 256:257], reads=[Cq], writes=[on], owner=Cq, allow_slow_non_contiguous=True)
                dv(lambda e, nd=nd: e.tensor_copy(st[:, 0:1], nd[:, 256:257]), [nd], [st])
                dv(lambda e: e.tensor_scalar_mul(st[:, 6:7], st[:, 0:1], -1.0), [st], [st])
                dv(lambda e: e.tensor_tensor(st[:, 0:1], st[:, 0:1], st[:, 6:7], ALU.max), [st], [st])
                dv(lambda e, h=h: e.tensor_tensor(st[:, 0:1], st[:, 0:1], gsm[:, 32 + h:33 + h], ALU.mult), [st, gsm], [st])
                dv(lambda e, h=h: e.tensor_tensor(st[:, 0:1], st[:, 0:1], gsm[:, 40 + h:41 + h], ALU.max), [st, gsm], [st])
                dv(lambda e: e.reciprocal(st[:, 0:1], st[:, 0:1]), [st], [st])
                dv(lambda e, h=h: e.tensor_tensor(st[:, 0:1], st[:, 0:1], gsm[:, 32 + h:33 + h], ALU.mult), [st, gsm], [st])
                dv(lambda e, nd=nd: e.tensor_scalar_mul(hm[:], nd[:, 0:256], st[:, 0:1]), [nd, st], [hm])
                dv(lambda e: e.tensor_reduce(st[:, 1:2], hm[:], AX.X, ALU.add), [hm], [st])
                dv(lambda e: e.tensor_scalar_mul(st[:, 1:2], st[:, 1:2], 1.0 / 256), [st], [st])
                dv(lambda e: e.tensor_scalar(hm[:], hm[:], st[:, 1:2], None, ALU.subtract), [hm, st], [hm]) if False else \
                    dv(lambda e: e.tensor_scalar_sub(hm[:], hm[:], st[:, 1:2]), [hm, st], [hm])
                dv(lambda e: e.tensor_tensor(hj[:], hm[:], hm[:], ALU.mult), [hm], [hj])
                dv(lambda e: e.tensor_reduce(st[:, 2:3], hj[:], AX.X, ALU.add), [hj], [st])
                rstd_col(st, 3, st[:, 2:3], 1.0 / 256, RMS_EPS)
                dv(lambda e, h=h: e.scalar_tensor_tensor(mixed[:, h * 256:(h + 1) * 256], hm[:], st[:, 3:4], MG[:, h * 256:(h + 1) * 256], ALU.mult, ALU.mult),
                   [hm, st, MG], [mixed])
            if s == 0:
                dv(lambda e: e.tensor_copy(Cb[:], Cst[:]), [Cst], [Cb])
                if i == NTP - 1:
                    for h in range(4):
                        for c in range(2):
                            cx.dma("act", oC[0, h, c * 128:(c + 1) * 128, :], Cst[:, h, c, 0:256], reads=[Cst], writes=[oC], owner=Cst)
                            cx.dma("act", on[0, h, c * 128:(c + 1) * 128].rearrange("(p o) -> p o", o=1), Cst[:, h, c, 256:257], reads=[Cst], writes=[on], owner=Cst)
            zc = 2056
            for k in range(4):
                cx.dma("sp", A[k][:], z_d[r0:r0 + 128, zc + k * 1024:zc + (k + 1) * 1024], reads=[z_d], writes=[A[k]], owner=A[k])
            ac(lambda e: e.activation(A[1][:], A[1][:], AF.Sigmoid), [A[1]], [A[1]])
            dv(lambda e: e.tensor_tensor(A[1][:], A[1][:], OMLB[:], ALU.mult), [A[1], OMLB], [A[1]])
            dv(lambda e: e.tensor_tensor(A[1][:], A[1][:], LB[:], ALU.add), [A[1], LB], [A[1]])
            ac(lambda e: e.activation(A[4][:], A[1][:], AF.Ln), [A[1]], [A[4]])
            pb = [nps(), nps()]
            for hf in range(2):
                mmg(pb[hf][:], [(trir[s][:], A[4][:, hf * 512:(hf + 1) * 512])], [trir[s], A[4]], pb[hf])
            pd = nps()
            for h in range(8):
                mmg(pd[:, h * 34:(h + 1) * 34], [(A[4][:, h * 128:(h + 1) * 128], ref[s][:])], [A[4], ref[s]], pd)
            ac(lambda e: e.activation(dec[:].rearrange("p a b -> p (a b)"), pd[:, 0:272], AF.Exp), [pd], [dec])
            for hf in range(2):
                ac(lambda e, hf=hf: e.activation(A[5][:, hf * 512:(hf + 1) * 512], pb[hf][:], AF.Exp), [pb[hf]], [A[5]])
                ac(lambda e, hf=hf: e.activation(A[4][:, hf * 512:(hf + 1) * 512], pb[hf][:], AF.Exp, scale=-1.0), [pb[hf]], [A[4]])
            ac(lambda e: e.activation(A[0][:], A[0][:], AF.Silu), [A[0]], [A[0]])
            dv(lambda e: e.scalar_tensor_tensor(A[0][:], A[0][:], float(128 ** -0.5), A[5][:], ALU.mult, ALU.mult), [A[0], A[5]], [A[0]])
            dv(lambda e: e.tensor_scalar(A[1][:], A[1][:], -1.0, 1.0, ALU.mult, ALU.add), [A[1]], [A[1]])
            dv(lambda e: e.tensor_tensor(A[1][:], A[1][:], A[4][:], ALU.mult), [A[1], A[4]], [A[1]])
            cx.op("pool", lambda e: e.tensor_copy(ktb[:], A[1][:]), reads=[A[1]], writes=[ktb])
            cx.op("pool", lambda e: e.tensor_copy(vb[:], A[2][:]), reads=[A[2]], writes=[vb])
            ac(lambda e: e.activation(A[3][:], A[3][:], AF.Silu), [A[3]], [A[3]])
            dv(lambda e: e.tensor_tensor(A[3][:], A[3][:], ggain[:], ALU.mult), [A[3], ggain], [A[3]])
            transpose_to(qtT, lambda c: qtT[:, c, :], A[0], lambda c: A[0][:, c * 128:(c + 1) * 128], 8)
            transpose_to(ktT, lambda c: ktT[:, c, :], A[1], lambda c: A[1][:, c * 128:(c + 1) * 128], 8)
            if s == 0:
                dv(lambda e: e.tensor_tensor(Smid[:], Sst[:], dec.v(0, [[34, 8], [0, 128]]), ALU.mult), [Sst, dec], [Smid])
                cx.op("pool", lambda e: e.tensor_copy(Smidb[:], Smid[:]), reads=[Smid], writes=[Smidb])
            for h in range(8):
                hc = slice(h * 128, (h + 1) * 128)
                p = nps()
                if s == 0:
                    mmg(p[0:64, 0:64], [(ktT[:, h, 0:64], qtT[:, h, 0:64])], [ktT, qtT], p)
                    mmg(p[:, 64:128], [(ktT[:, h, :], qtT[:, h, 64:128])], [ktT, qtT], p)
                    dv(lambda e, p=p: e.tensor_tensor(ATs[0:64, 0:64], p[0:64, 0:64], mle[s][0:64, 0:64], ALU.mult), [p, mle[s]], [ATs])
                    dv(lambda e, p=p: e.tensor_tensor(ATs[:, 64:128], p[:, 64:128], mle[s][:, 64:128], ALU.mult), [p, mle[s]], [ATs])
                    dv(lambda e: e.memset(ATs[64:128, 0:64], 0.0), [], [ATs])
                else:
                    mmg(p[:, 0:128], [(ktT[:, h, :], qtT[:, h, :])], [ktT, qtT], p)
                    dv(lambda e, p=p: e.tensor_tensor(ATs[:], p[:, 0:128], mle[s][:], ALU.mult), [p, mle[s]], [ATs])
                po = accps()
                if s == 0:
                    mmg(po[:, 0:128], [(ATs[:], vb[:, hc]), (qtT[:, h, :], Smidb[:, h, :])], [ATs, vb, qtT, Smidb], po)
                    pu = nps()
                    mmg(pu[:, 0:128], [(ktb[:, hc], vb[:, hc])], [ktb, vb], pu)
                    dv(lambda e, h=h, pu=pu: e.tensor_tensor(Sst[:, h, :], pu[:, 0:128], Smid[:, h, :], ALU.add), [pu, Smid], [Sst])
                    dv(lambda e, h=h: e.tensor_scalar_mul(Sst[:, h, :], Sst[:, h, :], dec[:, h, 17:18]), [Sst, dec], [Sst])
                else:
                    cx.op("pe", lambda e, hc=hc: e.matmul(po[:, 0:128], ATs[:], vb[:, hc], start=True, stop=False), reads=[ATs, vb], writes=[po])
                    for b in range(16):
                        Sq = Ss[ssi[0] % NSB]; Sqb = Ssb[ssi[0] % NSB]; ssi[0] += 1
                        cx.dma("sp", Sq[:], gS[b + 1, h], reads=[gS], writes=[Sq], owner=Sq)
                        cx.op("pool", lambda e, Sq=Sq, Sqb=Sqb: e.tensor_copy(Sqb[:], Sq[:]), reads=[Sq], writes=[Sqb])
                        dv(lambda e, b=b, h=h: e.tensor_tensor(qTm[:, 0, :], qtT[:, h, :], blkrow[:, b, :], ALU.mult), [qtT, blkrow], [qTm])
                        cx.op("pe", lambda e, b=b, Sqb=Sqb: e.matmul(po[:, 0:128], qTm[:, 0, :], Sqb[:], start=False, stop=(b == 15)), reads=[qTm, Sqb], writes=[po])
                        dv(lambda e, b=b, hc=hc: e.tensor_scalar_mul(khm[:, 0:128], ktb[:, hc], ref[1][:, 18 + b:19 + b]), [ktb, ref[1]], [khm])
                        pu = nps()
                        mmg(pu[:, 0:128], [(khm[:, 0:128], vb[:, hc])], [khm, vb], pu)
                        dv(lambda e, pu=pu, Sq=Sq: e.tensor_tensor(Sq[:], pu[:, 0:128], Sq[:], ALU.add), [pu, Sq], [Sq])
                        dv(lambda e, b=b, h=h, Sq=Sq: e.tensor_scalar_mul(Sq[:], Sq[:], dec[:, h, 17 + b + 1:17 + b + 2]), [Sq, dec], [Sq])
                        cx.dma("act", oS[b + 1, h], Sq[:], reads=[Sq], writes=[oS], owner=Sq)
                dv(lambda e, po=po: e.tensor_tensor(hj[:, 0:128], po[:, 0:128], po[:, 0:128], ALU.mult) if False else e.tensor_copy(hm[:, 0:128], po[:, 0:128]), [po], [hm])
                dv(lambda e: e.tensor_tensor(hj[:, 0:128], hm[:, 0:128], hm[:, 0:128], ALU.mult), [hm], [hj])
                dv(lambda e: e.tensor_reduce(st[:, 4:5], hj[:, 0:128], AX.X, ALU.add), [hj], [st])
                rstd_col(st, 5, st[:, 4:5], 1.0 / 128, RMS_EPS)
                dv(lambda e, hc=hc: e.scalar_tensor_tensor(mixed[:, 1024 + hc.start:1024 + hc.stop], hm[:, 0:128], st[:, 5:6], A[3][:, hc], ALU.mult, ALU.mult),
                   [hm, st, A[3]], [mixed])
            if s == 0 and i == NTP - 1:
                cx.dma("act", oS[0].rearrange("h k v -> k h v"), Sst[:], reads=[Sst], writes=[oS], owner=Sst)
            transpose_to(mixT, lambda c: mixT[:, c, r0:r0 + 128], mixed, lambda c: mixed[:, c * 128:(c + 1) * 128], 16)
        cx.dma("sp", oconv[0], ext_p[NP:NP + 3, :], reads=[ext_p], writes=[oconv], owner=oconv)
        cx.dma("sp", oconv[1:17], ext_s[:, 8:11, :], reads=[ext_s], writes=[oconv], owner=oconv)

    mix_d = dscr("mix_d", [6, 128, 16 * NTOK], BF16)

    @staged
    def stage_l1_mix(hT):
        shs = sb("shs", [NB, D]); shT = sb("shT", [128, 16, NB], BF16); muT = sb("muT", [128, 6, 16])
        xx = [sb("xx%d" % i, [128, NTOK], BF16) for i in range(2)]
        mo_ = [sb("mo%d" % i, [128, NTOK], BF16) for i in range(4)]
        cx.dma("sp", shs[:], rsh[:], writes=[shs], owner=shs)
        for c0 in range(0, 16, 4):
            p = nps()
            for j in range(4):
                c = c0 + j
                cx.op("pe", lambda e, j=j, c=c: e.transpose(p.v(j * 32, [[1, NB]]), shs[:, c * 128:(c + 1) * 128], ident[0:NB, 0:NB]),
                      reads=[shs, ident], writes=[p])
            for j in range(4):
                evac(shT[:, c0 + j, :], p.v(j * 32, [[1, NB]]), [p], [shT])
        for i in range(6):
            cx.dma("sp", muT[:, i, :], W["r_mu"][0, i].rearrange("(c p) -> p c", p=128), writes=[muT], owner=muT, allow_slow_non_contiguous=True)
        oi = 0
        for c in range(16):
            x = xx[c % 2]
            cx.op("dve", lambda e: e.tensor_tensor(x[:, 1:NTOK], hT[:, c, 0:NTOK - 1], hT[:, c, 1:NTOK], ALU.subtract), reads=[hT], writes=[x])
            cx.op("dve", lambda e: e.tensor_tensor(x[:, 0:1], shT[:, c, 0:1], hT[:, c, 0:1], ALU.subtract), reads=[hT, shT], writes=[x])
            cx.op("dve", lambda e: e.tensor_tensor(x.v(NP, [[8, 16]]), shT[:, c, 1:NB], hT.v(c * NTOK + NP, [[8, 16]]), ALU.subtract), reads=[hT, shT], writes=[x])
            for i in range(6):
                o = mo_[oi % 4]; oi += 1
                cx.op("dve", lambda e, i=i, o=o: e.scalar_tensor_tensor(o[:], x[:], muT[:, i, c:c + 1], hT[:, c, :], ALU.mult, ALU.add),
                      reads=[x, muT, hT], writes=[o])
                cx.dma("sp", mix_d[i, :, c * NTOK:(c + 1) * NTOK], o[:], reads=[o], writes=[mix_d], owner=o)

    def load_mix(i, hT):
        cx.dma("sp", hT[:].rearrange("p a b -> p (a b)"), mix_d[i], reads=[mix_d], writes=[hT], owner=hT)
        cx.barrier()

    @staged
    def lora1(aT, w1_ap, R, func, tT):
        w1 = sb("l1w", [128, 16, R], BF16)
        cx.dma("pool", w1[:], w1_ap.rearrange("(kc p) n -> p kc n", p=128), writes=[w1], owner=w1)
        for oc in range((R + 127) // 128):
            rc = min(128, R - oc * 128)
            for g0 in range(0, NTOK, 512):
                gn = min(512, NTOK - g0)
                p = nps()
                for kc in range(16):
                    cx.op("pe", lambda e, kc=kc: e.matmul(p[0:rc, 0:gn], w1[:, kc, oc * 128:oc * 128 + rc], aT[:, kc, g0:g0 + gn], start=(kc == 0), stop=(kc == 15)),
                          reads=[w1, aT], writes=[p])
                cx.op("act", lambda e: e.activation(tT[0:rc, oc, g0:g0 + gn], p[0:rc, 0:gn], func), reads=[p], writes=[tT])

    def stage_l1_proj(hT):
        def dst_rz(base):
            return lambda c0: (lambda i: [(rz_d, rz_d[i * 128:(i + 1) * 128, base + c0:base + c0 + 512], 0, 128)])
        def blocks_for(w, base):
            return [(w[:, c0:c0 + 512], dst_rz(base)(c0)) for c0 in range(0, D, 512)]
        stacks.append(contextlib.ExitStack()); stage_bufs.append([])
        tT = sb("tT", [128, 2, NTOK], BF16)
        load_mix(0, hT); proj_tok(hT, blocks_for(W["r_wr"][0], 0))
        load_mix(2, hT); proj_tok(hT, blocks_for(W["r_wk"][0], D))
        load_mix(3, hT); proj_tok(hT, blocks_for(W["r_wv"][0], 2 * D))
        load_mix(1, hT); lora1(hT, W["r_w1"][0], 96, AF.Tanh, tT)
        proj_tok(tT, blocks_for(W["r_w2"][0], 3 * D), K=1, kpart=96, bias_fn=lambda bi: W["r_w0"][0, bi * 512:(bi + 1) * 512], sigmoid=True)
        load_mix(4, hT); lora1(hT, W["r_a1"][0], 96, AF.Copy, tT)
        proj_tok(tT, blocks_for(W["r_a2"][0], 4 * D), K=1, kpart=96, bias_fn=lambda bi: W["r_a0"][0, bi * 512:(bi + 1) * 512], sigmoid=True)
        load_mix(5, hT); lora1(hT, W["r_g1"][0], 256, AF.Sigmoid, tT)
        proj_tok(tT, blocks_for(W["r_g2"][0], 5 * D), K=2, kpart=128)
        cx.barrier(); cx.release(stage_bufs.pop()); stacks.pop().close()

    @staged
    def stage_l1_rec(outT):
        npsmod[0] = 4
        dv = lambda fn, r, w: cx.op("dve", fn, reads=r, writes=w)
        ac = lambda fn, r, w: cx.op("act", fn, reads=r, writes=w)
        po_ = lambda fn, r, w: cx.op("pool", fn, reads=r, writes=w)
        mle = [sb("mle%d" % i, [128, 128]) for i in range(2)]; mlt = [sb("mlt%d" % i, [128, 128]) for i in range(2)]
        mgt = [sb("mgt%d" % i, [128, 128]) for i in range(2)]; refs = sb("refs", [128, 34])
        blkrow = sb("blkrow", [128, 16, 128], BF16)
        NBP = 4
        for i, (a_, b_, c_) in enumerate([("rle_p", "rlt_p", "rgt_p"), ("mle_s", "mlt_s", "mgt_s")]):
            cx.dma("sp", mle[i][:], CN[a_][:], writes=[mle[i]], owner=mle[i])
            cx.dma("sp", mlt[i][:], CN[b_][:], writes=[mlt[i]], owner=mlt[i])
            cx.dma("sp", mgt[i][:], CN[c_][:], writes=[mgt[i]], owner=mgt[i])
        cx.dma("sp", refs[:], CN["ref_s"][:], writes=[refs], owner=refs)
        rblk = sb("rblk", [128, NB]); rblkrow = sb("rblkrow", [128, NBP, 128], BF16)
        cx.dma("sp", rblk[:], CN["rblk_p"][:], writes=[rblk], owner=rblk)
        cx.dma("pool", rblkrow[:], CN["rblkrow_p"][:].rearrange("p (b t) -> p b t", b=NBP), writes=[rblkrow], owner=rblkrow)
        Hrd = [sb("Hrd%d" % i, [128, NBP, 64], BF16) for i in range(2)]
        rmk = sb("rmk", [128, NBP, 128], BF16)
        cx.dma("pool", blkrow[:], CN["blkrow"][:].rearrange("p (b t) -> p b t", b=16), writes=[blkrow], owner=blkrow)
        HW = 1024
        PRMH = [{k: sb("prm_%s%d" % (k, hf), [128, HW]) for k in ["r_kk", "r_ka", "r_rk"]} for hf in range(2)]
        for hf in range(2):
            for k_, b_ in PRMH[hf].items():
                cx.dma("sp", b_[:], W[k_][0, hf * HW:(hf + 1) * HW].partition_broadcast(128), writes=[b_], owner=b_)
        LNW = sb("prm_lnw", [128, HW]); LNB = sb("prm_lnb", [128, HW])
        Rb, Kb, Vb_, SW, Aa, Cc, KK, TMP, E1, E2 = [sb("rw%d" % i, [128, HW]) for i in range(10)]
        Gg = E2
        aTt, bTt, kTt, rTt = [sb("rt%d" % i, [128, 8, 128], BF16) for i in range(4)]
        btk, ktk, vtk = [sb("rk%d" % i, [128, HW], BF16) for i in range(3)]
        Hst = sb("Hst", [128, 16, 64]); Hbm = [sb("Hbm%d" % i, [128, 16, 64], BF16) for i in range(2)]
        bTm = [sb("bTm%d" % i, [128, 8, 128], BF16) for i in range(2)]; kTm = [sb("kTm%d" % i, [128, 8, 128], BF16) for i in range(2)]
        Hsbm = [[sb("Hsbm%d_%d" % (i, j), [128, 64], BF16) for j in range(2)] for i in range(3)]
        cx.op("dve", lambda e: e.memset(Hst[:], 0.0), writes=[Hst])
        for zb in Hbm + bTm + kTm + Hsbm[0] + Hsbm[1] + Hsbm[2] + Hrd:
            cx.op("dve", lambda e, zb=zb: e.memset(zb[:], 0.0), writes=[zb])
        Nb = [sb("Nb%d" % i, [128, 2, 128], BF16) for i in range(2)]; Mb = [sb("Mb%d" % i, [128, 2, 128], BF16) for i in range(2)]
        TtS = [sb("TtS%d" % i, [128, 2, 128], BF16) for i in range(3)]; AakS = [sb("AakS%d" % i, [128, 2, 128], BF16) for i in range(3)]
        ArbS = [sb("ArbS%d" % i, [128, 2, 128], BF16) for i in range(3)]; ArkS = [sb("ArkS%d" % i, [128, 2, 128], BF16) for i in range(3)]
        BSET = [dict(am=sb("am%d" % i, [128, 128], BF16), Xb=sb("Xbq%d" % i, [128, 128], BF16), Ubf=sb("Ubq%d" % i, [128, 128], BF16),
                     bm=sb("bmq%d" % i, [128, 128], BF16), km=sb("kmq%d" % i, [128, 128], BF16), rmk=sb("rmkq%d" % i, [128, 4, 128], BF16),
                     Hrd=[sb("Hrdq%d_%d" % (i, j), [128, 4, 64], BF16) for j in range(2)]) for i in range(2)]
        for i in range(2):
            for zb in BSET[i]["Hrd"]:
                cx.op("dve", lambda e, zb=zb: e.memset(zb[:], 0.0), writes=[zb])
        Tt, AakT, ArbT, ArkT = TtS[0], AakS[0], ArbS[0], ArkS[0]
        Xb = sb("Xb", [128, 128], BF16); Ub = sb("Ub", [128, 128], BF16); Ubf = Ub
        am = sb("am", [128, 128], BF16); rm = sb("rm", [128, 128], BF16); bm = sb("bm", [128, 128], BF16); km = sb("km", [128, 128], BF16)
        dL = sb("dL", [128, 8, NB]); stt_ = sb("stt", [128, 4, 16])
        NSB = 3
        Sin = [sb("Sin%d" % i, [64, 2, 64]) for i in range(NSB)]; Hs = [sb("Hs%d" % i, [128, 64]) for i in range(NSB)]
        Sout = [sb("Sout%d" % i, [64, 2, 64]) for i in range(NSB)]
        ssi = [0]
        h3 = lambda b_: b_[:].rearrange("p (h j) -> p h j", j=64)

        def prompt_pairs(half, ncp, Y):
            s = 0
            nit = 4

            def genA(cp, st):
                def pairmm(p, lT, rT_):
                    for hh in range(2):
                        lb = lT[hh] if isinstance(lT, list) else lT
                        rb = rT_[hh] if isinstance(rT_, list) else rT_
                        mmg(p[:, hh * 128:(hh + 1) * 128], [(lb[:, cp, :], rb[:, cp, :])], [lb, rb], p)

                def pairev(dst, p, mask):
                    dv(lambda e: e.tensor_tensor(dst[:], p[:, 0:256].rearrange("p (a b) -> p a b", a=2), mask.v(0, [[0, 2], [1, 128]]), ALU.mult), [p, mask], [dst])
                Tt_ = TtS[st]
                p = nps(); pairmm(p, aTt, bTm); pairev(Nb[0], p, mgt[s]); yield
                p = nps(); pairmm(p, bTm, aTt); pairev(Mb[0], p, mlt[s]); yield
                p = nps(); pairmm(p, kTm, aTt); pairev(AakS[st], p, mlt[s]); yield
                p = nps(); pairmm(p, bTm, rTt); pairev(ArbS[st], p, mle[s]); yield
                p = nps(); pairmm(p, kTm, rTt); pairev(ArkS[st], p, mle[s]); yield
                dv(lambda e: e.tensor_tensor(Tt_[:], Mb[0][:], identb.v(0, [[0, 2], [1, 128]]), ALU.add), [Mb[0], identb], [Tt_])
                cur = 0
                for it in range(nit):
                    nx = 1 - cur
                    p1 = nps(); p2 = nps()
                    for hh in range(2):
                        if it < nit - 1:
                            mmg(p1[:, hh * 128:(hh + 1) * 128], [(Nb[cur][:, hh, :], Mb[cur][:, hh, :])], [Nb[cur], Mb[cur]], p1)
                        mmg(p2[:, hh * 128:(hh + 1) * 128], [(Mb[cur][:, hh, :], Nb[cur][:, hh, :])], [Nb[cur], Mb[cur]], p2)
                    yield
                    if it < nit - 1:
                        ac(lambda e: e.activation(Mb[nx][:].rearrange("p a b -> p (a b)"), p1[:, 0:256], AF.Copy), [p1], [Mb[nx]])
                    dv(lambda e: e.tensor_copy(Nb[nx][:].rearrange("p a b -> p (a b)"), p2[:, 0:256]), [p2], [Nb[nx]])
                    p3 = nps()
                    for hh in range(2):
                        mmg(p3[:, hh * 128:(hh + 1) * 128], [(Nb[nx][:, hh, :], Tt_[:, hh, :])], [Nb[nx], Tt_], p3)
                    yield
                    dv(lambda e: e.tensor_tensor(Tt_[:].rearrange("p a b -> p (a b)"), p3[:, 0:256], Tt_[:].rearrange("p a b -> p (a b)"), ALU.add), [p3, Tt_], [Tt_])
                    cur = nx
                    yield

            def genB(cp, st, bs):
                gc = half * 8 + cp
                cs = slice(cp * 128, (cp + 1) * 128)
                Tt_, AakT_, ArbT_, ArkT_ = TtS[st], AakS[st], ArbS[st], ArkS[st]
                B_ = BSET[bs]
                am, Xb, Ubf, bm, km, rmk, Hrd = B_["am"], B_["Xb"], B_["Ubf"], B_["bm"], B_["km"], B_["rmk"], B_["Hrd"]
                px = pu = ph = PS[3 + 2 * bs]
                py = PS[4 + 2 * bs]
                for k in range(NBP):
                    dv(lambda e: e.tensor_tensor(am[:], aTt[:, cp, :], rblkrow[:, k, :], ALU.mult), [aTt, rblkrow], [am])
                    po_(lambda e: e.tensor_tensor(rmk[:, k, :], rTt[:, cp, :], rblkrow[:, k, :], ALU.mult), [rTt, rblkrow], [rmk])
                    Hsrc = [(Hbm[hh][:, gc, :], Hbm[hh]) if k == 0 else (Hrd[hh][:, k, :], Hrd[hh]) for hh in range(2)]
                    for hh in range(2):
                        mmg(px[:, hh * 64:(hh + 1) * 64], [(AakT_[:, hh, :], vtk[:, cp * 128 + hh * 64:cp * 128 + (hh + 1) * 64]),
                                                           (am[:], Hsrc[hh][0])], [AakT_, vtk, am, Hsrc[hh][1]], px)
                    yield
                    ac(lambda e: e.activation(Xb[:], px[:, 0:128], AF.Copy), [px], [Xb])
                    for hh in range(2):
                        mmg(pu[:, hh * 64:(hh + 1) * 64], [(Tt_[:, hh, :], Xb[:, hh * 64:(hh + 1) * 64])], [Tt_, Xb], pu)
                    yield
                    if k == 0:
                        dv(lambda e: e.tensor_scalar_mul(Ubf[:], pu[:, 0:128], rblk[:, k:k + 1]), [pu, rblk], [Ubf])
                    else:
                        dv(lambda e: e.scalar_tensor_tensor(Ubf[:], pu[:, 0:128], rblk[:, k:k + 1], Ubf[:], ALU.mult, ALU.add), [pu, rblk, Ubf], [Ubf])
                    ac(lambda e: e.activation(bm[:], btk[:, cs], AF.Copy, scale=rblk[:, k:k + 1]), [btk, rblk], [bm])
                    ac(lambda e: e.activation(km[:], ktk[:, cs], AF.Copy, scale=rblk[:, k:k + 1]), [ktk, rblk], [km])
                    mmg(ph[:, 0:128], [(bm[:], Ubf[:]), (km[:], vtk[:, cs])], [bm, Ubf, km, vtk], ph)
                    yield
                    for hh in range(2):
                        pb = 64 * hh
                        dv(lambda e: e.tensor_tensor(Hst[pb:pb + 64, gc, :], ph[pb:pb + 64, pb:pb + 64], Hst[pb:pb + 64, gc, :], ALU.add), [ph, Hst], [Hst])
                    ac(lambda e: e.activation(Hst[:, gc, :], Hst[:, gc, :], AF.Copy, scale=dL[:, cp, k:k + 1]), [Hst, dL], [Hst])
                    if k < NBP - 1:
                        for hh in range(2):
                            pb = 64 * hh
                            po_(lambda e: e.tensor_copy(Hrd[hh][pb:pb + 64, k + 1, :], Hst[pb:pb + 64, gc, :]), [Hst], [Hrd[hh]])
                    yield
                for hh in range(2):
                    vh = vtk[:, cp * 128 + hh * 64:cp * 128 + (hh + 1) * 64]
                    pairs = [(ArbT_[:, hh, :], Ubf[:, hh * 64:(hh + 1) * 64]), (ArkT_[:, hh, :], vh), (rmk[:, 0, :], Hbm[hh][:, gc, :])]
                    pairs += [(rmk[:, k, :], Hrd[hh][:, k, :]) for k in range(1, NBP)]
                    mmg(py[:, hh * 64:(hh + 1) * 64], pairs, [ArbT_, Ubf, ArkT_, vtk, rmk, Hbm[hh], Hrd[hh]], py)
                yield
                ac(lambda e: e.activation(Y[:, cs], py[:, 0:128], AF.Copy), [py], [Y])
                for hh in range(2):
                    pb = 64 * hh
                    po_(lambda e: e.tensor_copy(Hbm[hh][pb:pb + 64, gc, :], Hst[pb:pb + 64, gc, :]), [Hst], [Hbm[hh]])
                yield

            npsmod[0] = 3
            active = {}
            doneA = set(); doneB = set()
            nextA = 0; nextB = 0
            while len(doneB) < ncp:
                if nextA < ncp and "A" not in active and (nextA < 3 or (nextA - 3) in doneB):
                    active["A"] = (genA(nextA, nextA % 3), nextA); nextA += 1
                if nextB < ncp and nextB in doneA and ("B%d" % (nextB % 2)) not in active:
                    active["B%d" % (nextB % 2)] = (genB(nextB, nextB % 3, nextB % 2), nextB); nextB += 1
                for key in list(active):
                    g, idx = active[key]
                    try:
                        next(g)
                    except StopIteration:
                        del active[key]
                        (doneA if key == "A" else doneB).add(idx)
            npsmod[0] = 4

        def bc16(b_, col):
            return stt_.v(col * 16, [[1, 16], [0, 64]])

        import os
        for i in range(NT):
            s = 1 if i == NTP else 0
            if os.environ.get("K_REC") == "p" and s == 1:
                continue
            if os.environ.get("K_REC") == "s" and s == 0:
                continue
            r0 = i * 128
            nit = 2 if s else 4
            for half in range(2):
                f0 = half * HW
                PRM = dict(PRMH[half]); PRM["r_lnw"] = LNW; PRM["r_lnb"] = LNB
                for idx, b_ in enumerate([Rb, Kb, Vb_, SW, Aa]):
                    cx.dma("sp", b_[:], rz_d[r0:r0 + 128, idx * D + f0:idx * D + f0 + HW], reads=[rz_d], writes=[b_], owner=b_)
                cx.dma("sp", LNW[:], W["r_lnw"][0, f0:f0 + HW].partition_broadcast(128), writes=[LNW], owner=LNW)
                cx.dma("sp", LNB[:], W["r_lnb"][0, f0:f0 + HW].partition_broadcast(128), writes=[LNB], owner=LNB)
                dv(lambda e: e.tensor_scalar_mul(SW[:], SW[:], -0.6065306597126334), [SW], [SW])
                pc = [nps(), nps()]
                for hf in range(2):
                    mmg(pc[hf][:], [(mle[s][:], SW[:, hf * 512:(hf + 1) * 512])], [mle[s], SW], pc[hf])
                for hf in range(2):
                    evac(Cc[:, hf * 512:(hf + 1) * 512], pc[hf][:], [pc[hf]], [Cc])
                pd = nps()
                for cp in range(8):
                    mmg(pd[:, cp * NB:(cp + 1) * NB], [(SW[:, cp * 128:(cp + 1) * 128], refs[:, 17:34] if s else rblk[:])], [SW, refs, rblk], pd)
                ac(lambda e: e.activation(dL[:].rearrange("p a b -> p (a b)"), pd[:, 0:8 * NB], AF.Exp), [pd], [dL])
                dv(lambda e: e.tensor_tensor(KK[:], Kb[:], PRM["r_kk"][:], ALU.mult), [Kb, PRM["r_kk"]], [KK])
                ac(lambda e: e.activation(TMP[:], KK[:], AF.Square), [KK], [TMP])
                dv(lambda e: e.tensor_reduce(stt_[:, 0, :], h3(TMP), AX.X, ALU.add), [TMP], [stt_])
                dv(lambda e: e.tensor_scalar_max(stt_[:, 0, :], stt_[:, 0, :], 1e-24), [stt_], [stt_])
                ac(lambda e: e.activation(stt_[:, 0, :], stt_[:, 0, :], AF.Sqrt), [stt_], [stt_])
                dv(lambda e: e.reciprocal(stt_[:, 0, :], stt_[:, 0, :]), [stt_], [stt_])
                dv(lambda e: e.tensor_tensor(h3(KK), h3(KK), bc16(stt_, 0), ALU.mult), [KK, stt_], [KK])
                dv(lambda e: e.scalar_tensor_tensor(TMP[:], Aa[:], -1.0, PRM["r_ka"][:], ALU.add, ALU.mult), [Aa, PRM["r_ka"]], [TMP])
                dv(lambda e: e.scalar_tensor_tensor(Kb[:], TMP[:], 1.0, Kb[:], ALU.add, ALU.mult), [TMP, Kb], [Kb])
                po_(lambda e: e.tensor_tensor(TMP[:], Rb[:], Kb[:], ALU.mult), [Rb, Kb], [TMP])
                po_(lambda e: e.tensor_tensor(TMP[:], TMP[:], PRM["r_rk"][:], ALU.mult), [TMP, PRM["r_rk"]], [TMP])
                dv(lambda e: e.tensor_reduce(stt_[:, 1, :], h3(TMP), AX.X, ALU.add), [TMP], [stt_])
                dv(lambda e: e.tensor_tensor(Aa[:], KK[:], Aa[:], ALU.mult), [KK, Aa], [Aa])
                ac(lambda e: e.activation(E1[:], Cc[:], AF.Exp), [Cc], [E1])
                ac(lambda e: e.activation(E2[:], Cc[:], AF.Exp, scale=-1.0), [Cc], [E2])
                dv(lambda e: e.tensor_tensor(TMP[:], Cc[:], SW[:], ALU.subtract), [Cc, SW], [TMP])
                ac(lambda e: e.activation(TMP[:], TMP[:], AF.Exp), [TMP], [TMP])
                dv(lambda e: e.tensor_tensor(Rb[:], Rb[:], E1[:], ALU.mult), [Rb, E1], [Rb])
                dv(lambda e: e.scalar_tensor_tensor(KK[:], KK[:], -1.0, TMP[:], ALU.mult, ALU.mult), [KK, TMP], [KK])
                po_(lambda e: e.tensor_tensor(Aa[:], Aa[:], E2[:], ALU.mult), [Aa, E2], [Aa])
                dv(lambda e: e.tensor_tensor(Kb[:], Kb[:], E2[:], ALU.mult), [Kb, E2], [Kb])
                cx.dma("sp", Gg[:], rz_d[r0:r0 + 128, 5 * D + f0:5 * D + f0 + HW], reads=[rz_d], writes=[Gg], owner=Gg)
                ac(lambda e: e.activation(btk[:], Aa[:], AF.Copy), [Aa], [btk])
                ac(lambda e: e.activation(ktk[:], Kb[:], AF.Copy), [Kb], [ktk])
                ac(lambda e: e.activation(vtk[:], Vb_[:], AF.Copy), [Vb_], [vtk])
                transpose_to(aTt, lambda c: aTt[:, c, :], KK, lambda c: KK[:, c * 128:(c + 1) * 128], 8)
                transpose_to(bTt, lambda c: bTt[:, c, :], Aa, lambda c: Aa[:, c * 128:(c + 1) * 128], 8)
                transpose_to(kTt, lambda c: kTt[:, c, :], Kb, lambda c: Kb[:, c * 128:(c + 1) * 128], 8)
                transpose_to(rTt, lambda c: rTt[:, c, :], Rb, lambda c: Rb[:, c * 128:(c + 1) * 128], 8)
                for hh in range(2):
                    pb = 64 * hh
                    ac(lambda e, hh=hh, pb=pb: e.activation(bTm[hh][pb:pb + 64, :, :], bTt[pb:pb + 64, :, :], AF.Copy), [bTt], [bTm[hh]])
                    po_(lambda e, hh=hh, pb=pb: e.tensor_copy(kTm[hh][pb:pb + 64, :, :], kTt[pb:pb + 64, :, :]), [kTt], [kTm[hh]])
                Y = E1
                ncp = int(os.environ.get("K_NCP", "8"))
                if s == 0:
                    prompt_pairs(half, ncp, Y)
                for cp in range(ncp if s == 1 else 0):
                    gc = half * 8 + cp
                    cs = slice(cp * 128, (cp + 1) * 128)

                    def pairmm(p, lT, rT_):
                        for hh in range(2):
                            lb = lT[hh] if isinstance(lT, list) else lT
                            rb = rT_[hh] if isinstance(rT_, list) else rT_
                            mmg(p[:, hh * 128:(hh + 1) * 128], [(lb[:, cp, :], rb[:, cp, :])], [lb, rb], p)

                    def pairev(dst, p, mask):
                        dv(lambda e: e.tensor_tensor(dst[:], p[:, 0:256].rearrange("p (a b) -> p a b", a=2), mask.v(0, [[0, 2], [1, 128]]), ALU.mult), [p, mask], [dst])
                    p = nps(); pairmm(p, aTt, bTm); pairev(Nb[0], p, mgt[s])
                    p = nps(); pairmm(p, bTm, aTt); pairev(Mb[0], p, mlt[s])
                    p = nps(); pairmm(p, kTm, aTt); pairev(AakT, p, mlt[s])
                    p = nps(); pairmm(p, bTm, rTt); pairev(ArbT, p, mle[s])
                    p = nps(); pairmm(p, kTm, rTt); pairev(ArkT, p, mle[s])
                    dv(lambda e: e.tensor_tensor(Tt[:], Mb[0][:], identb.v(0, [[0, 2], [1, 128]]), ALU.add), [Mb[0], identb], [Tt])
                    cur = 0
                    for it in range(nit):
                        nx = 1 - cur
                        p1 = nps(); p2 = nps()
                        for hh in range(2):
                            if it < nit - 1:
                                mmg(p1[:, hh * 128:(hh + 1) * 128], [(Nb[cur][:, hh, :], Mb[cur][:, hh, :])], [Nb[cur], Mb[cur]], p1)
                            mmg(p2[:, hh * 128:(hh + 1) * 128], [(Mb[cur][:, hh, :], Nb[cur][:, hh, :])], [Nb[cur], Mb[cur]], p2)
                        if it < nit - 1:
                            ac(lambda e, p1=p1, nx=nx: e.activation(Mb[nx][:].rearrange("p a b -> p (a b)"), p1[:, 0:256], AF.Copy), [p1], [Mb[nx]])
                        dv(lambda e, p2=p2, nx=nx: e.tensor_copy(Nb[nx][:].rearrange("p a b -> p (a b)"), p2[:, 0:256]), [p2], [Nb[nx]])
                        p3 = nps()
                        for hh in range(2):
                            mmg(p3[:, hh * 128:(hh + 1) * 128], [(Nb[nx][:, hh, :], Tt[:, hh, :])], [Nb[nx], Tt], p3)
                        dv(lambda e, p3=p3: e.tensor_tensor(Tt[:].rearrange("p a b -> p (a b)"), p3[:, 0:256], Tt[:].rearrange("p a b -> p (a b)"), ALU.add), [p3, Tt], [Tt])
                        cur = nx
                    if s == 1:
                        pxs = [PS[4], PS[5]]; pys = [PS[6], PS[7]]
                        for hh in range(2):
                            cx.op("pe", lambda e, hh=hh: e.matmul(pxs[hh][:, 0:64], AakT[:, hh, :], vtk[:, cp * 128 + hh * 64:cp * 128 + (hh + 1) * 64], start=True, stop=False),
                                  reads=[AakT, vtk], writes=[pxs[hh]])
                    if s == 0:
                        for hh in range(2):
                            for k in range(1, NBP):
                                pass
                        for k in range(NBP):
                            dv(lambda e, k=k: e.tensor_tensor(am[:], aTt[:, cp, :], rblkrow[:, k, :], ALU.mult), [aTt, rblkrow], [am])
                            po_(lambda e, k=k: e.tensor_tensor(rmk[:, k, :], rTt[:, cp, :], rblkrow[:, k, :], ALU.mult), [rTt, rblkrow], [rmk])
                            Hsrc = [(Hbm[hh][:, gc, :], Hbm[hh]) if k == 0 else (Hrd[hh][:, k, :], Hrd[hh]) for hh in range(2)]
                            px = accps()
                            for hh in range(2):
                                mmg(px[:, hh * 64:(hh + 1) * 64], [(AakT[:, hh, :], vtk[:, cp * 128 + hh * 64:cp * 128 + (hh + 1) * 64]),
                                                                   (am[:], Hsrc[hh][0])], [AakT, vtk, am, Hsrc[hh][1]], px)
                            dv(lambda e, px=px: e.tensor_copy(Xb[:], px[:, 0:128]), [px], [Xb])
                            pu = nps()
                            for hh in range(2):
                                mmg(pu[:, hh * 64:(hh + 1) * 64], [(Tt[:, hh, :], Xb[:, hh * 64:(hh + 1) * 64])], [Tt, Xb], pu)
                            if k == 0:
                                dv(lambda e, pu=pu, k=k: e.tensor_scalar_mul(Ubf[:], pu[:, 0:128], rblk[:, k:k + 1]), [pu, rblk], [Ubf])
                            else:
                                dv(lambda e, pu=pu, k=k: e.scalar_tensor_tensor(Ubf[:], pu[:, 0:128], rblk[:, k:k + 1], Ubf[:], ALU.mult, ALU.add), [pu, rblk, Ubf], [Ubf])
                            dv(lambda e, k=k, cs=cs: e.tensor_scalar_mul(bm[:], btk[:, cs], rblk[:, k:k + 1]), [btk, rblk], [bm])
                            po_(lambda e, k=k, cs=cs: e.tensor_scalar_mul(km[:], ktk[:, cs], rblk[:, k:k + 1]), [ktk, rblk], [km])
                            ph = nps()
                            mmg(ph[:, 0:128], [(bm[:], Ubf[:]), (km[:], vtk[:, cs])], [bm, Ubf, km, vtk], ph)
                            for hh in range(2):
                                pb = 64 * hh
                                dv(lambda e, pb=pb, ph=ph: e.tensor_tensor(Hst[pb:pb + 64, gc, :], ph[pb:pb + 64, pb:pb + 64], Hst[pb:pb + 64, gc, :], ALU.add), [ph, Hst], [Hst])
                            dv(lambda e, k=k: e.tensor_scalar_mul(Hst[:, gc, :], Hst[:, gc, :], dL[:, cp, k:k + 1]), [Hst, dL], [Hst])
                            if k < NBP - 1:
                                for hh in range(2):
                                    pb = 64 * hh
                                    po_(lambda e, hh=hh, pb=pb, k=k: e.tensor_copy(Hrd[hh][pb:pb + 64, k + 1, :], Hst[pb:pb + 64, gc, :]), [Hst], [Hrd[hh]])
                        py = accps()
                        for hh in range(2):
                            vh = vtk[:, cp * 128 + hh * 64:cp * 128 + (hh + 1) * 64]
                            pairs = [(ArbT[:, hh, :], Ubf[:, hh * 64:(hh + 1) * 64]), (ArkT[:, hh, :], vh), (rmk[:, 0, :], Hbm[hh][:, gc, :])]
                            pairs += [(rmk[:, k, :], Hrd[hh][:, k, :]) for k in range(1, NBP)]
                            mmg(py[:, hh * 64:(hh + 1) * 64], pairs, [ArbT, Ubf, ArkT, vtk, rmk, Hbm[hh], Hrd[hh]], py)
                        dv(lambda e, py=py, cs=cs: e.tensor_copy(Y[:, cs], py[:, 0:128]), [py], [Y])
                        for hh in range(2):
                            pb = 64 * hh
                            po_(lambda e, hh=hh, pb=pb: e.tensor_copy(Hbm[hh][pb:pb + 64, gc, :], Hst[pb:pb + 64, gc, :]), [Hst], [Hbm[hh]])
                    else:
                        hs_list = []
                        for b in range(16):
                            k2 = ssi[0] % 3; ssi[0] += 1
                            cx.dma("sp", Sin[k2][:], rS[b + 1, 2 * gc:2 * gc + 2].rearrange("h i j -> i h j"), reads=[rS], writes=[Sin[k2]], owner=Sin[k2])
                            pt = nps()
                            cx.op("pe", lambda e, pt=pt, k2=k2: e.transpose(pt[:, 0:64], Sin[k2][:].rearrange("p a b -> p (a b)"), ident[0:64, 0:64]), reads=[Sin[k2], ident], writes=[pt])
                            for hh in range(2):
                                pb = 64 * hh
                                dv(lambda e, pt=pt, k2=k2, hh=hh, pb=pb: e.tensor_copy(Hsbm[k2][hh][pb:pb + 64, :], pt[pb:pb + 64, 0:64]), [pt], [Hsbm[k2][hh]])
                            dv(lambda e, b=b: e.tensor_tensor(am[:], aTt[:, cp, :], blkrow[:, b, :], ALU.mult), [aTt, blkrow], [am])
                            for hh in range(2):
                                cx.op("pe", lambda e, hh=hh, k2=k2, b=b: e.matmul(pxs[hh][:, 0:64], am[:], Hsbm[k2][hh][:], start=False, stop=(b == 15)),
                                      reads=[am, Hsbm[k2][hh]], writes=[pxs[hh]])
                        for hh in range(2):
                            dv(lambda e, hh=hh: e.tensor_copy(Xb[:, hh * 64:(hh + 1) * 64], pxs[hh][:, 0:64]), [pxs[hh]], [Xb])
                        pu = nps()
                        for hh in range(2):
                            mmg(pu[:, hh * 64:(hh + 1) * 64], [(Tt[:, hh, :], Xb[:, hh * 64:(hh + 1) * 64])], [Tt, Xb], pu)
                        ac(lambda e, pu=pu: e.activation(Ub[:], pu[:, 0:128], AF.Copy), [pu], [Ub])
                        for hh in range(2):
                            vh = vtk[:, cp * 128 + hh * 64:cp * 128 + (hh + 1) * 64]
                            cx.op("pe", lambda e, hh=hh: e.matmul(pys[hh][:, 0:64], ArbT[:, hh, :], Ub[:, hh * 64:(hh + 1) * 64], start=True, stop=False), reads=[ArbT, Ub], writes=[pys[hh]])
                            cx.op("pe", lambda e, hh=hh, vh=vh: e.matmul(pys[hh][:, 0:64], ArkT[:, hh, :], vh, start=False, stop=False), reads=[ArkT, vtk], writes=[pys[hh]])
                        for b in range(16):
                            k2 = ssi[0] % 3; ssi[0] += 1
                            cx.dma("sp", Sin[k2][:], rS[b + 1, 2 * gc:2 * gc + 2].rearrange("h i j -> i h j"), reads=[rS], writes=[Sin[k2]], owner=Sin[k2])
                            pt = nps()
                            cx.op("pe", lambda e, pt=pt, k2=k2: e.transpose(pt[:, 0:64], Sin[k2][:].rearrange("p a b -> p (a b)"), ident[0:64, 0:64]), reads=[Sin[k2], ident], writes=[pt])
                            for hh in range(2):
                                pb = 64 * hh
                                dv(lambda e, pt=pt, k2=k2, hh=hh, pb=pb: e.tensor_copy(Hsbm[k2][hh][pb:pb + 64, :], pt[pb:pb + 64, 0:64]), [pt], [Hsbm[k2][hh]])
                            ac(lambda e, pt=pt, k2=k2: e.activation(Hs[k2][:], pt[:, 0:64], AF.Copy), [pt], [Hs[k2]])
                            dv(lambda e, b=b: e.tensor_tensor(rm[:], rTt[:, cp, :], blkrow[:, b, :], ALU.mult), [rTt, blkrow], [rm])
                            for hh in range(2):
                                cx.op("pe", lambda e, hh=hh, k2=k2, b=b: e.matmul(pys[hh][:, 0:64], rm[:], Hsbm[k2][hh][:], start=False, stop=(b == 15)),
                                      reads=[rm, Hsbm[k2][hh]], writes=[pys[hh]])
                            dv(lambda e, b=b, cs=cs: e.tensor_scalar_mul(bm[:], btk[:, cs], refs[:, 18 + b:19 + b]), [btk, refs], [bm])
                            po_(lambda e, b=b, cs=cs: e.tensor_scalar_mul(km[:], ktk[:, cs], refs[:, 18 + b:19 + b]), [ktk, refs], [km])
                            ph = nps()
                            mmg(ph[:, 0:128], [(bm[:], Ub[:]), (km[:], vtk[:, cs])], [bm, Ub, km, vtk], ph)
                            for hh in range(2):
                                pb = 64 * hh
                                dv(lambda e, pb=pb, ph=ph, k2=k2: e.tensor_tensor(Hs[k2][pb:pb + 64, :], ph[pb:pb + 64, pb:pb + 64], Hs[k2][pb:pb + 64, :], ALU.add), [ph, Hs[k2]], [Hs[k2]])
                            dv(lambda e, k2=k2, b=b: e.tensor_scalar_mul(Hs[k2][:], Hs[k2][:], dL[:, cp, b + 1:b + 2]), [Hs[k2], dL], [Hs[k2]])
                            pt2 = nps()
                            cx.op("pe", lambda e, pt2=pt2, k2=k2: e.transpose(pt2[0:64, 0:128], Hs[k2][:], ident[:]), reads=[Hs[k2], ident], writes=[pt2])
                            ac(lambda e, pt2=pt2, k2=k2: e.activation(Sout[k2][:].rearrange("p a b -> p (a b)"), pt2[0:64, 0:128], AF.Copy), [pt2], [Sout[k2]])
                            cx.dma("act", orS[b + 1, 2 * gc:2 * gc + 2].rearrange("h i j -> i h j"), Sout[k2][:], reads=[Sout[k2]], writes=[orS], owner=Sout[k2])
                        for hh in range(2):
                            dv(lambda e, hh=hh, cp=cp: e.tensor_copy(Y[:, cp * 128 + hh * 64:cp * 128 + (hh + 1) * 64], pys[hh][:, 0:64]), [pys[hh]], [Y])
                dv(lambda e: e.tensor_reduce(stt_[:, 2, :], h3(Y), AX.X, ALU.add), [Y], [stt_])
                dv(lambda e: e.tensor_scalar_mul(stt_[:, 2, :], stt_[:, 2, :], 1.0 / 64), [stt_], [stt_])
                dv(lambda e: e.tensor_tensor(h3(Y), h3(Y), bc16(stt_, 2), ALU.subtract), [Y, stt_], [Y])
                ac(lambda e: e.activation(TMP[:], Y[:], AF.Square), [Y], [TMP])
                dv(lambda e: e.tensor_reduce(stt_[:, 3, :], h3(TMP), AX.X, ALU.add), [TMP], [stt_])
                dv(lambda e: e.tensor_scalar(stt_[:, 3, :], stt_[:, 3, :], 1.0 / 64, LN_X_EPS, ALU.mult, ALU.add), [stt_], [stt_])
                ac(lambda e: e.activation(stt_[:, 3, :], stt_[:, 3, :], AF.Sqrt), [stt_], [stt_])
                dv(lambda e: e.reciprocal(stt_[:, 3, :], stt_[:, 3, :]), [stt_], [stt_])
                dv(lambda e: e.tensor_tensor(h3(Y), h3(Y), bc16(stt_, 3), ALU.mult), [Y, stt_], [Y])
                dv(lambda e: e.tensor_tensor(Y[:], Y[:], PRM["r_lnw"][:], ALU.mult), [Y, PRM["r_lnw"]], [Y])
                po_(lambda e: e.tensor_tensor(Y[:], Y[:], PRM["r_lnb"][:], ALU.add), [Y, PRM["r_lnb"]], [Y])
                dv(lambda e: e.tensor_tensor(h3(TMP), h3(Vb_), bc16(stt_, 1), ALU.mult), [Vb_, stt_], [TMP])
                po_(lambda e: e.tensor_tensor(Y[:], Y[:], TMP[:], ALU.add), [Y, TMP], [Y])
                dv(lambda e: e.tensor_tensor(Y[:], Y[:], Gg[:], ALU.mult), [Y, Gg], [Y])
                transpose_to(outT, lambda c: outT[:, half * 8 + c, r0:r0 + 128], Y, lambda c: Y[:, c * 128:(c + 1) * 128], 8)
            if s == 0 and i == NTP - 1:
                for gc in range(16):
                    k2 = ssi[0] % 3; ssi[0] += 1
                    pt2 = nps()
                    cx.op("pe", lambda e, pt2=pt2, gc=gc: e.transpose(pt2[0:64, 0:128], Hst[:, gc, :], ident[:]), reads=[Hst, ident], writes=[pt2])
                    ac(lambda e, pt2=pt2, k2=k2: e.activation(Sout[k2][:].rearrange("p a b -> p (a b)"), pt2[0:64, 0:128], AF.Copy), [pt2], [Sout[k2]])
                    cx.dma("act", orS[0, 2 * gc:2 * gc + 2].rearrange("h i j -> i h j"), Sout[k2][:], reads=[Sout[k2]], writes=[orS], owner=Sout[k2])

    stage_mod()
    hT = sb("hT", [128, 16, NTOK], BF16)
    stage_norm(xin, 0, W["norm_mix"][0], 0, 1, hT)
    stage_l0_proj(hT)
    stage_l0_rec(hT)
    proj_resid(hT, lambda c0, n: W["ab_w_out"][0, :, c0:c0 + n], xin, x1_d, 0, 2)
    stage_norm(x1_d, 0, W["norm_ffn"][0], 3, 4, hT)
    stage_ffn(hT, 0, x1_d, x2_d)
    stage_norm(x2_d, 1, W["norm_mix"][1], 0, 1, hT, h_dram=h_d, shift_out=osh)
    import os
    stage_l1_mix(hT)
    if os.environ.get("K_SKIP") != "proj":
        stage_l1_proj(hT)
    if os.environ.get("K_SKIP") not in ("rec", "proj"):
        stage_l1_rec(hT)
    proj_resid(hT, lambda c0, n: W["r_wo"][0, :, c0:c0 + n], x2_d, x3_d, 1, 2)
    stage_norm(x3_d, 1, W["norm_ffn"][1], 3, 4, hT)
    stage_ffn(hT, 1, x3_d, x4_d)
    stage_final(x4_d)
    cx.barrier()
    return nc


def LAYER0(L):
    cx, xin, x1_d, NTOK = L["cx"], L["xin"], L["x1_d"], L["NTOK"]
    cx.dma("sp", x1_d[:], xin[:], reads=[xin], writes=[x1_d], owner=x1_d)
    cx.barrier()


def LAYER1(L):
    cx, x2_d, x3_d = L["cx"], L["x2_d"], L["x3_d"]
    cx.dma("sp", x3_d[:], x2_d[:], reads=[x2_d], writes=[x3_d], owner=x3_d)
    cx.barrier()


def make_consts(NTP):
    c = {}
    s = np.arange(128)
    blk = s // 8
    le_p = (s[:, None] <= s[None, :]).astype(np.float32)
    same = (blk[:, None] == blk[None, :])
    le_s = (le_p * same).astype(np.float32)
    lt_p = (s[:, None] < s[None, :]).astype(np.float32)
    lt_s = (lt_p * same).astype(np.float32)
    c["ident"] = np.eye(128, dtype=np.float32)
    c["ones"] = np.ones((128, 128), np.float32)
    c["mle_p"], c["mle_s"], c["mlt_p"], c["mlt_s"] = le_p, le_s, lt_p, lt_s
    c["mgt_p"], c["mgt_s"] = lt_p.T.copy(), lt_s.T.copy()
    c["neg_p"] = ((le_p.T - 1.0) * 1e30).astype(np.float32)
    c["neg_s"] = ((le_s.T - 1.0) * 1e30).astype(np.float32)
    c["trir_p"] = (le_p - (s[:, None] <= 63).astype(np.float32)).astype(np.float32)
    ref_p = np.zeros((128, 34), np.float32); ref_p[:, 0] = (s <= 63); ref_p[:, 17] = (s > 63)
    ref_s = np.zeros((128, 34), np.float32)
    for b in range(16):
        ref_s[8 * b:8 * b + 8, 17 + b + 1] = 1.0
    c["ref_p"], c["ref_s"] = ref_p, ref_s
    blkt_p = np.zeros((NB, 128), np.float32); blkt_p[0] = 1.0
    blkt_s = np.zeros((NB, 128), np.float32)
    sel_p = np.zeros((128, NB), np.float32); sel_p[127, 0] = 1.0
    sel_s = np.zeros((128, NB), np.float32)
    for b in range(16):
        blkt_s[b + 1, 8 * b:8 * b + 8] = 1.0
        sel_s[8 * b + 7, b + 1] = 1.0
    c["blkt_p"], c["blkt_s"], c["selend_p"], c["selend_s"] = blkt_p, blkt_s, sel_p, sel_s
    NBP = 4; LC = 128 // NBP
    pb_ = s // LC
    samep = (pb_[:, None] == pb_[None, :])
    c["rle_p"] = (le_p * samep).astype(np.float32); c["rlt_p"] = (lt_p * samep).astype(np.float32)
    c["rgt_p"] = c["rlt_p"].T.copy()
    rb = np.zeros((128, NB), np.float32)
    rbr = np.zeros((128, NBP, 128), np.float32)
    for k in range(NBP):
        rb[k * LC:(k + 1) * LC, k] = 1.0
        rbr[:, k, k * LC:(k + 1) * LC] = 1.0
    c["rblk_p"] = rb; c["rblkrow_p"] = rbr.reshape(128, NBP * 128)
    br = np.zeros((128, 16, 128), np.float32)
    for b in range(16):
        br[:, b, 8 * b:8 * b + 8] = 1.0
    c["blkrow"] = br.reshape(128, 16 * 128)
    return {"c_" + k: np.ascontiguousarray(v) for k, v in c.items()}


WNAMES = ["mod_w", "mod_b", "norm_mix", "norm_ffn", "ffn_w1", "ffn_w2", "final_norm", "ab_w_in", "ab_gate_b", "m_conv_w",
          "m_norm", "g_lb", "g_norm", "ab_w_out", "r_mu", "r_w0", "r_w1", "r_w2", "r_a0", "r_a1", "r_a2", "r_g1", "r_g2",
          "r_kk", "r_ka", "r_rk", "r_wr", "r_wk", "r_wv", "r_wo", "r_lnw", "r_lnb"]
_NC_CACHE = {}


def core_inputs(inp, core, NTP, consts):
    b = core // 2
    T = NTP * 128
    f = lambda a: np.ascontiguousarray(np.asarray(a, dtype=np.float32))
    sl = slice(16 * core, 16 * core + 16)
    m = {}
    m["xin"] = f(np.concatenate([inp["x_prompt"][b, :T], inp["x_sample"][sl].reshape(128, D)], axis=0))
    m["cin"] = f(np.concatenate([inp["c_prompt"][b:b + 1], inp["c_sample"][sl]], axis=0))

    def st(a):
        a = np.asarray(a)[0, sl]
        return f(np.concatenate([np.zeros((1,) + a.shape[1:], np.float32), a], axis=0))
    m["mC"] = st(inp["state_mlstm_C"]); m["mn"] = st(inp["state_mlstm_n"]); m["mm"] = st(inp["state_mlstm_m"])
    m["mconv"] = st(inp["state_mlstm_conv"]); m["gS"] = st(inp["state_hgrn_S"]); m["rS"] = st(inp["state_rwkv_S"])
    m["rsh"] = st(inp["state_rwkv_shift"])
    for n in WNAMES:
        a = f(inp[n])
        if n == "r_rk":
            a = a.reshape(1, D)
        m[n] = a
    m.update(consts)
    return m


def kernel(**inp):
    NTP = 16
    n = 8
    if NTP not in _NC_CACHE:
        _NC_CACHE[NTP] = build_nc(NTP)
    nc = _NC_CACHE[NTP]
    consts = make_consts(NTP)
    in_maps = [core_inputs(inp, c, NTP, consts) for c in range(n)]
    res = run_bass_kernel_spmd(nc, in_maps, core_ids=list(range(n)))
    R = res.results
    T = NTP * 128
    y_prompt = np.stack([R[2 * b]["y"][:T] for b in range(4)], axis=0)
    y_sample = np.concatenate([R[c]["y"][T:].reshape(16, 8, D) for c in range(n)], axis=0)

    def pst(k):
        return np.stack([R[2 * b][k][0] for b in range(4)], axis=0)[None]

    def sst(k):
        return np.concatenate([R[c][k][1:] for c in range(n)], axis=0)[None]
    keys = ["oC", "on", "om", "oconv", "oS", "orS", "osh"]
    outs = [y_prompt, y_sample] + [pst(k) for k in keys] + [sst(k) for k in keys]
    return tuple(np.ascontiguousarray(o, dtype=np.float32) for o in outs)
```

```python
import contextlib
import numpy as np
import concourse.bass as bass
import concourse.mybir as mybir
from concourse.bass_utils import run_bass_kernel_spmd

F32 = mybir.dt.float32
BF16 = mybir.dt.bfloat16
AF = mybir.ActivationFunctionType
ALU = mybir.AluOpType
AX = mybir.AxisListType

D = 2048
NB = 17
RMS_EPS = 1e-6
LN_X_EPS = 64e-5


class Res:
    __slots__ = ("name", "lw", "rd", "dsem")

    def __init__(self, name):
        self.name = name
        self.lw = None
        self.rd = {}
        self.dsem = {}


class Ctx:
    def __init__(self, nc):
        self.nc = nc
        self.eng = {"pe": nc.tensor, "act": nc.scalar, "dve": nc.vector, "pool": nc.gpsimd, "sp": nc.sync}
        self.sems = {}
        self.tot = {}
        self.isdma = {}
        self.seen = {e: {} for e in self.eng}
        for e in ("pe", "act", "dve", "pool"):
            self._newsem("E_" + e, False)
        self.free_dma = {"hw": [], "sw": []}
        self.ndma = 0

    def _newsem(self, key, isdma):
        self.sems[key] = self.nc.alloc_semaphore(name=key)
        self.tot[key] = 0
        self.isdma[key] = isdma
        return key

    def _dma_sem_for(self, res, q):
        kind = "sw" if q == "pool" else "hw"
        if kind not in res.dsem:
            if self.free_dma[kind]:
                res.dsem[kind] = self.free_dma[kind].pop()
            else:
                self.ndma += 1
                res.dsem[kind] = self._newsem("D%s%d" % (kind, self.ndma), True)
        return res.dsem[kind]

    def release(self, bufs):
        for b in bufs:
            r = b.r
            for kind, key in r.dsem.items():
                self.free_dma[kind].append(key)
            r.dsem = {}
            r.lw = None
            r.rd = {}

    def _need(self, e, deps):
        eng = self.eng[e]
        seen = self.seen[e]
        for key, val in deps:
            if self.isdma[key]:
                val = self.tot[key]
            elif key == "E_pe" and e == "pe":
                continue
            if seen.get(key, 0) >= val:
                continue
            eng.wait_ge(self.sems[key], val)
            seen[key] = val

    @staticmethod
    def _deps(reads, writes):
        deps = []
        for r in reads:
            if r.lw is not None:
                deps.append(r.lw)
        for w in writes:
            if w.lw is not None:
                deps.append(w.lw)
            deps.extend(w.rd.items())
        return deps

    @staticmethod
    def _commit(key, val, reads, writes):
        for w in writes:
            w.lw = (key, val)
            w.rd = {}
        for r in reads:
            if r in writes:
                continue
            if r.rd.get(key, 0) < val:
                r.rd[key] = val

    def op(self, e, fn, reads=(), writes=()):
        reads = [b.r for b in reads]
        writes = [b.r for b in writes]
        self._need(e, self._deps(reads, writes))
        inst = fn(self.eng[e])
        key = "E_" + e
        self.tot[key] += 1
        inst.then_inc(self.sems[key], 1)
        self._commit(key, self.tot[key], reads, writes)
        return inst

    def dma(self, q, out, in_, reads=(), writes=(), owner=None, **kw):
        reads = [b.r for b in reads]
        writes = [b.r for b in writes]
        self._need(q, self._deps(reads, writes))
        key = self._dma_sem_for(owner.r, q)
        inst = self.eng[q].dma_start(out=out, in_=in_, **kw)
        self.tot[key] += 16
        inst.then_inc(self.sems[key], 16)
        self._commit(key, self.tot[key], reads, writes)
        return inst

    def barrier(self):
        for e in self.eng:
            self._need(e, [(k, v) for k, v in self.tot.items() if v > 0])


class Buf:
    def __init__(self, t, name, shape):
        self.t = t
        self.r = Res(name)
        self.shape = list(shape)
        self.ps = int(np.prod(shape[1:]))

    def __getitem__(self, idx):
        return self.t[idx]

    def v(self, off, dims, p0=0, np_=128):
        return bass.AP(self.t, p0 * self.ps + off, [[self.ps, np_]] + [list(d) for d in dims])


class View:
    def __init__(self, parent, c0, n):
        self.parent = parent
        self.c0 = c0
        self.n = n
        self.r = parent.r

    def __getitem__(self, idx):
        if isinstance(idx, slice):
            assert idx == slice(None)
            return self.parent[:, self.c0:self.c0 + self.n]
        p, c = idx
        a = 0 if c.start is None else c.start
        b = self.n if c.stop is None else c.stop
        return self.parent[p, self.c0 + a:self.c0 + b]


def build_nc(NTP):
    NT = NTP + 1
    NP = NTP * 128
    NTOK = NT * 128
    nc = bass.Bass("TRN2", target_bir_lowering=False)
    cx = Ctx(nc)
    DT = {}

    def din(name, shape, dt=F32):
        b = Buf(nc.dram_tensor(name, list(shape), dt, kind="ExternalInput"), name, shape)
        DT[name] = b
        return b

    def dout(name, shape):
        b = Buf(nc.dram_tensor(name, list(shape), F32, kind="ExternalOutput"), name, shape)
        DT[name] = b
        return b

    def dscr(name, shape, dt=F32):
        return Buf(nc.dram_tensor(name, list(shape), dt), name, shape)

    xin = din("xin", [NTOK, D]); cin = din("cin", [NB, D])
    mC = din("mC", [NB, 4, 256, 256]); mn = din("mn", [NB, 4, 256]); mm = din("mm", [NB, 4])
    mconv = din("mconv", [NB, 3, D]); gS = din("gS", [NB, 8, 128, 128]); rS = din("rS", [NB, 32, 64, 64])
    rsh = din("rsh", [NB, D])
    W = {}
    for name, shape in [("mod_w", [2, D, 6 * D]), ("mod_b", [2, 6 * D]), ("norm_mix", [2, D]), ("norm_ffn", [2, D]),
                        ("ffn_w1", [2, D, 4 * D]), ("ffn_w2", [2, 4 * D, D]), ("final_norm", [D]),
                        ("ab_w_in", [1, D, 8200]), ("ab_gate_b", [1, 8]), ("m_conv_w", [1, 4, D]), ("m_norm", [1, 1024]),
                        ("g_lb", [2, 1024]), ("g_norm", [1, 1024]), ("ab_w_out", [1, D, D]),
                        ("r_mu", [1, 6, D]), ("r_w0", [1, D]), ("r_w1", [1, D, 96]), ("r_w2", [1, 96, D]),
                        ("r_a0", [1, D]), ("r_a1", [1, D, 96]), ("r_a2", [1, 96, D]), ("r_g1", [1, D, 256]),
                        ("r_g2", [1, 256, D]), ("r_kk", [1, D]), ("r_ka", [1, D]), ("r_rk", [1, D]),
                        ("r_wr", [1, D, D]), ("r_wk", [1, D, D]), ("r_wv", [1, D, D]), ("r_wo", [1, D, D]),
                        ("r_lnw", [1, D]), ("r_lnb", [1, D])]:
        W[name] = din(name, shape)
    CN = {}
    for name, shape in [("ident", [128, 128]), ("ones", [128, 128]), ("mle_p", [128, 128]), ("mle_s", [128, 128]),
                        ("mlt_p", [128, 128]), ("mlt_s", [128, 128]), ("mgt_p", [128, 128]), ("mgt_s", [128, 128]),
                        ("neg_p", [128, 128]), ("neg_s", [128, 128]), ("trir_p", [128, 128]),
                        ("ref_p", [128, 34]), ("ref_s", [128, 34]), ("blkt_p", [NB, 128]), ("blkt_s", [NB, 128]),
                        ("selend_p", [128, NB]), ("selend_s", [128, NB]), ("blkrow", [128, 16 * 128]),
                        ("rle_p", [128, 128]), ("rlt_p", [128, 128]), ("rgt_p", [128, 128]), ("rblk_p", [128, NB]),
                        ("rblkrow_p", [128, 4 * 128])]:
        CN[name] = din("c_" + name, shape)
    y = dout("y", [NTOK, D])
    oC = dout("oC", [NB, 4, 256, 256]); on = dout("on", [NB, 4, 256]); om = dout("om", [NB, 4])
    oconv = dout("oconv", [NB, 3, D]); oS = dout("oS", [NB, 8, 128, 128]); orS = dout("orS", [NB, 32, 64, 64])
    osh = dout("osh", [NB, D])
    mod_d = dscr("mod_d", [2, NB, 6 * D])
    ext_p = dscr("ext_p", [NP + 3, D]); ext_s = dscr("ext_s", [16, 11, D])
    z_d = dscr("z_d", [NTOK, 6152])
    x1_d = dscr("x1_d", [NTOK, D]); x2_d = dscr("x2_d", [NTOK, D]); x3_d = dscr("x3_d", [NTOK, D]); x4_d = dscr("x4_d", [NTOK, D])
    h_d = dscr("h_d", [NTOK, D])
    rz_d = dscr("rz_d", [NTOK, 6 * D])
    dummy = Buf(None, "dummy", [1, 1])

    stacks = [contextlib.ExitStack()]

    uid = [0]

    stage_bufs = [[]]

    def sb(name, shape, dt=F32):
        uid[0] += 1
        name = "%s_%d" % (name, uid[0])
        b = Buf(stacks[-1].enter_context(nc.sbuf_tensor(name, list(shape), dt)), name, shape)
        stage_bufs[-1].append(b)
        return b

    def staged(fn):
        def wrapper(*a, **k):
            stacks.append(contextlib.ExitStack())
            stage_bufs.append([])
            try:
                return fn(*a, **k)
            finally:
                cx.barrier()
                cx.release(stage_bufs.pop())
                stacks.pop().close()
                npsmod[0] = 8
        return wrapper

    ident = sb("ident", [128, 128]); ones = sb("ones", [128, 128])
    identb = sb("identb", [128, 128], BF16)
    PS = [Buf(nc.alloc_psum_tensor("ps%d" % i, [128, 512], F32), "ps%d" % i, [128, 512]) for i in range(8)]
    psi = [0]

    def nps():
        p = PS[psi[0] % npsmod[0]]
        psi[0] += 1
        return p

    npsmod = [8]

    acci = [0]

    def accps():
        p = PS[6 + acci[0] % 2]
        acci[0] += 1
        return p

    cx.dma("sp", ident[:], CN["ident"][:], writes=[ident], owner=ident)
    cx.dma("sp", ones[:], CN["ones"][:], writes=[ones], owner=ones)
    cx.op("dve", lambda e: e.tensor_copy(identb[:], ident[:]), reads=[ident], writes=[identb])

    evq = [0]

    def evac(out_ap, in_ap, reads, writes, scale=None):
        evq[0] += 1
        if evq[0] % 2 == 0:
            if scale is None:
                cx.op("act", lambda e: e.activation(out_ap, in_ap, AF.Copy), reads=reads, writes=writes)
            else:
                cx.op("act", lambda e: e.activation(out_ap, in_ap, AF.Copy, scale=float(scale)), reads=reads, writes=writes)
        else:
            if scale is None:
                cx.op("dve", lambda e: e.tensor_copy(out_ap, in_ap), reads=reads, writes=writes)
            else:
                cx.op("dve", lambda e: e.tensor_scalar(out_ap, in_ap, float(scale), None, ALU.mult), reads=reads, writes=writes)

    def transpose_to(dst, dst_ap_fn, src, src_ap_fn, nchunks, scale_fn=None, npart=128):
        for c0 in range(0, nchunks, 4):
            n = min(4, nchunks - c0)
            p = nps()
            for j in range(n):
                cx.op("pe", lambda e, j=j: e.transpose(p.v(j * 128, [[1, npart]]),
                                                      src_ap_fn(c0 + j), ident[0:npart, 0:npart]),
                      reads=[src, ident], writes=[p])
            for j in range(n):
                sc = None if scale_fn is None else scale_fn(c0 + j)
                evac(dst_ap_fn(c0 + j), p.v(j * 128, [[1, npart]]), [p], [dst], scale=sc)

    def rows_bcast(dst, src_dram_rows_fn, tile_is_sample, width, col0=0, q="sp"):
        if not tile_is_sample:
            cx.dma(q, dst[:, col0:col0 + width], src_dram_rows_fn(0).partition_broadcast(128), writes=[dst], owner=dst)
        else:
            for b in range(16):
                cx.dma(q, dst[8 * b:8 * b + 8, col0:col0 + width], src_dram_rows_fn(b + 1).partition_broadcast(8), writes=[dst], owner=dst)

    @staged
    def stage_mod():
        csb = sb("csb", [NB, D]); scT = sb("scT", [128, 16, NB], BF16)
        modsb = sb("modsb", [NB, 6 * D]); biasb = sb("biasb", [NB, 6 * D])
        wb = [sb("modw%d" % i, [128, 16, 512], BF16) for i in range(2)]
        cx.dma("sp", csb[:], cin[:], writes=[csb], owner=csb)
        cx.op("act", lambda e: e.activation(csb[:], csb[:], AF.Silu), reads=[csb], writes=[csb])
        for c0 in range(0, 16, 4):
            p = nps()
            for j in range(4):
                c = c0 + j
                cx.op("pe", lambda e, j=j, c=c: e.transpose(p.v(j * 32, [[1, NB]]), csb[:, c * 128:(c + 1) * 128], ident[0:NB, 0:NB]),
                      reads=[csb, ident], writes=[p])
            for j in range(4):
                evac(scT[:, c0 + j, :], p.v(j * 32, [[1, NB]]), [p], [scT])
        for l in range(2):
            cx.dma("sp", biasb[:], W["mod_b"][l].partition_broadcast(NB), writes=[biasb], owner=biasb)
            for cb in range(24):
                w = wb[cb % 2]
                cx.dma("pool", w[:], W["mod_w"][l, :, cb * 512:(cb + 1) * 512].rearrange("(kc p) n -> p kc n", p=128),
                       writes=[w], owner=w)
                p = nps()
                for kc in range(16):
                    cx.op("pe", lambda e, kc=kc: e.matmul(p[0:NB, :], scT[:, kc, :], w[:, kc, :], start=(kc == 0), stop=(kc == 15)),
                          reads=[scT, w], writes=[p])
                cx.op("dve", lambda e: e.tensor_tensor(modsb[:, cb * 512:(cb + 1) * 512], p[0:NB, :], biasb[:, cb * 512:(cb + 1) * 512], ALU.add),
                      reads=[p, biasb], writes=[modsb])
            cx.dma("sp", mod_d[l], modsb[:], reads=[modsb], writes=[mod_d], owner=modsb)
        cx.barrier()
        cx.release([csb, scT, modsb, biasb] + wb)
        return [csb, scT, modsb, biasb] + wb

    @staged
    def stage_norm(x_d, l, normw_ap, ish, isc, hT, h_dram=None, shift_out=None):
        G = [sb("nG%d" % i, [128, D]) for i in range(2)]; SH = [sb("nSH%d" % i, [128, D]) for i in range(2)]
        nw = sb("nnw", [128, D])
        xt = [sb("nxt%d" % i, [128, D]) for i in range(2)]; ht = [sb("nht%d" % i, [128, D]) for i in range(2)]
        junk = sb("njunk", [128, D]); ss = sb("nss", [128, 2])
        cx.dma("sp", nw[:], normw_ap.partition_broadcast(128), writes=[nw], owner=nw)
        for s in range(2):
            rows_bcast(G[s], lambda b: mod_d[l, b, isc * D:(isc + 1) * D], s == 1, D)
            rows_bcast(SH[s], lambda b: mod_d[l, b, ish * D:(ish + 1) * D], s == 1, D)
            cx.op("dve", lambda e, s=s: e.scalar_tensor_tensor(G[s][:], G[s][:], 1.0, nw[:], ALU.add, ALU.mult), reads=[G[s], nw], writes=[G[s]])
        for i in range(NT):
            s = 1 if i == NTP else 0
            x = xt[i % 2]; h = ht[i % 2]
            cx.dma("sp", x[:], x_d[i * 128:(i + 1) * 128, :], reads=[x_d], writes=[x], owner=x)
            cx.op("act", lambda e: e.activation(junk[:], x[:], AF.Square, accum_out=ss[:, 0:1]), reads=[x], writes=[junk, ss])
            cx.op("dve", lambda e: e.tensor_scalar(ss[:, 1:2], ss[:, 0:1], 1.0 / D, RMS_EPS, ALU.mult, ALU.add), reads=[ss], writes=[ss])
            cx.op("act", lambda e: e.activation(ss[:, 1:2], ss[:, 1:2], AF.Sqrt), reads=[ss], writes=[ss])
            cx.op("dve", lambda e: e.reciprocal(ss[:, 1:2], ss[:, 1:2]), reads=[ss], writes=[ss])
            cx.op("dve", lambda e: e.scalar_tensor_tensor(h[:], x[:], ss[:, 1:2], G[s][:], ALU.mult, ALU.mult), reads=[x, ss, G[s]], writes=[h])
            if SH is not None:
                cx.op("dve", lambda e: e.tensor_tensor(h[:], h[:], SH[s][:], ALU.add), reads=[h, SH[s]], writes=[h])
            if h_dram is not None:
                cx.dma("sp", h_dram[i * 128:(i + 1) * 128, :], h[:], reads=[h], writes=[h_dram], owner=h)
            if shift_out is not None:
                if s == 0 and i == NTP - 1:
                    cx.dma("sp", shift_out[0:1, :], h[127:128, :], reads=[h], writes=[shift_out], owner=h)
                if s == 1:
                    for b in range(16):
                        cx.dma("sp", shift_out[b + 1:b + 2, :], h[8 * b + 7:8 * b + 8, :], reads=[h], writes=[shift_out], owner=h)
            transpose_to(hT, lambda c: hT[:, c, i * 128:(i + 1) * 128], h, lambda c: h[:, c * 128:(c + 1) * 128], 16)
        cx.barrier()
        tmp = G + SH + [nw, junk, ss] + xt + ht
        cx.release(tmp)

    @staged
    def proj_tok(aT, blocks, K=16, kpart=128, bias_fn=None, sigmoid=False):
        wb = [sb("pw%d" % i, [128, K, 512], BF16) for i in range(2)]
        ob = [sb("po%d" % i, [128, 512]) for i in range(3)]
        bb_ = [sb("pb%d" % i, [128, 512]) for i in range(2)] if bias_fn is not None else None
        oi = 0
        for bi, (w_ap, dst_fn) in enumerate(blocks):
            n = w_ap.shape[-1]
            w = wb[bi % 2]
            wv = w_ap.rearrange("(kc p) n -> p kc n", p=kpart)
            for k0 in range(0, K, 4):
                k1 = min(K, k0 + 4)
                cx.dma("pool", w[0:kpart, k0:k1, 0:n], wv[:, k0:k1, :], writes=[w], owner=w)
            if bias_fn is not None:
                bt = bb_[bi % 2]
                cx.dma("sp", bt[:, 0:n], bias_fn(bi).partition_broadcast(128), writes=[bt], owner=bt)
            for i in range(NT):
                p = nps()
                for kc in range(K):
                    cx.op("pe", lambda e, kc=kc: e.matmul(p[:, 0:n], aT[0:kpart, kc, i * 128:(i + 1) * 128], w[0:kpart, kc, 0:n],
                                                          start=(kc == 0), stop=(kc == K - 1)), reads=[aT, w], writes=[p])
                o = ob[oi % 3]; oi += 1
                if bias_fn is None:
                    cx.op("dve", lambda e: e.tensor_copy(o[:, 0:n], p[:, 0:n]), reads=[p], writes=[o])
                else:
                    cx.op("dve", lambda e: e.tensor_tensor(o[:, 0:n], p[:, 0:n], bt[:, 0:n], ALU.add), reads=[p, bt], writes=[o])
                    if sigmoid:
                        cx.op("act", lambda e: e.activation(o[:, 0:n], o[:, 0:n], AF.Sigmoid), reads=[o], writes=[o])
                for (dbuf, dap, p0, p1) in dst_fn(i):
                    if p0 == "3d":
                        cx.dma("act", dap, o[:, 0:n], reads=[o], writes=[dbuf], owner=o)
                    else:
                        cx.dma("act", dap, o[p0:p1, 0:n], reads=[o], writes=[dbuf], owner=o)

    @staged
    def proj_resid(aT, w_fn, x_old, x_new, l, igate, K=16):
        wb = [sb("rw%d" % i, [128, K, 512], BF16) for i in range(2)]
        xb = [sb("rx%d" % i, [128, 512]) for i in range(3)]
        GT = [sb("rgt%d" % i, [128, D]) for i in range(2)]
        for s in range(2):
            rows_bcast(GT[s], lambda b: mod_d[l, b, igate * D:(igate + 1) * D], s == 1, D)
        oi = 0
        for cb in range(4):
            w = wb[cb % 2]
            cx.dma("pool", w[:], w_fn(cb * 512, 512).rearrange("(kc p) n -> p kc n", p=128), writes=[w], owner=w)
            for i in range(NT):
                s = 1 if i == NTP else 0
                xo = xb[oi % 3]; oi += 1
                cx.dma("sp", xo[:], x_old[i * 128:(i + 1) * 128, cb * 512:(cb + 1) * 512], reads=[x_old], writes=[xo], owner=xo)
                p = nps()
                for kc in range(K):
                    cx.op("pe", lambda e, kc=kc: e.matmul(p[:], aT[:, kc, i * 128:(i + 1) * 128], w[:, kc, :], start=(kc == 0), stop=(kc == K - 1)),
                          reads=[aT, w], writes=[p])
                t = sbtmp512[oi % 2]
                cx.op("dve", lambda e: e.tensor_tensor(t[:], p[:], GT[s][:, cb * 512:(cb + 1) * 512], ALU.mult), reads=[p, GT[s]], writes=[t])
                cx.op("dve", lambda e: e.tensor_tensor(xo[:], xo[:], t[:], ALU.add), reads=[xo, t], writes=[xo])
                cx.dma("act", x_new[i * 128:(i + 1) * 128, cb * 512:(cb + 1) * 512], xo[:], reads=[xo], writes=[x_new], owner=xo)
        cx.barrier()
        cx.release(wb + xb + GT)

    sbtmp512 = [sb("tmp512_%d" % i, [128, 512]) for i in range(2)]

    @staged
    def stage_ffn(hT, l, x_old, x_new):
        FB = 2048
        nfb = 4 * D // FB
        KC = FB // 128
        hidT = sb("hidT", [128, KC, NTOK], BF16)
        w1b = [sb("fw1_%d" % i, [128, 16, 256], BF16) for i in range(2)]
        w2b = [sb("fw2_%d" % i, [128, KC, 512], BF16) for i in range(2)]
        xb = [sb("fx%d" % i, [128, 512]) for i in range(4)]
        GTs = [[sb("fgt%d_%d" % (i, j), [128, 512]) for j in range(2)] for i in range(2)]
        groups = [(g, min(512, NTOK - g)) for g in range(0, NTOK, 512)]
        w1i = 0; w2i = 0; oi = 0
        for fb in range(nfb):
            for blk in range(FB // 256):
                w = w1b[w1i % 2]; w1i += 1
                c0 = fb * FB + blk * 256
                wv = W["ffn_w1"][l, :, c0:c0 + 256].rearrange("(kc p) n -> p kc n", p=128)
                for k0 in range(0, 16, 8):
                    cx.dma("pool", w[:, k0:k0 + 8, :], wv[:, k0:k0 + 8, :], writes=[w], owner=w)
                for oc in range(2):
                    for (g0, gn) in groups:
                        p = nps()
                        for kc in range(16):
                            cx.op("pe", lambda e, kc=kc: e.matmul(p[:, 0:gn], w[:, kc, oc * 128:(oc + 1) * 128], hT[:, kc, g0:g0 + gn],
                                                                  start=(kc == 0), stop=(kc == 15)), reads=[hT, w], writes=[p])
                        t = sbtmp512[oi % 2]; oi += 1
                        cx.op("act", lambda e: e.activation(t[:, 0:gn], p[:, 0:gn], AF.Relu), reads=[p], writes=[t])
                        cx.op("dve", lambda e: e.tensor_tensor(hidT[:, blk * 2 + oc, g0:g0 + gn], t[:, 0:gn], t[:, 0:gn], ALU.mult),
                              reads=[t], writes=[hidT])
            for cb in range(4):
                w = w2b[w2i % 2]; w2i += 1
                wv = W["ffn_w2"][l, fb * FB:(fb + 1) * FB, cb * 512:(cb + 1) * 512].rearrange("(kc p) n -> p kc n", p=128)
                for k0 in range(0, KC, 4):
                    cx.dma("pool", w[:, k0:k0 + 4, :], wv[:, k0:k0 + 4, :], writes=[w], owner=w)
                GT = GTs[(fb * 4 + cb) % 2]
                for s_ in range(2):
                    rows_bcast(GT[s_], lambda b: mod_d[l, b, 5 * D + cb * 512:5 * D + (cb + 1) * 512], s_ == 1, 512)
                for i in range(NT):
                    s_ = 1 if i == NTP else 0
                    xo = xb[oi % 4]; oi += 1
                    src = x_old if fb == 0 else x_new
                    cx.dma("sp", xo[:], src[i * 128:(i + 1) * 128, cb * 512:(cb + 1) * 512], reads=[src], writes=[xo], owner=xo)
                    p = nps()
                    for kc in range(KC):
                        cx.op("pe", lambda e, kc=kc: e.matmul(p[:], hidT[:, kc, i * 128:(i + 1) * 128], w[:, kc, :], start=(kc == 0), stop=(kc == KC - 1)),
                              reads=[hidT, w], writes=[p])
                    t = sbtmp512[oi % 2]
                    cx.op("dve", lambda e: e.tensor_tensor(t[:], p[:], GT[s_][:], ALU.mult), reads=[p, GT[s_]], writes=[t])
                    cx.op("dve", lambda e: e.tensor_tensor(xo[:], xo[:], t[:], ALU.add), reads=[xo, t], writes=[xo])
                    cx.dma("act", x_new[i * 128:(i + 1) * 128, cb * 512:(cb + 1) * 512], xo[:], reads=[xo], writes=[x_new], owner=xo)

    @staged
    def stage_final(x_d):
        nw = sb("fnw", [128, D]); xt = [sb("fnx%d" % i, [128, D]) for i in range(2)]
        junk = sb("fnj", [128, D]); ss = sb("fns", [128, 2])
        cx.dma("sp", nw[:], W["final_norm"][:].partition_broadcast(128), writes=[nw], owner=nw)
        for i in range(NT):
            x = xt[i % 2]
            cx.dma("sp", x[:], x_d[i * 128:(i + 1) * 128, :], reads=[x_d], writes=[x], owner=x)
            cx.op("act", lambda e: e.activation(junk[:], x[:], AF.Square, accum_out=ss[:, 0:1]), reads=[x], writes=[junk, ss])
            cx.op("dve", lambda e: e.tensor_scalar(ss[:, 1:2], ss[:, 0:1], 1.0 / D, RMS_EPS, ALU.mult, ALU.add), reads=[ss], writes=[ss])
            cx.op("act", lambda e: e.activation(ss[:, 1:2], ss[:, 1:2], AF.Sqrt), reads=[ss], writes=[ss])
            cx.op("dve", lambda e: e.reciprocal(ss[:, 1:2], ss[:, 1:2]), reads=[ss], writes=[ss])
            cx.op("dve", lambda e: e.scalar_tensor_tensor(x[:], x[:], ss[:, 1:2], nw[:], ALU.mult, ALU.mult), reads=[x, ss, nw], writes=[x])
            cx.dma("sp", y[i * 128:(i + 1) * 128, :], x[:], reads=[x], writes=[y], owner=x)
        cx.barrier()
        cx.release([nw, junk, ss] + xt)

    def stage_l0_proj(hT):
        Wi = W["ab_w_in"][0]
        cx.dma("sp", ext_p[0:3, :], mconv[0], reads=[mconv], writes=[ext_p], owner=ext_p)
        cx.dma("sp", ext_s[:, 0:3, :], mconv[1:17], reads=[mconv], writes=[ext_s], owner=ext_s)
        blocks = []

        def dst_ext(c0, n):
            def f(i):
                if i < NTP:
                    return [(ext_p, ext_p[3 + i * 128:3 + (i + 1) * 128, c0:c0 + n], 0, 128)]
                return [(ext_s, ext_s[:, 3:11, c0:c0 + n], "3d", n)]
            return f

        def dst_z(c0, n):
            return lambda i: [(z_d, z_d[i * 128:(i + 1) * 128, c0:c0 + n], 0, 128)]
        for c0 in range(0, 2048, 512):
            blocks.append((Wi[:, c0:c0 + 512], dst_ext(c0, 512)))
        for c0 in range(0, 2048, 512):
            blocks.append((Wi[:, 2048 + c0:2048 + c0 + 512], dst_z(c0, 512)))
        blocks.append((Wi[:, 4096:4104], dst_z(2048, 8)))
        for c0 in range(0, 4096, 512):
            blocks.append((Wi[:, 4104 + c0:4104 + c0 + 512], dst_z(2056 + c0, 512)))
        proj_tok(hT, blocks)

    def mmg(p_ap, pairs, reads, pbuf):
        n = len(pairs)
        for idx, (l_ap, r_ap) in enumerate(pairs):
            cx.op("pe", lambda e, l_ap=l_ap, r_ap=r_ap, idx=idx: e.matmul(p_ap, l_ap, r_ap, start=(idx == 0), stop=(idx == n - 1)),
                  reads=reads, writes=[pbuf])

    def rstd_col(dst, col, src_ap, inv_n, eps):
        cx.op("dve", lambda e: e.tensor_scalar(dst[:, col:col + 1], src_ap, float(inv_n), float(eps), ALU.mult, ALU.add), reads=[dst], writes=[dst])
        cx.op("act", lambda e: e.activation(dst[:, col:col + 1], dst[:, col:col + 1], AF.Sqrt), reads=[dst], writes=[dst])
        cx.op("dve", lambda e: e.reciprocal(dst[:, col:col + 1], dst[:, col:col + 1]), reads=[dst], writes=[dst])

    @staged
    def stage_l0_rec(mixT):
        npsmod[0] = 4
        dv = lambda fn, r, w: cx.op("dve", fn, reads=r, writes=w)
        ac = lambda fn, r, w: cx.op("act", fn, reads=r, writes=w)
        mle = [sb("mle%d" % i, [128, 128]) for i in range(2)]; neg = [sb("neg%d" % i, [128, 128]) for i in range(2)]
        trirp = sb("trirp", [128, 128]); ref = [sb("ref%d" % i, [128, 34]) for i in range(2)]
        blkt = [sb("blkt%d" % i, [NB, 128]) for i in range(2)]; selend = [sb("selend%d" % i, [128, NB]) for i in range(2)]
        blkrow = sb("blkrow", [128, 16, 128], BF16)
        for i, sfx in enumerate(["p", "s"]):
            cx.dma("sp", mle[i][:], CN["mle_" + sfx][:], writes=[mle[i]], owner=mle[i])
            cx.dma("sp", neg[i][:], CN["neg_" + sfx][:], writes=[neg[i]], owner=neg[i])
            cx.dma("sp", ref[i][:], CN["ref_" + sfx][:], writes=[ref[i]], owner=ref[i])
            cx.dma("sp", blkt[i][:], CN["blkt_" + sfx][:], writes=[blkt[i]], owner=blkt[i])
            cx.dma("sp", selend[i][:], CN["selend_" + sfx][:], writes=[selend[i]], owner=selend[i])
        cx.dma("sp", trirp[:], CN["trir_p"][:], writes=[trirp], owner=trirp)
        cx.dma("pool", blkrow[:], CN["blkrow"][:].rearrange("p (b t) -> p b t", b=16), writes=[blkrow], owner=blkrow)
        trir = [trirp, mle[1]]
        gb = sb("gb", [128, 8]); mgain = sb("mgain", [128, 1024]); ggain = sb("ggain", [128, 1024])
        LB = sb("LB", [128, 1024]); OMLB = sb("OMLB", [128, 1024])
        cx.dma("sp", gb[:], W["ab_gate_b"][0].partition_broadcast(128), writes=[gb], owner=gb)
        cx.dma("sp", mgain[:], W["m_norm"][0].partition_broadcast(128), writes=[mgain], owner=mgain)
        cx.dma("sp", ggain[:], W["g_norm"][0].partition_broadcast(128), writes=[ggain], owner=ggain)
        cx.dma("sp", LB[:], W["g_lb"][0].partition_broadcast(128), writes=[LB], owner=LB)
        cx.dma("sp", OMLB[:], W["g_lb"][1].partition_broadcast(128), writes=[OMLB], owner=OMLB)
        dv(lambda e: e.tensor_tensor(LB[:], LB[:], OMLB[:], ALU.subtract), [LB, OMLB], [LB])
        ac(lambda e: e.activation(LB[:], LB[:], AF.Sigmoid), [LB], [LB])
        dv(lambda e: e.tensor_scalar(OMLB[:], LB[:], -1.0, 1.0, ALU.mult, ALU.add), [LB], [OMLB])
        mst = sb("mst", [NB, 4]); mend = sb("mend", [NB, 4])
        cx.dma("sp", mst[:], mm[:], writes=[mst], owner=mst)
        Cst = sb("Cst", [128, 4, 2, 257]); Cb = sb("Cb", [128, 4, 2, 257], BF16)
        Sst = sb("Sst", [128, 8, 128]); Smid = sb("Smid", [128, 8, 128]); Smidb = sb("Smidb", [128, 8, 128], BF16)
        for h in range(4):
            for c in range(2):
                cx.dma("sp", Cst[:, h, c, 0:256], mC[0, h, c * 128:(c + 1) * 128, :], writes=[Cst], owner=Cst)
                cx.dma("sp", Cst[:, h, c, 256:257], mn[0, h, c * 128:(c + 1) * 128].rearrange("(p o) -> p o", o=1), writes=[Cst], owner=Cst)
        cx.dma("sp", Sst[:], gS[0].rearrange("h k v -> k h v"), writes=[Sst], owner=Sst)
        dv(lambda e: e.tensor_copy(Cb[:], Cst[:]), [Cst], [Cb])
        NSB = 4
        Cs = [sb("Cs%d" % i, [128, 2, 257]) for i in range(NSB)]; Csb = [sb("Csb%d" % i, [128, 2, 257], BF16) for i in range(NSB)]
        Ss = [sb("Ss%d" % i, [128, 128]) for i in range(NSB)]; Ssb = [sb("Ssb%d" % i, [128, 128], BF16) for i in range(NSB)]
        gsm = sb("gsm", [128, 64]); diag4 = sb("diag4", [128, 4, 128]); dtmp = sb("dtmp", [128, 4, 128])
        Rt = sb("Rt", [128, NB, 4]); bend = sb("bend", [128, NB * 4])
        CQ = 256
        taps = [sb("tap%d" % j, [128, CQ]) for j in range(4)]; cw = [sb("cw%d" % j, [128, CQ]) for j in range(4)]
        qk = sb("qk", [128, 2048]); qT = sb("qT", [128, 8, 128], BF16); kT = sb("kT", [128, 8, 128], BF16)
        khat = sb("khat", [128, 1024], BF16); khm = sb("khm", [128, 1024], BF16)
        zmv = sb("zmv", [128, 2048]); Vp = sb("Vp", [128, 4, 257], BF16); MG = sb("MG", [128, 1024])
        sTs = sb("sTs", [128, 128], BF16); qTm = sb("qTm", [128, 1, 128], BF16)
        hm = sb("hm", [128, 256]); hj = sb("hj", [128, 256]); st = sb("st", [128, 8])
        hm2 = sb("hm2", [128, 128]); hj2 = sb("hj2", [128, 128]); st2 = sb("st2", [128, 8])
        A = [View(qk, 0, 1024), View(qk, 1024, 1024)] + [sb("A%d" % i, [128, 1024]) for i in range(2, 6)]
        ktb = sb("ktb", [128, 1024], BF16); vb = sb("vb", [128, 1024], BF16)
        qtT = sb("qtT", [128, 8, 128], BF16); ktT = sb("ktT", [128, 8, 128], BF16)
        dec = sb("dec", [128, 8, 34]); ATs = sb("ATs", [128, 128], BF16)
        mixed = zmv
        cx.op("pool", lambda e: e.memset(Vp[:], 1.0), writes=[Vp])
        IG, LF, BB, GG_, CMX, MP, CM, AL, BE, MT, FL, T1, T2, AL16 = [slice(4 * k, 4 * k + 4) for k in range(14)]
        ssi = [0]

        for i in range(NT):
            s = 1 if i == NTP else 0
            r0 = i * 128
            cx.dma("sp", gsm[:, 0:8], z_d[r0:r0 + 128, 2048:2056], reads=[z_d], writes=[gsm], owner=gsm)
            dv(lambda e: e.tensor_tensor(gsm[:, 0:8], gsm[:, 0:8], gb[:], ALU.add), [gsm, gb], [gsm])
            ac(lambda e: e.activation(gsm[:, T1], gsm[:, LF], AF.Exp, scale=-1.0), [gsm], [gsm])
            ac(lambda e: e.activation(gsm[:, T1], gsm[:, T1], AF.Ln, bias=1.0), [gsm], [gsm])
            dv(lambda e: e.tensor_scalar_mul(gsm[:, LF], gsm[:, T1], -1.0), [gsm], [gsm])
            p = nps()
            mmg(p[:, 0:4], [(trir[1][:] if s else mle[0][:], gsm[:, LF])], [mle[s], gsm], p)
            dv(lambda e: e.tensor_copy(gsm[:, BB], p[:, 0:4]), [p], [gsm])
            dv(lambda e: e.tensor_tensor(gsm[:, GG_], gsm[:, IG], gsm[:, BB], ALU.subtract), [gsm], [gsm])
            for h in range(4):
                dv(lambda e, h=h: e.tensor_scalar_mul(diag4[:, h, :], ident[:], gsm[:, 12 + h:13 + h]), [ident, gsm], [diag4])
            p = nps()
            for h in range(4):
                mmg(p[:, h * 128:(h + 1) * 128], [(ones[:], diag4[:, h, :])], [ones, diag4], p)
            dv(lambda e: e.tensor_tensor(dtmp[:], p[:].rearrange("p (a b) -> p a b", a=4), neg[s].v(0, [[0, 4], [1, 128]]), ALU.add), [p, neg[s]], [dtmp])
            dv(lambda e: e.tensor_reduce(gsm[:, CMX], dtmp[:], AX.X, ALU.max), [dtmp], [gsm])
            p = nps()
            mmg(p[:, 0:4], [(blkt[s][:], mst[:])], [blkt[s], mst], p)
            dv(lambda e: e.tensor_copy(gsm[:, MP], p[:, 0:4]), [p], [gsm])
            dv(lambda e: e.tensor_tensor(gsm[:, CM], gsm[:, CMX], gsm[:, MP], ALU.max), [gsm], [gsm])
            dv(lambda e: e.tensor_tensor(gsm[:, T1], gsm[:, GG_], gsm[:, MP], ALU.subtract), [gsm], [gsm])
            ac(lambda e: e.activation(gsm[:, AL], gsm[:, T1], AF.Exp), [gsm], [gsm])
            dv(lambda e: e.tensor_tensor(gsm[:, T1], gsm[:, MP], gsm[:, CM], ALU.subtract), [gsm], [gsm])
            ac(lambda e: e.activation(gsm[:, BE], gsm[:, T1], AF.Exp), [gsm], [gsm])
            dv(lambda e: e.tensor_tensor(gsm[:, MT], gsm[:, BB], gsm[:, CM], ALU.add), [gsm], [gsm])
            ac(lambda e: e.activation(gsm[:, FL], gsm[:, MT], AF.Exp, scale=-1.0), [gsm], [gsm])
            dv(lambda e: e.tensor_scalar_mul(gsm[:, AL16], gsm[:, AL], 0.0625), [gsm], [gsm])
            p = nps()
            mmg(p[0:NB, 0:4], [(selend[s][:], gsm[:, MT])], [selend[s], gsm], p)
            dv(lambda e: e.tensor_copy(mend[:], p[0:NB, 0:4]), [p], [mend])
            if s == 0:
                dv(lambda e: e.tensor_copy(mst[0:1, :], mend[0:1, :]), [mend], [mst])
                if i == NTP - 1:
                    cx.dma("act", om[0:1, :], mend[0:1, :], reads=[mend], writes=[om], owner=mend)
            else:
                cx.dma("act", om[1:NB, :], mend[1:NB, :], reads=[mend], writes=[om], owner=mend)
            dv(lambda e: e.tensor_tensor(Rt[:], selend[s].v(0, [[1, NB], [0, 4]]), gsm.v(32, [[0, NB], [1, 4]]), ALU.mult), [selend[s], gsm], [Rt])
            p = nps()
            mmg(p[:, 0:NB * 4], [(ones[:], Rt[:].rearrange("p a b -> p (a b)"))], [ones, Rt], p)
            dv(lambda e: e.tensor_copy(bend[:], p[:, 0:NB * 4]), [p], [bend])
            for qd in range(2048 // CQ):
                c0 = qd * CQ
                for j in range(4):
                    if s == 0:
                        cx.dma("sp", taps[j][:], ext_p[r0 + j:r0 + j + 128, c0:c0 + CQ], reads=[ext_p], writes=[taps[j]], owner=taps[j])
                    else:
                        cx.dma("sp", taps[j][:], ext_s[:, j:j + 8, c0:c0 + CQ], reads=[ext_s], writes=[taps[j]], owner=taps[j])
                    cx.dma("sp", cw[j][:], W["m_conv_w"][0, j, c0:c0 + CQ].partition_broadcast(128), writes=[cw[j]], owner=cw[j])
                for j in range(4):
                    cx.op("pool" if j % 2 else "dve", lambda e, j=j: e.tensor_tensor(taps[j][:], taps[j][:], cw[j][:], ALU.mult), reads=[taps[j], cw[j]], writes=[taps[j]])
                dv(lambda e: e.tensor_tensor(taps[0][:], taps[0][:], taps[1][:], ALU.add), [taps[0], taps[1]], [taps[0]])
                cx.op("pool", lambda e: e.tensor_tensor(taps[2][:], taps[2][:], taps[3][:], ALU.add), reads=[taps[2], taps[3]], writes=[taps[2]])
                dv(lambda e: e.tensor_tensor(taps[0][:], taps[0][:], taps[2][:], ALU.add), [taps[0], taps[2]], [taps[0]])
                ac(lambda e, c0=c0: e.activation(qk[:, c0:c0 + CQ], taps[0][:], AF.Silu), [taps[0]], [qk])
            transpose_to(qT, lambda c: qT[:, c, :], qk, lambda c: qk[:, c * 128:(c + 1) * 128], 8)
            transpose_to(kT, lambda c: kT[:, c, :], qk, lambda c: qk[:, 1024 + c * 128:1024 + (c + 1) * 128], 8, scale_fn=lambda c: 0.0625)
            for h in range(4):
                dv(lambda e, h=h: e.tensor_scalar_mul(khat[:, h * 256:(h + 1) * 256], qk[:, 1024 + h * 256:1024 + (h + 1) * 256], gsm[:, 52 + h:53 + h]),
                   [qk, gsm], [khat])
            cx.dma("sp", zmv[:], z_d[r0:r0 + 128, 0:2048], reads=[z_d], writes=[zmv], owner=zmv)
            dv(lambda e: e.tensor_copy(Vp.v(0, [[257, 4], [1, 256]]), zmv.v(0, [[256, 4], [1, 256]])), [zmv], [Vp])
            ac(lambda e: e.activation(MG[:], zmv[:, 1024:2048], AF.Sigmoid), [zmv], [MG])
            dv(lambda e: e.tensor_tensor(MG[:], MG[:], mgain[:], ALU.mult), [MG, mgain], [MG])
            for h in range(4 if s == 1 else 0):
                p = nps()
                mmg(p[:, 0:128], [(kT[:, 2 * h + c, :], qT[:, 2 * h + c, :]) for c in range(2)], [kT, qT], p)
                dv(lambda e, h=h, p=p: e.scalar_tensor_tensor(sTs[:], p[:, 0:128], gsm[:, 28 + h:29 + h], mle[s][:], ALU.mult, ALU.mult), [p, gsm, mle[s]], [sTs])
                nd = accps()
                if s == 0:
                    pairs = [(sTs[:], Vp[:, h, :])] + [(qT[:, 2 * h + c, :], Cb[:, h, c, :]) for c in range(2)]
                    mmg(nd[:, 0:257], pairs, [sTs, Vp, qT, Cb], nd)
                    for c in range(2):
                        pu = nps()
                        mmg(pu[:, 0:257], [(khat[:, h * 256 + c * 128:h * 256 + (c + 1) * 128], Vp[:, h, :])], [khat, Vp], pu)
                        dv(lambda e, h=h, c=c, pu=pu: e.tensor_tensor(Cst[:, h, c, :], pu[:, 0:257], Cst[:, h, c, :], ALU.add), [pu, Cst], [Cst])
                        dv(lambda e, h=h, c=c: e.tensor_scalar_mul(Cst[:, h, c, :], Cst[:, h, c, :], bend[:, h:h + 1]), [Cst, bend], [Cst])
                else:
                    cx.op("pe", lambda e, h=h: e.matmul(nd[:, 0:257], sTs[:], Vp[:, h, :], start=True, stop=False), reads=[sTs, Vp], writes=[nd])
                    for c in range(2):
                        pass
                    for b in range(16):
                        Cq = Cs[ssi[0] % NSB]; Cqb = Csb[ssi[0] % NSB]; ssi[0] += 1
                        cx.dma("sp", Cq[:, :, 0:256], mC[b + 1, h].rearrange("(c p) e -> p c e", p=128), reads=[mC], writes=[Cq], owner=Cq)
                        cx.dma("sp", Cq[:, :, 256:257], mn[b + 1, h].rearrange("(c p o) -> p c o", p=128, o=1), reads=[mn], writes=[Cq], owner=Cq, allow_slow_non_contiguous=True)
                        cx.op("pool", lambda e, Cq=Cq, Cqb=Cqb: e.tensor_copy(Cqb[:], Cq[:]), reads=[Cq], writes=[Cqb])
                        dv(lambda e, b=b: e.tensor_scalar_mul(khm[:, h * 256:(h + 1) * 256], khat[:, h * 256:(h + 1) * 256], ref[1][:, 18 + b:19 + b]), [khat, ref[1]], [khm])
                        for c in range(2):
                            dv(lambda e, c=c, b=b: e.tensor_tensor(qTm[:, 0, :], qT[:, 2 * h + c, :], blkrow[:, b, :], ALU.mult), [qT, blkrow], [qTm])
                            cx.op("pe", lambda e, c=c, b=b, Cqb=Cqb: e.matmul(nd[:, 0:257], qTm[:, 0, :], Cqb[:, c, :], start=False, stop=(b == 15 and c == 1)),
                                  reads=[qTm, Cqb], writes=[nd])
                        for c in range(2):
                            pu = nps()
                            mmg(pu[:, 0:257], [(khm[:, h * 256 + c * 128:h * 256 + (c + 1) * 128], Vp[:, h, :])], [khm, Vp], pu)
                            dv(lambda e, c=c, pu=pu, Cq=Cq: e.tensor_tensor(Cq[:, c, :], pu[:, 0:257], Cq[:, c, :], ALU.add), [pu, Cq], [Cq])
                            dv(lambda e, c=c, b=b, Cq=Cq: e.tensor_scalar_mul(Cq[:, c, :], Cq[:, c, :], bend[:, (b + 1) * 4 + h:(b + 1) * 4 + h + 1]), [Cq, bend], [Cq])
                            if c == 1:
                                cx.dma("act", oC[b + 1, h].rearrange("(c p) e -> p c e", p=128), Cq[:, :, 0:256], reads=[Cq], writes=[oC], owner=Cq)
                                cx.dma("act", on[b + 1, h].rearrange("(c p o) -> p c o", p=128, o=1), Cq[:, :, 256:257], reads=[Cq], writes=[on], owner=Cq, allow_slow_non_contiguous=True)
                dv(lambda e, nd=nd: e.tensor_copy(st[:, 0:1], nd[:, 256:257]), [nd], [st])
                dv(lambda e: e.tensor_scalar_mul(st[:, 6:7], st[:, 0:1], -1.0), [st], [st])
                dv(lambda e: e.tensor_tensor(st[:, 0:1], st[:, 0:1], st[:, 6:7], ALU.max), [st], [st])
                dv(lambda e, h=h: e.tensor_tensor(st[:, 0:1], st[:, 0:1], gsm[:, 32 + h:33 + h], ALU.mult), [st, gsm], [st])
                dv(lambda e, h=h: e.tensor_tensor(st[:, 0:1], st[:, 0:1], gsm[:, 40 + h:41 + h], ALU.max), [st, gsm], [st])
                dv(lambda e: e.reciprocal(st[:, 0:1], st[:, 0:1]), [st], [st])
                dv(lambda e, h=h: e.tensor_tensor(st[:, 0:1], st[:, 0:1], gsm[:, 32 + h:33 + h], ALU.mult), [st, gsm], [st])
                dv(lambda e, nd=nd: e.tensor_scalar_mul(hm[:], nd[:, 0:256], st[:, 0:1]), [nd, st], [hm])
                dv(lambda e: e.tensor_reduce(st[:, 1:2], hm[:], AX.X, ALU.add), [hm], [st])
                dv(lambda e: e.tensor_scalar_mul(st[:, 1:2], st[:, 1:2], 1.0 / 256), [st], [st])
                dv(lambda e: e.tensor_scalar(hm[:], hm[:], st[:, 1:2], None, ALU.subtract), [hm, st], [hm]) if False else \
                    dv(lambda e: e.tensor_scalar_sub(hm[:], hm[:], st[:, 1:2]), [hm, st], [hm])
                dv(lambda e: e.tensor_tensor(hj[:], hm[:], hm[:], ALU.mult), [hm], [hj])
                dv(lambda e: e.tensor_reduce(st[:, 2:3], hj[:], AX.X, ALU.add), [hj], [st])
                rstd_col(st, 3, st[:, 2:3], 1.0 / 256, RMS_EPS)
                dv(lambda e, h=h: e.scalar_tensor_tensor(mixed[:, h * 256:(h + 1) * 256], hm[:], st[:, 3:4], MG[:, h * 256:(h + 1) * 256], ALU.mult, ALU.mult),
                   [hm, st, MG], [mixed])
            if s == 0 and False:
                dv(lambda e: e.tensor_copy(Cb[:], Cst[:]), [Cst], [Cb])
                if i == NTP - 1:
                    for h in range(4):
                        for c in range(2):
                            cx.dma("act", oC[0, h, c * 128:(c + 1) * 128, :], Cst[:, h, c, 0:256], reads=[Cst], writes=[oC], owner=Cst)
                            cx.dma("act", on[0, h, c * 128:(c + 1) * 128].rearrange("(p o) -> p o", o=1), Cst[:, h, c, 256:257], reads=[Cst], writes=[on], owner=Cst)
            zc = 2056
            for k in range(4):
                cx.dma("sp", A[k][:], z_d[r0:r0 + 128, zc + k * 1024:zc + (k + 1) * 1024], reads=[z_d], writes=[A[k]], owner=A[k])
            ac(lambda e: e.activation(A[1][:], A[1][:], AF.Sigmoid), [A[1]], [A[1]])
            dv(lambda e: e.tensor_tensor(A[1][:], A[1][:], OMLB[:], ALU.mult), [A[1], OMLB], [A[1]])
            dv(lambda e: e.tensor_tensor(A[1][:], A[1][:], LB[:], ALU.add), [A[1], LB], [A[1]])
            ac(lambda e: e.activation(A[4][:], A[1][:], AF.Ln), [A[1]], [A[4]])
            pb = [nps(), nps()]
            for hf in range(2):
                mmg(pb[hf][:], [(trir[s][:], A[4][:, hf * 512:(hf + 1) * 512])], [trir[s], A[4]], pb[hf])
            pd = nps()
            for h in range(8):
                mmg(pd[:, h * 34:(h + 1) * 34], [(A[4][:, h * 128:(h + 1) * 128], ref[s][:])], [A[4], ref[s]], pd)
            ac(lambda e: e.activation(dec[:].rearrange("p a b -> p (a b)"), pd[:, 0:272], AF.Exp), [pd], [dec])
            for hf in range(2):
                ac(lambda e, hf=hf: e.activation(A[5][:, hf * 512:(hf + 1) * 512], pb[hf][:], AF.Exp), [pb[hf]], [A[5]])
                ac(lambda e, hf=hf: e.activation(A[4][:, hf * 512:(hf + 1) * 512], pb[hf][:], AF.Exp, scale=-1.0), [pb[hf]], [A[4]])
            ac(lambda e: e.activation(A[0][:], A[0][:], AF.Silu), [A[0]], [A[0]])
            dv(lambda e: e.scalar_tensor_tensor(A[0][:], A[0][:], float(128 ** -0.5), A[5][:], ALU.mult, ALU.mult), [A[0], A[5]], [A[0]])
            dv(lambda e: e.tensor_scalar(A[1][:], A[1][:], -1.0, 1.0, ALU.mult, ALU.add), [A[1]], [A[1]])
            dv(lambda e: e.tensor_tensor(A[1][:], A[1][:], A[4][:], ALU.mult), [A[1], A[4]], [A[1]])
            cx.op("pool", lambda e: e.tensor_copy(ktb[:], A[1][:]), reads=[A[1]], writes=[ktb])
            cx.op("pool", lambda e: e.tensor_copy(vb[:], A[2][:]), reads=[A[2]], writes=[vb])
            ac(lambda e: e.activation(A[3][:], A[3][:], AF.Silu), [A[3]], [A[3]])
            dv(lambda e: e.tensor_tensor(A[3][:], A[3][:], ggain[:], ALU.mult), [A[3], ggain], [A[3]])
            transpose_to(qtT, lambda c: qtT[:, c, :], A[0], lambda c: A[0][:, c * 128:(c + 1) * 128], 8)
            transpose_to(ktT, lambda c: ktT[:, c, :], A[1], lambda c: A[1][:, c * 128:(c + 1) * 128], 8)
            if s == 0:
                dv(lambda e: e.tensor_tensor(Smid[:], Sst[:], dec.v(0, [[34, 8], [0, 128]]), ALU.mult), [Sst, dec], [Smid])
                cx.op("pool", lambda e: e.tensor_copy(Smidb[:], Smid[:]), reads=[Smid], writes=[Smidb])
            if s == 0:
                def gen_m():
                    banks = [PS[0], PS[1]]; bi = [0]

                    def mps():
                        bi[0] += 1
                        return banks[bi[0] % 2]
                    for h in range(4):
                        p = mps()
                        mmg(p[:, 0:128], [(kT[:, 2 * h + c, :], qT[:, 2 * h + c, :]) for c in range(2)], [kT, qT], p)
                        dv(lambda e: e.scalar_tensor_tensor(sTs[:], p[:, 0:128], gsm[:, 28 + h:29 + h], mle[s][:], ALU.mult, ALU.mult), [p, gsm, mle[s]], [sTs])
                        nd = PS[6]
                        pairs = [(sTs[:], Vp[:, h, :])] + [(qT[:, 2 * h + c, :], Cb[:, h, c, :]) for c in range(2)]
                        mmg(nd[:, 0:257], pairs, [sTs, Vp, qT, Cb], nd)
                        yield
                        for c in range(2):
                            pu = mps()
                            mmg(pu[:, 0:257], [(khat[:, h * 256 + c * 128:h * 256 + (c + 1) * 128], Vp[:, h, :])], [khat, Vp], pu)
                            dv(lambda e: e.tensor_tensor(Cst[:, h, c, :], pu[:, 0:257], Cst[:, h, c, :], ALU.add), [pu, Cst], [Cst])
                            dv(lambda e: e.tensor_scalar_mul(Cst[:, h, c, :], Cst[:, h, c, :], bend[:, h:h + 1]), [Cst, bend], [Cst])
                            yield
                        dv(lambda e: e.tensor_copy(st[:, 0:1], nd[:, 256:257]), [nd], [st])
                        dv(lambda e: e.tensor_scalar_mul(st[:, 6:7], st[:, 0:1], -1.0), [st], [st])
                        dv(lambda e: e.tensor_tensor(st[:, 0:1], st[:, 0:1], st[:, 6:7], ALU.max), [st], [st])
                        dv(lambda e: e.tensor_tensor(st[:, 0:1], st[:, 0:1], gsm[:, 32 + h:33 + h], ALU.mult), [st, gsm], [st])
                        yield
                        dv(lambda e: e.tensor_tensor(st[:, 0:1], st[:, 0:1], gsm[:, 40 + h:41 + h], ALU.max), [st, gsm], [st])
                        dv(lambda e: e.reciprocal(st[:, 0:1], st[:, 0:1]), [st], [st])
                        dv(lambda e: e.tensor_tensor(st[:, 0:1], st[:, 0:1], gsm[:, 32 + h:33 + h], ALU.mult), [st, gsm], [st])
                        dv(lambda e: e.tensor_scalar_mul(hm[:], nd[:, 0:256], st[:, 0:1]), [nd, st], [hm])
                        yield
                        dv(lambda e: e.tensor_reduce(st[:, 1:2], hm[:], AX.X, ALU.add), [hm], [st])
                        dv(lambda e: e.tensor_scalar_mul(st[:, 1:2], st[:, 1:2], 1.0 / 256), [st], [st])
                        dv(lambda e: e.tensor_scalar_sub(hm[:], hm[:], st[:, 1:2]), [hm, st], [hm])
                        yield
                        dv(lambda e: e.tensor_tensor(hj[:], hm[:], hm[:], ALU.mult), [hm], [hj])
                        dv(lambda e: e.tensor_reduce(st[:, 2:3], hj[:], AX.X, ALU.add), [hj], [st])
                        rstd_col(st, 3, st[:, 2:3], 1.0 / 256, RMS_EPS)
                        yield
                        dv(lambda e: e.scalar_tensor_tensor(mixed[:, h * 256:(h + 1) * 256], hm[:], st[:, 3:4], MG[:, h * 256:(h + 1) * 256], ALU.mult, ALU.mult),
                           [hm, st, MG], [mixed])
                        yield

                def gen_g():
                    banks = [PS[2], PS[3]]; bi = [0]

                    def gps():
                        bi[0] += 1
                        return banks[bi[0] % 2]
                    for h in range(8):
                        hc = slice(h * 128, (h + 1) * 128)
                        p = gps()
                        mmg(p[0:64, 0:64], [(ktT[:, h, 0:64], qtT[:, h, 0:64])], [ktT, qtT], p)
                        mmg(p[:, 64:128], [(ktT[:, h, :], qtT[:, h, 64:128])], [ktT, qtT], p)
                        dv(lambda e: e.tensor_tensor(ATs[0:64, 0:64], p[0:64, 0:64], mle[s][0:64, 0:64], ALU.mult), [p, mle[s]], [ATs])
                        dv(lambda e: e.tensor_tensor(ATs[:, 64:128], p[:, 64:128], mle[s][:, 64:128], ALU.mult), [p, mle[s]], [ATs])
                        dv(lambda e: e.memset(ATs[64:128, 0:64], 0.0), [], [ATs])
                        yield
                        po = PS[7]
                        mmg(po[:, 0:128], [(ATs[:], vb[:, hc]), (qtT[:, h, :], Smidb[:, h, :])], [ATs, vb, qtT, Smidb], po)
                        pu = gps()
                        mmg(pu[:, 0:128], [(ktb[:, hc], vb[:, hc])], [ktb, vb], pu)
                        yield
                        dv(lambda e: e.tensor_tensor(Sst[:, h, :], pu[:, 0:128], Smid[:, h, :], ALU.add), [pu, Smid], [Sst])
                        dv(lambda e: e.tensor_scalar_mul(Sst[:, h, :], Sst[:, h, :], dec[:, h, 17:18]), [Sst, dec], [Sst])
                        yield
                        ac(lambda e: e.activation(hm2[:], po[:, 0:128], AF.Copy), [po], [hm2])
                        cx.op("pool", lambda e: e.tensor_tensor(hj2[:], hm2[:], hm2[:], ALU.mult), reads=[hm2], writes=[hj2])
                        yield
                        dv(lambda e: e.tensor_reduce(st2[:, 4:5], hj2[:], AX.X, ALU.add), [hj2], [st2])
                        rstd_col(st2, 5, st2[:, 4:5], 1.0 / 128, RMS_EPS)
                        yield
                        dv(lambda e: e.scalar_tensor_tensor(mixed[:, 1024 + hc.start:1024 + hc.stop], hm2[:], st2[:, 5:6], A[3][:, hc], ALU.mult, ALU.mult),
                           [hm2, st2, A[3]], [mixed])
                        yield
                gens = [gen_m(), gen_g()]
                while gens:
                    for g in list(gens):
                        try:
                            next(g)
                        except StopIteration:
                            gens.remove(g)
                dv(lambda e: e.tensor_copy(Cb[:], Cst[:]), [Cst], [Cb])
                if i == NTP - 1:
                    for h in range(4):
                        for c in range(2):
                            cx.dma("act", oC[0, h, c * 128:(c + 1) * 128, :], Cst[:, h, c, 0:256], reads=[Cst], writes=[oC], owner=Cst)
                            cx.dma("act", on[0, h, c * 128:(c + 1) * 128].rearrange("(p o) -> p o", o=1), Cst[:, h, c, 256:257], reads=[Cst], writes=[on], owner=Cst)
            for h in range(8 if s == 1 else 0):
                hc = slice(h * 128, (h + 1) * 128)
                p = nps()
                if s == 0:
                    mmg(p[0:64, 0:64], [(ktT[:, h, 0:64], qtT[:, h, 0:64])], [ktT, qtT], p)
                    mmg(p[:, 64:128], [(ktT[:, h, :], qtT[:, h, 64:128])], [ktT, qtT], p)
                    dv(lambda e, p=p: e.tensor_tensor(ATs[0:64, 0:64], p[0:64, 0:64], mle[s][0:64, 0:64], ALU.mult), [p, mle[s]], [ATs])
                    dv(lambda e, p=p: e.tensor_tensor(ATs[:, 64:128], p[:, 64:128], mle[s][:, 64:128], ALU.mult), [p, mle[s]], [ATs])
                    dv(lambda e: e.memset(ATs[64:128, 0:64], 0.0), [], [ATs])
                else:
                    mmg(p[:, 0:128], [(ktT[:, h, :], qtT[:, h, :])], [ktT, qtT], p)
                    dv(lambda e, p=p: e.tensor_tensor(ATs[:], p[:, 0:128], mle[s][:], ALU.mult), [p, mle[s]], [ATs])
                po = accps()
                if s == 0:
                    mmg(po[:, 0:128], [(ATs[:], vb[:, hc]), (qtT[:, h, :], Smidb[:, h, :])], [ATs, vb, qtT, Smidb], po)
                    pu = nps()
                    mmg(pu[:, 0:128], [(ktb[:, hc], vb[:, hc])], [ktb, vb], pu)
                    dv(lambda e, h=h, pu=pu: e.tensor_tensor(Sst[:, h, :], pu[:, 0:128], Smid[:, h, :], ALU.add), [pu, Smid], [Sst])
                    dv(lambda e, h=h: e.tensor_scalar_mul(Sst[:, h, :], Sst[:, h, :], dec[:, h, 17:18]), [Sst, dec], [Sst])
                else:
                    cx.op("pe", lambda e, hc=hc: e.matmul(po[:, 0:128], ATs[:], vb[:, hc], start=True, stop=False), reads=[ATs, vb], writes=[po])
                    for b in range(16):
                        Sq = Ss[ssi[0] % NSB]; Sqb = Ssb[ssi[0] % NSB]; ssi[0] += 1
                        cx.dma("sp", Sq[:], gS[b + 1, h], reads=[gS], writes=[Sq], owner=Sq)
                        cx.op("pool", lambda e, Sq=Sq, Sqb=Sqb: e.tensor_copy(Sqb[:], Sq[:]), reads=[Sq], writes=[Sqb])
                        dv(lambda e, b=b, h=h: e.tensor_tensor(qTm[:, 0, :], qtT[:, h, :], blkrow[:, b, :], ALU.mult), [qtT, blkrow], [qTm])
                        cx.op("pe", lambda e, b=b, Sqb=Sqb: e.matmul(po[:, 0:128], qTm[:, 0, :], Sqb[:], start=False, stop=(b == 15)), reads=[qTm, Sqb], writes=[po])
                        dv(lambda e, b=b, hc=hc: e.tensor_scalar_mul(khm[:, 0:128], ktb[:, hc], ref[1][:, 18 + b:19 + b]), [ktb, ref[1]], [khm])
                        pu = nps()
                        mmg(pu[:, 0:128], [(khm[:, 0:128], vb[:, hc])], [khm, vb], pu)
                        dv(lambda e, pu=pu, Sq=Sq: e.tensor_tensor(Sq[:], pu[:, 0:128], Sq[:], ALU.add), [pu, Sq], [Sq])
                        dv(lambda e, b=b, h=h, Sq=Sq: e.tensor_scalar_mul(Sq[:], Sq[:], dec[:, h, 17 + b + 1:17 + b + 2]), [Sq, dec], [Sq])
                        cx.dma("act", oS[b + 1, h], Sq[:], reads=[Sq], writes=[oS], owner=Sq)
                dv(lambda e, po=po: e.tensor_tensor(hj[:, 0:128], po[:, 0:128], po[:, 0:128], ALU.mult) if False else e.tensor_copy(hm[:, 0:128], po[:, 0:128]), [po], [hm])
                dv(lambda e: e.tensor_tensor(hj[:, 0:128], hm[:, 0:128], hm[:, 0:128], ALU.mult), [hm], [hj])
                dv(lambda e: e.tensor_reduce(st[:, 4:5], hj[:, 0:128], AX.X, ALU.add), [hj], [st])
                rstd_col(st, 5, st[:, 4:5], 1.0 / 128, RMS_EPS)
                dv(lambda e, hc=hc: e.scalar_tensor_tensor(mixed[:, 1024 + hc.start:1024 + hc.stop], hm[:, 0:128], st[:, 5:6], A[3][:, hc], ALU.mult, ALU.mult),
                   [hm, st, A[3]], [mixed])
            if s == 0 and i == NTP - 1:
                cx.dma("act", oS[0].rearrange("h k v -> k h v"), Sst[:], reads=[Sst], writes=[oS], owner=Sst)
            transpose_to(mixT, lambda c: mixT[:, c, r0:r0 + 128], mixed, lambda c: mixed[:, c * 128:(c + 1) * 128], 16)
        cx.dma("sp", oconv[0], ext_p[NP:NP + 3, :], reads=[ext_p], writes=[oconv], owner=oconv)
        cx.dma("sp", oconv[1:17], ext_s[:, 8:11, :], reads=[ext_s], writes=[oconv], owner=oconv)

    mix_d = dscr("mix_d", [6, 128, 16 * NTOK], BF16)

    @staged
    def stage_l1_mix(hT):
        shs = sb("shs", [NB, D]); shT = sb("shT", [128, 16, NB], BF16); muT = sb("muT", [128, 6, 16])
        xx = [sb("xx%d" % i, [128, NTOK], BF16) for i in range(2)]
        mo_ = [sb("mo%d" % i, [128, NTOK], BF16) for i in range(4)]
        cx.dma("sp", shs[:], rsh[:], writes=[shs], owner=shs)
        for c0 in range(0, 16, 4):
            p = nps()
            for j in range(4):
                c = c0 + j
                cx.op("pe", lambda e, j=j, c=c: e.transpose(p.v(j * 32, [[1, NB]]), shs[:, c * 128:(c + 1) * 128], ident[0:NB, 0:NB]),
                      reads=[shs, ident], writes=[p])
            for j in range(4):
                evac(shT[:, c0 + j, :], p.v(j * 32, [[1, NB]]), [p], [shT])
        for i in range(6):
            cx.dma("sp", muT[:, i, :], W["r_mu"][0, i].rearrange("(c p) -> p c", p=128), writes=[muT], owner=muT, allow_slow_non_contiguous=True)
        oi = 0
        for c in range(16):
            x = xx[c % 2]
            cx.op("dve", lambda e: e.tensor_tensor(x[:, 1:NTOK], hT[:, c, 0:NTOK - 1], hT[:, c, 1:NTOK], ALU.subtract), reads=[hT], writes=[x])
            cx.op("dve", lambda e: e.tensor_tensor(x[:, 0:1], shT[:, c, 0:1], hT[:, c, 0:1], ALU.subtract), reads=[hT, shT], writes=[x])
            cx.op("dve", lambda e: e.tensor_tensor(x.v(NP, [[8, 16]]), shT[:, c, 1:NB], hT.v(c * NTOK + NP, [[8, 16]]), ALU.subtract), reads=[hT, shT], writes=[x])
            for i in range(6):
                o = mo_[oi % 4]; oi += 1
                cx.op("dve", lambda e, i=i, o=o: e.scalar_tensor_tensor(o[:], x[:], muT[:, i, c:c + 1], hT[:, c, :], ALU.mult, ALU.add),
                      reads=[x, muT, hT], writes=[o])
                cx.dma("sp", mix_d[i, :, c * NTOK:(c + 1) * NTOK], o[:], reads=[o], writes=[mix_d], owner=o)

    def load_mix(i, hT):
        cx.dma("sp", hT[:].rearrange("p a b -> p (a b)"), mix_d[i], reads=[mix_d], writes=[hT], owner=hT)
        cx.barrier()

    @staged
    def lora1(aT, w1_ap, R, func, tT):
        w1 = sb("l1w", [128, 16, R], BF16)
        cx.dma("pool", w1[:], w1_ap.rearrange("(kc p) n -> p kc n", p=128), writes=[w1], owner=w1)
        for oc in range((R + 127) // 128):
            rc = min(128, R - oc * 128)
            for g0 in range(0, NTOK, 512):
                gn = min(512, NTOK - g0)
                p = nps()
                for kc in range(16):
                    cx.op("pe", lambda e, kc=kc: e.matmul(p[0:rc, 0:gn], w1[:, kc, oc * 128:oc * 128 + rc], aT[:, kc, g0:g0 + gn], start=(kc == 0), stop=(kc == 15)),
                          reads=[w1, aT], writes=[p])
                cx.op("act", lambda e: e.activation(tT[0:rc, oc, g0:g0 + gn], p[0:rc, 0:gn], func), reads=[p], writes=[tT])

    def stage_l1_proj(hT):
        def dst_rz(base):
            return lambda c0: (lambda i: [(rz_d, rz_d[i * 128:(i + 1) * 128, base + c0:base + c0 + 512], 0, 128)])
        def blocks_for(w, base):
            return [(w[:, c0:c0 + 512], dst_rz(base)(c0)) for c0 in range(0, D, 512)]
        stacks.append(contextlib.ExitStack()); stage_bufs.append([])
        tT = sb("tT", [128, 2, NTOK], BF16)
        load_mix(0, hT); proj_tok(hT, blocks_for(W["r_wr"][0], 0))
        load_mix(2, hT); proj_tok(hT, blocks_for(W["r_wk"][0], D))
        load_mix(3, hT); proj_tok(hT, blocks_for(W["r_wv"][0], 2 * D))
        load_mix(1, hT); lora1(hT, W["r_w1"][0], 96, AF.Tanh, tT)
        proj_tok(tT, blocks_for(W["r_w2"][0], 3 * D), K=1, kpart=96, bias_fn=lambda bi: W["r_w0"][0, bi * 512:(bi + 1) * 512], sigmoid=True)
        load_mix(4, hT); lora1(hT, W["r_a1"][0], 96, AF.Copy, tT)
        proj_tok(tT, blocks_for(W["r_a2"][0], 4 * D), K=1, kpart=96, bias_fn=lambda bi: W["r_a0"][0, bi * 512:(bi + 1) * 512], sigmoid=True)
        load_mix(5, hT); lora1(hT, W["r_g1"][0], 256, AF.Sigmoid, tT)
        proj_tok(tT, blocks_for(W["r_g2"][0], 5 * D), K=2, kpart=128)
        cx.barrier(); cx.release(stage_bufs.pop()); stacks.pop().close()

    @staged
    def stage_l1_rec(outT):
        npsmod[0] = 4
        dv = lambda fn, r, w: cx.op("dve", fn, reads=r, writes=w)
        ac = lambda fn, r, w: cx.op("act", fn, reads=r, writes=w)
        po_ = lambda fn, r, w: cx.op("pool", fn, reads=r, writes=w)
        mle = [sb("mle%d" % i, [128, 128]) for i in range(2)]; mlt = [sb("mlt%d" % i, [128, 128]) for i in range(2)]
        mgt = [sb("mgt%d" % i, [128, 128]) for i in range(2)]; refs = sb("refs", [128, 34])
        blkrow = sb("blkrow", [128, 16, 128], BF16)
        NBP = 4
        for i, (a_, b_, c_) in enumerate([("rle_p", "rlt_p", "rgt_p"), ("mle_s", "mlt_s", "mgt_s")]):
            cx.dma("sp", mle[i][:], CN[a_][:], writes=[mle[i]], owner=mle[i])
            cx.dma("sp", mlt[i][:], CN[b_][:], writes=[mlt[i]], owner=mlt[i])
            cx.dma("sp", mgt[i][:], CN[c_][:], writes=[mgt[i]], owner=mgt[i])
        cx.dma("sp", refs[:], CN["ref_s"][:], writes=[refs], owner=refs)
        rblk = sb("rblk", [128, NB]); rblkrow = sb("rblkrow", [128, NBP, 128], BF16)
        cx.dma("sp", rblk[:], CN["rblk_p"][:], writes=[rblk], owner=rblk)
        cx.dma("pool", rblkrow[:], CN["rblkrow_p"][:].rearrange("p (b t) -> p b t", b=NBP), writes=[rblkrow], owner=rblkrow)
        Hrd = [sb("Hrd%d" % i, [128, NBP, 64], BF16) for i in range(2)]
        rmk = sb("rmk", [128, NBP, 128], BF16)
        cx.dma("pool", blkrow[:], CN["blkrow"][:].rearrange("p (b t) -> p b t", b=16), writes=[blkrow], owner=blkrow)
        HW = 1024
        PRMH = [{k: sb("prm_%s%d" % (k, hf), [128, HW]) for k in ["r_kk", "r_ka", "r_rk"]} for hf in range(2)]
        for hf in range(2):
            for k_, b_ in PRMH[hf].items():
                cx.dma("sp", b_[:], W[k_][0, hf * HW:(hf + 1) * HW].partition_broadcast(128), writes=[b_], owner=b_)
        LNW = sb("prm_lnw", [128, HW]); LNB = sb("prm_lnb", [128, HW])
        Rb, Kb, Vb_, SW, Aa, Cc, KK, TMP, E1, E2 = [sb("rw%d" % i, [128, HW]) for i in range(10)]
        Gg = E2
        aTt, bTt, kTt, rTt = [sb("rt%d" % i, [128, 8, 128], BF16) for i in range(4)]
        btk, ktk, vtk = [sb("rk%d" % i, [128, HW], BF16) for i in range(3)]
        Hst = sb("Hst", [128, 16, 64]); Hbm = [sb("Hbm%d" % i, [128, 16, 64], BF16) for i in range(2)]
        bTm = [sb("bTm%d" % i, [128, 8, 128], BF16) for i in range(2)]; kTm = [sb("kTm%d" % i, [128, 8, 128], BF16) for i in range(2)]
        Hsbm = [[sb("Hsbm%d_%d" % (i, j), [128, 64], BF16) for j in range(2)] for i in range(3)]
        cx.op("dve", lambda e: e.memset(Hst[:], 0.0), writes=[Hst])
        for zb in Hbm + bTm + kTm + Hsbm[0] + Hsbm[1] + Hsbm[2] + Hrd:
            cx.op("dve", lambda e, zb=zb: e.memset(zb[:], 0.0), writes=[zb])
        Nb = [sb("Nb%d" % i, [128, 2, 128], BF16) for i in range(2)]; Mb = [sb("Mb%d" % i, [128, 2, 128], BF16) for i in range(2)]
        TtS = [sb("TtS%d" % i, [128, 2, 128], BF16) for i in range(3)]; AakS = [sb("AakS%d" % i, [128, 2, 128], BF16) for i in range(3)]
        ArbS = [sb("ArbS%d" % i, [128, 2, 128], BF16) for i in range(3)]; ArkS = [sb("ArkS%d" % i, [128, 2, 128], BF16) for i in range(3)]
        BSET = [dict(am=sb("am%d" % i, [128, 128], BF16), Xb=sb("Xbq%d" % i, [128, 128], BF16), Ubf=sb("Ubq%d" % i, [128, 128], BF16),
                     bm=sb("bmq%d" % i, [128, 128], BF16), km=sb("kmq%d" % i, [128, 128], BF16), rmk=sb("rmkq%d" % i, [128, 4, 128], BF16),
                     Hrd=[sb("Hrdq%d_%d" % (i, j), [128, 4, 64], BF16) for j in range(2)]) for i in range(2)]
        for i in range(2):
            for zb in BSET[i]["Hrd"]:
                cx.op("dve", lambda e, zb=zb: e.memset(zb[:], 0.0), writes=[zb])
        Tt, AakT, ArbT, ArkT = TtS[0], AakS[0], ArbS[0], ArkS[0]
        Xb = sb("Xb", [128, 128], BF16); Ub = sb("Ub", [128, 128], BF16); Ubf = Ub
        am = sb("am", [128, 128], BF16); rm = sb("rm", [128, 128], BF16); bm = sb("bm", [128, 128], BF16); km = sb("km", [128, 128], BF16)
        dL = sb("dL", [128, 8, NB]); stt_ = sb("stt", [128, 4, 16])
        NSB = 3
        Sin = [sb("Sin%d" % i, [64, 2, 64]) for i in range(NSB)]; Hs = [sb("Hs%d" % i, [128, 64]) for i in range(NSB)]
        Sout = [sb("Sout%d" % i, [64, 2, 64]) for i in range(NSB)]
        ssi = [0]
        h3 = lambda b_: b_[:].rearrange("p (h j) -> p h j", j=64)

        def prompt_pairs(half, ncp, Y):
            s = 0
            nit = 4

            def genA(cp, st):
                def pairmm(p, lT, rT_):
                    for hh in range(2):
                        lb = lT[hh] if isinstance(lT, list) else lT
                        rb = rT_[hh] if isinstance(rT_, list) else rT_
                        mmg(p[:, hh * 128:(hh + 1) * 128], [(lb[:, cp, :], rb[:, cp, :])], [lb, rb], p)

                def pairev(dst, p, mask):
                    dv(lambda e: e.tensor_tensor(dst[:], p[:, 0:256].rearrange("p (a b) -> p a b", a=2), mask.v(0, [[0, 2], [1, 128]]), ALU.mult), [p, mask], [dst])
                Tt_ = TtS[st]
                p = nps(); pairmm(p, aTt, bTm); pairev(Nb[0], p, mgt[s]); yield
                p = nps(); pairmm(p, bTm, aTt); pairev(Mb[0], p, mlt[s]); yield
                p = nps(); pairmm(p, kTm, aTt); pairev(AakS[st], p, mlt[s]); yield
                p = nps(); pairmm(p, bTm, rTt); pairev(ArbS[st], p, mle[s]); yield
                p = nps(); pairmm(p, kTm, rTt); pairev(ArkS[st], p, mle[s]); yield
                dv(lambda e: e.tensor_tensor(Tt_[:], Mb[0][:], identb.v(0, [[0, 2], [1, 128]]), ALU.add), [Mb[0], identb], [Tt_])
                cur = 0
                for it in range(nit):
                    nx = 1 - cur
                    p1 = nps(); p2 = nps()
                    for hh in range(2):
                        if it < nit - 1:
                            mmg(p1[:, hh * 128:(hh + 1) * 128], [(Nb[cur][:, hh, :], Mb[cur][:, hh, :])], [Nb[cur], Mb[cur]], p1)
                        mmg(p2[:, hh * 128:(hh + 1) * 128], [(Mb[cur][:, hh, :], Nb[cur][:, hh, :])], [Nb[cur], Mb[cur]], p2)
                    yield
                    if it < nit - 1:
                        ac(lambda e: e.activation(Mb[nx][:].rearrange("p a b -> p (a b)"), p1[:, 0:256], AF.Copy), [p1], [Mb[nx]])
                    dv(lambda e: e.tensor_copy(Nb[nx][:].rearrange("p a b -> p (a b)"), p2[:, 0:256]), [p2], [Nb[nx]])
                    p3 = nps()
                    for hh in range(2):
                        mmg(p3[:, hh * 128:(hh + 1) * 128], [(Nb[nx][:, hh, :], Tt_[:, hh, :])], [Nb[nx], Tt_], p3)
                    yield
                    dv(lambda e: e.tensor_tensor(Tt_[:].rearrange("p a b -> p (a b)"), p3[:, 0:256], Tt_[:].rearrange("p a b -> p (a b)"), ALU.add), [p3, Tt_], [Tt_])
                    cur = nx
                    yield

            def genB(cp, st, bs):
                gc = half * 8 + cp
                cs = slice(cp * 128, (cp + 1) * 128)
                Tt_, AakT_, ArbT_, ArkT_ = TtS[st], AakS[st], ArbS[st], ArkS[st]
                B_ = BSET[bs]
                am, Xb, Ubf, bm, km, rmk, Hrd = B_["am"], B_["Xb"], B_["Ubf"], B_["bm"], B_["km"], B_["rmk"], B_["Hrd"]
                px = pu = ph = PS[3 + 2 * bs]
                py = PS[4 + 2 * bs]
                for k in range(NBP):
                    dv(lambda e: e.tensor_tensor(am[:], aTt[:, cp, :], rblkrow[:, k, :], ALU.mult), [aTt, rblkrow], [am])
                    po_(lambda e: e.tensor_tensor(rmk[:, k, :], rTt[:, cp, :], rblkrow[:, k, :], ALU.mult), [rTt, rblkrow], [rmk])
                    Hsrc = [(Hbm[hh][:, gc, :], Hbm[hh]) if k == 0 else (Hrd[hh][:, k, :], Hrd[hh]) for hh in range(2)]
                    for hh in range(2):
                        mmg(px[:, hh * 64:(hh + 1) * 64], [(AakT_[:, hh, :], vtk[:, cp * 128 + hh * 64:cp * 128 + (hh + 1) * 64]),
                                                           (am[:], Hsrc[hh][0])], [AakT_, vtk, am, Hsrc[hh][1]], px)
                    yield
                    ac(lambda e: e.activation(Xb[:], px[:, 0:128], AF.Copy), [px], [Xb])
                    for hh in range(2):
                        mmg(pu[:, hh * 64:(hh + 1) * 64], [(Tt_[:, hh, :], Xb[:, hh * 64:(hh + 1) * 64])], [Tt_, Xb], pu)
                    yield
                    if k == 0:
                        dv(lambda e: e.tensor_scalar_mul(Ubf[:], pu[:, 0:128], rblk[:, k:k + 1]), [pu, rblk], [Ubf])
                    else:
                        dv(lambda e: e.scalar_tensor_tensor(Ubf[:], pu[:, 0:128], rblk[:, k:k + 1], Ubf[:], ALU.mult, ALU.add), [pu, rblk, Ubf], [Ubf])
                    ac(lambda e: e.activation(bm[:], btk[:, cs], AF.Copy, scale=rblk[:, k:k + 1]), [btk, rblk], [bm])
                    ac(lambda e: e.activation(km[:], ktk[:, cs], AF.Copy, scale=rblk[:, k:k + 1]), [ktk, rblk], [km])
                    mmg(ph[:, 0:128], [(bm[:], Ubf[:]), (km[:], vtk[:, cs])], [bm, Ubf, km, vtk], ph)
                    yield
                    for hh in range(2):
                        pb = 64 * hh
                        dv(lambda e: e.tensor_tensor(Hst[pb:pb + 64, gc, :], ph[pb:pb + 64, pb:pb + 64], Hst[pb:pb + 64, gc, :], ALU.add), [ph, Hst], [Hst])
                    ac(lambda e: e.activation(Hst[:, gc, :], Hst[:, gc, :], AF.Copy, scale=dL[:, cp, k:k + 1]), [Hst, dL], [Hst])
                    if k < NBP - 1:
                        for hh in range(2):
                            pb = 64 * hh
                            po_(lambda e: e.tensor_copy(Hrd[hh][pb:pb + 64, k + 1, :], Hst[pb:pb + 64, gc, :]), [Hst], [Hrd[hh]])
                    yield
                for hh in range(2):
                    vh = vtk[:, cp * 128 + hh * 64:cp * 128 + (hh + 1) * 64]
                    pairs = [(ArbT_[:, hh, :], Ubf[:, hh * 64:(hh + 1) * 64]), (ArkT_[:, hh, :], vh), (rmk[:, 0, :], Hbm[hh][:, gc, :])]
                    pairs += [(rmk[:, k, :], Hrd[hh][:, k, :]) for k in range(1, NBP)]
                    mmg(py[:, hh * 64:(hh + 1) * 64], pairs, [ArbT_, Ubf, ArkT_, vtk, rmk, Hbm[hh], Hrd[hh]], py)
                yield
                ac(lambda e: e.activation(Y[:, cs], py[:, 0:128], AF.Copy), [py], [Y])
                for hh in range(2):
                    pb = 64 * hh
                    po_(lambda e: e.tensor_copy(Hbm[hh][pb:pb + 64, gc, :], Hst[pb:pb + 64, gc, :]), [Hst], [Hbm[hh]])
                yield

            npsmod[0] = 3
            active = {}
            doneA = set(); doneB = set()
            nextA = 0; nextB = 0
            while len(doneB) < ncp:
                if nextA < ncp and "A" not in active and (nextA < 3 or (nextA - 3) in doneB):
                    active["A"] = (genA(nextA, nextA % 3), nextA); nextA += 1
                if nextB < ncp and nextB in doneA and ("B%d" % (nextB % 2)) not in active:
                    active["B%d" % (nextB % 2)] = (genB(nextB, nextB % 3, nextB % 2), nextB); nextB += 1
                for key in list(active):
                    g, idx = active[key]
                    try:
                        next(g)
                    except StopIteration:
                        del active[key]
                        (doneA if key == "A" else doneB).add(idx)
            npsmod[0] = 4

        def bc16(b_, col):
            return stt_.v(col * 16, [[1, 16], [0, 64]])

        import os
        for i in range(NT):
            s = 1 if i == NTP else 0
            if os.environ.get("K_REC") == "p" and s == 1:
                continue
            if os.environ.get("K_REC") == "s" and s == 0:
                continue
            r0 = i * 128
            nit = 2 if s else 4
            for half in range(2):
                f0 = half * HW
                PRM = dict(PRMH[half]); PRM["r_lnw"] = LNW; PRM["r_lnb"] = LNB
                for idx, b_ in enumerate([Rb, Kb, Vb_, SW, Aa]):
                    cx.dma("sp", b_[:], rz_d[r0:r0 + 128, idx * D + f0:idx * D + f0 + HW], reads=[rz_d], writes=[b_], owner=b_)
                cx.dma("sp", LNW[:], W["r_lnw"][0, f0:f0 + HW].partition_broadcast(128), writes=[LNW], owner=LNW)
                cx.dma("sp", LNB[:], W["r_lnb"][0, f0:f0 + HW].partition_broadcast(128), writes=[LNB], owner=LNB)
                dv(lambda e: e.tensor_scalar_mul(SW[:], SW[:], -0.6065306597126334), [SW], [SW])
                pc = [nps(), nps()]
                for hf in range(2):
                    mmg(pc[hf][:], [(mle[s][:], SW[:, hf * 512:(hf + 1) * 512])], [mle[s], SW], pc[hf])
                for hf in range(2):
                    evac(Cc[:, hf * 512:(hf + 1) * 512], pc[hf][:], [pc[hf]], [Cc])
                pd = nps()
                for cp in range(8):
                    mmg(pd[:, cp * NB:(cp + 1) * NB], [(SW[:, cp * 128:(cp + 1) * 128], refs[:, 17:34] if s else rblk[:])], [SW, refs, rblk], pd)
                ac(lambda e: e.activation(dL[:].rearrange("p a b -> p (a b)"), pd[:, 0:8 * NB], AF.Exp), [pd], [dL])
                dv(lambda e: e.tensor_tensor(KK[:], Kb[:], PRM["r_kk"][:], ALU.mult), [Kb, PRM["r_kk"]], [KK])
                ac(lambda e: e.activation(TMP[:], KK[:], AF.Square), [KK], [TMP])
                dv(lambda e: e.tensor_reduce(stt_[:, 0, :], h3(TMP), AX.X, ALU.add), [TMP], [stt_])
                dv(lambda e: e.tensor_scalar_max(stt_[:, 0, :], stt_[:, 0, :], 1e-24), [stt_], [stt_])
                ac(lambda e: e.activation(stt_[:, 0, :], stt_[:, 0, :], AF.Sqrt), [stt_], [stt_])
                dv(lambda e: e.reciprocal(stt_[:, 0, :], stt_[:, 0, :]), [stt_], [stt_])
                dv(lambda e: e.tensor_tensor(h3(KK), h3(KK), bc16(stt_, 0), ALU.mult), [KK, stt_], [KK])
                dv(lambda e: e.scalar_tensor_tensor(TMP[:], Aa[:], -1.0, PRM["r_ka"][:], ALU.add, ALU.mult), [Aa, PRM["r_ka"]], [TMP])
                dv(lambda e: e.scalar_tensor_tensor(Kb[:], TMP[:], 1.0, Kb[:], ALU.add, ALU.mult), [TMP, Kb], [Kb])
                po_(lambda e: e.tensor_tensor(TMP[:], Rb[:], Kb[:], ALU.mult), [Rb, Kb], [TMP])
                po_(lambda e: e.tensor_tensor(TMP[:], TMP[:], PRM["r_rk"][:], ALU.mult), [TMP, PRM["r_rk"]], [TMP])
                dv(lambda e: e.tensor_reduce(stt_[:, 1, :], h3(TMP), AX.X, ALU.add), [TMP], [stt_])
                dv(lambda e: e.tensor_tensor(Aa[:], KK[:], Aa[:], ALU.mult), [KK, Aa], [Aa])
                ac(lambda e: e.activation(E1[:], Cc[:], AF.Exp), [Cc], [E1])
                ac(lambda e: e.activation(E2[:], Cc[:], AF.Exp, scale=-1.0), [Cc], [E2])
                dv(lambda e: e.tensor_tensor(TMP[:], Cc[:], SW[:], ALU.subtract), [Cc, SW], [TMP])
                ac(lambda e: e.activation(TMP[:], TMP[:], AF.Exp), [TMP], [TMP])
                dv(lambda e: e.tensor_tensor(Rb[:], Rb[:], E1[:], ALU.mult), [Rb, E1], [Rb])
                dv(lambda e: e.scalar_tensor_tensor(KK[:], KK[:], -1.0, TMP[:], ALU.mult, ALU.mult), [KK, TMP], [KK])
                po_(lambda e: e.tensor_tensor(Aa[:], Aa[:], E2[:], ALU.mult), [Aa, E2], [Aa])
                dv(lambda e: e.tensor_tensor(Kb[:], Kb[:], E2[:], ALU.mult), [Kb, E2], [Kb])
                cx.dma("sp", Gg[:], rz_d[r0:r0 + 128, 5 * D + f0:5 * D + f0 + HW], reads=[rz_d], writes=[Gg], owner=Gg)
                ac(lambda e: e.activation(btk[:], Aa[:], AF.Copy), [Aa], [btk])
                ac(lambda e: e.activation(ktk[:], Kb[:], AF.Copy), [Kb], [ktk])
                ac(lambda e: e.activation(vtk[:], Vb_[:], AF.Copy), [Vb_], [vtk])
                transpose_to(aTt, lambda c: aTt[:, c, :], KK, lambda c: KK[:, c * 128:(c + 1) * 128], 8)
                transpose_to(bTt, lambda c: bTt[:, c, :], Aa, lambda c: Aa[:, c * 128:(c + 1) * 128], 8)
                transpose_to(kTt, lambda c: kTt[:, c, :], Kb, lambda c: Kb[:, c * 128:(c + 1) * 128], 8)
                transpose_to(rTt, lambda c: rTt[:, c, :], Rb, lambda c: Rb[:, c * 128:(c + 1) * 128], 8)
                for hh in range(2):
                    pb = 64 * hh
                    ac(lambda e, hh=hh, pb=pb: e.activation(bTm[hh][pb:pb + 64, :, :], bTt[pb:pb + 64, :, :], AF.Copy), [bTt], [bTm[hh]])
                    po_(lambda e, hh=hh, pb=pb: e.tensor_copy(kTm[hh][pb:pb + 64, :, :], kTt[pb:pb + 64, :, :]), [kTt], [kTm[hh]])
                Y = E1
                ncp = int(os.environ.get("K_NCP", "8"))
                if s == 0:
                    prompt_pairs(half, ncp, Y)
                for cp in range(ncp if s == 1 else 0):
                    gc = half * 8 + cp
                    cs = slice(cp * 128, (cp + 1) * 128)

                    def pairmm(p, lT, rT_):
                        for hh in range(2):
                            lb = lT[hh] if isinstance(lT, list) else lT
                            rb = rT_[hh] if isinstance(rT_, list) else rT_
                            mmg(p[:, hh * 128:(hh + 1) * 128], [(lb[:, cp, :], rb[:, cp, :])], [lb, rb], p)

                    def pairev(dst, p, mask):
                        dv(lambda e: e.tensor_tensor(dst[:], p[:, 0:256].rearrange("p (a b) -> p a b", a=2), mask.v(0, [[0, 2], [1, 128]]), ALU.mult), [p, mask], [dst])
                    p = nps(); pairmm(p, aTt, bTm); pairev(Nb[0], p, mgt[s])
                    p = nps(); pairmm(p, bTm, aTt); pairev(Mb[0], p, mlt[s])
                    p = nps(); pairmm(p, kTm, aTt); pairev(AakT, p, mlt[s])
                    p = nps(); pairmm(p, bTm, rTt); pairev(ArbT, p, mle[s])
                    p = nps(); pairmm(p, kTm, rTt); pairev(ArkT, p, mle[s])
                    dv(lambda e: e.tensor_tensor(Tt[:], Mb[0][:], identb.v(0, [[0, 2], [1, 128]]), ALU.add), [Mb[0], identb], [Tt])
                    cur = 0
                    for it in range(nit):
                        nx = 1 - cur
                        p1 = nps(); p2 = nps()
                        for hh in range(2):
                            if it < nit - 1:
                                mmg(p1[:, hh * 128:(hh + 1) * 128], [(Nb[cur][:, hh, :], Mb[cur][:, hh, :])], [Nb[cur], Mb[cur]], p1)
                            mmg(p2[:, hh * 128:(hh + 1) * 128], [(Mb[cur][:, hh, :], Nb[cur][:, hh, :])], [Nb[cur], Mb[cur]], p2)
                        if it < nit - 1:
                            ac(lambda e, p1=p1, nx=nx: e.activation(Mb[nx][:].rearrange("p a b -> p (a b)"), p1[:, 0:256], AF.Copy), [p1], [Mb[nx]])
                        dv(lambda e, p2=p2, nx=nx: e.tensor_copy(Nb[nx][:].rearrange("p a b -> p (a b)"), p2[:, 0:256]), [p2], [Nb[nx]])
                        p3 = nps()
                        for hh in range(2):
                            mmg(p3[:, hh * 128:(hh + 1) * 128], [(Nb[nx][:, hh, :], Tt[:, hh, :])], [Nb[nx], Tt], p3)
                        dv(lambda e, p3=p3: e.tensor_tensor(Tt[:].rearrange("p a b -> p (a b)"), p3[:, 0:256], Tt[:].rearrange("p a b -> p (a b)"), ALU.add), [p3, Tt], [Tt])
                        cur = nx
                    if s == 1:
                        pxs = [PS[4], PS[5]]; pys = [PS[6], PS[7]]
                        for hh in range(2):
                            cx.op("pe", lambda e, hh=hh: e.matmul(pxs[hh][:, 0:64], AakT[:, hh, :], vtk[:, cp * 128 + hh * 64:cp * 128 + (hh + 1) * 64], start=True, stop=False),
                                  reads=[AakT, vtk], writes=[pxs[hh]])
                    if s == 0:
                        for hh in range(2):
                            for k in range(1, NBP):
                                pass
                        for k in range(NBP):
                            dv(lambda e, k=k: e.tensor_tensor(am[:], aTt[:, cp, :], rblkrow[:, k, :], ALU.mult), [aTt, rblkrow], [am])
                            po_(lambda e, k=k: e.tensor_tensor(rmk[:, k, :], rTt[:, cp, :], rblkrow[:, k, :], ALU.mult), [rTt, rblkrow], [rmk])
                            Hsrc = [(Hbm[hh][:, gc, :], Hbm[hh]) if k == 0 else (Hrd[hh][:, k, :], Hrd[hh]) for hh in range(2)]
                            px = accps()
                            for hh in range(2):
                                mmg(px[:, hh * 64:(hh + 1) * 64], [(AakT[:, hh, :], vtk[:, cp * 128 + hh * 64:cp * 128 + (hh + 1) * 64]),
                                                                   (am[:], Hsrc[hh][0])], [AakT, vtk, am, Hsrc[hh][1]], px)
                            dv(lambda e, px=px: e.tensor_copy(Xb[:], px[:, 0:128]), [px], [Xb])
                            pu = nps()
                            for hh in range(2):
                                mmg(pu[:, hh * 64:(hh + 1) * 64], [(Tt[:, hh, :], Xb[:, hh * 64:(hh + 1) * 64])], [Tt, Xb], pu)
                            if k == 0:
                                dv(lambda e, pu=pu, k=k: e.tensor_scalar_mul(Ubf[:], pu[:, 0:128], rblk[:, k:k + 1]), [pu, rblk], [Ubf])
                            else:
                                dv(lambda e, pu=pu, k=k: e.scalar_tensor_tensor(Ubf[:], pu[:, 0:128], rblk[:, k:k + 1], Ubf[:], ALU.mult, ALU.add), [pu, rblk, Ubf], [Ubf])
                            dv(lambda e, k=k, cs=cs: e.tensor_scalar_mul(bm[:], btk[:, cs], rblk[:, k:k + 1]), [btk, rblk], [bm])
                            po_(lambda e, k=k, cs=cs: e.tensor_scalar_mul(km[:], ktk[:, cs], rblk[:, k:k + 1]), [ktk, rblk], [km])
                            ph = nps()
                            mmg(ph[:, 0:128], [(bm[:], Ubf[:]), (km[:], vtk[:, cs])], [bm, Ubf, km, vtk], ph)
                            for hh in range(2):
                                pb = 64 * hh
                                dv(lambda e, pb=pb, ph=ph: e.tensor_tensor(Hst[pb:pb + 64, gc, :], ph[pb:pb + 64, pb:pb + 64], Hst[pb:pb + 64, gc, :], ALU.add), [ph, Hst], [Hst])
                            dv(lambda e, k=k: e.tensor_scalar_mul(Hst[:, gc, :], Hst[:, gc, :], dL[:, cp, k:k + 1]), [Hst, dL], [Hst])
                            if k < NBP - 1:
                                for hh in range(2):
                                    pb = 64 * hh
                                    po_(lambda e, hh=hh, pb=pb, k=k: e.tensor_copy(Hrd[hh][pb:pb + 64, k + 1, :], Hst[pb:pb + 64, gc, :]), [Hst], [Hrd[hh]])
                        py = accps()
                        for hh in range(2):
                            vh = vtk[:, cp * 128 + hh * 64:cp * 128 + (hh + 1) * 64]
                            pairs = [(ArbT[:, hh, :], Ubf[:, hh * 64:(hh + 1) * 64]), (ArkT[:, hh, :], vh), (rmk[:, 0, :], Hbm[hh][:, gc, :])]
                            pairs += [(rmk[:, k, :], Hrd[hh][:, k, :]) for k in range(1, NBP)]
                            mmg(py[:, hh * 64:(hh + 1) * 64], pairs, [ArbT, Ubf, ArkT, vtk, rmk, Hbm[hh], Hrd[hh]], py)
                        dv(lambda e, py=py, cs=cs: e.tensor_copy(Y[:, cs], py[:, 0:128]), [py], [Y])
                        for hh in range(2):
                            pb = 64 * hh
                            po_(lambda e, hh=hh, pb=pb: e.tensor_copy(Hbm[hh][pb:pb + 64, gc, :], Hst[pb:pb + 64, gc, :]), [Hst], [Hbm[hh]])
                    else:
                        hs_list = []
                        for b in range(16):
                            k2 = ssi[0] % 3; ssi[0] += 1
                            cx.dma("sp", Sin[k2][:], rS[b + 1, 2 * gc:2 * gc + 2].rearrange("h i j -> i h j"), reads=[rS], writes=[Sin[k2]], owner=Sin[k2])
                            pt = nps()
                            cx.op("pe", lambda e, pt=pt, k2=k2: e.transpose(pt[:, 0:64], Sin[k2][:].rearrange("p a b -> p (a b)"), ident[0:64, 0:64]), reads=[Sin[k2], ident], writes=[pt])
                            for hh in range(2):
                                pb = 64 * hh
                                dv(lambda e, pt=pt, k2=k2, hh=hh, pb=pb: e.tensor_copy(Hsbm[k2][hh][pb:pb + 64, :], pt[pb:pb + 64, 0:64]), [pt], [Hsbm[k2][hh]])
                            dv(lambda e, b=b: e.tensor_tensor(am[:], aTt[:, cp, :], blkrow[:, b, :], ALU.mult), [aTt, blkrow], [am])
                            for hh in range(2):
                                cx.op("pe", lambda e, hh=hh, k2=k2, b=b: e.matmul(pxs[hh][:, 0:64], am[:], Hsbm[k2][hh][:], start=False, stop=(b == 15)),
                                      reads=[am, Hsbm[k2][hh]], writes=[pxs[hh]])
                        for hh in range(2):
                            dv(lambda e, hh=hh: e.tensor_copy(Xb[:, hh * 64:(hh + 1) * 64], pxs[hh][:, 0:64]), [pxs[hh]], [Xb])
                        pu = nps()
                        for hh in range(2):
                            mmg(pu[:, hh * 64:(hh + 1) * 64], [(Tt[:, hh, :], Xb[:, hh * 64:(hh + 1) * 64])], [Tt, Xb], pu)
                        ac(lambda e, pu=pu: e.activation(Ub[:], pu[:, 0:128], AF.Copy), [pu], [Ub])
                        for hh in range(2):
                            vh = vtk[:, cp * 128 + hh * 64:cp * 128 + (hh + 1) * 64]
                            cx.op("pe", lambda e, hh=hh: e.matmul(pys[hh][:, 0:64], ArbT[:, hh, :], Ub[:, hh * 64:(hh + 1) * 64], start=True, stop=False), reads=[ArbT, Ub], writes=[pys[hh]])
                            cx.op("pe", lambda e, hh=hh, vh=vh: e.matmul(pys[hh][:, 0:64], ArkT[:, hh, :], vh, start=False, stop=False), reads=[ArkT, vtk], writes=[pys[hh]])
                        for b in range(16):
                            k2 = ssi[0] % 3; ssi[0] += 1
                            cx.dma("sp", Sin[k2][:], rS[b + 1, 2 * gc:2 * gc + 2].rearrange("h i j -> i h j"), reads=[rS], writes=[Sin[k2]], owner=Sin[k2])
                            pt = nps()
                            cx.op("pe", lambda e, pt=pt, k2=k2: e.transpose(pt[:, 0:64], Sin[k2][:].rearrange("p a b -> p (a b)"), ident[0:64, 0:64]), reads=[Sin[k2], ident], writes=[pt])
                            for hh in range(2):
                                pb = 64 * hh
                                dv(lambda e, pt=pt, k2=k2, hh=hh, pb=pb: e.tensor_copy(Hsbm[k2][hh][pb:pb + 64, :], pt[pb:pb + 64, 0:64]), [pt], [Hsbm[k2][hh]])
                            ac(lambda e, pt=pt, k2=k2: e.activation(Hs[k2][:], pt[:, 0:64], AF.Copy), [pt], [Hs[k2]])
                            dv(lambda e, b=b: e.tensor_tensor(rm[:], rTt[:, cp, :], blkrow[:, b, :], ALU.mult), [rTt, blkrow], [rm])
                            for hh in range(2):
                                cx.op("pe", lambda e, hh=hh, k2=k2, b=b: e.matmul(pys[hh][:, 0:64], rm[:], Hsbm[k2][hh][:], start=False, stop=(b == 15)),
                                      reads=[rm, Hsbm[k2][hh]], writes=[pys[hh]])
                            dv(lambda e, b=b, cs=cs: e.tensor_scalar_mul(bm[:], btk[:, cs], refs[:, 18 + b:19 + b]), [btk, refs], [bm])
                            po_(lambda e, b=b, cs=cs: e.tensor_scalar_mul(km[:], ktk[:, cs], refs[:, 18 + b:19 + b]), [ktk, refs], [km])
                            ph = nps()
                            mmg(ph[:, 0:128], [(bm[:], Ub[:]), (km[:], vtk[:, cs])], [bm, Ub, km, vtk], ph)
                            for hh in range(2):
                                pb = 64 * hh
                                dv(lambda e, pb=pb, ph=ph, k2=k2: e.tensor_tensor(Hs[k2][pb:pb + 64, :], ph[pb:pb + 64, pb:pb + 64], Hs[k2][pb:pb + 64, :], ALU.add), [ph, Hs[k2]], [Hs[k2]])
                            dv(lambda e, k2=k2, b=b: e.tensor_scalar_mul(Hs[k2][:], Hs[k2][:], dL[:, cp, b + 1:b + 2]), [Hs[k2], dL], [Hs[k2]])
                            pt2 = nps()
                            cx.op("pe", lambda e, pt2=pt2, k2=k2: e.transpose(pt2[0:64, 0:128], Hs[k2][:], ident[:]), reads=[Hs[k2], ident], writes=[pt2])
                            ac(lambda e, pt2=pt2, k2=k2: e.activation(Sout[k2][:].rearrange("p a b -> p (a b)"), pt2[0:64, 0:128], AF.Copy), [pt2], [Sout[k2]])
                            cx.dma("act", orS[b + 1, 2 * gc:2 * gc + 2].rearrange("h i j -> i h j"), Sout[k2][:], reads=[Sout[k2]], writes=[orS], owner=Sout[k2])
                        for hh in range(2):
                            dv(lambda e, hh=hh, cp=cp: e.tensor_copy(Y[:, cp * 128 + hh * 64:cp * 128 + (hh + 1) * 64], pys[hh][:, 0:64]), [pys[hh]], [Y])
                dv(lambda e: e.tensor_reduce(stt_[:, 2, :], h3(Y), AX.X, ALU.add), [Y], [stt_])
                dv(lambda e: e.tensor_scalar_mul(stt_[:, 2, :], stt_[:, 2, :], 1.0 / 64), [stt_], [stt_])
                dv(lambda e: e.tensor_tensor(h3(Y), h3(Y), bc16(stt_, 2), ALU.subtract), [Y, stt_], [Y])
                ac(lambda e: e.activation(TMP[:], Y[:], AF.Square), [Y], [TMP])
                dv(lambda e: e.tensor_reduce(stt_[:, 3, :], h3(TMP), AX.X, ALU.add), [TMP], [stt_])
                dv(lambda e: e.tensor_scalar(stt_[:, 3, :], stt_[:, 3, :], 1.0 / 64, LN_X_EPS, ALU.mult, ALU.add), [stt_], [stt_])
                ac(lambda e: e.activation(stt_[:, 3, :], stt_[:, 3, :], AF.Sqrt), [stt_], [stt_])
                dv(lambda e: e.reciprocal(stt_[:, 3, :], stt_[:, 3, :]), [stt_], [stt_])
                dv(lambda e: e.tensor_tensor(h3(Y), h3(Y), bc16(stt_, 3), ALU.mult), [Y, stt_], [Y])
                dv(lambda e: e.tensor_tensor(Y[:], Y[:], PRM["r_lnw"][:], ALU.mult), [Y, PRM["r_lnw"]], [Y])
                po_(lambda e: e.tensor_tensor(Y[:], Y[:], PRM["r_lnb"][:], ALU.add), [Y, PRM["r_lnb"]], [Y])
                dv(lambda e: e.tensor_tensor(h3(TMP), h3(Vb_), bc16(stt_, 1), ALU.mult), [Vb_, stt_], [TMP])
                po_(lambda e: e.tensor_tensor(Y[:], Y[:], TMP[:], ALU.add), [Y, TMP], [Y])
                dv(lambda e: e.tensor_tensor(Y[:], Y[:], Gg[:], ALU.mult), [Y, Gg], [Y])
                transpose_to(outT, lambda c: outT[:, half * 8 + c, r0:r0 + 128], Y, lambda c: Y[:, c * 128:(c + 1) * 128], 8)
            if s == 0 and i == NTP - 1:
                for gc in range(16):
                    k2 = ssi[0] % 3; ssi[0] += 1
                    pt2 = nps()
                    cx.op("pe", lambda e, pt2=pt2, gc=gc: e.transpose(pt2[0:64, 0:128], Hst[:, gc, :], ident[:]), reads=[Hst, ident], writes=[pt2])
                    ac(lambda e, pt2=pt2, k2=k2: e.activation(Sout[k2][:].rearrange("p a b -> p (a b)"), pt2[0:64, 0:128], AF.Copy), [pt2], [Sout[k2]])
                    cx.dma("act", orS[0, 2 * gc:2 * gc + 2].rearrange("h i j -> i h j"), Sout[k2][:], reads=[Sout[k2]], writes=[orS], owner=Sout[k2])

    stage_mod()
    hT = sb("hT", [128, 16, NTOK], BF16)
    stage_norm(xin, 0, W["norm_mix"][0], 0, 1, hT)
    stage_l0_proj(hT)
    stage_l0_rec(hT)
    proj_resid(hT, lambda c0, n: W["ab_w_out"][0, :, c0:c0 + n], xin, x1_d, 0, 2)
    stage_norm(x1_d, 0, W["norm_ffn"][0], 3, 4, hT)
    stage_ffn(hT, 0, x1_d, x2_d)
    stage_norm(x2_d, 1, W["norm_mix"][1], 0, 1, hT, h_dram=h_d, shift_out=osh)
    import os
    stage_l1_mix(hT)
    if os.environ.get("K_SKIP") != "proj":
        stage_l1_proj(hT)
    if os.environ.get("K_SKIP") not in ("rec", "proj"):
        stage_l1_rec(hT)
    proj_resid(hT, lambda c0, n: W["r_wo"][0, :, c0:c0 + n], x2_d, x3_d, 1, 2)
    stage_norm(x3_d, 1, W["norm_ffn"][1], 3, 4, hT)
    stage_ffn(hT, 1, x3_d, x4_d)
    stage_final(x4_d)
    cx.barrier()
    return nc


def LAYER0(L):
    cx, xin, x1_d, NTOK = L["cx"], L["xin"], L["x1_d"], L["NTOK"]
    cx.dma("sp", x1_d[:], xin[:], reads=[xin], writes=[x1_d], owner=x1_d)
    cx.barrier()


def LAYER1(L):
    cx, x2_d, x3_d = L["cx"], L["x2_d"], L["x3_d"]
    cx.dma("sp", x3_d[:], x2_d[:], reads=[x2_d], writes=[x3_d], owner=x3_d)
    cx.barrier()


def make_consts(NTP):
    c = {}
    s = np.arange(128)
    blk = s // 8
    le_p = (s[:, None] <= s[None, :]).astype(np.float32)
    same = (blk[:, None] == blk[None, :])
    le_s = (le_p * same).astype(np.float32)
    lt_p = (s[:, None] < s[None, :]).astype(np.float32)
    lt_s = (lt_p * same).astype(np.float32)
    c["ident"] = np.eye(128, dtype=np.float32)
    c["ones"] = np.ones((128, 128), np.float32)
    c["mle_p"], c["mle_s"], c["mlt_p"], c["mlt_s"] = le_p, le_s, lt_p, lt_s
    c["mgt_p"], c["mgt_s"] = lt_p.T.copy(), lt_s.T.copy()
    c["neg_p"] = ((le_p.T - 1.0) * 1e30).astype(np.float32)
    c["neg_s"] = ((le_s.T - 1.0) * 1e30).astype(np.float32)
    c["trir_p"] = (le_p - (s[:, None] <= 63).astype(np.float32)).astype(np.float32)
    ref_p = np.zeros((128, 34), np.float32); ref_p[:, 0] = (s <= 63); ref_p[:, 17] = (s > 63)
    ref_s = np.zeros((128, 34), np.float32)
    for b in range(16):
        ref_s[8 * b:8 * b + 8, 17 + b + 1] = 1.0
    c["ref_p"], c["ref_s"] = ref_p, ref_s
    blkt_p = np.zeros((NB, 128), np.float32); blkt_p[0] = 1.0
    blkt_s = np.zeros((NB, 128), np.float32)
    sel_p = np.zeros((128, NB), np.float32); sel_p[127, 0] = 1.0
    sel_s = np.zeros((128, NB), np.float32)
    for b in range(16):
        blkt_s[b + 1, 8 * b:8 * b + 8] = 1.0
        sel_s[8 * b + 7, b + 1] = 1.0
    c["blkt_p"], c["blkt_s"], c["selend_p"], c["selend_s"] = blkt_p, blkt_s, sel_p, sel_s
    NBP = 4; LC = 128 // NBP
    pb_ = s // LC
    samep = (pb_[:, None] == pb_[None, :])
    c["rle_p"] = (le_p * samep).astype(np.float32); c["rlt_p"] = (lt_p * samep).astype(np.float32)
    c["rgt_p"] = c["rlt_p"].T.copy()
    rb = np.zeros((128, NB), np.float32)
    rbr = np.zeros((128, NBP, 128), np.float32)
    for k in range(NBP):
        rb[k * LC:(k + 1) * LC, k] = 1.0
        rbr[:, k, k * LC:(k + 1) * LC] = 1.0
    c["rblk_p"] = rb; c["rblkrow_p"] = rbr.reshape(128, NBP * 128)
    br = np.zeros((128, 16, 128), np.float32)
    for b in range(16):
        br[:, b, 8 * b:8 * b + 8] = 1.0
    c["blkrow"] = br.reshape(128, 16 * 128)
    return {"c_" + k: np.ascontiguousarray(v) for k, v in c.items()}


WNAMES = ["mod_w", "mod_b", "norm_mix", "norm_ffn", "ffn_w1", "ffn_w2", "final_norm", "ab_w_in", "ab_gate_b", "m_conv_w",
          "m_norm", "g_lb", "g_norm", "ab_w_out", "r_mu", "r_w0", "r_w1", "r_w2", "r_a0", "r_a1", "r_a2", "r_g1", "r_g2",
          "r_kk", "r_ka", "r_rk", "r_wr", "r_wk", "r_wv", "r_wo", "r_lnw", "r_lnb"]
_NC_CACHE = {}


def core_inputs(inp, core, NTP, consts):
    b = core // 2
    T = NTP * 128
    f = lambda a: np.ascontiguousarray(np.asarray(a, dtype=np.float32))
    sl = slice(16 * core, 16 * core + 16)
    m = {}
    m["xin"] = f(np.concatenate([inp["x_prompt"][b, :T], inp["x_sample"][sl].reshape(128, D)], axis=0))
    m["cin"] = f(np.concatenate([inp["c_prompt"][b:b + 1], inp["c_sample"][sl]], axis=0))

    def st(a):
        a = np.asarray(a)[0, sl]
        return f(np.concatenate([np.zeros((1,) + a.shape[1:], np.float32), a], axis=0))
    m["mC"] = st(inp["state_mlstm_C"]); m["mn"] = st(inp["state_mlstm_n"]); m["mm"] = st(inp["state_mlstm_m"])
    m["mconv"] = st(inp["state_mlstm_conv"]); m["gS"] = st(inp["state_hgrn_S"]); m["rS"] = st(inp["state_rwkv_S"])
    m["rsh"] = st(inp["state_rwkv_shift"])
    for n in WNAMES:
        a = f(inp[n])
        if n == "r_rk":
            a = a.reshape(1, D)
        m[n] = a
    m.update(consts)
    return m


def kernel(**inp):
    NTP = 16
    n = 8
    if NTP not in _NC_CACHE:
        _NC_CACHE[NTP] = build_nc(NTP)
    nc = _NC_CACHE[NTP]
    consts = make_consts(NTP)
    in_maps = [core_inputs(inp, c, NTP, consts) for c in range(n)]
    res = run_bass_kernel_spmd(nc, in_maps, core_ids=list(range(n)))
    R = res.results
    T = NTP * 128
    y_prompt = np.stack([R[2 * b]["y"][:T] for b in range(4)], axis=0)
    y_sample = np.concatenate([R[c]["y"][T:].reshape(16, 8, D) for c in range(n)], axis=0)

    def pst(k):
        return np.stack([R[2 * b][k][0] for b in range(4)], axis=0)[None]

    def sst(k):
        return np.concatenate([R[c][k][1:] for c in range(n)], axis=0)[None]
    keys = ["oC", "on", "om", "oconv", "oS", "orS", "osh"]
    outs = [y_prompt, y_sample] + [pst(k) for k in keys] + [sst(k) for k in keys]
    return tuple(np.ascontiguousarray(o, dtype=np.float32) for o in outs)
```

```python
import contextlib
import numpy as np
import concourse.bass as bass
import concourse.mybir as mybir
from concourse.bass_utils import run_bass_kernel_spmd

F32 = mybir.dt.float32
BF16 = mybir.dt.bfloat16
AF = mybir.ActivationFunctionType
ALU = mybir.AluOpType
AX = mybir.AxisListType

D = 2048
NB = 17
RMS_EPS = 1e-6
LN_X_EPS = 64e-5


class Res:
    __slots__ = ("name", "lw", "rd", "dsem")

    def __init__(self, name):
        self.name = name
        self.lw = None
        self.rd = {}
        self.dsem = {}


class Ctx:
    def __init__(self, nc):
        self.nc = nc
        self.eng = {"pe": nc.tensor, "act": nc.scalar, "dve": nc.vector, "pool": nc.gpsimd, "sp": nc.sync}
        self.sems = {}
        self.tot = {}
        self.isdma = {}
        self.seen = {e: {} for e in self.eng}
        for e in ("pe", "act", "dve", "pool"):
            self._newsem("E_" + e, False)
        self.free_dma = {"hw": [], "sw": []}
        self.ndma = 0

    def _newsem(self, key, isdma):
        self.sems[key] = self.nc.alloc_semaphore(name=key)
        self.tot[key] = 0
        self.isdma[key] = isdma
        return key

    def _dma_sem_for(self, res, q):
        kind = "sw" if q == "pool" else "hw"
        if kind not in res.dsem:
            if self.free_dma[kind]:
                res.dsem[kind] = self.free_dma[kind].pop()
            else:
                self.ndma += 1
                res.dsem[kind] = self._newsem("D%s%d" % (kind, self.ndma), True)
        return res.dsem[kind]

    def release(self, bufs):
        for b in bufs:
            r = b.r
            for kind, key in r.dsem.items():
                self.free_dma[kind].append(key)
            r.dsem = {}
            r.lw = None
            r.rd = {}

    def _need(self, e, deps):
        eng = self.eng[e]
        seen = self.seen[e]
        for key, val in deps:
            if self.isdma[key]:
                val = self.tot[key]
            elif key == "E_pe" and e == "pe":
                continue
            if seen.get(key, 0) >= val:
                continue
            eng.wait_ge(self.sems[key], val)
            seen[key] = val

    @staticmethod
    def _deps(reads, writes):
        deps = []
        for r in reads:
            if r.lw is not None:
                deps.append(r.lw)
        for w in writes:
            if w.lw is not None:
                deps.append(w.lw)
            deps.extend(w.rd.items())
        return deps

    @staticmethod
    def _commit(key, val, reads, writes):
        for w in writes:
            w.lw = (key, val)
            w.rd = {}
        for r in reads:
            if r in writes:
                continue
            if r.rd.get(key, 0) < val:
                r.rd[key] = val

    def op(self, e, fn, reads=(), writes=()):
        reads = [b.r for b in reads]
        writes = [b.r for b in writes]
        self._need(e, self._deps(reads, writes))
        inst = fn(self.eng[e])
        key = "E_" + e
        self.tot[key] += 1
        inst.then_inc(self.sems[key], 1)
        self._commit(key, self.tot[key], reads, writes)
        return inst

    def dma(self, q, out, in_, reads=(), writes=(), owner=None, **kw):
        reads = [b.r for b in reads]
        writes = [b.r for b in writes]
        self._need(q, self._deps(reads, writes))
        key = self._dma_sem_for(owner.r, q)
        inst = self.eng[q].dma_start(out=out, in_=in_, **kw)
        self.tot[key] += 16
        inst.then_inc(self.sems[key], 16)
        self._commit(key, self.tot[key], reads, writes)
        return inst

    def barrier(self):
        for e in self.eng:
            self._need(e, [(k, v) for k, v in self.tot.items() if v > 0])


class Buf:
    def __init__(self, t, name, shape):
        self.t = t
        self.r = Res(name)
        self.shape = list(shape)
        self.ps = int(np.prod(shape[1:]))

    def __getitem__(self, idx):
        return self.t[idx]

    def v(self, off, dims, p0=0, np_=128):
        return bass.AP(self.t, p0 * self.ps + off, [[self.ps, np_]] + [list(d) for d in dims])


class View:
    def __init__(self, parent, c0, n):
        self.parent = parent
        self.c0 = c0
        self.n = n
        self.r = parent.r

    def __getitem__(self, idx):
        if isinstance(idx, slice):
            assert idx == slice(None)
            return self.parent[:, self.c0:self.c0 + self.n]
        p, c = idx
        a = 0 if c.start is None else c.start
        b = self.n if c.stop is None else c.stop
        return self.parent[p, self.c0 + a:self.c0 + b]


def build_nc(NTP):
    NT = NTP + 1
    NP = NTP * 128
    NTOK = NT * 128
    nc = bass.Bass("TRN2", target_bir_lowering=False)
    cx = Ctx(nc)
    DT = {}

    def din(name, shape, dt=F32):
        b = Buf(nc.dram_tensor(name, list(shape), dt, kind="ExternalInput"), name, shape)
        DT[name] = b
        return b

    def dout(name, shape):
        b = Buf(nc.dram_tensor(name, list(shape), F32, kind="ExternalOutput"), name, shape)
        DT[name] = b
        return b

    def dscr(name, shape, dt=F32):
        return Buf(nc.dram_tensor(name, list(shape), dt), name, shape)

    xin = din("xin", [NTOK, D]); cin = din("cin", [NB, D])
    mC = din("mC", [NB, 4, 256, 256]); mn = din("mn", [NB, 4, 256]); mm = din("mm", [NB, 4])
    mconv = din("mconv", [NB, 3, D]); gS = din("gS", [NB, 8, 128, 128]); rS = din("rS", [NB, 32, 64, 64])
    rsh = din("rsh", [NB, D])
    W = {}
    for name, shape in [("mod_w", [2, D, 6 * D]), ("mod_b", [2, 6 * D]), ("norm_mix", [2, D]), ("norm_ffn", [2, D]),
                        ("ffn_w1", [2, D, 4 * D]), ("ffn_w2", [2, 4 * D, D]), ("final_norm", [D]),
                        ("ab_w_in", [1, D, 8200]), ("ab_gate_b", [1, 8]), ("m_conv_w", [1, 4, D]), ("m_norm", [1, 1024]),
                        ("g_lb", [2, 1024]), ("g_norm", [1, 1024]), ("ab_w_out", [1, D, D]),
                        ("r_mu", [1, 6, D]), ("r_w0", [1, D]), ("r_w1", [1, D, 96]), ("r_w2", [1, 96, D]),
                        ("r_a0", [1, D]), ("r_a1", [1, D, 96]), ("r_a2", [1, 96, D]), ("r_g1", [1, D, 256]),
                        ("r_g2", [1, 256, D]), ("r_kk", [1, D]), ("r_ka", [1, D]), ("r_rk", [1, D]),
                        ("r_wr", [1, D, D]), ("r_wk", [1, D, D]), ("r_wv", [1, D, D]), ("r_wo", [1, D, D]),
                        ("r_lnw", [1, D]), ("r_lnb", [1, D])]:
        W[name] = din(name, shape)
    CN = {}
    for name, shape in [("ident", [128, 128]), ("ones", [128, 128]), ("mle_p", [128, 128]), ("mle_s", [128, 128]),
                        ("mlt_p", [128, 128]), ("mlt_s", [128, 128]), ("mgt_p", [128, 128]), ("mgt_s", [128, 128]),
                        ("neg_p", [128, 128]), ("neg_s", [128, 128]), ("trir_p", [128, 128]),
                        ("ref_p", [128, 34]), ("ref_s", [128, 34]), ("blkt_p", [NB, 128]), ("blkt_s", [NB, 128]),
                        ("selend_p", [128, NB]), ("selend_s", [128, NB]), ("blkrow", [128, 16 * 128]),
                        ("rle_p", [128, 128]), ("rlt_p", [128, 128]), ("rgt_p", [128, 128]), ("rblk_p", [128, NB]),
                        ("rblkrow_p", [128, 4 * 128])]:
        CN[name] = din("c_" + name, shape)
    y = dout("y", [NTOK, D])
    oC = dout("oC", [NB, 4, 256, 256]); on = dout("on", [NB, 4, 256]); om = dout("om", [NB, 4])
    oconv = dout("oconv", [NB, 3, D]); oS = dout("oS", [NB, 8, 128, 128]); orS = dout("orS", [NB, 32, 64, 64])
    osh = dout("osh", [NB, D])
    mod_d = dscr("mod_d", [2, NB, 6 * D])
    ext_p = dscr("ext_p", [NP + 3, D]); ext_s = dscr("ext_s", [16, 11, D])
    z_d = dscr("z_d", [NTOK, 6152])
    x1_d = dscr("x1_d", [NTOK, D]); x2_d = dscr("x2_d", [NTOK, D]); x3_d = dscr("x3_d", [NTOK, D]); x4_d = dscr("x4_d", [NTOK, D])
    h_d = dscr("h_d", [NTOK, D])
    rz_d = dscr("rz_d", [NTOK, 6 * D])
    dummy = Buf(None, "dummy", [1, 1])

    stacks = [contextlib.ExitStack()]

    uid = [0]

    stage_bufs = [[]]

    def sb(name, shape, dt=F32):
        uid[0] += 1
        name = "%s_%d" % (name, uid[0])
        b = Buf(stacks[-1].enter_context(nc.sbuf_tensor(name, list(shape), dt)), name, shape)
        stage_bufs[-1].append(b)
        return b

    def staged(fn):
        def wrapper(*a, **k):
            stacks.append(contextlib.ExitStack())
            stage_bufs.append([])
            try:
                return fn(*a, **k)
            finally:
                cx.barrier()
                cx.release(stage_bufs.pop())
                stacks.pop().close()
                npsmod[0] = 8
        return wrapper

    ident = sb("ident", [128, 128]); ones = sb("ones", [128, 128])
    identb = sb("identb", [128, 128], BF16)
    PS = [Buf(nc.alloc_psum_tensor("ps%d" % i, [128, 512], F32), "ps%d" % i, [128, 512]) for i in range(8)]
    psi = [0]

    def nps():
        p = PS[psi[0] % npsmod[0]]
        psi[0] += 1
        return p

    npsmod = [8]

    acci = [0]

    def accps():
        p = PS[6 + acci[0] % 2]
        acci[0] += 1
        return p

    cx.dma("sp", ident[:], CN["ident"][:], writes=[ident], owner=ident)
    cx.dma("sp", ones[:], CN["ones"][:], writes=[ones], owner=ones)
    cx.op("dve", lambda e: e.tensor_copy(identb[:], ident[:]), reads=[ident], writes=[identb])

    evq = [0]

    def evac(out_ap, in_ap, reads, writes, scale=None):
        evq[0] += 1
        if evq[0] % 2 == 0:
            if scale is None:
                cx.op("act", lambda e: e.activation(out_ap, in_ap, AF.Copy), reads=reads, writes=writes)
            else:
                cx.op("act", lambda e: e.activation(out_ap, in_ap, AF.Copy, scale=float(scale)), reads=reads, writes=writes)
        else:
            if scale is None:
                cx.op("dve", lambda e: e.tensor_copy(out_ap, in_ap), reads=reads, writes=writes)
            else:
                cx.op("dve", lambda e: e.tensor_scalar(out_ap, in_ap, float(scale), None, ALU.mult), reads=reads, writes=writes)

    def transpose_to(dst, dst_ap_fn, src, src_ap_fn, nchunks, scale_fn=None, npart=128):
        for c0 in range(0, nchunks, 4):
            n = min(4, nchunks - c0)
            p = nps()
            for j in range(n):
                cx.op("pe", lambda e, j=j: e.transpose(p.v(j * 128, [[1, npart]]),
                                                      src_ap_fn(c0 + j), ident[0:npart, 0:npart]),
                      reads=[src, ident], writes=[p])
            for j in range(n):
                sc = None if scale_fn is None else scale_fn(c0 + j)
                evac(dst_ap_fn(c0 + j), p.v(j * 128, [[1, npart]]), [p], [dst], scale=sc)

    def rows_bcast(dst, src_dram_rows_fn, tile_is_sample, width, col0=0, q="sp"):
        if not tile_is_sample:
            cx.dma(q, dst[:, col0:col0 + width], src_dram_rows_fn(0).partition_broadcast(128), writes=[dst], owner=dst)
        else:
            a1 = src_dram_rows_fn(1); a2 = src_dram_rows_fn(2)
            src3 = bass.AP(a1.tensor, a1.offset, [[a2.offset - a1.offset, 16], [0, 8], [1, width]])
            cx.dma(q, dst[:, col0:col0 + width], src3, writes=[dst], owner=dst)

    @staged
    def stage_mod():
        csb = sb("csb", [NB, D]); scT = sb("scT", [128, 16, NB], BF16)
        modsb = sb("modsb", [NB, 6 * D]); biasb = sb("biasb", [NB, 6 * D])
        wb = [sb("modw%d" % i, [128, 16, 512], BF16) for i in range(2)]
        cx.dma("sp", csb[:], cin[:], writes=[csb], owner=csb)
        cx.op("act", lambda e: e.activation(csb[:], csb[:], AF.Silu), reads=[csb], writes=[csb])
        for c0 in range(0, 16, 4):
            p = nps()
            for j in range(4):
                c = c0 + j
                cx.op("pe", lambda e, j=j, c=c: e.transpose(p.v(j * 32, [[1, NB]]), csb[:, c * 128:(c + 1) * 128], ident[0:NB, 0:NB]),
                      reads=[csb, ident], writes=[p])
            for j in range(4):
                evac(scT[:, c0 + j, :], p.v(j * 32, [[1, NB]]), [p], [scT])
        for l in range(2):
            cx.dma("sp", biasb[:], W["mod_b"][l].partition_broadcast(NB), writes=[biasb], owner=biasb)
            for cb in range(24):
                w = wb[cb % 2]
                cx.dma("pool", w[:], W["mod_w"][l, :, cb * 512:(cb + 1) * 512].rearrange("(kc p) n -> p kc n", p=128),
                       writes=[w], owner=w)
                p = nps()
                for kc in range(16):
                    cx.op("pe", lambda e, kc=kc: e.matmul(p[0:NB, :], scT[:, kc, :], w[:, kc, :], start=(kc == 0), stop=(kc == 15)),
                          reads=[scT, w], writes=[p])
                cx.op("dve", lambda e: e.tensor_tensor(modsb[:, cb * 512:(cb + 1) * 512], p[0:NB, :], biasb[:, cb * 512:(cb + 1) * 512], ALU.add),
                      reads=[p, biasb], writes=[modsb])
            cx.dma("sp", mod_d[l], modsb[:], reads=[modsb], writes=[mod_d], owner=modsb)
        cx.barrier()
        cx.release([csb, scT, modsb, biasb] + wb)
        return [csb, scT, modsb, biasb] + wb

    @staged
    def stage_norm(x_d, l, normw_ap, ish, isc, hT, h_dram=None, shift_out=None):
        G = [sb("nG%d" % i, [128, D]) for i in range(2)]; SH = [sb("nSH%d" % i, [128, D]) for i in range(2)]
        nw = sb("nnw", [128, D])
        xt = [sb("nxt%d" % i, [128, D]) for i in range(2)]; ht = [sb("nht%d" % i, [128, D]) for i in range(2)]
        junk = sb("njunk", [128, D]); ss = sb("nss", [128, 2])
        cx.dma("sp", nw[:], normw_ap.partition_broadcast(128), writes=[nw], owner=nw)
        for s in range(2):
            rows_bcast(G[s], lambda b: mod_d[l, b, isc * D:(isc + 1) * D], s == 1, D)
            rows_bcast(SH[s], lambda b: mod_d[l, b, ish * D:(ish + 1) * D], s == 1, D)
            cx.op("dve", lambda e, s=s: e.scalar_tensor_tensor(G[s][:], G[s][:], 1.0, nw[:], ALU.add, ALU.mult), reads=[G[s], nw], writes=[G[s]])
        for i in range(NT):
            s = 1 if i == NTP else 0
            x = xt[i % 2]; h = ht[i % 2]
            cx.dma("sp", x[:], x_d[i * 128:(i + 1) * 128, :], reads=[x_d], writes=[x], owner=x)
            cx.op("act", lambda e: e.activation(junk[:], x[:], AF.Square, accum_out=ss[:, 0:1]), reads=[x], writes=[junk, ss])
            cx.op("dve", lambda e: e.tensor_scalar(ss[:, 1:2], ss[:, 0:1], 1.0 / D, RMS_EPS, ALU.mult, ALU.add), reads=[ss], writes=[ss])
            cx.op("act", lambda e: e.activation(ss[:, 1:2], ss[:, 1:2], AF.Sqrt), reads=[ss], writes=[ss])
            cx.op("dve", lambda e: e.reciprocal(ss[:, 1:2], ss[:, 1:2]), reads=[ss], writes=[ss])
            cx.op("dve", lambda e: e.scalar_tensor_tensor(h[:], x[:], ss[:, 1:2], G[s][:], ALU.mult, ALU.mult), reads=[x, ss, G[s]], writes=[h])
            if SH is not None:
                cx.op("dve", lambda e: e.tensor_tensor(h[:], h[:], SH[s][:], ALU.add), reads=[h, SH[s]], writes=[h])
            if h_dram is not None:
                cx.dma("sp", h_dram[i * 128:(i + 1) * 128, :], h[:], reads=[h], writes=[h_dram], owner=h)
            if shift_out is not None:
                if s == 0 and i == NTP - 1:
                    cx.dma("sp", shift_out[0:1, :], h[127:128, :], reads=[h], writes=[shift_out], owner=h)
                if s == 1:
                    for b in range(16):
                        cx.dma("sp", shift_out[b + 1:b + 2, :], h[8 * b + 7:8 * b + 8, :], reads=[h], writes=[shift_out], owner=h)
            transpose_to(hT, lambda c: hT[:, c, i * 128:(i + 1) * 128], h, lambda c: h[:, c * 128:(c + 1) * 128], 16)
        cx.barrier()
        tmp = G + SH + [nw, junk, ss] + xt + ht
        cx.release(tmp)

    @staged
    def proj_tok(aT, blocks, K=16, kpart=128, bias_fn=None, sigmoid=False):
        wb = [sb("pw%d" % i, [128, K, 512], BF16) for i in range(2)]
        ob = [sb("po%d" % i, [128, 512]) for i in range(3)]
        bb_ = [sb("pb%d" % i, [128, 512]) for i in range(2)] if bias_fn is not None else None
        oi = 0
        for bi, (w_ap, dst_fn) in enumerate(blocks):
            n = w_ap.shape[-1]
            w = wb[bi % 2]
            wv = w_ap.rearrange("(kc p) n -> p kc n", p=kpart)
            for k0 in range(0, K, 4):
                k1 = min(K, k0 + 4)
                cx.dma("pool", w[0:kpart, k0:k1, 0:n], wv[:, k0:k1, :], writes=[w], owner=w)
            if bias_fn is not None:
                bt = bb_[bi % 2]
                cx.dma("sp", bt[:, 0:n], bias_fn(bi).partition_broadcast(128), writes=[bt], owner=bt)
            for i in range(NT):
                p = nps()
                for kc in range(K):
                    cx.op("pe", lambda e, kc=kc: e.matmul(p[:, 0:n], aT[0:kpart, kc, i * 128:(i + 1) * 128], w[0:kpart, kc, 0:n],
                                                          start=(kc == 0), stop=(kc == K - 1)), reads=[aT, w], writes=[p])
                o = ob[oi % 3]; oi += 1
                if bias_fn is None:
                    cx.op("dve", lambda e: e.tensor_copy(o[:, 0:n], p[:, 0:n]), reads=[p], writes=[o])
                else:
                    cx.op("dve", lambda e: e.tensor_tensor(o[:, 0:n], p[:, 0:n], bt[:, 0:n], ALU.add), reads=[p, bt], writes=[o])
                    if sigmoid:
                        cx.op("act", lambda e: e.activation(o[:, 0:n], o[:, 0:n], AF.Sigmoid), reads=[o], writes=[o])
                for (dbuf, dap, p0, p1) in dst_fn(i):
                    if p0 == "3d":
                        cx.dma("act", dap, o[:, 0:n], reads=[o], writes=[dbuf], owner=o)
                    else:
                        cx.dma("act", dap, o[p0:p1, 0:n], reads=[o], writes=[dbuf], owner=o)

    @staged
    def proj_resid(aT, w_fn, x_old, x_new, l, igate, K=16):
        wb = [sb("rw%d" % i, [128, K, 512], BF16) for i in range(2)]
        xb = [sb("rx%d" % i, [128, 512]) for i in range(3)]
        GT = [sb("rgt%d" % i, [128, D]) for i in range(2)]
        for s in range(2):
            rows_bcast(GT[s], lambda b: mod_d[l, b, igate * D:(igate + 1) * D], s == 1, D)
        oi = 0
        for cb in range(4):
            w = wb[cb % 2]
            cx.dma("pool", w[:], w_fn(cb * 512, 512).rearrange("(kc p) n -> p kc n", p=128), writes=[w], owner=w)
            for i in range(NT):
                s = 1 if i == NTP else 0
                xo = xb[oi % 3]; oi += 1
                cx.dma("sp", xo[:], x_old[i * 128:(i + 1) * 128, cb * 512:(cb + 1) * 512], reads=[x_old], writes=[xo], owner=xo)
                p = nps()
                for kc in range(K):
                    cx.op("pe", lambda e, kc=kc: e.matmul(p[:], aT[:, kc, i * 128:(i + 1) * 128], w[:, kc, :], start=(kc == 0), stop=(kc == K - 1)),
                          reads=[aT, w], writes=[p])
                t = sbtmp512[oi % 2]
                cx.op("dve", lambda e: e.tensor_tensor(t[:], p[:], GT[s][:, cb * 512:(cb + 1) * 512], ALU.mult), reads=[p, GT[s]], writes=[t])
                cx.op("dve", lambda e: e.tensor_tensor(xo[:], xo[:], t[:], ALU.add), reads=[xo, t], writes=[xo])
                cx.dma("act", x_new[i * 128:(i + 1) * 128, cb * 512:(cb + 1) * 512], xo[:], reads=[xo], writes=[x_new], owner=xo)
        cx.barrier()
        cx.release(wb + xb + GT)

    sbtmp512 = [sb("tmp512_%d" % i, [128, 512]) for i in range(2)]

    @staged
    def stage_ffn(hT, l, x_old, x_new):
        FB = 2048
        nfb = 4 * D // FB
        KC = FB // 128
        hidT = sb("hidT", [128, KC, NTOK], BF16)
        w1b = [sb("fw1_%d" % i, [128, 16, 256], BF16) for i in range(2)]
        w2b = [sb("fw2_%d" % i, [128, KC, 512], BF16) for i in range(2)]
        xb = [sb("fx%d" % i, [128, 512]) for i in range(4)]
        GTs = [[sb("fgt%d_%d" % (i, j), [128, 512]) for j in range(2)] for i in range(2)]
        groups = [(g, min(512, NTOK - g)) for g in range(0, NTOK, 512)]
        w1i = 0; w2i = 0; oi = 0
        for fb in range(nfb):
            for blk in range(FB // 256):
                w = w1b[w1i % 2]; w1i += 1
                c0 = fb * FB + blk * 256
                wv = W["ffn_w1"][l, :, c0:c0 + 256].rearrange("(kc p) n -> p kc n", p=128)
                for k0 in range(0, 16, 8):
                    cx.dma("pool", w[:, k0:k0 + 8, :], wv[:, k0:k0 + 8, :], writes=[w], owner=w)
                for oc in range(2):
                    for (g0, gn) in groups:
                        p = nps()
                        for kc in range(16):
                            cx.op("pe", lambda e, kc=kc: e.matmul(p[:, 0:gn], w[:, kc, oc * 128:(oc + 1) * 128], hT[:, kc, g0:g0 + gn],
                                                                  start=(kc == 0), stop=(kc == 15)), reads=[hT, w], writes=[p])
                        t = sbtmp512[oi % 2]; oi += 1
                        cx.op("act", lambda e: e.activation(t[:, 0:gn], p[:, 0:gn], AF.Relu), reads=[p], writes=[t])
                        cx.op("dve", lambda e: e.tensor_tensor(hidT[:, blk * 2 + oc, g0:g0 + gn], t[:, 0:gn], t[:, 0:gn], ALU.mult),
                              reads=[t], writes=[hidT])
            for cb in range(4):
                w = w2b[w2i % 2]; w2i += 1
                wv = W["ffn_w2"][l, fb * FB:(fb + 1) * FB, cb * 512:(cb + 1) * 512].rearrange("(kc p) n -> p kc n", p=128)
                for k0 in range(0, KC, 4):
                    cx.dma("pool", w[:, k0:k0 + 4, :], wv[:, k0:k0 + 4, :], writes=[w], owner=w)
                GT = GTs[(fb * 4 + cb) % 2]
                for s_ in range(2):
                    rows_bcast(GT[s_], lambda b: mod_d[l, b, 5 * D + cb * 512:5 * D + (cb + 1) * 512], s_ == 1, 512)
                for i in range(NT):
                    s_ = 1 if i == NTP else 0
                    xo = xb[oi % 4]; oi += 1
                    src = x_old if fb == 0 else x_new
                    cx.dma("sp", xo[:], src[i * 128:(i + 1) * 128, cb * 512:(cb + 1) * 512], reads=[src], writes=[xo], owner=xo)
                    p = nps()
                    for kc in range(KC):
                        cx.op("pe", lambda e, kc=kc: e.matmul(p[:], hidT[:, kc, i * 128:(i + 1) * 128], w[:, kc, :], start=(kc == 0), stop=(kc == KC - 1)),
                              reads=[hidT, w], writes=[p])
                    t = sbtmp512[oi % 2]
                    cx.op("dve", lambda e: e.tensor_tensor(t[:], p[:], GT[s_][:], ALU.mult), reads=[p, GT[s_]], writes=[t])
                    cx.op("dve", lambda e: e.tensor_tensor(xo[:], xo[:], t[:], ALU.add), reads=[xo, t], writes=[xo])
                    cx.dma("act", x_new[i * 128:(i + 1) * 128, cb * 512:(cb + 1) * 512], xo[:], reads=[xo], writes=[x_new], owner=xo)

    @staged
    def stage_final(x_d):
        nw = sb("fnw", [128, D]); xt = [sb("fnx%d" % i, [128, D]) for i in range(2)]
        junk = sb("fnj", [128, D]); ss = sb("fns", [128, 2])
        cx.dma("sp", nw[:], W["final_norm"][:].partition_broadcast(128), writes=[nw], owner=nw)
        for i in range(NT):
            x = xt[i % 2]
            cx.dma("sp", x[:], x_d[i * 128:(i + 1) * 128, :], reads=[x_d], writes=[x], owner=x)
            cx.op("act", lambda e: e.activation(junk[:], x[:], AF.Square, accum_out=ss[:, 0:1]), reads=[x], writes=[junk, ss])
            cx.op("dve", lambda e: e.tensor_scalar(ss[:, 1:2], ss[:, 0:1], 1.0 / D, RMS_EPS, ALU.mult, ALU.add), reads=[ss], writes=[ss])
            cx.op("act", lambda e: e.activation(ss[:, 1:2], ss[:, 1:2], AF.Sqrt), reads=[ss], writes=[ss])
            cx.op("dve", lambda e: e.reciprocal(ss[:, 1:2], ss[:, 1:2]), reads=[ss], writes=[ss])
            cx.op("dve", lambda e: e.scalar_tensor_tensor(x[:], x[:], ss[:, 1:2], nw[:], ALU.mult, ALU.mult), reads=[x, ss, nw], writes=[x])
            cx.dma("sp", y[i * 128:(i + 1) * 128, :], x[:], reads=[x], writes=[y], owner=x)
        cx.barrier()
        cx.release([nw, junk, ss] + xt)

    def stage_l0_proj(hT):
        Wi = W["ab_w_in"][0]
        cx.dma("sp", ext_p[0:3, :], mconv[0], reads=[mconv], writes=[ext_p], owner=ext_p)
        cx.dma("sp", ext_s[:, 0:3, :], mconv[1:17], reads=[mconv], writes=[ext_s], owner=ext_s)
        blocks = []

        def dst_ext(c0, n):
            def f(i):
                if i < NTP:
                    return [(ext_p, ext_p[3 + i * 128:3 + (i + 1) * 128, c0:c0 + n], 0, 128)]
                return [(ext_s, ext_s[:, 3:11, c0:c0 + n], "3d", n)]
            return f

        def dst_z(c0, n):
            return lambda i: [(z_d, z_d[i * 128:(i + 1) * 128, c0:c0 + n], 0, 128)]
        for c0 in range(0, 2048, 512):
            blocks.append((Wi[:, c0:c0 + 512], dst_ext(c0, 512)))
        for c0 in range(0, 2048, 512):
            blocks.append((Wi[:, 2048 + c0:2048 + c0 + 512], dst_z(c0, 512)))
        blocks.append((Wi[:, 4096:4104], dst_z(2048, 8)))
        for c0 in range(0, 4096, 512):
            blocks.append((Wi[:, 4104 + c0:4104 + c0 + 512], dst_z(2056 + c0, 512)))
        proj_tok(hT, blocks)

    def mmg(p_ap, pairs, reads, pbuf):
        n = len(pairs)
        for idx, (l_ap, r_ap) in enumerate(pairs):
            cx.op("pe", lambda e, l_ap=l_ap, r_ap=r_ap, idx=idx: e.matmul(p_ap, l_ap, r_ap, start=(idx == 0), stop=(idx == n - 1)),
                  reads=reads, writes=[pbuf])

    def rstd_col(dst, col, src_ap, inv_n, eps):
        cx.op("dve", lambda e: e.tensor_scalar(dst[:, col:col + 1], src_ap, float(inv_n), float(eps), ALU.mult, ALU.add), reads=[dst], writes=[dst])
        cx.op("act", lambda e: e.activation(dst[:, col:col + 1], dst[:, col:col + 1], AF.Sqrt), reads=[dst], writes=[dst])
        cx.op("dve", lambda e: e.reciprocal(dst[:, col:col + 1], dst[:, col:col + 1]), reads=[dst], writes=[dst])

    @staged
    def stage_l0_rec(mixT):
        npsmod[0] = 4
        dv = lambda fn, r, w: cx.op("dve", fn, reads=r, writes=w)
        ac = lambda fn, r, w: cx.op("act", fn, reads=r, writes=w)
        mle = [sb("mle%d" % i, [128, 128]) for i in range(2)]; neg = [sb("neg%d" % i, [128, 128]) for i in range(2)]
        trirp = sb("trirp", [128, 128]); ref = [sb("ref%d" % i, [128, 34]) for i in range(2)]
        blkt = [sb("blkt%d" % i, [NB, 128]) for i in range(2)]; selend = [sb("selend%d" % i, [128, NB]) for i in range(2)]
        blkrow = sb("blkrow", [128, 16, 128], BF16)
        for i, sfx in enumerate(["p", "s"]):
            cx.dma("sp", mle[i][:], CN["mle_" + sfx][:], writes=[mle[i]], owner=mle[i])
            cx.dma("sp", neg[i][:], CN["neg_" + sfx][:], writes=[neg[i]], owner=neg[i])
            cx.dma("sp", ref[i][:], CN["ref_" + sfx][:], writes=[ref[i]], owner=ref[i])
            cx.dma("sp", blkt[i][:], CN["blkt_" + sfx][:], writes=[blkt[i]], owner=blkt[i])
            cx.dma("sp", selend[i][:], CN["selend_" + sfx][:], writes=[selend[i]], owner=selend[i])
        cx.dma("sp", trirp[:], CN["trir_p"][:], writes=[trirp], owner=trirp)
        cx.dma("pool", blkrow[:], CN["blkrow"][:].rearrange("p (b t) -> p b t", b=16), writes=[blkrow], owner=blkrow)
        trir = [trirp, mle[1]]
        gb = sb("gb", [128, 8]); mgain = sb("mgain", [128, 1024]); ggain = sb("ggain", [128, 1024])
        LB = sb("LB", [128, 1024]); OMLB = sb("OMLB", [128, 1024])
        cx.dma("sp", gb[:], W["ab_gate_b"][0].partition_broadcast(128), writes=[gb], owner=gb)
        cx.dma("sp", mgain[:], W["m_norm"][0].partition_broadcast(128), writes=[mgain], owner=mgain)
        cx.dma("sp", ggain[:], W["g_norm"][0].partition_broadcast(128), writes=[ggain], owner=ggain)
        cx.dma("sp", LB[:], W["g_lb"][0].partition_broadcast(128), writes=[LB], owner=LB)
        cx.dma("sp", OMLB[:], W["g_lb"][1].partition_broadcast(128), writes=[OMLB], owner=OMLB)
        dv(lambda e: e.tensor_tensor(LB[:], LB[:], OMLB[:], ALU.subtract), [LB, OMLB], [LB])
        ac(lambda e: e.activation(LB[:], LB[:], AF.Sigmoid), [LB], [LB])
        dv(lambda e: e.tensor_scalar(OMLB[:], LB[:], -1.0, 1.0, ALU.mult, ALU.add), [LB], [OMLB])
        mst = sb("mst", [NB, 4]); mend = sb("mend", [NB, 4])
        cx.dma("sp", mst[:], mm[:], writes=[mst], owner=mst)
        Cst = sb("Cst", [128, 4, 2, 257]); Cb = sb("Cb", [128, 4, 2, 257], BF16)
        Sst = sb("Sst", [128, 8, 128]); Smid = sb("Smid", [128, 8, 128]); Smidb = sb("Smidb", [128, 8, 128], BF16)
        for h in range(4):
            for c in range(2):
                cx.dma("sp", Cst[:, h, c, 0:256], mC[0, h, c * 128:(c + 1) * 128, :], writes=[Cst], owner=Cst)
                cx.dma("sp", Cst[:, h, c, 256:257], mn[0, h, c * 128:(c + 1) * 128].rearrange("(p o) -> p o", o=1), writes=[Cst], owner=Cst)
        cx.dma("sp", Sst[:], gS[0].rearrange("h k v -> k h v"), writes=[Sst], owner=Sst)
        dv(lambda e: e.tensor_copy(Cb[:], Cst[:]), [Cst], [Cb])
        NSB = 4
        Cs = [sb("Cs%d" % i, [128, 2, 257]) for i in range(NSB)]; Csb = [sb("Csb%d" % i, [128, 2, 257], BF16) for i in range(NSB)]
        Ss = [sb("Ss%d" % i, [128, 128]) for i in range(NSB)]; Ssb = [sb("Ssb%d" % i, [128, 128], BF16) for i in range(NSB)]
        gsm = sb("gsm", [128, 64]); diag4 = sb("diag4", [128, 4, 128]); dtmp = sb("dtmp", [128, 4, 128])
        Rt = sb("Rt", [128, NB, 4]); bend = sb("bend", [128, NB * 4])
        CQ = 256
        taps = [sb("tap%d" % j, [128, CQ]) for j in range(4)]; cw = [sb("cw%d" % j, [128, CQ]) for j in range(4)]
        qk = sb("qk", [128, 2048]); qT = sb("qT", [128, 8, 128], BF16); kT = sb("kT", [128, 8, 128], BF16)
        khat = sb("khat", [128, 1024], BF16); khm = sb("khm", [128, 1024], BF16)
        zmv = sb("zmv", [128, 2048]); Vp = sb("Vp", [128, 4, 257], BF16); MG = sb("MG", [128, 1024])
        sTs = sb("sTs", [128, 128], BF16); qTm = sb("qTm", [128, 1, 128], BF16)
        hm = sb("hm", [128, 256]); hj = sb("hj", [128, 256]); st = sb("st", [128, 8])
        hm2 = sb("hm2", [128, 128]); hj2 = sb("hj2", [128, 128]); st2 = sb("st2", [128, 8])
        A = [View(qk, 0, 1024), View(qk, 1024, 1024)] + [sb("A%d" % i, [128, 1024]) for i in range(2, 6)]
        ktb = sb("ktb", [128, 1024], BF16); vb = sb("vb", [128, 1024], BF16)
        qtT = sb("qtT", [128, 8, 128], BF16); ktT = sb("ktT", [128, 8, 128], BF16)
        dec = sb("dec", [128, 8, 34]); ATs = sb("ATs", [128, 128], BF16)
        mixed = zmv
        cx.op("pool", lambda e: e.memset(Vp[:], 1.0), writes=[Vp])
        IG, LF, BB, GG_, CMX, MP, CM, AL, BE, MT, FL, T1, T2, AL16 = [slice(4 * k, 4 * k + 4) for k in range(14)]
        ssi = [0]

        for i in range(NT):
            s = 1 if i == NTP else 0
            r0 = i * 128
            cx.dma("sp", gsm[:, 0:8], z_d[r0:r0 + 128, 2048:2056], reads=[z_d], writes=[gsm], owner=gsm)
            dv(lambda e: e.tensor_tensor(gsm[:, 0:8], gsm[:, 0:8], gb[:], ALU.add), [gsm, gb], [gsm])
            ac(lambda e: e.activation(gsm[:, T1], gsm[:, LF], AF.Exp, scale=-1.0), [gsm], [gsm])
            ac(lambda e: e.activation(gsm[:, T1], gsm[:, T1], AF.Ln, bias=1.0), [gsm], [gsm])
            dv(lambda e: e.tensor_scalar_mul(gsm[:, LF], gsm[:, T1], -1.0), [gsm], [gsm])
            p = nps()
            mmg(p[:, 0:4], [(trir[1][:] if s else mle[0][:], gsm[:, LF])], [mle[s], gsm], p)
            dv(lambda e: e.tensor_copy(gsm[:, BB], p[:, 0:4]), [p], [gsm])
            dv(lambda e: e.tensor_tensor(gsm[:, GG_], gsm[:, IG], gsm[:, BB], ALU.subtract), [gsm], [gsm])
            for h in range(4):
                dv(lambda e, h=h: e.tensor_scalar_mul(diag4[:, h, :], ident[:], gsm[:, 12 + h:13 + h]), [ident, gsm], [diag4])
            p = nps()
            for h in range(4):
                mmg(p[:, h * 128:(h + 1) * 128], [(ones[:], diag4[:, h, :])], [ones, diag4], p)
            dv(lambda e: e.tensor_tensor(dtmp[:], p[:].rearrange("p (a b) -> p a b", a=4), neg[s].v(0, [[0, 4], [1, 128]]), ALU.add), [p, neg[s]], [dtmp])
            dv(lambda e: e.tensor_reduce(gsm[:, CMX], dtmp[:], AX.X, ALU.max), [dtmp], [gsm])
            p = nps()
            mmg(p[:, 0:4], [(blkt[s][:], mst[:])], [blkt[s], mst], p)
            dv(lambda e: e.tensor_copy(gsm[:, MP], p[:, 0:4]), [p], [gsm])
            dv(lambda e: e.tensor_tensor(gsm[:, CM], gsm[:, CMX], gsm[:, MP], ALU.max), [gsm], [gsm])
            dv(lambda e: e.tensor_tensor(gsm[:, T1], gsm[:, GG_], gsm[:, MP], ALU.subtract), [gsm], [gsm])
            ac(lambda e: e.activation(gsm[:, AL], gsm[:, T1], AF.Exp), [gsm], [gsm])
            dv(lambda e: e.tensor_tensor(gsm[:, T1], gsm[:, MP], gsm[:, CM], ALU.subtract), [gsm], [gsm])
            ac(lambda e: e.activation(gsm[:, BE], gsm[:, T1], AF.Exp), [gsm], [gsm])
            dv(lambda e: e.tensor_tensor(gsm[:, MT], gsm[:, BB], gsm[:, CM], ALU.add), [gsm], [gsm])
            ac(lambda e: e.activation(gsm[:, FL], gsm[:, MT], AF.Exp, scale=-1.0), [gsm], [gsm])
            dv(lambda e: e.tensor_scalar_mul(gsm[:, AL16], gsm[:, AL], 0.0625), [gsm], [gsm])
            p = nps()
            mmg(p[0:NB, 0:4], [(selend[s][:], gsm[:, MT])], [selend[s], gsm], p)
            dv(lambda e: e.tensor_copy(mend[:], p[0:NB, 0:4]), [p], [mend])
            if s == 0:
                dv(lambda e: e.tensor_copy(mst[0:1, :], mend[0:1, :]), [mend], [mst])
                if i == NTP - 1:
                    cx.dma("act", om[0:1, :], mend[0:1, :], reads=[mend], writes=[om], owner=mend)
            else:
                cx.dma("act", om[1:NB, :], mend[1:NB, :], reads=[mend], writes=[om], owner=mend)
            dv(lambda e: e.tensor_tensor(Rt[:], selend[s].v(0, [[1, NB], [0, 4]]), gsm.v(32, [[0, NB], [1, 4]]), ALU.mult), [selend[s], gsm], [Rt])
            p = nps()
            mmg(p[:, 0:NB * 4], [(ones[:], Rt[:].rearrange("p a b -> p (a b)"))], [ones, Rt], p)
            dv(lambda e: e.tensor_copy(bend[:], p[:, 0:NB * 4]), [p], [bend])
            for qd in range(2048 // CQ):
                c0 = qd * CQ
                for j in range(4):
                    if s == 0:
                        cx.dma("sp", taps[j][:], ext_p[r0 + j:r0 + j + 128, c0:c0 + CQ], reads=[ext_p], writes=[taps[j]], owner=taps[j])
                    else:
                        cx.dma("sp", taps[j][:], ext_s[:, j:j + 8, c0:c0 + CQ], reads=[ext_s], writes=[taps[j]], owner=taps[j])
                    cx.dma("sp", cw[j][:], W["m_conv_w"][0, j, c0:c0 + CQ].partition_broadcast(128), writes=[cw[j]], owner=cw[j])
                for j in range(4):
                    cx.op("pool" if j % 2 else "dve", lambda e, j=j: e.tensor_tensor(taps[j][:], taps[j][:], cw[j][:], ALU.mult), reads=[taps[j], cw[j]], writes=[taps[j]])
                dv(lambda e: e.tensor_tensor(taps[0][:], taps[0][:], taps[1][:], ALU.add), [taps[0], taps[1]], [taps[0]])
                cx.op("pool", lambda e: e.tensor_tensor(taps[2][:], taps[2][:], taps[3][:], ALU.add), reads=[taps[2], taps[3]], writes=[taps[2]])
                dv(lambda e: e.tensor_tensor(taps[0][:], taps[0][:], taps[2][:], ALU.add), [taps[0], taps[2]], [taps[0]])
                ac(lambda e, c0=c0: e.activation(qk[:, c0:c0 + CQ], taps[0][:], AF.Silu), [taps[0]], [qk])
            transpose_to(qT, lambda c: qT[:, c, :], qk, lambda c: qk[:, c * 128:(c + 1) * 128], 8)
            transpose_to(kT, lambda c: kT[:, c, :], qk, lambda c: qk[:, 1024 + c * 128:1024 + (c + 1) * 128], 8, scale_fn=lambda c: 0.0625)
            for h in range(4):
                dv(lambda e, h=h: e.tensor_scalar_mul(khat[:, h * 256:(h + 1) * 256], qk[:, 1024 + h * 256:1024 + (h + 1) * 256], gsm[:, 52 + h:53 + h]),
                   [qk, gsm], [khat])
            cx.dma("sp", zmv[:], z_d[r0:r0 + 128, 0:2048], reads=[z_d], writes=[zmv], owner=zmv)
            dv(lambda e: e.tensor_copy(Vp.v(0, [[257, 4], [1, 256]]), zmv.v(0, [[256, 4], [1, 256]])), [zmv], [Vp])
            ac(lambda e: e.activation(MG[:], zmv[:, 1024:2048], AF.Sigmoid), [zmv], [MG])
            dv(lambda e: e.tensor_tensor(MG[:], MG[:], mgain[:], ALU.mult), [MG, mgain], [MG])
            for h in range(4 if s == 1 else 0):
                p = nps()
                mmg(p[:, 0:128], [(kT[:, 2 * h + c, :], qT[:, 2 * h + c, :]) for c in range(2)], [kT, qT], p)
                dv(lambda e, h=h, p=p: e.scalar_tensor_tensor(sTs[:], p[:, 0:128], gsm[:, 28 + h:29 + h], mle[s][:], ALU.mult, ALU.mult), [p, gsm, mle[s]], [sTs])
                nd = accps()
                if s == 0:
                    pairs = [(sTs[:], Vp[:, h, :])] + [(qT[:, 2 * h + c, :], Cb[:, h, c, :]) for c in range(2)]
                    mmg(nd[:, 0:257], pairs, [sTs, Vp, qT, Cb], nd)
                    for c in range(2):
                        pu = nps()
                        mmg(pu[:, 0:257], [(khat[:, h * 256 + c * 128:h * 256 + (c + 1) * 128], Vp[:, h, :])], [khat, Vp], pu)
                        dv(lambda e, h=h, c=c, pu=pu: e.tensor_tensor(Cst[:, h, c, :], pu[:, 0:257], Cst[:, h, c, :], ALU.add), [pu, Cst], [Cst])
                        dv(lambda e, h=h, c=c: e.tensor_scalar_mul(Cst[:, h, c, :], Cst[:, h, c, :], bend[:, h:h + 1]), [Cst, bend], [Cst])
                else:
                    cx.op("pe", lambda e, h=h: e.matmul(nd[:, 0:257], sTs[:], Vp[:, h, :], start=True, stop=False), reads=[sTs, Vp], writes=[nd])
                    for c in range(2):
                        pass
                    for b in range(16):
                        Cq = Cs[ssi[0] % NSB]; Cqb = Csb[ssi[0] % NSB]; ssi[0] += 1
                        cx.dma("sp", Cq[:, :, 0:256], mC[b + 1, h].rearrange("(c p) e -> p c e", p=128), reads=[mC], writes=[Cq], owner=Cq)
                        cx.dma("sp", Cq[:, :, 256:257], mn[b + 1, h].rearrange("(c p o) -> p c o", p=128, o=1), reads=[mn], writes=[Cq], owner=Cq, allow_slow_non_contiguous=True)
                        cx.op("pool", lambda e, Cq=Cq, Cqb=Cqb: e.tensor_copy(Cqb[:], Cq[:]), reads=[Cq], writes=[Cqb])
                        dv(lambda e, b=b: e.tensor_scalar_mul(khm[:, h * 256:(h + 1) * 256], khat[:, h * 256:(h + 1) * 256], ref[1][:, 18 + b:19 + b]), [khat, ref[1]], [khm])
                        for c in range(2):
                            dv(lambda e, c=c, b=b: e.tensor_tensor(qTm[:, 0, :], qT[:, 2 * h + c, :], blkrow[:, b, :], ALU.mult), [qT, blkrow], [qTm])
                            cx.op("pe", lambda e, c=c, b=b, Cqb=Cqb: e.matmul(nd[:, 0:257], qTm[:, 0, :], Cqb[:, c, :], start=False, stop=(b == 15 and c == 1)),
                                  reads=[qTm, Cqb], writes=[nd])
                        for c in range(2):
                            pu = nps()
                            mmg(pu[:, 0:257], [(khm[:, h * 256 + c * 128:h * 256 + (c + 1) * 128], Vp[:, h, :])], [khm, Vp], pu)
                            dv(lambda e, c=c, pu=pu, Cq=Cq: e.tensor_tensor(Cq[:, c, :], pu[:, 0:257], Cq[:, c, :], ALU.add), [pu, Cq], [Cq])
                            dv(lambda e, c=c, b=b, Cq=Cq: e.tensor_scalar_mul(Cq[:, c, :], Cq[:, c, :], bend[:, (b + 1) * 4 + h:(b + 1) * 4 + h + 1]), [Cq, bend], [Cq])
                            if c == 1:
                                cx.dma("act", oC[b + 1, h].rearrange("(c p) e -> p c e", p=128), Cq[:, :, 0:256], reads=[Cq], writes=[oC], owner=Cq)
                                cx.dma("act", on[b + 1, h].rearrange("(c p o) -> p c o", p=128, o=1), Cq[:, :, 256:257], reads=[Cq], writes=[on], owner=Cq, allow_slow_non_contiguous=True)
                dv(lambda e, nd=nd: e.tensor_copy(st[:, 0:1], nd[:, 256:257]), [nd], [st])
                dv(lambda e: e.tensor_scalar_mul(st[:, 6:7], st[:, 0:1], -1.0), [st], [st])
                dv(lambda e: e.tensor_tensor(st[:, 0:1], st[:, 0:1], st[:, 6:7], ALU.max), [st], [st])
                dv(lambda e, h=h: e.tensor_tensor(st[:, 0:1], st[:, 0:1], gsm[:, 32 + h:33 + h], ALU.mult), [st, gsm], [st])
                dv(lambda e, h=h: e.tensor_tensor(st[:, 0:1], st[:, 0:1], gsm[:, 40 + h:41 + h], ALU.max), [st, gsm], [st])
                dv(lambda e: e.reciprocal(st[:, 0:1], st[:, 0:1]), [st], [st])
                dv(lambda e, h=h: e.tensor_tensor(st[:, 0:1], st[:, 0:1], gsm[:, 32 + h:33 + h], ALU.mult), [st, gsm], [st])
                dv(lambda e, nd=nd: e.tensor_scalar_mul(hm[:], nd[:, 0:256], st[:, 0:1]), [nd, st], [hm])
                dv(lambda e: e.tensor_reduce(st[:, 1:2], hm[:], AX.X, ALU.add), [hm], [st])
                dv(lambda e: e.tensor_scalar_mul(st[:, 1:2], st[:, 1:2], 1.0 / 256), [st], [st])
                dv(lambda e: e.tensor_scalar(hm[:], hm[:], st[:, 1:2], None, ALU.subtract), [hm, st], [hm]) if False else \
                    dv(lambda e: e.tensor_scalar_sub(hm[:], hm[:], st[:, 1:2]), [hm, st], [hm])
                dv(lambda e: e.tensor_tensor(hj[:], hm[:], hm[:], ALU.mult), [hm], [hj])
                dv(lambda e: e.tensor_reduce(st[:, 2:3], hj[:], AX.X, ALU.add), [hj], [st])
                rstd_col(st, 3, st[:, 2:3], 1.0 / 256, RMS_EPS)
                dv(lambda e, h=h: e.scalar_tensor_tensor(mixed[:, h * 256:(h + 1) * 256], hm[:], st[:, 3:4], MG[:, h * 256:(h + 1) * 256], ALU.mult, ALU.mult),
                   [hm, st, MG], [mixed])
            if s == 0 and False:
                dv(lambda e: e.tensor_copy(Cb[:], Cst[:]), [Cst], [Cb])
                if i == NTP - 1:
                    for h in range(4):
                        for c in range(2):
                            cx.dma("act", oC[0, h, c * 128:(c + 1) * 128, :], Cst[:, h, c, 0:256], reads=[Cst], writes=[oC], owner=Cst)
                            cx.dma("act", on[0, h, c * 128:(c + 1) * 128].rearrange("(p o) -> p o", o=1), Cst[:, h, c, 256:257], reads=[Cst], writes=[on], owner=Cst)
            zc = 2056
            for k in range(4):
                cx.dma("sp", A[k][:], z_d[r0:r0 + 128, zc + k * 1024:zc + (k + 1) * 1024], reads=[z_d], writes=[A[k]], owner=A[k])
            ac(lambda e: e.activation(A[1][:], A[1][:], AF.Sigmoid), [A[1]], [A[1]])
            dv(lambda e: e.tensor_tensor(A[1][:], A[1][:], OMLB[:], ALU.mult), [A[1], OMLB], [A[1]])
            dv(lambda e: e.tensor_tensor(A[1][:], A[1][:], LB[:], ALU.add), [A[1], LB], [A[1]])
            ac(lambda e: e.activation(A[4][:], A[1][:], AF.Ln), [A[1]], [A[4]])
            pb = [nps(), nps()]
            for hf in range(2):
                mmg(pb[hf][:], [(trir[s][:], A[4][:, hf * 512:(hf + 1) * 512])], [trir[s], A[4]], pb[hf])
            pd = nps()
            for h in range(8):
                mmg(pd[:, h * 34:(h + 1) * 34], [(A[4][:, h * 128:(h + 1) * 128], ref[s][:])], [A[4], ref[s]], pd)
            ac(lambda e: e.activation(dec[:].rearrange("p a b -> p (a b)"), pd[:, 0:272], AF.Exp), [pd], [dec])
            for hf in range(2):
                ac(lambda e, hf=hf: e.activation(A[5][:, hf * 512:(hf + 1) * 512], pb[hf][:], AF.Exp), [pb[hf]], [A[5]])
                ac(lambda e, hf=hf: e.activation(A[4][:, hf * 512:(hf + 1) * 512], pb[hf][:], AF.Exp, scale=-1.0), [pb[hf]], [A[4]])
            ac(lambda e: e.activation(A[0][:], A[0][:], AF.Silu), [A[0]], [A[0]])
            dv(lambda e: e.scalar_tensor_tensor(A[0][:], A[0][:], float(128 ** -0.5), A[5][:], ALU.mult, ALU.mult), [A[0], A[5]], [A[0]])
            dv(lambda e: e.tensor_scalar(A[1][:], A[1][:], -1.0, 1.0, ALU.mult, ALU.add), [A[1]], [A[1]])
            dv(lambda e: e.tensor_tensor(A[1][:], A[1][:], A[4][:], ALU.mult), [A[1], A[4]], [A[1]])
            cx.op("pool", lambda e: e.tensor_copy(ktb[:], A[1][:]), reads=[A[1]], writes=[ktb])
            cx.op("pool", lambda e: e.tensor_copy(vb[:], A[2][:]), reads=[A[2]], writes=[vb])
            ac(lambda e: e.activation(A[3][:], A[3][:], AF.Silu), [A[3]], [A[3]])
            dv(lambda e: e.tensor_tensor(A[3][:], A[3][:], ggain[:], ALU.mult), [A[3], ggain], [A[3]])
            transpose_to(qtT, lambda c: qtT[:, c, :], A[0], lambda c: A[0][:, c * 128:(c + 1) * 128], 8)
            transpose_to(ktT, lambda c: ktT[:, c, :], A[1], lambda c: A[1][:, c * 128:(c + 1) * 128], 8)
            if s == 0:
                dv(lambda e: e.tensor_tensor(Smid[:], Sst[:], dec.v(0, [[34, 8], [0, 128]]), ALU.mult), [Sst, dec], [Smid])
                cx.op("pool", lambda e: e.tensor_copy(Smidb[:], Smid[:]), reads=[Smid], writes=[Smidb])
            if s == 0:
                def gen_m():
                    banks = [PS[0], PS[1]]; bi = [0]

                    def mps():
                        bi[0] += 1
                        return banks[bi[0] % 2]
                    for h in range(4):
                        p = mps()
                        mmg(p[:, 0:128], [(kT[:, 2 * h + c, :], qT[:, 2 * h + c, :]) for c in range(2)], [kT, qT], p)
                        dv(lambda e: e.scalar_tensor_tensor(sTs[:], p[:, 0:128], gsm[:, 28 + h:29 + h], mle[s][:], ALU.mult, ALU.mult), [p, gsm, mle[s]], [sTs])
                        nd = PS[6]
                        pairs = [(sTs[:], Vp[:, h, :])] + [(qT[:, 2 * h + c, :], Cb[:, h, c, :]) for c in range(2)]
                        mmg(nd[:, 0:257], pairs, [sTs, Vp, qT, Cb], nd)
                        yield
                        for c in range(2):
                            pu = mps()
                            mmg(pu[:, 0:257], [(khat[:, h * 256 + c * 128:h * 256 + (c + 1) * 128], Vp[:, h, :])], [khat, Vp], pu)
                            dv(lambda e: e.tensor_tensor(Cst[:, h, c, :], pu[:, 0:257], Cst[:, h, c, :], ALU.add), [pu, Cst], [Cst])
                            dv(lambda e: e.tensor_scalar_mul(Cst[:, h, c, :], Cst[:, h, c, :], bend[:, h:h + 1]), [Cst, bend], [Cst])
                            yield
                        dv(lambda e: e.tensor_copy(st[:, 0:1], nd[:, 256:257]), [nd], [st])
                        dv(lambda e: e.tensor_scalar_mul(st[:, 6:7], st[:, 0:1], -1.0), [st], [st])
                        dv(lambda e: e.tensor_tensor(st[:, 0:1], st[:, 0:1], st[:, 6:7], ALU.max), [st], [st])
                        dv(lambda e: e.tensor_tensor(st[:, 0:1], st[:, 0:1], gsm[:, 32 + h:33 + h], ALU.mult), [st, gsm], [st])
                        yield
                        dv(lambda e: e.tensor_tensor(st[:, 0:1], st[:, 0:1], gsm[:, 40 + h:41 + h], ALU.max), [st, gsm], [st])
                        dv(lambda e: e.reciprocal(st[:, 0:1], st[:, 0:1]), [st], [st])
                        dv(lambda e: e.tensor_tensor(st[:, 0:1], st[:, 0:1], gsm[:, 32 + h:33 + h], ALU.mult), [st, gsm], [st])
                        dv(lambda e: e.tensor_scalar_mul(hm[:], nd[:, 0:256], st[:, 0:1]), [nd, st], [hm])
                        yield
                        dv(lambda e: e.tensor_reduce(st[:, 1:2], hm[:], AX.X, ALU.add), [hm], [st])
                        dv(lambda e: e.tensor_scalar_mul(st[:, 1:2], st[:, 1:2], 1.0 / 256), [st], [st])
                        dv(lambda e: e.tensor_scalar_sub(hm[:], hm[:], st[:, 1:2]), [hm, st], [hm])
                        yield
                        dv(lambda e: e.tensor_tensor(hj[:], hm[:], hm[:], ALU.mult), [hm], [hj])
                        dv(lambda e: e.tensor_reduce(st[:, 2:3], hj[:], AX.X, ALU.add), [hj], [st])
                        rstd_col(st, 3, st[:, 2:3], 1.0 / 256, RMS_EPS)
                        yield
                        dv(lambda e: e.scalar_tensor_tensor(mixed[:, h * 256:(h + 1) * 256], hm[:], st[:, 3:4], MG[:, h * 256:(h + 1) * 256], ALU.mult, ALU.mult),
                           [hm, st, MG], [mixed])
                        yield

                def gen_g():
                    banks = [PS[2], PS[3]]; bi = [0]

                    def gps():
                        bi[0] += 1
                        return banks[bi[0] % 2]
                    for h in range(8):
                        hc = slice(h * 128, (h + 1) * 128)
                        p = gps()
                        mmg(p[0:64, 0:64], [(ktT[:, h, 0:64], qtT[:, h, 0:64])], [ktT, qtT], p)
                        mmg(p[:, 64:128], [(ktT[:, h, :], qtT[:, h, 64:128])], [ktT, qtT], p)
                        dv(lambda e: e.tensor_tensor(ATs[0:64, 0:64], p[0:64, 0:64], mle[s][0:64, 0:64], ALU.mult), [p, mle[s]], [ATs])
                        dv(lambda e: e.tensor_tensor(ATs[:, 64:128], p[:, 64:128], mle[s][:, 64:128], ALU.mult), [p, mle[s]], [ATs])
                        dv(lambda e: e.memset(ATs[64:128, 0:64], 0.0), [], [ATs])
                        yield
                        po = PS[7]
                        mmg(po[:, 0:128], [(ATs[:], vb[:, hc]), (qtT[:, h, :], Smidb[:, h, :])], [ATs, vb, qtT, Smidb], po)
                        pu = gps()
                        mmg(pu[:, 0:128], [(ktb[:, hc], vb[:, hc])], [ktb, vb], pu)
                        yield
                        dv(lambda e: e.tensor_tensor(Sst[:, h, :], pu[:, 0:128], Smid[:, h, :], ALU.add), [pu, Smid], [Sst])
                        dv(lambda e: e.tensor_scalar_mul(Sst[:, h, :], Sst[:, h, :], dec[:, h, 17:18]), [Sst, dec], [Sst])
                        yield
                        ac(lambda e: e.activation(hm2[:], po[:, 0:128], AF.Copy), [po], [hm2])
                        cx.op("pool", lambda e: e.tensor_tensor(hj2[:], hm2[:], hm2[:], ALU.mult), reads=[hm2], writes=[hj2])
                        yield
                        dv(lambda e: e.tensor_reduce(st2[:, 4:5], hj2[:], AX.X, ALU.add), [hj2], [st2])
                        rstd_col(st2, 5, st2[:, 4:5], 1.0 / 128, RMS_EPS)
                        yield
                        dv(lambda e: e.scalar_tensor_tensor(mixed[:, 1024 + hc.start:1024 + hc.stop], hm2[:], st2[:, 5:6], A[3][:, hc], ALU.mult, ALU.mult),
                           [hm2, st2, A[3]], [mixed])
                        yield
                gens = [gen_m(), gen_g()]
                while gens:
                    for g in list(gens):
                        try:
                            next(g)
                        except StopIteration:
                            gens.remove(g)
                dv(lambda e: e.tensor_copy(Cb[:], Cst[:]), [Cst], [Cb])
                if i == NTP - 1:
                    for h in range(4):
                        for c in range(2):
                            cx.dma("act", oC[0, h, c * 128:(c + 1) * 128, :], Cst[:, h, c, 0:256], reads=[Cst], writes=[oC], owner=Cst)
                            cx.dma("act", on[0, h, c * 128:(c + 1) * 128].rearrange("(p o) -> p o", o=1), Cst[:, h, c, 256:257], reads=[Cst], writes=[on], owner=Cst)
            for h in range(8 if s == 1 else 0):
                hc = slice(h * 128, (h + 1) * 128)
                p = nps()
                if s == 0:
                    mmg(p[0:64, 0:64], [(ktT[:, h, 0:64], qtT[:, h, 0:64])], [ktT, qtT], p)
                    mmg(p[:, 64:128], [(ktT[:, h, :], qtT[:, h, 64:128])], [ktT, qtT], p)
                    dv(lambda e, p=p: e.tensor_tensor(ATs[0:64, 0:64], p[0:64, 0:64], mle[s][0:64, 0:64], ALU.mult), [p, mle[s]], [ATs])
                    dv(lambda e, p=p: e.tensor_tensor(ATs[:, 64:128], p[:, 64:128], mle[s][:, 64:128], ALU.mult), [p, mle[s]], [ATs])
                    dv(lambda e: e.memset(ATs[64:128, 0:64], 0.0), [], [ATs])
                else:
                    mmg(p[:, 0:128], [(ktT[:, h, :], qtT[:, h, :])], [ktT, qtT], p)
                    dv(lambda e, p=p: e.tensor_tensor(ATs[:], p[:, 0:128], mle[s][:], ALU.mult), [p, mle[s]], [ATs])
                po = accps()
                if s == 0:
                    mmg(po[:, 0:128], [(ATs[:], vb[:, hc]), (qtT[:, h, :], Smidb[:, h, :])], [ATs, vb, qtT, Smidb], po)
                    pu = nps()
                    mmg(pu[:, 0:128], [(ktb[:, hc], vb[:, hc])], [ktb, vb], pu)
                    dv(lambda e, h=h, pu=pu: e.tensor_tensor(Sst[:, h, :], pu[:, 0:128], Smid[:, h, :], ALU.add), [pu, Smid], [Sst])
                    dv(lambda e, h=h: e.tensor_scalar_mul(Sst[:, h, :], Sst[:, h, :], dec[:, h, 17:18]), [Sst, dec], [Sst])
                else:
                    cx.op("pe", lambda e, hc=hc: e.matmul(po[:, 0:128], ATs[:], vb[:, hc], start=True, stop=False), reads=[ATs, vb], writes=[po])
                    for b in range(16):
                        Sq = Ss[ssi[0] % NSB]; Sqb = Ssb[ssi[0] % NSB]; ssi[0] += 1
                        cx.dma("sp", Sq[:], gS[b + 1, h], reads=[gS], writes=[Sq], owner=Sq)
                        cx.op("pool", lambda e, Sq=Sq, Sqb=Sqb: e.tensor_copy(Sqb[:], Sq[:]), reads=[Sq], writes=[Sqb])
                        dv(lambda e, b=b, h=h: e.tensor_tensor(qTm[:, 0, :], qtT[:, h, :], blkrow[:, b, :], ALU.mult), [qtT, blkrow], [qTm])
                        cx.op("pe", lambda e, b=b, Sqb=Sqb: e.matmul(po[:, 0:128], qTm[:, 0, :], Sqb[:], start=False, stop=(b == 15)), reads=[qTm, Sqb], writes=[po])
                        dv(lambda e, b=b, hc=hc: e.tensor_scalar_mul(khm[:, 0:128], ktb[:, hc], ref[1][:, 18 + b:19 + b]), [ktb, ref[1]], [khm])
                        pu = nps()
                        mmg(pu[:, 0:128], [(khm[:, 0:128], vb[:, hc])], [khm, vb], pu)
                        dv(lambda e, pu=pu, Sq=Sq: e.tensor_tensor(Sq[:], pu[:, 0:128], Sq[:], ALU.add), [pu, Sq], [Sq])
                        dv(lambda e, b=b, h=h, Sq=Sq: e.tensor_scalar_mul(Sq[:], Sq[:], dec[:, h, 17 + b + 1:17 + b + 2]), [Sq, dec], [Sq])
                        cx.dma("act", oS[b + 1, h], Sq[:], reads=[Sq], writes=[oS], owner=Sq)
                dv(lambda e, po=po: e.tensor_tensor(hj[:, 0:128], po[:, 0:128], po[:, 0:128], ALU.mult) if False else e.tensor_copy(hm[:, 0:128], po[:, 0:128]), [po], [hm])
                dv(lambda e: e.tensor_tensor(hj[:, 0:128], hm[:, 0:128], hm[:, 0:128], ALU.mult), [hm], [hj])
                dv(lambda e: e.tensor_reduce(st[:, 4:5], hj[:, 0:128], AX.X, ALU.add), [hj], [st])
                rstd_col(st, 5, st[:, 4:5], 1.0 / 128, RMS_EPS)
                dv(lambda e, hc=hc: e.scalar_tensor_tensor(mixed[:, 1024 + hc.start:1024 + hc.stop], hm[:, 0:128], st[:, 5:6], A[3][:, hc], ALU.mult, ALU.mult),
                   [hm, st, A[3]], [mixed])
            if s == 0 and i == NTP - 1:
                cx.dma("act", oS[0].rearrange("h k v -> k h v"), Sst[:], reads=[Sst], writes=[oS], owner=Sst)
            transpose_to(mixT, lambda c: mixT[:, c, r0:r0 + 128], mixed, lambda c: mixed[:, c * 128:(c + 1) * 128], 16)
        cx.dma("sp", oconv[0], ext_p[NP:NP + 3, :], reads=[ext_p], writes=[oconv], owner=oconv)
        cx.dma("sp", oconv[1:17], ext_s[:, 8:11, :], reads=[ext_s], writes=[oconv], owner=oconv)

    mix_d = dscr("mix_d", [6, 128, 16 * NTOK], BF16)

    @staged
    def stage_l1_mix(hT):
        shs = sb("shs", [NB, D]); shT = sb("shT", [128, 16, NB], BF16); muT = sb("muT", [128, 6, 16])
        xx = [sb("xx%d" % i, [128, NTOK], BF16) for i in range(2)]
        mo_ = [sb("mo%d" % i, [128, NTOK], BF16) for i in range(4)]
        cx.dma("sp", shs[:], rsh[:], writes=[shs], owner=shs)
        for c0 in range(0, 16, 4):
            p = nps()
            for j in range(4):
                c = c0 + j
                cx.op("pe", lambda e, j=j, c=c: e.transpose(p.v(j * 32, [[1, NB]]), shs[:, c * 128:(c + 1) * 128], ident[0:NB, 0:NB]),
                      reads=[shs, ident], writes=[p])
            for j in range(4):
                evac(shT[:, c0 + j, :], p.v(j * 32, [[1, NB]]), [p], [shT])
        for i in range(6):
            cx.dma("sp", muT[:, i, :], W["r_mu"][0, i].rearrange("(c p) -> p c", p=128), writes=[muT], owner=muT, allow_slow_non_contiguous=True)
        oi = 0
        for c in range(16):
            x = xx[c % 2]
            cx.op("dve", lambda e: e.tensor_tensor(x[:, 1:NTOK], hT[:, c, 0:NTOK - 1], hT[:, c, 1:NTOK], ALU.subtract), reads=[hT], writes=[x])
            cx.op("dve", lambda e: e.tensor_tensor(x[:, 0:1], shT[:, c, 0:1], hT[:, c, 0:1], ALU.subtract), reads=[hT, shT], writes=[x])
            cx.op("dve", lambda e: e.tensor_tensor(x.v(NP, [[8, 16]]), shT[:, c, 1:NB], hT.v(c * NTOK + NP, [[8, 16]]), ALU.subtract), reads=[hT, shT], writes=[x])
            for i in range(6):
                o = mo_[oi % 4]; oi += 1
                cx.op("dve", lambda e, i=i, o=o: e.scalar_tensor_tensor(o[:], x[:], muT[:, i, c:c + 1], hT[:, c, :], ALU.mult, ALU.add),
                      reads=[x, muT, hT], writes=[o])
                cx.dma("sp", mix_d[i, :, c * NTOK:(c + 1) * NTOK], o[:], reads=[o], writes=[mix_d], owner=o)

    def load_mix(i, hT):
        cx.dma("sp", hT[:].rearrange("p a b -> p (a b)"), mix_d[i], reads=[mix_d], writes=[hT], owner=hT)
        cx.barrier()

    @staged
    def lora1(aT, w1_ap, R, func, tT):
        w1 = sb("l1w", [128, 16, R], BF16)
        cx.dma("pool", w1[:], w1_ap.rearrange("(kc p) n -> p kc n", p=128), writes=[w1], owner=w1)
        for oc in range((R + 127) // 128):
            rc = min(128, R - oc * 128)
            for g0 in range(0, NTOK, 512):
                gn = min(512, NTOK - g0)
                p = nps()
                for kc in range(16):
                    cx.op("pe", lambda e, kc=kc: e.matmul(p[0:rc, 0:gn], w1[:, kc, oc * 128:oc * 128 + rc], aT[:, kc, g0:g0 + gn], start=(kc == 0), stop=(kc == 15)),
                          reads=[w1, aT], writes=[p])
                cx.op("act", lambda e: e.activation(tT[0:rc, oc, g0:g0 + gn], p[0:rc, 0:gn], func), reads=[p], writes=[tT])

    def stage_l1_proj(hT):
        def dst_rz(base):
            return lambda c0: (lambda i: [(rz_d, rz_d[i * 128:(i + 1) * 128, base + c0:base + c0 + 512], 0, 128)])
        def blocks_for(w, base):
            return [(w[:, c0:c0 + 512], dst_rz(base)(c0)) for c0 in range(0, D, 512)]
        stacks.append(contextlib.ExitStack()); stage_bufs.append([])
        tT = sb("tT", [128, 2, NTOK], BF16)
        load_mix(0, hT); proj_tok(hT, blocks_for(W["r_wr"][0], 0))
        load_mix(2, hT); proj_tok(hT, blocks_for(W["r_wk"][0], D))
        load_mix(3, hT); proj_tok(hT, blocks_for(W["r_wv"][0], 2 * D))
        load_mix(1, hT); lora1(hT, W["r_w1"][0], 96, AF.Tanh, tT)
        proj_tok(tT, blocks_for(W["r_w2"][0], 3 * D), K=1, kpart=96, bias_fn=lambda bi: W["r_w0"][0, bi * 512:(bi + 1) * 512], sigmoid=True)
        load_mix(4, hT); lora1(hT, W["r_a1"][0], 96, AF.Copy, tT)
        proj_tok(tT, blocks_for(W["r_a2"][0], 4 * D), K=1, kpart=96, bias_fn=lambda bi: W["r_a0"][0, bi * 512:(bi + 1) * 512], sigmoid=True)
        load_mix(5, hT); lora1(hT, W["r_g1"][0], 256, AF.Sigmoid, tT)
        proj_tok(tT, blocks_for(W["r_g2"][0], 5 * D), K=2, kpart=128)
        cx.barrier(); cx.release(stage_bufs.pop()); stacks.pop().close()

    @staged
    def stage_l1_rec(outT):
        npsmod[0] = 4
        dv = lambda fn, r, w: cx.op("dve", fn, reads=r, writes=w)
        ac = lambda fn, r, w: cx.op("act", fn, reads=r, writes=w)
        po_ = lambda fn, r, w: cx.op("pool", fn, reads=r, writes=w)
        mle = [sb("mle%d" % i, [128, 128]) for i in range(2)]; mlt = [sb("mlt%d" % i, [128, 128]) for i in range(2)]
        mgt = [sb("mgt%d" % i, [128, 128]) for i in range(2)]; refs = sb("refs", [128, 34])
        blkrow = sb("blkrow", [128, 16, 128], BF16)
        NBP = 4
        for i, (a_, b_, c_) in enumerate([("rle_p", "rlt_p", "rgt_p"), ("mle_s", "mlt_s", "mgt_s")]):
            cx.dma("sp", mle[i][:], CN[a_][:], writes=[mle[i]], owner=mle[i])
            cx.dma("sp", mlt[i][:], CN[b_][:], writes=[mlt[i]], owner=mlt[i])
            cx.dma("sp", mgt[i][:], CN[c_][:], writes=[mgt[i]], owner=mgt[i])
        cx.dma("sp", refs[:], CN["ref_s"][:], writes=[refs], owner=refs)
        rblk = sb("rblk", [128, NB]); rblkrow = sb("rblkrow", [128, NBP, 128], BF16)
        cx.dma("sp", rblk[:], CN["rblk_p"][:], writes=[rblk], owner=rblk)
        cx.dma("pool", rblkrow[:], CN["rblkrow_p"][:].rearrange("p (b t) -> p b t", b=NBP), writes=[rblkrow], owner=rblkrow)
        Hrd = [sb("Hrd%d" % i, [128, NBP, 64], BF16) for i in range(2)]
        rmk = sb("rmk", [128, NBP, 128], BF16)
        cx.dma("pool", blkrow[:], CN["blkrow"][:].rearrange("p (b t) -> p b t", b=16), writes=[blkrow], owner=blkrow)
        HW = 1024
        PRMH = [{k: sb("prm_%s%d" % (k, hf), [128, HW]) for k in ["r_kk", "r_ka", "r_rk"]} for hf in range(2)]
        for hf in range(2):
            for k_, b_ in PRMH[hf].items():
                cx.dma("sp", b_[:], W[k_][0, hf * HW:(hf + 1) * HW].partition_broadcast(128), writes=[b_], owner=b_)
        LNW = sb("prm_lnw", [128, HW]); LNB = sb("prm_lnb", [128, HW])
        Rb, Kb, Vb_, SW, Aa, Cc, KK, TMP, E1, E2 = [sb("rw%d" % i, [128, HW]) for i in range(10)]
        Gg = E2
        aTt, bTt, kTt, rTt = [sb("rt%d" % i, [128, 8, 128], BF16) for i in range(4)]
        btk, ktk, vtk = [sb("rk%d" % i, [128, HW], BF16) for i in range(3)]
        Hst = sb("Hst", [128, 16, 64]); Hbm = [sb("Hbm%d" % i, [128, 16, 64], BF16) for i in range(2)]
        bTm = [sb("bTm%d" % i, [128, 8, 128], BF16) for i in range(2)]; kTm = [sb("kTm%d" % i, [128, 8, 128], BF16) for i in range(2)]
        Hsbm = [[sb("Hsbm%d_%d" % (i, j), [128, 64], BF16) for j in range(2)] for i in range(3)]
        cx.op("dve", lambda e: e.memset(Hst[:], 0.0), writes=[Hst])
        for zb in Hbm + bTm + kTm + Hsbm[0] + Hsbm[1] + Hsbm[2] + Hrd:
            cx.op("dve", lambda e, zb=zb: e.memset(zb[:], 0.0), writes=[zb])
        Nb = [sb("Nb%d" % i, [128, 2, 128], BF16) for i in range(2)]; Mb = [sb("Mb%d" % i, [128, 2, 128], BF16) for i in range(2)]
        TtS = [sb("TtS%d" % i, [128, 2, 128], BF16) for i in range(3)]; AakS = [sb("AakS%d" % i, [128, 2, 128], BF16) for i in range(3)]
        ArbS = [sb("ArbS%d" % i, [128, 2, 128], BF16) for i in range(3)]; ArkS = [sb("ArkS%d" % i, [128, 2, 128], BF16) for i in range(3)]
        BSET = [dict(am=sb("am%d" % i, [128, 128], BF16), Xb=sb("Xbq%d" % i, [128, 128], BF16), Ubf=sb("Ubq%d" % i, [128, 128], BF16),
                     bm=sb("bmq%d" % i, [128, 128], BF16), km=sb("kmq%d" % i, [128, 128], BF16), rmk=sb("rmkq%d" % i, [128, 4, 128], BF16),
                     Hrd=[sb("Hrdq%d_%d" % (i, j), [128, 4, 64], BF16) for j in range(2)]) for i in range(2)]
        for i in range(2):
            for zb in BSET[i]["Hrd"]:
                cx.op("dve", lambda e, zb=zb: e.memset(zb[:], 0.0), writes=[zb])
        Tt, AakT, ArbT, ArkT = TtS[0], AakS[0], ArbS[0], ArkS[0]
        Xb = sb("Xb", [128, 128], BF16); Ub = sb("Ub", [128, 128], BF16); Ubf = Ub
        am = sb("am", [128, 128], BF16); rm = sb("rm", [128, 128], BF16); bm = sb("bm", [128, 128], BF16); km = sb("km", [128, 128], BF16)
        dL = sb("dL", [128, 8, NB]); stt_ = sb("stt", [128, 4, 16])
        NSB = 3
        Sin = [sb("Sin%d" % i, [64, 2, 64]) for i in range(NSB)]; Hs = [sb("Hs%d" % i, [128, 64]) for i in range(NSB)]
        Sout = [sb("Sout%d" % i, [64, 2, 64]) for i in range(NSB)]
        ssi = [0]
        h3 = lambda b_: b_[:].rearrange("p (h j) -> p h j", j=64)

        def prompt_pairs(half, ncp, Y):
            s = 0
            nit = 4

            def genA(cp, st):
                def pairmm(p, lT, rT_):
                    for hh in range(2):
                        lb = lT[hh] if isinstance(lT, list) else lT
                        rb = rT_[hh] if isinstance(rT_, list) else rT_
                        mmg(p[:, hh * 128:(hh + 1) * 128], [(lb[:, cp, :], rb[:, cp, :])], [lb, rb], p)

                def pairev(dst, p, mask):
                    dv(lambda e: e.tensor_tensor(dst[:], p[:, 0:256].rearrange("p (a b) -> p a b", a=2), mask.v(0, [[0, 2], [1, 128]]), ALU.mult), [p, mask], [dst])
                Tt_ = TtS[st]
                p = nps(); pairmm(p, aTt, bTm); pairev(Nb[0], p, mgt[s]); yield
                p = nps(); pairmm(p, bTm, aTt); pairev(Mb[0], p, mlt[s]); yield
                p = nps(); pairmm(p, kTm, aTt); pairev(AakS[st], p, mlt[s]); yield
                p = nps(); pairmm(p, bTm, rTt); pairev(ArbS[st], p, mle[s]); yield
                p = nps(); pairmm(p, kTm, rTt); pairev(ArkS[st], p, mle[s]); yield
                dv(lambda e: e.tensor_tensor(Tt_[:], Mb[0][:], identb.v(0, [[0, 2], [1, 128]]), ALU.add), [Mb[0], identb], [Tt_])
                cur = 0
                for it in range(nit):
                    nx = 1 - cur
                    p1 = nps(); p2 = nps()
                    for hh in range(2):
                        if it < nit - 1:
                            mmg(p1[:, hh * 128:(hh + 1) * 128], [(Nb[cur][:, hh, :], Mb[cur][:, hh, :])], [Nb[cur], Mb[cur]], p1)
                        mmg(p2[:, hh * 128:(hh + 1) * 128], [(Mb[cur][:, hh, :], Nb[cur][:, hh, :])], [Nb[cur], Mb[cur]], p2)
                    yield
                    if it < nit - 1:
                        ac(lambda e: e.activation(Mb[nx][:].rearrange("p a b -> p (a b)"), p1[:, 0:256], AF.Copy), [p1], [Mb[nx]])
                    dv(lambda e: e.tensor_copy(Nb[nx][:].rearrange("p a b -> p (a b)"), p2[:, 0:256]), [p2], [Nb[nx]])
                    p3 = nps()
                    for hh in range(2):
                        mmg(p3[:, hh * 128:(hh + 1) * 128], [(Nb[nx][:, hh, :], Tt_[:, hh, :])], [Nb[nx], Tt_], p3)
                    yield
                    dv(lambda e: e.tensor_tensor(Tt_[:].rearrange("p a b -> p (a b)"), p3[:, 0:256], Tt_[:].rearrange("p a b -> p (a b)"), ALU.add), [p3, Tt_], [Tt_])
                    cur = nx
                    yield

            def genB(cp, st, bs):
                gc = half * 8 + cp
                cs = slice(cp * 128, (cp + 1) * 128)
                Tt_, AakT_, ArbT_, ArkT_ = TtS[st], AakS[st], ArbS[st], ArkS[st]
                B_ = BSET[bs]
                am, Xb, Ubf, bm, km, rmk, Hrd = B_["am"], B_["Xb"], B_["Ubf"], B_["bm"], B_["km"], B_["rmk"], B_["Hrd"]
                px = pu = ph = PS[3 + 2 * bs]
                py = PS[4 + 2 * bs]
                for k in range(NBP):
                    dv(lambda e: e.tensor_tensor(am[:], aTt[:, cp, :], rblkrow[:, k, :], ALU.mult), [aTt, rblkrow], [am])
                    po_(lambda e: e.tensor_tensor(rmk[:, k, :], rTt[:, cp, :], rblkrow[:, k, :], ALU.mult), [rTt, rblkrow], [rmk])
                    Hsrc = [(Hbm[hh][:, gc, :], Hbm[hh]) if k == 0 else (Hrd[hh][:, k, :], Hrd[hh]) for hh in range(2)]
                    for hh in range(2):
                        mmg(px[:, hh * 64:(hh + 1) * 64], [(AakT_[:, hh, :], vtk[:, cp * 128 + hh * 64:cp * 128 + (hh + 1) * 64]),
                                                           (am[:], Hsrc[hh][0])], [AakT_, vtk, am, Hsrc[hh][1]], px)
                    yield
                    ac(lambda e: e.activation(Xb[:], px[:, 0:128], AF.Copy), [px], [Xb])
                    for hh in range(2):
                        mmg(pu[:, hh * 64:(hh + 1) * 64], [(Tt_[:, hh, :], Xb[:, hh * 64:(hh + 1) * 64])], [Tt_, Xb], pu)
                    yield
                    if k == 0:
                        dv(lambda e: e.tensor_scalar_mul(Ubf[:], pu[:, 0:128], rblk[:, k:k + 1]), [pu, rblk], [Ubf])
                    else:
                        dv(lambda e: e.scalar_tensor_tensor(Ubf[:], pu[:, 0:128], rblk[:, k:k + 1], Ubf[:], ALU.mult, ALU.add), [pu, rblk, Ubf], [Ubf])
                    ac(lambda e: e.activation(bm[:], btk[:, cs], AF.Copy, scale=rblk[:, k:k + 1]), [btk, rblk], [bm])
                    ac(lambda e: e.activation(km[:], ktk[:, cs], AF.Copy, scale=rblk[:, k:k + 1]), [ktk, rblk], [km])
                    mmg(ph[:, 0:128], [(bm[:], Ubf[:]), (km[:], vtk[:, cs])], [bm, Ubf, km, vtk], ph)
                    yield
                    for hh in range(2):
                        pb = 64 * hh
                        dv(lambda e: e.tensor_tensor(Hst[pb:pb + 64, gc, :], ph[pb:pb + 64, pb:pb + 64], Hst[pb:pb + 64, gc, :], ALU.add), [ph, Hst], [Hst])
                    ac(lambda e: e.activation(Hst[:, gc, :], Hst[:, gc, :], AF.Copy, scale=dL[:, cp, k:k + 1]), [Hst, dL], [Hst])
                    if k < NBP - 1:
                        for hh in range(2):
                            pb = 64 * hh
                            po_(lambda e: e.tensor_copy(Hrd[hh][pb:pb + 64, k + 1, :], Hst[pb:pb + 64, gc, :]), [Hst], [Hrd[hh]])
                    yield
                for hh in range(2):
                    vh = vtk[:, cp * 128 + hh * 64:cp * 128 + (hh + 1) * 64]
                    pairs = [(ArbT_[:, hh, :], Ubf[:, hh * 64:(hh + 1) * 64]), (ArkT_[:, hh, :], vh), (rmk[:, 0, :], Hbm[hh][:, gc, :])]
                    pairs += [(rmk[:, k, :], Hrd[hh][:, k, :]) for k in range(1, NBP)]
                    mmg(py[:, hh * 64:(hh + 1) * 64], pairs, [ArbT_, Ubf, ArkT_, vtk, rmk, Hbm[hh], Hrd[hh]], py)
                yield
                ac(lambda e: e.activation(Y[:, cs], py[:, 0:128], AF.Copy), [py], [Y])
                for hh in range(2):
                    pb = 64 * hh
                    po_(lambda e: e.tensor_copy(Hbm[hh][pb:pb + 64, gc, :], Hst[pb:pb + 64, gc, :]), [Hst], [Hbm[hh]])
                yield

            npsmod[0] = 3
            active = {}
            doneA = set(); doneB = set()
            nextA = 0; nextB = 0
            while len(doneB) < ncp:
                if nextA < ncp and "A" not in active and (nextA < 3 or (nextA - 3) in doneB):
                    active["A"] = (genA(nextA, nextA % 3), nextA); nextA += 1
                if nextB < ncp and nextB in doneA and ("B%d" % (nextB % 2)) not in active:
                    active["B%d" % (nextB % 2)] = (genB(nextB, nextB % 3, nextB % 2), nextB); nextB += 1
                for key in list(active):
                    g, idx = active[key]
                    try:
                        next(g)
                    except StopIteration:
                        del active[key]
                        (doneA if key == "A" else doneB).add(idx)
            npsmod[0] = 4

        def bc16(b_, col):
            return stt_.v(col * 16, [[1, 16], [0, 64]])

        import os
        for i in range(NT):
            s = 1 if i == NTP else 0
            if os.environ.get("K_REC") == "p" and s == 1:
                continue
            if os.environ.get("K_REC") == "s" and s == 0:
                continue
            r0 = i * 128
            nit = 2 if s else 4
            for half in range(2):
                f0 = half * HW
                PRM = dict(PRMH[half]); PRM["r_lnw"] = LNW; PRM["r_lnb"] = LNB
                for idx, b_ in enumerate([Rb, Kb, Vb_, SW, Aa]):
                    cx.dma("sp", b_[:], rz_d[r0:r0 + 128, idx * D + f0:idx * D + f0 + HW], reads=[rz_d], writes=[b_], owner=b_)
                cx.dma("sp", LNW[:], W["r_lnw"][0, f0:f0 + HW].partition_broadcast(128), writes=[LNW], owner=LNW)
                cx.dma("sp", LNB[:], W["r_lnb"][0, f0:f0 + HW].partition_broadcast(128), writes=[LNB], owner=LNB)
                dv(lambda e: e.tensor_scalar_mul(SW[:], SW[:], -0.6065306597126334), [SW], [SW])
                pc = [nps(), nps()]
                for hf in range(2):
                    mmg(pc[hf][:], [(mle[s][:], SW[:, hf * 512:(hf + 1) * 512])], [mle[s], SW], pc[hf])
                for hf in range(2):
                    evac(Cc[:, hf * 512:(hf + 1) * 512], pc[hf][:], [pc[hf]], [Cc])
                pd = nps()
                for cp in range(8):
                    mmg(pd[:, cp * NB:(cp + 1) * NB], [(SW[:, cp * 128:(cp + 1) * 128], refs[:, 17:34] if s else rblk[:])], [SW, refs, rblk], pd)
                ac(lambda e: e.activation(dL[:].rearrange("p a b -> p (a b)"), pd[:, 0:8 * NB], AF.Exp), [pd], [dL])
                dv(lambda e: e.tensor_tensor(KK[:], Kb[:], PRM["r_kk"][:], ALU.mult), [Kb, PRM["r_kk"]], [KK])
                ac(lambda e: e.activation(TMP[:], KK[:], AF.Square), [KK], [TMP])
                dv(lambda e: e.tensor_reduce(stt_[:, 0, :], h3(TMP), AX.X, ALU.add), [TMP], [stt_])
                dv(lambda e: e.tensor_scalar_max(stt_[:, 0, :], stt_[:, 0, :], 1e-24), [stt_], [stt_])
                ac(lambda e: e.activation(stt_[:, 0, :], stt_[:, 0, :], AF.Sqrt), [stt_], [stt_])
                dv(lambda e: e.reciprocal(stt_[:, 0, :], stt_[:, 0, :]), [stt_], [stt_])
                dv(lambda e: e.tensor_tensor(h3(KK), h3(KK), bc16(stt_, 0), ALU.mult), [KK, stt_], [KK])
                dv(lambda e: e.scalar_tensor_tensor(TMP[:], Aa[:], -1.0, PRM["r_ka"][:], ALU.add, ALU.mult), [Aa, PRM["r_ka"]], [TMP])
                dv(lambda e: e.scalar_tensor_tensor(Kb[:], TMP[:], 1.0, Kb[:], ALU.add, ALU.mult), [TMP, Kb], [Kb])
                po_(lambda e: e.tensor_tensor(TMP[:], Rb[:], Kb[:], ALU.mult), [Rb, Kb], [TMP])
                po_(lambda e: e.tensor_tensor(TMP[:], TMP[:], PRM["r_rk"][:], ALU.mult), [TMP, PRM["r_rk"]], [TMP])
                dv(lambda e: e.tensor_reduce(stt_[:, 1, :], h3(TMP), AX.X, ALU.add), [TMP], [stt_])
                dv(lambda e: e.tensor_tensor(Aa[:], KK[:], Aa[:], ALU.mult), [KK, Aa], [Aa])
                ac(lambda e: e.activation(E1[:], Cc[:], AF.Exp), [Cc], [E1])
                ac(lambda e: e.activation(E2[:], Cc[:], AF.Exp, scale=-1.0), [Cc], [E2])
                dv(lambda e: e.tensor_tensor(TMP[:], Cc[:], SW[:], ALU.subtract), [Cc, SW], [TMP])
                ac(lambda e: e.activation(TMP[:], TMP[:], AF.Exp), [TMP], [TMP])
                dv(lambda e: e.tensor_tensor(Rb[:], Rb[:], E1[:], ALU.mult), [Rb, E1], [Rb])
                dv(lambda e: e.scalar_tensor_tensor(KK[:], KK[:], -1.0, TMP[:], ALU.mult, ALU.mult), [KK, TMP], [KK])
                po_(lambda e: e.tensor_tensor(Aa[:], Aa[:], E2[:], ALU.mult), [Aa, E2], [Aa])
                dv(lambda e: e.tensor_tensor(Kb[:], Kb[:], E2[:], ALU.mult), [Kb, E2], [Kb])
                cx.dma("sp", Gg[:], rz_d[r0:r0 + 128, 5 * D + f0:5 * D + f0 + HW], reads=[rz_d], writes=[Gg], owner=Gg)
                ac(lambda e: e.activation(btk[:], Aa[:], AF.Copy), [Aa], [btk])
                ac(lambda e: e.activation(ktk[:], Kb[:], AF.Copy), [Kb], [ktk])
                ac(lambda e: e.activation(vtk[:], Vb_[:], AF.Copy), [Vb_], [vtk])
                transpose_to(aTt, lambda c: aTt[:, c, :], KK, lambda c: KK[:, c * 128:(c + 1) * 128], 8)
                transpose_to(bTt, lambda c: bTt[:, c, :], Aa, lambda c: Aa[:, c * 128:(c + 1) * 128], 8)
                transpose_to(kTt, lambda c: kTt[:, c, :], Kb, lambda c: Kb[:, c * 128:(c + 1) * 128], 8)
                transpose_to(rTt, lambda c: rTt[:, c, :], Rb, lambda c: Rb[:, c * 128:(c + 1) * 128], 8)
                for hh in range(2):
                    pb = 64 * hh
                    ac(lambda e, hh=hh, pb=pb: e.activation(bTm[hh][pb:pb + 64, :, :], bTt[pb:pb + 64, :, :], AF.Copy), [bTt], [bTm[hh]])
                    po_(lambda e, hh=hh, pb=pb: e.tensor_copy(kTm[hh][pb:pb + 64, :, :], kTt[pb:pb + 64, :, :]), [kTt], [kTm[hh]])
                Y = E1
                ncp = int(os.environ.get("K_NCP", "8"))
                if s == 0:
                    prompt_pairs(half, ncp, Y)
                for cp in range(ncp if s == 1 else 0):
                    gc = half * 8 + cp
                    cs = slice(cp * 128, (cp + 1) * 128)

                    def pairmm(p, lT, rT_):
                        for hh in range(2):
                            lb = lT[hh] if isinstance(lT, list) else lT
                            rb = rT_[hh] if isinstance(rT_, list) else rT_
                            mmg(p[:, hh * 128:(hh + 1) * 128], [(lb[:, cp, :], rb[:, cp, :])], [lb, rb], p)

                    def pairev(dst, p, mask):
                        dv(lambda e: e.tensor_tensor(dst[:], p[:, 0:256].rearrange("p (a b) -> p a b", a=2), mask.v(0, [[0, 2], [1, 128]]), ALU.mult), [p, mask], [dst])
                    p = nps(); pairmm(p, aTt, bTm); pairev(Nb[0], p, mgt[s])
                    p = nps(); pairmm(p, bTm, aTt); pairev(Mb[0], p, mlt[s])
                    p = nps(); pairmm(p, kTm, aTt); pairev(AakT, p, mlt[s])
                    p = nps(); pairmm(p, bTm, rTt); pairev(ArbT, p, mle[s])
                    p = nps(); pairmm(p, kTm, rTt); pairev(ArkT, p, mle[s])
                    dv(lambda e: e.tensor_tensor(Tt[:], Mb[0][:], identb.v(0, [[0, 2], [1, 128]]), ALU.add), [Mb[0], identb], [Tt])
                    cur = 0
                    for it in range(nit):
                        nx = 1 - cur
                        p1 = nps(); p2 = nps()
                        for hh in range(2):
                            if it < nit - 1:
                                mmg(p1[:, hh * 128:(hh + 1) * 128], [(Nb[cur][:, hh, :], Mb[cur][:, hh, :])], [Nb[cur], Mb[cur]], p1)
                            mmg(p2[:, hh * 128:(hh + 1) * 128], [(Mb[cur][:, hh, :], Nb[cur][:, hh, :])], [Nb[cur], Mb[cur]], p2)
                        if it < nit - 1:
                            ac(lambda e, p1=p1, nx=nx: e.activation(Mb[nx][:].rearrange("p a b -> p (a b)"), p1[:, 0:256], AF.Copy), [p1], [Mb[nx]])
                        dv(lambda e, p2=p2, nx=nx: e.tensor_copy(Nb[nx][:].rearrange("p a b -> p (a b)"), p2[:, 0:256]), [p2], [Nb[nx]])
                        p3 = nps()
                        for hh in range(2):
                            mmg(p3[:, hh * 128:(hh + 1) * 128], [(Nb[nx][:, hh, :], Tt[:, hh, :])], [Nb[nx], Tt], p3)
                        dv(lambda e, p3=p3: e.tensor_tensor(Tt[:].rearrange("p a b -> p (a b)"), p3[:, 0:256], Tt[:].rearrange("p a b -> p (a b)"), ALU.add), [p3, Tt], [Tt])
                        cur = nx
                    if s == 1:
                        pxs = [PS[4], PS[5]]; pys = [PS[6], PS[7]]
                        for hh in range(2):
                            cx.op("pe", lambda e, hh=hh: e.matmul(pxs[hh][:, 0:64], AakT[:, hh, :], vtk[:, cp * 128 + hh * 64:cp * 128 + (hh + 1) * 64], start=True, stop=False),
                                  reads=[AakT, vtk], writes=[pxs[hh]])
                    if s == 0:
                        for hh in range(2):
                            for k in range(1, NBP):
                                pass
                        for k in range(NBP):
                            dv(lambda e, k=k: e.tensor_tensor(am[:], aTt[:, cp, :], rblkrow[:, k, :], ALU.mult), [aTt, rblkrow], [am])
                            po_(lambda e, k=k: e.tensor_tensor(rmk[:, k, :], rTt[:, cp, :], rblkrow[:, k, :], ALU.mult), [rTt, rblkrow], [rmk])
                            Hsrc = [(Hbm[hh][:, gc, :], Hbm[hh]) if k == 0 else (Hrd[hh][:, k, :], Hrd[hh]) for hh in range(2)]
                            px = accps()
                            for hh in range(2):
                                mmg(px[:, hh * 64:(hh + 1) * 64], [(AakT[:, hh, :], vtk[:, cp * 128 + hh * 64:cp * 128 + (hh + 1) * 64]),
                                                                   (am[:], Hsrc[hh][0])], [AakT, vtk, am, Hsrc[hh][1]], px)
                            dv(lambda e, px=px: e.tensor_copy(Xb[:], px[:, 0:128]), [px], [Xb])
                            pu = nps()
                            for hh in range(2):
                                mmg(pu[:, hh * 64:(hh + 1) * 64], [(Tt[:, hh, :], Xb[:, hh * 64:(hh + 1) * 64])], [Tt, Xb], pu)
                            if k == 0:
                                dv(lambda e, pu=pu, k=k: e.tensor_scalar_mul(Ubf[:], pu[:, 0:128], rblk[:, k:k + 1]), [pu, rblk], [Ubf])
                            else:
                                dv(lambda e, pu=pu, k=k: e.scalar_tensor_tensor(Ubf[:], pu[:, 0:128], rblk[:, k:k + 1], Ubf[:], ALU.mult, ALU.add), [pu, rblk, Ubf], [Ubf])
                            dv(lambda e, k=k, cs=cs: e.tensor_scalar_mul(bm[:], btk[:, cs], rblk[:, k:k + 1]), [btk, rblk], [bm])
                            po_(lambda e, k=k, cs=cs: e.tensor_scalar_mul(km[:], ktk[:, cs], rblk[:, k:k + 1]), [ktk, rblk], [km])
                            ph = nps()
                            mmg(ph[:, 0:128], [(bm[:], Ubf[:]), (km[:], vtk[:, cs])], [bm, Ubf, km, vtk], ph)
                            for hh in range(2):
                                pb = 64 * hh
                                dv(lambda e, pb=pb, ph=ph: e.tensor_tensor(Hst[pb:pb + 64, gc, :], ph[pb:pb + 64, pb:pb + 64], Hst[pb:pb + 64, gc, :], ALU.add), [ph, Hst], [Hst])
                            dv(lambda e, k=k: e.tensor_scalar_mul(Hst[:, gc, :], Hst[:, gc, :], dL[:, cp, k:k + 1]), [Hst, dL], [Hst])
                            if k < NBP - 1:
                                for hh in range(2):
                                    pb = 64 * hh
                                    po_(lambda e, hh=hh, pb=pb, k=k: e.tensor_copy(Hrd[hh][pb:pb + 64, k + 1, :], Hst[pb:pb + 64, gc, :]), [Hst], [Hrd[hh]])
                        py = accps()
                        for hh in range(2):
                            vh = vtk[:, cp * 128 + hh * 64:cp * 128 + (hh + 1) * 64]
                            pairs = [(ArbT[:, hh, :], Ubf[:, hh * 64:(hh + 1) * 64]), (ArkT[:, hh, :], vh), (rmk[:, 0, :], Hbm[hh][:, gc, :])]
                            pairs += [(rmk[:, k, :], Hrd[hh][:, k, :]) for k in range(1, NBP)]
                            mmg(py[:, hh * 64:(hh + 1) * 64], pairs, [ArbT, Ubf, ArkT, vtk, rmk, Hbm[hh], Hrd[hh]], py)
                        dv(lambda e, py=py, cs=cs: e.tensor_copy(Y[:, cs], py[:, 0:128]), [py], [Y])
                        for hh in range(2):
                            pb = 64 * hh
                            po_(lambda e, hh=hh, pb=pb: e.tensor_copy(Hbm[hh][pb:pb + 64, gc, :], Hst[pb:pb + 64, gc, :]), [Hst], [Hbm[hh]])
                    else:
                        hs_list = []
                        for b in range(16):
                            k2 = ssi[0] % 3; ssi[0] += 1
                            cx.dma("sp", Sin[k2][:], rS[b + 1, 2 * gc:2 * gc + 2].rearrange("h i j -> i h j"), reads=[rS], writes=[Sin[k2]], owner=Sin[k2])
                            pt = nps()
                            cx.op("pe", lambda e, pt=pt, k2=k2: e.transpose(pt[:, 0:64], Sin[k2][:].rearrange("p a b -> p (a b)"), ident[0:64, 0:64]), reads=[Sin[k2], ident], writes=[pt])
                            for hh in range(2):
                                pb = 64 * hh
                                dv(lambda e, pt=pt, k2=k2, hh=hh, pb=pb: e.tensor_copy(Hsbm[k2][hh][pb:pb + 64, :], pt[pb:pb + 64, 0:64]), [pt], [Hsbm[k2][hh]])
                            dv(lambda e, b=b: e.tensor_tensor(am[:], aTt[:, cp, :], blkrow[:, b, :], ALU.mult), [aTt, blkrow], [am])
                            for hh in range(2):
                                cx.op("pe", lambda e, hh=hh, k2=k2, b=b: e.matmul(pxs[hh][:, 0:64], am[:], Hsbm[k2][hh][:], start=False, stop=(b == 15)),
                                      reads=[am, Hsbm[k2][hh]], writes=[pxs[hh]])
                        for hh in range(2):
                            dv(lambda e, hh=hh: e.tensor_copy(Xb[:, hh * 64:(hh + 1) * 64], pxs[hh][:, 0:64]), [pxs[hh]], [Xb])
                        pu = nps()
                        for hh in range(2):
                            mmg(pu[:, hh * 64:(hh + 1) * 64], [(Tt[:, hh, :], Xb[:, hh * 64:(hh + 1) * 64])], [Tt, Xb], pu)
                        ac(lambda e, pu=pu: e.activation(Ub[:], pu[:, 0:128], AF.Copy), [pu], [Ub])
                        for hh in range(2):
                            vh = vtk[:, cp * 128 + hh * 64:cp * 128 + (hh + 1) * 64]
                            cx.op("pe", lambda e, hh=hh: e.matmul(pys[hh][:, 0:64], ArbT[:, hh, :], Ub[:, hh * 64:(hh + 1) * 64], start=True, stop=False), reads=[ArbT, Ub], writes=[pys[hh]])
                            cx.op("pe", lambda e, hh=hh, vh=vh: e.matmul(pys[hh][:, 0:64], ArkT[:, hh, :], vh, start=False, stop=False), reads=[ArkT, vtk], writes=[pys[hh]])
                        for b in range(16):
                            k2 = ssi[0] % 3; ssi[0] += 1
                            cx.dma("sp", Sin[k2][:], rS[b + 1, 2 * gc:2 * gc + 2].rearrange("h i j -> i h j"), reads=[rS], writes=[Sin[k2]], owner=Sin[k2])
                            pt = nps()
                            cx.op("pe", lambda e, pt=pt, k2=k2: e.transpose(pt[:, 0:64], Sin[k2][:].rearrange("p a b -> p (a b)"), ident[0:64, 0:64]), reads=[Sin[k2], ident], writes=[pt])
                            for hh in range(2):
                                pb = 64 * hh
                                dv(lambda e, pt=pt, k2=k2, hh=hh, pb=pb: e.tensor_copy(Hsbm[k2][hh][pb:pb + 64, :], pt[pb:pb + 64, 0:64]), [pt], [Hsbm[k2][hh]])
                            ac(lambda e, pt=pt, k2=k2: e.activation(Hs[k2][:], pt[:, 0:64], AF.Copy), [pt], [Hs[k2]])
                            dv(lambda e, b=b: e.tensor_tensor(rm[:], rTt[:, cp, :], blkrow[:, b, :], ALU.mult), [rTt, blkrow], [rm])
                            for hh in range(2):
                                cx.op("pe", lambda e, hh=hh, k2=k2, b=b: e.matmul(pys[hh][:, 0:64], rm[:], Hsbm[k2][hh][:], start=False, stop=(b == 15)),
                                      reads=[rm, Hsbm[k2][hh]], writes=[pys[hh]])
                            dv(lambda e, b=b, cs=cs: e.tensor_scalar_mul(bm[:], btk[:, cs], refs[:, 18 + b:19 + b]), [btk, refs], [bm])
                            po_(lambda e, b=b, cs=cs: e.tensor_scalar_mul(km[:], ktk[:, cs], refs[:, 18 + b:19 + b]), [ktk, refs], [km])
                            ph = nps()
                            mmg(ph[:, 0:128], [(bm[:], Ub[:]), (km[:], vtk[:, cs])], [bm, Ub, km, vtk], ph)
                            for hh in range(2):
                                pb = 64 * hh
                                dv(lambda e, pb=pb, ph=ph, k2=k2: e.tensor_tensor(Hs[k2][pb:pb + 64, :], ph[pb:pb + 64, pb:pb + 64], Hs[k2][pb:pb + 64, :], ALU.add), [ph, Hs[k2]], [Hs[k2]])
                            dv(lambda e, k2=k2, b=b: e.tensor_scalar_mul(Hs[k2][:], Hs[k2][:], dL[:, cp, b + 1:b + 2]), [Hs[k2], dL], [Hs[k2]])
                            pt2 = nps()
                            cx.op("pe", lambda e, pt2=pt2, k2=k2: e.transpose(pt2[0:64, 0:128], Hs[k2][:], ident[:]), reads=[Hs[k2], ident], writes=[pt2])
                            ac(lambda e, pt2=pt2, k2=k2: e.activation(Sout[k2][:].rearrange("p a b -> p (a b)"), pt2[0:64, 0:128], AF.Copy), [pt2], [Sout[k2]])
                            cx.dma("act", orS[b + 1, 2 * gc:2 * gc + 2].rearrange("h i j -> i h j"), Sout[k2][:], reads=[Sout[k2]], writes=[orS], owner=Sout[k2])
                        for hh in range(2):
                            dv(lambda e, hh=hh, cp=cp: e.tensor_copy(Y[:, cp * 128 + hh * 64:cp * 128 + (hh + 1) * 64], pys[hh][:, 0:64]), [pys[hh]], [Y])
                dv(lambda e: e.tensor_reduce(stt_[:, 2, :], h3(Y), AX.X, ALU.add), [Y], [stt_])
                dv(lambda e: e.tensor_scalar_mul(stt_[:, 2, :], stt_[:, 2, :], 1.0 / 64), [stt_], [stt_])
                dv(lambda e: e.tensor_tensor(h3(Y), h3(Y), bc16(stt_, 2), ALU.subtract), [Y, stt_], [Y])
                ac(lambda e: e.activation(TMP[:], Y[:], AF.Square), [Y], [TMP])
                dv(lambda e: e.tensor_reduce(stt_[:, 3, :], h3(TMP), AX.X, ALU.add), [TMP], [stt_])
                dv(lambda e: e.tensor_scalar(stt_[:, 3, :], stt_[:, 3, :], 1.0 / 64, LN_X_EPS, ALU.mult, ALU.add), [stt_], [stt_])
                ac(lambda e: e.activation(stt_[:, 3, :], stt_[:, 3, :], AF.Sqrt), [stt_], [stt_])
                dv(lambda e: e.reciprocal(stt_[:, 3, :], stt_[:, 3, :]), [stt_], [stt_])
                dv(lambda e: e.tensor_tensor(h3(Y), h3(Y), bc16(stt_, 3), ALU.mult), [Y, stt_], [Y])
                dv(lambda e: e.tensor_tensor(Y[:], Y[:], PRM["r_lnw"][:], ALU.mult), [Y, PRM["r_lnw"]], [Y])
                po_(lambda e: e.tensor_tensor(Y[:], Y[:], PRM["r_lnb"][:], ALU.add), [Y, PRM["r_lnb"]], [Y])
                dv(lambda e: e.tensor_tensor(h3(TMP), h3(Vb_), bc16(stt_, 1), ALU.mult), [Vb_, stt_], [TMP])
                po_(lambda e: e.tensor_tensor(Y[:], Y[:], TMP[:], ALU.add), [Y, TMP], [Y])
                dv(lambda e: e.tensor_tensor(Y[:], Y[:], Gg[:], ALU.mult), [Y, Gg], [Y])
                transpose_to(outT, lambda c: outT[:, half * 8 + c, r0:r0 + 128], Y, lambda c: Y[:, c * 128:(c + 1) * 128], 8)
            if s == 0 and i == NTP - 1:
                for gc in range(16):
                    k2 = ssi[0] % 3; ssi[0] += 1
                    pt2 = nps()
                    cx.op("pe", lambda e, pt2=pt2, gc=gc: e.transpose(pt2[0:64, 0:128], Hst[:, gc, :], ident[:]), reads=[Hst, ident], writes=[pt2])
                    ac(lambda e, pt2=pt2, k2=k2: e.activation(Sout[k2][:].rearrange("p a b -> p (a b)"), pt2[0:64, 0:128], AF.Copy), [pt2], [Sout[k2]])
                    cx.dma("act", orS[0, 2 * gc:2 * gc + 2].rearrange("h i j -> i h j"), Sout[k2][:], reads=[Sout[k2]], writes=[orS], owner=Sout[k2])

    stage_mod()
    hT = sb("hT", [128, 16, NTOK], BF16)
    stage_norm(xin, 0, W["norm_mix"][0], 0, 1, hT)
    stage_l0_proj(hT)
    stage_l0_rec(hT)
    proj_resid(hT, lambda c0, n: W["ab_w_out"][0, :, c0:c0 + n], xin, x1_d, 0, 2)
    stage_norm(x1_d, 0, W["norm_ffn"][0], 3, 4, hT)
    stage_ffn(hT, 0, x1_d, x2_d)
    stage_norm(x2_d, 1, W["norm_mix"][1], 0, 1, hT, h_dram=h_d, shift_out=osh)
    import os
    stage_l1_mix(hT)
    if os.environ.get("K_SKIP") != "proj":
        stage_l1_proj(hT)
    if os.environ.get("K_SKIP") not in ("rec", "proj"):
        stage_l1_rec(hT)
    proj_resid(hT, lambda c0, n: W["r_wo"][0, :, c0:c0 + n], x2_d, x3_d, 1, 2)
    stage_norm(x3_d, 1, W["norm_ffn"][1], 3, 4, hT)
    stage_ffn(hT, 1, x3_d, x4_d)
    stage_final(x4_d)
    cx.barrier()
    return nc


def LAYER0(L):
    cx, xin, x1_d, NTOK = L["cx"], L["xin"], L["x1_d"], L["NTOK"]
    cx.dma("sp", x1_d[:], xin[:], reads=[xin], writes=[x1_d], owner=x1_d)
    cx.barrier()


def LAYER1(L):
    cx, x2_d, x3_d = L["cx"], L["x2_d"], L["x3_d"]
    cx.dma("sp", x3_d[:], x2_d[:], reads=[x2_d], writes=[x3_d], owner=x3_d)
    cx.barrier()


def make_consts(NTP):
    c = {}
    s = np.arange(128)
    blk = s // 8
    le_p = (s[:, None] <= s[None, :]).astype(np.float32)
    same = (blk[:, None] == blk[None, :])
    le_s = (le_p * same).astype(np.float32)
    lt_p = (s[:, None] < s[None, :]).astype(np.float32)
    lt_s = (lt_p * same).astype(np.float32)
    c["ident"] = np.eye(128, dtype=np.float32)
    c["ones"] = np.ones((128, 128), np.float32)
    c["mle_p"], c["mle_s"], c["mlt_p"], c["mlt_s"] = le_p, le_s, lt_p, lt_s
    c["mgt_p"], c["mgt_s"] = lt_p.T.copy(), lt_s.T.copy()
    c["neg_p"] = ((le_p.T - 1.0) * 1e30).astype(np.float32)
    c["neg_s"] = ((le_s.T - 1.0) * 1e30).astype(np.float32)
    c["trir_p"] = (le_p - (s[:, None] <= 63).astype(np.float32)).astype(np.float32)
    ref_p = np.zeros((128, 34), np.float32); ref_p[:, 0] = (s <= 63); ref_p[:, 17] = (s > 63)
    ref_s = np.zeros((128, 34), np.float32)
    for b in range(16):
        ref_s[8 * b:8 * b + 8, 17 + b + 1] = 1.0
    c["ref_p"], c["ref_s"] = ref_p, ref_s
    blkt_p = np.zeros((NB, 128), np.float32); blkt_p[0] = 1.0
    blkt_s = np.zeros((NB, 128), np.float32)
    sel_p = np.zeros((128, NB), np.float32); sel_p[127, 0] = 1.0
    sel_s = np.zeros((128, NB), np.float32)
    for b in range(16):
        blkt_s[b + 1, 8 * b:8 * b + 8] = 1.0
        sel_s[8 * b + 7, b + 1] = 1.0
    c["blkt_p"], c["blkt_s"], c["selend_p"], c["selend_s"] = blkt_p, blkt_s, sel_p, sel_s
    NBP = 4; LC = 128 // NBP
    pb_ = s // LC
    samep = (pb_[:, None] == pb_[None, :])
    c["rle_p"] = (le_p * samep).astype(np.float32); c["rlt_p"] = (lt_p * samep).astype(np.float32)
    c["rgt_p"] = c["rlt_p"].T.copy()
    rb = np.zeros((128, NB), np.float32)
    rbr = np.zeros((128, NBP, 128), np.float32)
    for k in range(NBP):
        rb[k * LC:(k + 1) * LC, k] = 1.0
        rbr[:, k, k * LC:(k + 1) * LC] = 1.0
    c["rblk_p"] = rb; c["rblkrow_p"] = rbr.reshape(128, NBP * 128)
    br = np.zeros((128, 16, 128), np.float32)
    for b in range(16):
        br[:, b, 8 * b:8 * b + 8] = 1.0
    c["blkrow"] = br.reshape(128, 16 * 128)
    return {"c_" + k: np.ascontiguousarray(v) for k, v in c.items()}


WNAMES = ["mod_w", "mod_b", "norm_mix", "norm_ffn", "ffn_w1", "ffn_w2", "final_norm", "ab_w_in", "ab_gate_b", "m_conv_w",
          "m_norm", "g_lb", "g_norm", "ab_w_out", "r_mu", "r_w0", "r_w1", "r_w2", "r_a0", "r_a1", "r_a2", "r_g1", "r_g2",
          "r_kk", "r_ka", "r_rk", "r_wr", "r_wk", "r_wv", "r_wo", "r_lnw", "r_lnb"]
_NC_CACHE = {}


def core_inputs(inp, core, NTP, consts):
    b = core // 2
    T = NTP * 128
    f = lambda a: np.ascontiguousarray(np.asarray(a, dtype=np.float32))
    sl = slice(16 * core, 16 * core + 16)
    m = {}
    m["xin"] = f(np.concatenate([inp["x_prompt"][b, :T], inp["x_sample"][sl].reshape(128, D)], axis=0))
    m["cin"] = f(np.concatenate([inp["c_prompt"][b:b + 1], inp["c_sample"][sl]], axis=0))

    def st(a):
        a = np.asarray(a)[0, sl]
        return f(np.concatenate([np.zeros((1,) + a.shape[1:], np.float32), a], axis=0))
    m["mC"] = st(inp["state_mlstm_C"]); m["mn"] = st(inp["state_mlstm_n"]); m["mm"] = st(inp["state_mlstm_m"])
    m["mconv"] = st(inp["state_mlstm_conv"]); m["gS"] = st(inp["state_hgrn_S"]); m["rS"] = st(inp["state_rwkv_S"])
    m["rsh"] = st(inp["state_rwkv_shift"])
    for n in WNAMES:
        a = f(inp[n])
        if n == "r_rk":
            a = a.reshape(1, D)
        m[n] = a
    m.update(consts)
    return m


def kernel(**inp):
    NTP = 16
    n = 8
    if NTP not in _NC_CACHE:
        _NC_CACHE[NTP] = build_nc(NTP)
    nc = _NC_CACHE[NTP]
    consts = make_consts(NTP)
    in_maps = [core_inputs(inp, c, NTP, consts) for c in range(n)]
    res = run_bass_kernel_spmd(nc, in_maps, core_ids=list(range(n)))
    R = res.results
    T = NTP * 128
    y_prompt = np.stack([R[2 * b]["y"][:T] for b in range(4)], axis=0)
    y_sample = np.concatenate([R[c]["y"][T:].reshape(16, 8, D) for c in range(n)], axis=0)

    def pst(k):
        return np.stack([R[2 * b][k][0] for b in range(4)], axis=0)[None]

    def sst(k):
        return np.concatenate([R[c][k][1:] for c in range(n)], axis=0)[None]
    keys = ["oC", "on", "om", "oconv", "oS", "orS", "osh"]
    outs = [y_prompt, y_sample] + [pst(k) for k in keys] + [sst(k) for k in keys]
    return tuple(np.ascontiguousarray(o, dtype=np.float32) for o in outs)
```

```python
import contextlib
import numpy as np
import concourse.bass as bass
import concourse.mybir as mybir
from concourse.bass_utils import run_bass_kernel_spmd

F32 = mybir.dt.float32
BF16 = mybir.dt.bfloat16
AF = mybir.ActivationFunctionType
ALU = mybir.AluOpType
AX = mybir.AxisListType

D = 2048
NB = 17
RMS_EPS = 1e-6
LN_X_EPS = 64e-5


class Res:
    __slots__ = ("name", "lw", "rd", "dsem")

    def __init__(self, name):
        self.name = name
        self.lw = None
        self.rd = {}
        self.dsem = {}


class Ctx:
    def __init__(self, nc):
        self.nc = nc
        self.eng = {"pe": nc.tensor, "act": nc.scalar, "dve": nc.vector, "pool": nc.gpsimd, "sp": nc.sync}
        self.sems = {}
        self.tot = {}
        self.isdma = {}
        self.seen = {e: {} for e in self.eng}
        for e in ("pe", "act", "dve", "pool"):
            self._newsem("E_" + e, False)
        self.free_dma = {"hw": [], "sw": []}
        self.ndma = 0

    def _newsem(self, key, isdma):
        self.sems[key] = self.nc.alloc_semaphore(name=key)
        self.tot[key] = 0
        self.isdma[key] = isdma
        return key

    def _dma_sem_for(self, res, q):
        kind = "sw" if q == "pool" else "hw"
        if kind not in res.dsem:
            if self.free_dma[kind]:
                res.dsem[kind] = self.free_dma[kind].pop()
            else:
                self.ndma += 1
                res.dsem[kind] = self._newsem("D%s%d" % (kind, self.ndma), True)
        return res.dsem[kind]

    def release(self, bufs):
        for b in bufs:
            r = b.r
            for kind, key in r.dsem.items():
                self.free_dma[kind].append(key)
            r.dsem = {}
            r.lw = None
            r.rd = {}

    def _need(self, e, deps):
        eng = self.eng[e]
        seen = self.seen[e]
        for key, val in deps:
            if self.isdma[key]:
                val = self.tot[key]
            elif key == "E_pe" and e == "pe":
                continue
            if seen.get(key, 0) >= val:
                continue
            eng.wait_ge(self.sems[key], val)
            seen[key] = val

    @staticmethod
    def _deps(reads, writes):
        deps = []
        for r in reads:
            if r.lw is not None:
                deps.append(r.lw)
        for w in writes:
            if w.lw is not None:
                deps.append(w.lw)
            deps.extend(w.rd.items())
        return deps

    @staticmethod
    def _commit(key, val, reads, writes):
        for w in writes:
            w.lw = (key, val)
            w.rd = {}
        for r in reads:
            if r in writes:
                continue
            if r.rd.get(key, 0) < val:
                r.rd[key] = val

    def op(self, e, fn, reads=(), writes=()):
        reads = [b.r for b in reads]
        writes = [b.r for b in writes]
        self._need(e, self._deps(reads, writes))
        inst = fn(self.eng[e])
        key = "E_" + e
        self.tot[key] += 1
        inst.then_inc(self.sems[key], 1)
        self._commit(key, self.tot[key], reads, writes)
        return inst

    def dma(self, q, out, in_, reads=(), writes=(), owner=None, **kw):
        reads = [b.r for b in reads]
        writes = [b.r for b in writes]
        self._need(q, self._deps(reads, writes))
        key = self._dma_sem_for(owner.r, q)
        inst = self.eng[q].dma_start(out=out, in_=in_, **kw)
        self.tot[key] += 16
        inst.then_inc(self.sems[key], 16)
        self._commit(key, self.tot[key], reads, writes)
        return inst

    def barrier(self):
        for e in self.eng:
            self._need(e, [(k, v) for k, v in self.tot.items() if v > 0])


class Buf:
    def __init__(self, t, name, shape):
        self.t = t
        self.r = Res(name)
        self.shape = list(shape)
        self.ps = int(np.prod(shape[1:]))

    def __getitem__(self, idx):
        return self.t[idx]

    def v(self, off, dims, p0=0, np_=128):
        return bass.AP(self.t, p0 * self.ps + off, [[self.ps, np_]] + [list(d) for d in dims])


class View:
    def __init__(self, parent, c0, n):
        self.parent = parent
        self.c0 = c0
        self.n = n
        self.r = parent.r

    def __getitem__(self, idx):
        if isinstance(idx, slice):
            assert idx == slice(None)
            return self.parent[:, self.c0:self.c0 + self.n]
        p, c = idx
        a = 0 if c.start is None else c.start
        b = self.n if c.stop is None else c.stop
        return self.parent[p, self.c0 + a:self.c0 + b]


def build_nc(NTP):
    NT = NTP + 1
    NP = NTP * 128
    NTOK = NT * 128
    nc = bass.Bass("TRN2", target_bir_lowering=False)
    cx = Ctx(nc)
    DT = {}

    def din(name, shape, dt=F32):
        b = Buf(nc.dram_tensor(name, list(shape), dt, kind="ExternalInput"), name, shape)
        DT[name] = b
        return b

    def dout(name, shape):
        b = Buf(nc.dram_tensor(name, list(shape), F32, kind="ExternalOutput"), name, shape)
        DT[name] = b
        return b

    def dscr(name, shape, dt=F32):
        return Buf(nc.dram_tensor(name, list(shape), dt), name, shape)

    xin = din("xin", [NTOK, D]); cin = din("cin", [NB, D])
    mC = din("mC", [NB, 4, 256, 256]); mn = din("mn", [NB, 4, 256]); mm = din("mm", [NB, 4])
    mconv = din("mconv", [NB, 3, D]); gS = din("gS", [NB, 8, 128, 128]); rS = din("rS", [NB, 32, 64, 64])
    rsh = din("rsh", [NB, D])
    W = {}
    for name, shape in [("mod_w", [2, D, 6 * D]), ("mod_b", [2, 6 * D]), ("norm_mix", [2, D]), ("norm_ffn", [2, D]),
                        ("ffn_w1", [2, D, 4 * D]), ("ffn_w2", [2, 4 * D, D]), ("final_norm", [D]),
                        ("ab_w_in", [1, D, 8200]), ("ab_gate_b", [1, 8]), ("m_conv_w", [1, 4, D]), ("m_norm", [1, 1024]),
                        ("g_lb", [2, 1024]), ("g_norm", [1, 1024]), ("ab_w_out", [1, D, D]),
                        ("r_mu", [1, 6, D]), ("r_w0", [1, D]), ("r_w1", [1, D, 96]), ("r_w2", [1, 96, D]),
                        ("r_a0", [1, D]), ("r_a1", [1, D, 96]), ("r_a2", [1, 96, D]), ("r_g1", [1, D, 256]),
                        ("r_g2", [1, 256, D]), ("r_kk", [1, D]), ("r_ka", [1, D]), ("r_rk", [1, D]),
                        ("r_wr", [1, D, D]), ("r_wk", [1, D, D]), ("r_wv", [1, D, D]), ("r_wo", [1, D, D]),
                        ("r_lnw", [1, D]), ("r_lnb", [1, D])]:
        W[name] = din(name, shape)
    CN = {}
    for name, shape in [("ident", [128, 128]), ("ones", [128, 128]), ("mle_p", [128, 128]), ("mle_s", [128, 128]),
                        ("mlt_p", [128, 128]), ("mlt_s", [128, 128]), ("mgt_p", [128, 128]), ("mgt_s", [128, 128]),
                        ("neg_p", [128, 128]), ("neg_s", [128, 128]), ("trir_p", [128, 128]),
                        ("ref_p", [128, 34]), ("ref_s", [128, 34]), ("blkt_p", [NB, 128]), ("blkt_s", [NB, 128]),
                        ("selend_p", [128, NB]), ("selend_s", [128, NB]), ("blkrow", [128, 16 * 128]),
                        ("rle_p", [128, 128]), ("rlt_p", [128, 128]), ("rgt_p", [128, 128]), ("rblk_p", [128, NB]),
                        ("rblkrow_p", [128, 4 * 128])]:
        CN[name] = din("c_" + name, shape)
    y = dout("y", [NTOK, D])
    oC = dout("oC", [NB, 4, 256, 256]); on = dout("on", [NB, 4, 256]); om = dout("om", [NB, 4])
    oconv = dout("oconv", [NB, 3, D]); oS = dout("oS", [NB, 8, 128, 128]); orS = dout("orS", [NB, 32, 64, 64])
    osh = dout("osh", [NB, D])
    mod_d = dscr("mod_d", [2, NB, 6 * D])
    ext_p = dscr("ext_p", [NP + 3, D]); ext_s = dscr("ext_s", [16, 11, D])
    z_d = dscr("z_d", [NTOK, 6152])
    x1_d = dscr("x1_d", [NTOK, D]); x2_d = dscr("x2_d", [NTOK, D]); x3_d = dscr("x3_d", [NTOK, D]); x4_d = dscr("x4_d", [NTOK, D])
    h_d = dscr("h_d", [NTOK, D])
    rz_d = dscr("rz_d", [NTOK, 6 * D])
    dummy = Buf(None, "dummy", [1, 1])

    stacks = [contextlib.ExitStack()]

    uid = [0]

    stage_bufs = [[]]

    def sb(name, shape, dt=F32):
        uid[0] += 1
        name = "%s_%d" % (name, uid[0])
        b = Buf(stacks[-1].enter_context(nc.sbuf_tensor(name, list(shape), dt)), name, shape)
        stage_bufs[-1].append(b)
        return b

    def staged(fn):
        def wrapper(*a, **k):
            stacks.append(contextlib.ExitStack())
            stage_bufs.append([])
            try:
                return fn(*a, **k)
            finally:
                cx.barrier()
                cx.release(stage_bufs.pop())
                stacks.pop().close()
                npsmod[0] = 8
        return wrapper

    ident = sb("ident", [128, 128]); ones = sb("ones", [128, 128])
    identb = sb("identb", [128, 128], BF16)
    PS = [Buf(nc.alloc_psum_tensor("ps%d" % i, [128, 512], F32), "ps%d" % i, [128, 512]) for i in range(8)]
    psi = [0]

    def nps():
        p = PS[psi[0] % npsmod[0]]
        psi[0] += 1
        return p

    npsmod = [8]

    acci = [0]

    def accps():
        p = PS[6 + acci[0] % 2]
        acci[0] += 1
        return p

    cx.dma("sp", ident[:], CN["ident"][:], writes=[ident], owner=ident)
    cx.dma("sp", ones[:], CN["ones"][:], writes=[ones], owner=ones)
    cx.op("dve", lambda e: e.tensor_copy(identb[:], ident[:]), reads=[ident], writes=[identb])

    evq = [0]

    def evac(out_ap, in_ap, reads, writes, scale=None):
        evq[0] += 1
        if evq[0] % 2 == 0:
            if scale is None:
                cx.op("act", lambda e: e.activation(out_ap, in_ap, AF.Copy), reads=reads, writes=writes)
            else:
                cx.op("act", lambda e: e.activation(out_ap, in_ap, AF.Copy, scale=float(scale)), reads=reads, writes=writes)
        else:
            if scale is None:
                cx.op("dve", lambda e: e.tensor_copy(out_ap, in_ap), reads=reads, writes=writes)
            else:
                cx.op("dve", lambda e: e.tensor_scalar(out_ap, in_ap, float(scale), None, ALU.mult), reads=reads, writes=writes)

    def transpose_to(dst, dst_ap_fn, src, src_ap_fn, nchunks, scale_fn=None, npart=128):
        for c0 in range(0, nchunks, 4):
            n = min(4, nchunks - c0)
            p = nps()
            for j in range(n):
                cx.op("pe", lambda e, j=j: e.transpose(p.v(j * 128, [[1, npart]]),
                                                      src_ap_fn(c0 + j), ident[0:npart, 0:npart]),
                      reads=[src, ident], writes=[p])
            for j in range(n):
                sc = None if scale_fn is None else scale_fn(c0 + j)
                evac(dst_ap_fn(c0 + j), p.v(j * 128, [[1, npart]]), [p], [dst], scale=sc)

    def rows_bcast(dst, src_dram_rows_fn, tile_is_sample, width, col0=0, q="sp"):
        if not tile_is_sample:
            cx.dma(q, dst[:, col0:col0 + width], src_dram_rows_fn(0).partition_broadcast(128), writes=[dst], owner=dst)
        else:
            a1 = src_dram_rows_fn(1); a2 = src_dram_rows_fn(2)
            src3 = bass.AP(a1.tensor, a1.offset, [[a2.offset - a1.offset, 16], [0, 8], [1, width]])
            cx.dma(q, dst[:, col0:col0 + width], src3, writes=[dst], owner=dst)

    @staged
    def stage_mod():
        csb = sb("csb", [NB, D]); scT = sb("scT", [128, 16, NB], BF16)
        modsb = sb("modsb", [NB, 6 * D]); biasb = sb("biasb", [NB, 6 * D])
        wb = [sb("modw%d" % i, [128, 16, 512], BF16) for i in range(2)]
        cx.dma("sp", csb[:], cin[:], writes=[csb], owner=csb)
        cx.op("act", lambda e: e.activation(csb[:], csb[:], AF.Silu), reads=[csb], writes=[csb])
        for c0 in range(0, 16, 4):
            p = nps()
            for j in range(4):
                c = c0 + j
                cx.op("pe", lambda e, j=j, c=c: e.transpose(p.v(j * 32, [[1, NB]]), csb[:, c * 128:(c + 1) * 128], ident[0:NB, 0:NB]),
                      reads=[csb, ident], writes=[p])
            for j in range(4):
                evac(scT[:, c0 + j, :], p.v(j * 32, [[1, NB]]), [p], [scT])
        for l in range(2):
            cx.dma("sp", biasb[:], W["mod_b"][l].partition_broadcast(NB), writes=[biasb], owner=biasb)
            for cb in range(24):
                w = wb[cb % 2]
                cx.dma("pool", w[:], W["mod_w"][l, :, cb * 512:(cb + 1) * 512].rearrange("(kc p) n -> p kc n", p=128),
                       writes=[w], owner=w)
                p = nps()
                for kc in range(16):
                    cx.op("pe", lambda e, kc=kc: e.matmul(p[0:NB, :], scT[:, kc, :], w[:, kc, :], start=(kc == 0), stop=(kc == 15)),
                          reads=[scT, w], writes=[p])
                cx.op("dve", lambda e: e.tensor_tensor(modsb[:, cb * 512:(cb + 1) * 512], p[0:NB, :], biasb[:, cb * 512:(cb + 1) * 512], ALU.add),
                      reads=[p, biasb], writes=[modsb])
            cx.dma("sp", mod_d[l], modsb[:], reads=[modsb], writes=[mod_d], owner=modsb)
        cx.barrier()
        cx.release([csb, scT, modsb, biasb] + wb)
        return [csb, scT, modsb, biasb] + wb

    @staged
    def stage_norm(x_d, l, normw_ap, ish, isc, hT, h_dram=None, shift_out=None):
        G = [sb("nG%d" % i, [128, D]) for i in range(2)]; SH = [sb("nSH%d" % i, [128, D]) for i in range(2)]
        nw = sb("nnw", [128, D])
        xt = [sb("nxt%d" % i, [128, D]) for i in range(2)]; ht = [sb("nht%d" % i, [128, D]) for i in range(2)]
        junk = sb("njunk", [128, D]); ss = sb("nss", [128, 2])
        cx.dma("sp", nw[:], normw_ap.partition_broadcast(128), writes=[nw], owner=nw)
        for s in range(2):
            rows_bcast(G[s], lambda b: mod_d[l, b, isc * D:(isc + 1) * D], s == 1, D)
            rows_bcast(SH[s], lambda b: mod_d[l, b, ish * D:(ish + 1) * D], s == 1, D)
            cx.op("dve", lambda e, s=s: e.scalar_tensor_tensor(G[s][:], G[s][:], 1.0, nw[:], ALU.add, ALU.mult), reads=[G[s], nw], writes=[G[s]])
        for i in range(NT):
            s = 1 if i == NTP else 0
            x = xt[i % 2]; h = ht[i % 2]
            cx.dma("sp", x[:], x_d[i * 128:(i + 1) * 128, :], reads=[x_d], writes=[x], owner=x)
            cx.op("act", lambda e: e.activation(junk[:], x[:], AF.Square, accum_out=ss[:, 0:1]), reads=[x], writes=[junk, ss])
            cx.op("dve", lambda e: e.tensor_scalar(ss[:, 1:2], ss[:, 0:1], 1.0 / D, RMS_EPS, ALU.mult, ALU.add), reads=[ss], writes=[ss])
            cx.op("act", lambda e: e.activation(ss[:, 1:2], ss[:, 1:2], AF.Sqrt), reads=[ss], writes=[ss])
            cx.op("dve", lambda e: e.reciprocal(ss[:, 1:2], ss[:, 1:2]), reads=[ss], writes=[ss])
            cx.op("dve", lambda e: e.scalar_tensor_tensor(h[:], x[:], ss[:, 1:2], G[s][:], ALU.mult, ALU.mult), reads=[x, ss, G[s]], writes=[h])
            if SH is not None:
                cx.op("dve", lambda e: e.tensor_tensor(h[:], h[:], SH[s][:], ALU.add), reads=[h, SH[s]], writes=[h])
            if h_dram is not None:
                cx.dma("sp", h_dram[i * 128:(i + 1) * 128, :], h[:], reads=[h], writes=[h_dram], owner=h)
            if shift_out is not None:
                if s == 0 and i == NTP - 1:
                    cx.dma("sp", shift_out[0:1, :], h[127:128, :], reads=[h], writes=[shift_out], owner=h)
                if s == 1:
                    for b in range(16):
                        cx.dma("sp", shift_out[b + 1:b + 2, :], h[8 * b + 7:8 * b + 8, :], reads=[h], writes=[shift_out], owner=h)
            transpose_to(hT, lambda c: hT[:, c, i * 128:(i + 1) * 128], h, lambda c: h[:, c * 128:(c + 1) * 128], 16)
        cx.barrier()
        tmp = G + SH + [nw, junk, ss] + xt + ht
        cx.release(tmp)

    @staged
    def proj_tok(aT, blocks, K=16, kpart=128, bias_fn=None, sigmoid=False):
        wb = [sb("pw%d" % i, [128, K, 512], BF16) for i in range(2)]
        ob = [sb("po%d" % i, [128, 512]) for i in range(3)]
        bb_ = [sb("pb%d" % i, [128, 512]) for i in range(2)] if bias_fn is not None else None
        oi = 0
        for bi, (w_ap, dst_fn) in enumerate(blocks):
            n = w_ap.shape[-1]
            w = wb[bi % 2]
            wv = w_ap.rearrange("(kc p) n -> p kc n", p=kpart)
            for k0 in range(0, K, 4):
                k1 = min(K, k0 + 4)
                cx.dma("pool", w[0:kpart, k0:k1, 0:n], wv[:, k0:k1, :], writes=[w], owner=w)
            if bias_fn is not None:
                bt = bb_[bi % 2]
                cx.dma("sp", bt[:, 0:n], bias_fn(bi).partition_broadcast(128), writes=[bt], owner=bt)
            for i in range(NT):
                p = nps()
                for kc in range(K):
                    cx.op("pe", lambda e, kc=kc: e.matmul(p[:, 0:n], aT[0:kpart, kc, i * 128:(i + 1) * 128], w[0:kpart, kc, 0:n],
                                                          start=(kc == 0), stop=(kc == K - 1)), reads=[aT, w], writes=[p])
                o = ob[oi % 3]; oi += 1
                if bias_fn is None:
                    cx.op("dve", lambda e: e.tensor_copy(o[:, 0:n], p[:, 0:n]), reads=[p], writes=[o])
                else:
                    cx.op("dve", lambda e: e.tensor_tensor(o[:, 0:n], p[:, 0:n], bt[:, 0:n], ALU.add), reads=[p, bt], writes=[o])
                    if sigmoid:
                        cx.op("act", lambda e: e.activation(o[:, 0:n], o[:, 0:n], AF.Sigmoid), reads=[o], writes=[o])
                for (dbuf, dap, p0, p1) in dst_fn(i):
                    if p0 == "3d":
                        cx.dma("act", dap, o[:, 0:n], reads=[o], writes=[dbuf], owner=o)
                    else:
                        cx.dma("act", dap, o[p0:p1, 0:n], reads=[o], writes=[dbuf], owner=o)

    @staged
    def proj_resid(aT, w_fn, x_old, x_new, l, igate, K=16):
        wb = [sb("rw%d" % i, [128, K, 512], BF16) for i in range(2)]
        xb = [sb("rx%d" % i, [128, 512]) for i in range(3)]
        GT = [sb("rgt%d" % i, [128, D]) for i in range(2)]
        for s in range(2):
            rows_bcast(GT[s], lambda b: mod_d[l, b, igate * D:(igate + 1) * D], s == 1, D)
        oi = 0
        for cb in range(4):
            w = wb[cb % 2]
            cx.dma("pool", w[:], w_fn(cb * 512, 512).rearrange("(kc p) n -> p kc n", p=128), writes=[w], owner=w)
            for i in range(NT):
                s = 1 if i == NTP else 0
                xo = xb[oi % 3]; oi += 1
                cx.dma("sp", xo[:], x_old[i * 128:(i + 1) * 128, cb * 512:(cb + 1) * 512], reads=[x_old], writes=[xo], owner=xo)
                p = nps()
                for kc in range(K):
                    cx.op("pe", lambda e, kc=kc: e.matmul(p[:], aT[:, kc, i * 128:(i + 1) * 128], w[:, kc, :], start=(kc == 0), stop=(kc == K - 1)),
                          reads=[aT, w], writes=[p])
                t = sbtmp512[oi % 2]
                cx.op("dve", lambda e: e.tensor_tensor(t[:], p[:], GT[s][:, cb * 512:(cb + 1) * 512], ALU.mult), reads=[p, GT[s]], writes=[t])
                cx.op("dve", lambda e: e.tensor_tensor(xo[:], xo[:], t[:], ALU.add), reads=[xo, t], writes=[xo])
                cx.dma("act", x_new[i * 128:(i + 1) * 128, cb * 512:(cb + 1) * 512], xo[:], reads=[xo], writes=[x_new], owner=xo)
        cx.barrier()
        cx.release(wb + xb + GT)

    sbtmp512 = [sb("tmp512_%d" % i, [128, 512]) for i in range(2)]

    @staged
    def stage_ffn(hT, l, x_old, x_new):
        FB = 2048
        nfb = 4 * D // FB
        KC = FB // 128
        hidT = sb("hidT", [128, KC, NTOK], BF16)
        w1b = [sb("fw1_%d" % i, [128, 16, 256], BF16) for i in range(2)]
        w2b = [sb("fw2_%d" % i, [128, KC, 512], BF16) for i in range(2)]
        xb = [sb("fx%d" % i, [128, 512]) for i in range(4)]
        GTs = [[sb("fgt%d_%d" % (i, j), [128, 512]) for j in range(2)] for i in range(2)]
        groups = [(g, min(512, NTOK - g)) for g in range(0, NTOK, 512)]
        w1i = 0; w2i = 0; oi = 0
        for fb in range(nfb):
            for blk in range(FB // 256):
                w = w1b[w1i % 2]; w1i += 1
                c0 = fb * FB + blk * 256
                wv = W["ffn_w1"][l, :, c0:c0 + 256].rearrange("(kc p) n -> p kc n", p=128)
                for k0 in range(0, 16, 8):
                    cx.dma("pool", w[:, k0:k0 + 8, :], wv[:, k0:k0 + 8, :], writes=[w], owner=w)
                for oc in range(2):
                    for (g0, gn) in groups:
                        p = nps()
                        for kc in range(16):
                            cx.op("pe", lambda e, kc=kc: e.matmul(p[:, 0:gn], w[:, kc, oc * 128:(oc + 1) * 128], hT[:, kc, g0:g0 + gn],
                                                                  start=(kc == 0), stop=(kc == 15)), reads=[hT, w], writes=[p])
                        t = sbtmp512[oi % 2]; oi += 1
                        cx.op("act", lambda e: e.activation(t[:, 0:gn], p[:, 0:gn], AF.Relu), reads=[p], writes=[t])
                        cx.op("dve", lambda e: e.tensor_tensor(hidT[:, blk * 2 + oc, g0:g0 + gn], t[:, 0:gn], t[:, 0:gn], ALU.mult),
                              reads=[t], writes=[hidT])
            for cb in range(4):
                w = w2b[w2i % 2]; w2i += 1
                wv = W["ffn_w2"][l, fb * FB:(fb + 1) * FB, cb * 512:(cb + 1) * 512].rearrange("(kc p) n -> p kc n", p=128)
                for k0 in range(0, KC, 4):
                    cx.dma("pool", w[:, k0:k0 + 4, :], wv[:, k0:k0 + 4, :], writes=[w], owner=w)
                GT = GTs[(fb * 4 + cb) % 2]
                for s_ in range(2):
                    rows_bcast(GT[s_], lambda b: mod_d[l, b, 5 * D + cb * 512:5 * D + (cb + 1) * 512], s_ == 1, 512)
                for i in range(NT):
                    s_ = 1 if i == NTP else 0
                    xo = xb[oi % 4]; oi += 1
                    src = x_old if fb == 0 else x_new
                    cx.dma("sp", xo[:], src[i * 128:(i + 1) * 128, cb * 512:(cb + 1) * 512], reads=[src], writes=[xo], owner=xo)
                    p = nps()
                    for kc in range(KC):
                        cx.op("pe", lambda e, kc=kc: e.matmul(p[:], hidT[:, kc, i * 128:(i + 1) * 128], w[:, kc, :], start=(kc == 0), stop=(kc == KC - 1)),
                              reads=[hidT, w], writes=[p])
                    t = sbtmp512[oi % 2]
                    cx.op("dve", lambda e: e.tensor_tensor(t[:], p[:], GT[s_][:], ALU.mult), reads=[p, GT[s_]], writes=[t])
                    cx.op("dve", lambda e: e.tensor_tensor(xo[:], xo[:], t[:], ALU.add), reads=[xo, t], writes=[xo])
                    cx.dma("act", x_new[i * 128:(i + 1) * 128, cb * 512:(cb + 1) * 512], xo[:], reads=[xo], writes=[x_new], owner=xo)

    @staged
    def stage_final(x_d):
        nw = sb("fnw", [128, D]); xt = [sb("fnx%d" % i, [128, D]) for i in range(2)]
        junk = sb("fnj", [128, D]); ss = sb("fns", [128, 2])
        cx.dma("sp", nw[:], W["final_norm"][:].partition_broadcast(128), writes=[nw], owner=nw)
        for i in range(NT):
            x = xt[i % 2]
            cx.dma("sp", x[:], x_d[i * 128:(i + 1) * 128, :], reads=[x_d], writes=[x], owner=x)
            cx.op("act", lambda e: e.activation(junk[:], x[:], AF.Square, accum_out=ss[:, 0:1]), reads=[x], writes=[junk, ss])
            cx.op("dve", lambda e: e.tensor_scalar(ss[:, 1:2], ss[:, 0:1], 1.0 / D, RMS_EPS, ALU.mult, ALU.add), reads=[ss], writes=[ss])
            cx.op("act", lambda e: e.activation(ss[:, 1:2], ss[:, 1:2], AF.Sqrt), reads=[ss], writes=[ss])
            cx.op("dve", lambda e: e.reciprocal(ss[:, 1:2], ss[:, 1:2]), reads=[ss], writes=[ss])
            cx.op("dve", lambda e: e.scalar_tensor_tensor(x[:], x[:], ss[:, 1:2], nw[:], ALU.mult, ALU.mult), reads=[x, ss, nw], writes=[x])
            cx.dma("sp", y[i * 128:(i + 1) * 128, :], x[:], reads=[x], writes=[y], owner=x)
        cx.barrier()
        cx.release([nw, junk, ss] + xt)

    def stage_l0_proj(hT):
        Wi = W["ab_w_in"][0]
        cx.dma("sp", ext_p[0:3, :], mconv[0], reads=[mconv], writes=[ext_p], owner=ext_p)
        cx.dma("sp", ext_s[:, 0:3, :], mconv[1:17], reads=[mconv], writes=[ext_s], owner=ext_s)
        blocks = []

        def dst_ext(c0, n):
            def f(i):
                if i < NTP:
                    return [(ext_p, ext_p[3 + i * 128:3 + (i + 1) * 128, c0:c0 + n], 0, 128)]
                return [(ext_s, ext_s[:, 3:11, c0:c0 + n], "3d", n)]
            return f

        def dst_z(c0, n):
            return lambda i: [(z_d, z_d[i * 128:(i + 1) * 128, c0:c0 + n], 0, 128)]
        for c0 in range(0, 2048, 512):
            blocks.append((Wi[:, c0:c0 + 512], dst_ext(c0, 512)))
        for c0 in range(0, 2048, 512):
            blocks.append((Wi[:, 2048 + c0:2048 + c0 + 512], dst_z(c0, 512)))
        blocks.append((Wi[:, 4096:4104], dst_z(2048, 8)))
        for c0 in range(0, 4096, 512):
            blocks.append((Wi[:, 4104 + c0:4104 + c0 + 512], dst_z(2056 + c0, 512)))
        proj_tok(hT, blocks)

    def mmg(p_ap, pairs, reads, pbuf):
        n = len(pairs)
        for idx, (l_ap, r_ap) in enumerate(pairs):
            cx.op("pe", lambda e, l_ap=l_ap, r_ap=r_ap, idx=idx: e.matmul(p_ap, l_ap, r_ap, start=(idx == 0), stop=(idx == n - 1)),
                  reads=reads, writes=[pbuf])

    def rstd_col(dst, col, src_ap, inv_n, eps):
        cx.op("dve", lambda e: e.tensor_scalar(dst[:, col:col + 1], src_ap, float(inv_n), float(eps), ALU.mult, ALU.add), reads=[dst], writes=[dst])
        cx.op("act", lambda e: e.activation(dst[:, col:col + 1], dst[:, col:col + 1], AF.Sqrt), reads=[dst], writes=[dst])
        cx.op("dve", lambda e: e.reciprocal(dst[:, col:col + 1], dst[:, col:col + 1]), reads=[dst], writes=[dst])

    @staged
    def stage_l0_rec(mixT):
        npsmod[0] = 4
        dv = lambda fn, r, w: cx.op("dve", fn, reads=r, writes=w)
        ac = lambda fn, r, w: cx.op("act", fn, reads=r, writes=w)
        mle = [sb("mle%d" % i, [128, 128]) for i in range(2)]; neg = [sb("neg%d" % i, [128, 128]) for i in range(2)]
        trirp = sb("trirp", [128, 128]); ref = [sb("ref%d" % i, [128, 34]) for i in range(2)]
        blkt = [sb("blkt%d" % i, [NB, 128]) for i in range(2)]; selend = [sb("selend%d" % i, [128, NB]) for i in range(2)]
        blkrow = sb("blkrow", [128, 16, 128], BF16)
        for i, sfx in enumerate(["p", "s"]):
            cx.dma("sp", mle[i][:], CN["mle_" + sfx][:], writes=[mle[i]], owner=mle[i])
            cx.dma("sp", neg[i][:], CN["neg_" + sfx][:], writes=[neg[i]], owner=neg[i])
            cx.dma("sp", ref[i][:], CN["ref_" + sfx][:], writes=[ref[i]], owner=ref[i])
            cx.dma("sp", blkt[i][:], CN["blkt_" + sfx][:], writes=[blkt[i]], owner=blkt[i])
            cx.dma("sp", selend[i][:], CN["selend_" + sfx][:], writes=[selend[i]], owner=selend[i])
        cx.dma("sp", trirp[:], CN["trir_p"][:], writes=[trirp], owner=trirp)
        cx.dma("pool", blkrow[:], CN["blkrow"][:].rearrange("p (b t) -> p b t", b=16), writes=[blkrow], owner=blkrow)
        trir = [trirp, mle[1]]
        gb = sb("gb", [128, 8]); mgain = sb("mgain", [128, 1024]); ggain = sb("ggain", [128, 1024])
        LB = sb("LB", [128, 1024]); OMLB = sb("OMLB", [128, 1024])
        cx.dma("sp", gb[:], W["ab_gate_b"][0].partition_broadcast(128), writes=[gb], owner=gb)
        cx.dma("sp", mgain[:], W["m_norm"][0].partition_broadcast(128), writes=[mgain], owner=mgain)
        cx.dma("sp", ggain[:], W["g_norm"][0].partition_broadcast(128), writes=[ggain], owner=ggain)
        cx.dma("sp", LB[:], W["g_lb"][0].partition_broadcast(128), writes=[LB], owner=LB)
        cx.dma("sp", OMLB[:], W["g_lb"][1].partition_broadcast(128), writes=[OMLB], owner=OMLB)
        dv(lambda e: e.tensor_tensor(LB[:], LB[:], OMLB[:], ALU.subtract), [LB, OMLB], [LB])
        ac(lambda e: e.activation(LB[:], LB[:], AF.Sigmoid), [LB], [LB])
        dv(lambda e: e.tensor_scalar(OMLB[:], LB[:], -1.0, 1.0, ALU.mult, ALU.add), [LB], [OMLB])
        mst = sb("mst", [NB, 4]); mend = sb("mend", [NB, 4])
        cx.dma("sp", mst[:], mm[:], writes=[mst], owner=mst)
        Cst = sb("Cst", [128, 4, 2, 257]); Cb = sb("Cb", [128, 4, 2, 257], BF16)
        Sst = sb("Sst", [128, 8, 128]); Smid = sb("Smid", [128, 8, 128]); Smidb = sb("Smidb", [128, 8, 128], BF16)
        for h in range(4):
            for c in range(2):
                cx.dma("sp", Cst[:, h, c, 0:256], mC[0, h, c * 128:(c + 1) * 128, :], writes=[Cst], owner=Cst)
                cx.dma("sp", Cst[:, h, c, 256:257], mn[0, h, c * 128:(c + 1) * 128].rearrange("(p o) -> p o", o=1), writes=[Cst], owner=Cst)
        cx.dma("sp", Sst[:], gS[0].rearrange("h k v -> k h v"), writes=[Sst], owner=Sst)
        dv(lambda e: e.tensor_copy(Cb[:], Cst[:]), [Cst], [Cb])
        NSB = 4
        Cs = [sb("Cs%d" % i, [128, 2, 257]) for i in range(NSB)]; Csb = [sb("Csb%d" % i, [128, 2, 257], BF16) for i in range(NSB)]
        Ss = [sb("Ss%d" % i, [128, 128]) for i in range(NSB)]; Ssb = [sb("Ssb%d" % i, [128, 128], BF16) for i in range(NSB)]
        gsm = sb("gsm", [128, 64]); diag4 = sb("diag4", [128, 4, 128]); dtmp = sb("dtmp", [128, 4, 128])
        Rt = sb("Rt", [128, NB, 4]); bend = sb("bend", [128, NB * 4])
        CQ = 256
        taps = [sb("tap%d" % j, [128, CQ]) for j in range(4)]; cw = [sb("cw%d" % j, [128, CQ]) for j in range(4)]
        qk = sb("qk", [128, 2048]); qT = sb("qT", [128, 8, 128], BF16); kT = sb("kT", [128, 8, 128], BF16)
        khat = sb("khat", [128, 1024], BF16); khm = sb("khm", [128, 1024], BF16)
        zmv = sb("zmv", [128, 2048]); Vp = sb("Vp", [128, 4, 257], BF16); MG = sb("MG", [128, 1024])
        sTs = sb("sTs", [128, 128], BF16); qTm = sb("qTm", [128, 1, 128], BF16)
        hm = sb("hm", [128, 256]); hj = sb("hj", [128, 256]); st = sb("st", [128, 8])
        hm2 = sb("hm2", [128, 128]); hj2 = sb("hj2", [128, 128]); st2 = sb("st2", [128, 8])
        A = [View(qk, 0, 1024), View(qk, 1024, 1024)] + [sb("A%d" % i, [128, 1024]) for i in range(2, 6)]
        ktb = sb("ktb", [128, 1024], BF16); vb = sb("vb", [128, 1024], BF16)
        qtT = sb("qtT", [128, 8, 128], BF16); ktT = sb("ktT", [128, 8, 128], BF16)
        dec = sb("dec", [128, 8, 34]); ATs = sb("ATs", [128, 128], BF16)
        mixed = zmv
        cx.op("pool", lambda e: e.memset(Vp[:], 1.0), writes=[Vp])
        IG, LF, BB, GG_, CMX, MP, CM, AL, BE, MT, FL, T1, T2, AL16 = [slice(4 * k, 4 * k + 4) for k in range(14)]
        ssi = [0]

        for i in range(NT):
            s = 1 if i == NTP else 0
            r0 = i * 128
            cx.dma("sp", gsm[:, 0:8], z_d[r0:r0 + 128, 2048:2056], reads=[z_d], writes=[gsm], owner=gsm)
            dv(lambda e: e.tensor_tensor(gsm[:, 0:8], gsm[:, 0:8], gb[:], ALU.add), [gsm, gb], [gsm])
            ac(lambda e: e.activation(gsm[:, T1], gsm[:, LF], AF.Exp, scale=-1.0), [gsm], [gsm])
            ac(lambda e: e.activation(gsm[:, T1], gsm[:, T1], AF.Ln, bias=1.0), [gsm], [gsm])
            dv(lambda e: e.tensor_scalar_mul(gsm[:, LF], gsm[:, T1], -1.0), [gsm], [gsm])
            p = nps()
            mmg(p[:, 0:4], [(trir[1][:] if s else mle[0][:], gsm[:, LF])], [mle[s], gsm], p)
            dv(lambda e: e.tensor_copy(gsm[:, BB], p[:, 0:4]), [p], [gsm])
            dv(lambda e: e.tensor_tensor(gsm[:, GG_], gsm[:, IG], gsm[:, BB], ALU.subtract), [gsm], [gsm])
            for h in range(4):
                dv(lambda e, h=h: e.tensor_scalar_mul(diag4[:, h, :], ident[:], gsm[:, 12 + h:13 + h]), [ident, gsm], [diag4])
            p = nps()
            for h in range(4):
                mmg(p[:, h * 128:(h + 1) * 128], [(ones[:], diag4[:, h, :])], [ones, diag4], p)
            dv(lambda e: e.tensor_tensor(dtmp[:], p[:].rearrange("p (a b) -> p a b", a=4), neg[s].v(0, [[0, 4], [1, 128]]), ALU.add), [p, neg[s]], [dtmp])
            dv(lambda e: e.tensor_reduce(gsm[:, CMX], dtmp[:], AX.X, ALU.max), [dtmp], [gsm])
            p = nps()
            mmg(p[:, 0:4], [(blkt[s][:], mst[:])], [blkt[s], mst], p)
            dv(lambda e: e.tensor_copy(gsm[:, MP], p[:, 0:4]), [p], [gsm])
            dv(lambda e: e.tensor_tensor(gsm[:, CM], gsm[:, CMX], gsm[:, MP], ALU.max), [gsm], [gsm])
            dv(lambda e: e.tensor_tensor(gsm[:, T1], gsm[:, GG_], gsm[:, MP], ALU.subtract), [gsm], [gsm])
            ac(lambda e: e.activation(gsm[:, AL], gsm[:, T1], AF.Exp), [gsm], [gsm])
            dv(lambda e: e.tensor_tensor(gsm[:, T1], gsm[:, MP], gsm[:, CM], ALU.subtract), [gsm], [gsm])
            ac(lambda e: e.activation(gsm[:, BE], gsm[:, T1], AF.Exp), [gsm], [gsm])
            dv(lambda e: e.tensor_tensor(gsm[:, MT], gsm[:, BB], gsm[:, CM], ALU.add), [gsm], [gsm])
            ac(lambda e: e.activation(gsm[:, FL], gsm[:, MT], AF.Exp, scale=-1.0), [gsm], [gsm])
            dv(lambda e: e.tensor_scalar_mul(gsm[:, AL16], gsm[:, AL], 0.0625), [gsm], [gsm])
            p = nps()
            mmg(p[0:NB, 0:4], [(selend[s][:], gsm[:, MT])], [selend[s], gsm], p)
            dv(lambda e: e.tensor_copy(mend[:], p[0:NB, 0:4]), [p], [mend])
            if s == 0:
                dv(lambda e: e.tensor_copy(mst[0:1, :], mend[0:1, :]), [mend], [mst])
                if i == NTP - 1:
                    cx.dma("act", om[0:1, :], mend[0:1, :], reads=[mend], writes=[om], owner=mend)
            else:
                cx.dma("act", om[1:NB, :], mend[1:NB, :], reads=[mend], writes=[om], owner=mend)
            dv(lambda e: e.tensor_tensor(Rt[:], selend[s].v(0, [[1, NB], [0, 4]]), gsm.v(32, [[0, NB], [1, 4]]), ALU.mult), [selend[s], gsm], [Rt])
            p = nps()
            mmg(p[:, 0:NB * 4], [(ones[:], Rt[:].rearrange("p a b -> p (a b)"))], [ones, Rt], p)
            dv(lambda e: e.tensor_copy(bend[:], p[:, 0:NB * 4]), [p], [bend])
            for qd in range(2048 // CQ):
                c0 = qd * CQ
                for j in range(4):
                    if s == 0:
                        cx.dma("sp", taps[j][:], ext_p[r0 + j:r0 + j + 128, c0:c0 + CQ], reads=[ext_p], writes=[taps[j]], owner=taps[j])
                    else:
                        cx.dma("sp", taps[j][:], ext_s[:, j:j + 8, c0:c0 + CQ], reads=[ext_s], writes=[taps[j]], owner=taps[j])
                    cx.dma("pool", cw[j][:], W["m_conv_w"][0, j, c0:c0 + CQ].partition_broadcast(128), writes=[cw[j]], owner=cw[j])
                for j in range(4):
                    cx.op("pool" if j % 2 else "dve", lambda e, j=j: e.tensor_tensor(taps[j][:], taps[j][:], cw[j][:], ALU.mult), reads=[taps[j], cw[j]], writes=[taps[j]])
                dv(lambda e: e.tensor_tensor(taps[0][:], taps[0][:], taps[1][:], ALU.add), [taps[0], taps[1]], [taps[0]])
                cx.op("pool", lambda e: e.tensor_tensor(taps[2][:], taps[2][:], taps[3][:], ALU.add), reads=[taps[2], taps[3]], writes=[taps[2]])
                dv(lambda e: e.tensor_tensor(taps[0][:], taps[0][:], taps[2][:], ALU.add), [taps[0], taps[2]], [taps[0]])
                ac(lambda e, c0=c0: e.activation(qk[:, c0:c0 + CQ], taps[0][:], AF.Silu), [taps[0]], [qk])
            transpose_to(qT, lambda c: qT[:, c, :], qk, lambda c: qk[:, c * 128:(c + 1) * 128], 8)
            transpose_to(kT, lambda c: kT[:, c, :], qk, lambda c: qk[:, 1024 + c * 128:1024 + (c + 1) * 128], 8, scale_fn=lambda c: 0.0625)
            for h in range(4):
                dv(lambda e, h=h: e.tensor_scalar_mul(khat[:, h * 256:(h + 1) * 256], qk[:, 1024 + h * 256:1024 + (h + 1) * 256], gsm[:, 52 + h:53 + h]),
                   [qk, gsm], [khat])
            cx.dma("sp", zmv[:], z_d[r0:r0 + 128, 0:2048], reads=[z_d], writes=[zmv], owner=zmv)
            dv(lambda e: e.tensor_copy(Vp.v(0, [[257, 4], [1, 256]]), zmv.v(0, [[256, 4], [1, 256]])), [zmv], [Vp])
            ac(lambda e: e.activation(MG[:], zmv[:, 1024:2048], AF.Sigmoid), [zmv], [MG])
            dv(lambda e: e.tensor_tensor(MG[:], MG[:], mgain[:], ALU.mult), [MG, mgain], [MG])
            for h in range(4 if s == 1 else 0):
                p = nps()
                mmg(p[:, 0:128], [(kT[:, 2 * h + c, :], qT[:, 2 * h + c, :]) for c in range(2)], [kT, qT], p)
                dv(lambda e, h=h, p=p: e.scalar_tensor_tensor(sTs[:], p[:, 0:128], gsm[:, 28 + h:29 + h], mle[s][:], ALU.mult, ALU.mult), [p, gsm, mle[s]], [sTs])
                nd = accps()
                if s == 0:
                    pairs = [(sTs[:], Vp[:, h, :])] + [(qT[:, 2 * h + c, :], Cb[:, h, c, :]) for c in range(2)]
                    mmg(nd[:, 0:257], pairs, [sTs, Vp, qT, Cb], nd)
                    for c in range(2):
                        pu = nps()
                        mmg(pu[:, 0:257], [(khat[:, h * 256 + c * 128:h * 256 + (c + 1) * 128], Vp[:, h, :])], [khat, Vp], pu)
                        dv(lambda e, h=h, c=c, pu=pu: e.tensor_tensor(Cst[:, h, c, :], pu[:, 0:257], Cst[:, h, c, :], ALU.add), [pu, Cst], [Cst])
                        dv(lambda e, h=h, c=c: e.tensor_scalar_mul(Cst[:, h, c, :], Cst[:, h, c, :], bend[:, h:h + 1]), [Cst, bend], [Cst])
                else:
                    cx.op("pe", lambda e, h=h: e.matmul(nd[:, 0:257], sTs[:], Vp[:, h, :], start=True, stop=False), reads=[sTs, Vp], writes=[nd])
                    for c in range(2):
                        pass
                    for b in range(16):
                        Cq = Cs[ssi[0] % NSB]; Cqb = Csb[ssi[0] % NSB]; ssi[0] += 1
                        cx.dma("sp", Cq[:, :, 0:256], mC[b + 1, h].rearrange("(c p) e -> p c e", p=128), reads=[mC], writes=[Cq], owner=Cq)
                        cx.dma("sp", Cq[:, :, 256:257], mn[b + 1, h].rearrange("(c p o) -> p c o", p=128, o=1), reads=[mn], writes=[Cq], owner=Cq, allow_slow_non_contiguous=True)
                        cx.op("pool", lambda e, Cq=Cq, Cqb=Cqb: e.tensor_copy(Cqb[:], Cq[:]), reads=[Cq], writes=[Cqb])
                        dv(lambda e, b=b: e.tensor_scalar_mul(khm[:, h * 256:(h + 1) * 256], khat[:, h * 256:(h + 1) * 256], ref[1][:, 18 + b:19 + b]), [khat, ref[1]], [khm])
                        for c in range(2):
                            dv(lambda e, c=c, b=b: e.tensor_tensor(qTm[:, 0, :], qT[:, 2 * h + c, :], blkrow[:, b, :], ALU.mult), [qT, blkrow], [qTm])
                            cx.op("pe", lambda e, c=c, b=b, Cqb=Cqb: e.matmul(nd[:, 0:257], qTm[:, 0, :], Cqb[:, c, :], start=False, stop=(b == 15 and c == 1)),
                                  reads=[qTm, Cqb], writes=[nd])
                        for c in range(2):
                            pu = nps()
                            mmg(pu[:, 0:257], [(khm[:, h * 256 + c * 128:h * 256 + (c + 1) * 128], Vp[:, h, :])], [khm, Vp], pu)
                            dv(lambda e, c=c, pu=pu, Cq=Cq: e.tensor_tensor(Cq[:, c, :], pu[:, 0:257], Cq[:, c, :], ALU.add), [pu, Cq], [Cq])
                            dv(lambda e, c=c, b=b, Cq=Cq: e.tensor_scalar_mul(Cq[:, c, :], Cq[:, c, :], bend[:, (b + 1) * 4 + h:(b + 1) * 4 + h + 1]), [Cq, bend], [Cq])
                            if c == 1:
                                cx.dma("act", oC[b + 1, h].rearrange("(c p) e -> p c e", p=128), Cq[:, :, 0:256], reads=[Cq], writes=[oC], owner=Cq)
                                cx.dma("act", on[b + 1, h].rearrange("(c p o) -> p c o", p=128, o=1), Cq[:, :, 256:257], reads=[Cq], writes=[on], owner=Cq, allow_slow_non_contiguous=True)
                dv(lambda e, nd=nd: e.tensor_copy(st[:, 0:1], nd[:, 256:257]), [nd], [st])
                dv(lambda e: e.tensor_scalar_mul(st[:, 6:7], st[:, 0:1], -1.0), [st], [st])
                dv(lambda e: e.tensor_tensor(st[:, 0:1], st[:, 0:1], st[:, 6:7], ALU.max), [st], [st])
                dv(lambda e, h=h: e.tensor_tensor(st[:, 0:1], st[:, 0:1], gsm[:, 32 + h:33 + h], ALU.mult), [st, gsm], [st])
                dv(lambda e, h=h: e.tensor_tensor(st[:, 0:1], st[:, 0:1], gsm[:, 40 + h:41 + h], ALU.max), [st, gsm], [st])
                dv(lambda e: e.reciprocal(st[:, 0:1], st[:, 0:1]), [st], [st])
                dv(lambda e, h=h: e.tensor_tensor(st[:, 0:1], st[:, 0:1], gsm[:, 32 + h:33 + h], ALU.mult), [st, gsm], [st])
                dv(lambda e, nd=nd: e.tensor_scalar_mul(hm[:], nd[:, 0:256], st[:, 0:1]), [nd, st], [hm])
                dv(lambda e: e.tensor_reduce(st[:, 1:2], hm[:], AX.X, ALU.add), [hm], [st])
                dv(lambda e: e.tensor_scalar_mul(st[:, 1:2], st[:, 1:2], 1.0 / 256), [st], [st])
                dv(lambda e: e.tensor_scalar(hm[:], hm[:], st[:, 1:2], None, ALU.subtract), [hm, st], [hm]) if False else \
                    dv(lambda e: e.tensor_scalar_sub(hm[:], hm[:], st[:, 1:2]), [hm, st], [hm])
                dv(lambda e: e.tensor_tensor(hj[:], hm[:], hm[:], ALU.mult), [hm], [hj])
                dv(lambda e: e.tensor_reduce(st[:, 2:3], hj[:], AX.X, ALU.add), [hj], [st])
                rstd_col(st, 3, st[:, 2:3], 1.0 / 256, RMS_EPS)
                dv(lambda e, h=h: e.scalar_tensor_tensor(mixed[:, h * 256:(h + 1) * 256], hm[:], st[:, 3:4], MG[:, h * 256:(h + 1) * 256], ALU.mult, ALU.mult),
                   [hm, st, MG], [mixed])
            if s == 0 and False:
                dv(lambda e: e.tensor_copy(Cb[:], Cst[:]), [Cst], [Cb])
                if i == NTP - 1:
                    for h in range(4):
                        for c in range(2):
                            cx.dma("act", oC[0, h, c * 128:(c + 1) * 128, :], Cst[:, h, c, 0:256], reads=[Cst], writes=[oC], owner=Cst)
                            cx.dma("act", on[0, h, c * 128:(c + 1) * 128].rearrange("(p o) -> p o", o=1), Cst[:, h, c, 256:257], reads=[Cst], writes=[on], owner=Cst)
            zc = 2056
            for k in range(4):
                cx.dma("sp", A[k][:], z_d[r0:r0 + 128, zc + k * 1024:zc + (k + 1) * 1024], reads=[z_d], writes=[A[k]], owner=A[k])
            ac(lambda e: e.activation(A[1][:], A[1][:], AF.Sigmoid), [A[1]], [A[1]])
            dv(lambda e: e.tensor_tensor(A[1][:], A[1][:], OMLB[:], ALU.mult), [A[1], OMLB], [A[1]])
            dv(lambda e: e.tensor_tensor(A[1][:], A[1][:], LB[:], ALU.add), [A[1], LB], [A[1]])
            ac(lambda e: e.activation(A[4][:], A[1][:], AF.Ln), [A[1]], [A[4]])
            pb = [nps(), nps()]
            for hf in range(2):
                mmg(pb[hf][:], [(trir[s][:], A[4][:, hf * 512:(hf + 1) * 512])], [trir[s], A[4]], pb[hf])
            pd = nps()
            for h in range(8):
                mmg(pd[:, h * 34:(h + 1) * 34], [(A[4][:, h * 128:(h + 1) * 128], ref[s][:])], [A[4], ref[s]], pd)
            ac(lambda e: e.activation(dec[:].rearrange("p a b -> p (a b)"), pd[:, 0:272], AF.Exp), [pd], [dec])
            for hf in range(2):
                ac(lambda e, hf=hf: e.activation(A[5][:, hf * 512:(hf + 1) * 512], pb[hf][:], AF.Exp), [pb[hf]], [A[5]])
                ac(lambda e, hf=hf: e.activation(A[4][:, hf * 512:(hf + 1) * 512], pb[hf][:], AF.Exp, scale=-1.0), [pb[hf]], [A[4]])
            ac(lambda e: e.activation(A[0][:], A[0][:], AF.Silu), [A[0]], [A[0]])
            dv(lambda e: e.scalar_tensor_tensor(A[0][:], A[0][:], float(128 ** -0.5), A[5][:], ALU.mult, ALU.mult), [A[0], A[5]], [A[0]])
            dv(lambda e: e.tensor_scalar(A[1][:], A[1][:], -1.0, 1.0, ALU.mult, ALU.add), [A[1]], [A[1]])
            dv(lambda e: e.tensor_tensor(A[1][:], A[1][:], A[4][:], ALU.mult), [A[1], A[4]], [A[1]])
            cx.op("pool", lambda e: e.tensor_copy(ktb[:], A[1][:]), reads=[A[1]], writes=[ktb])
            cx.op("pool", lambda e: e.tensor_copy(vb[:], A[2][:]), reads=[A[2]], writes=[vb])
            ac(lambda e: e.activation(A[3][:], A[3][:], AF.Silu), [A[3]], [A[3]])
            dv(lambda e: e.tensor_tensor(A[3][:], A[3][:], ggain[:], ALU.mult), [A[3], ggain], [A[3]])
            transpose_to(qtT, lambda c: qtT[:, c, :], A[0], lambda c: A[0][:, c * 128:(c + 1) * 128], 8)
            transpose_to(ktT, lambda c: ktT[:, c, :], A[1], lambda c: A[1][:, c * 128:(c + 1) * 128], 8)
            if s == 0:
                dv(lambda e: e.tensor_tensor(Smid[:], Sst[:], dec.v(0, [[34, 8], [0, 128]]), ALU.mult), [Sst, dec], [Smid])
                cx.op("pool", lambda e: e.tensor_copy(Smidb[:], Smid[:]), reads=[Smid], writes=[Smidb])
            if s == 0:
                def gen_m():
                    banks = [PS[0], PS[1]]; bi = [0]

                    def mps():
                        bi[0] += 1
                        return banks[bi[0] % 2]
                    for h in range(4):
                        p = mps()
                        mmg(p[:, 0:128], [(kT[:, 2 * h + c, :], qT[:, 2 * h + c, :]) for c in range(2)], [kT, qT], p)
                        dv(lambda e: e.scalar_tensor_tensor(sTs[:], p[:, 0:128], gsm[:, 28 + h:29 + h], mle[s][:], ALU.mult, ALU.mult), [p, gsm, mle[s]], [sTs])
                        nd = PS[6]
                        pairs = [(sTs[:], Vp[:, h, :])] + [(qT[:, 2 * h + c, :], Cb[:, h, c, :]) for c in range(2)]
                        mmg(nd[:, 0:257], pairs, [sTs, Vp, qT, Cb], nd)
                        yield
                        for c in range(2):
                            pu = mps()
                            mmg(pu[:, 0:257], [(khat[:, h * 256 + c * 128:h * 256 + (c + 1) * 128], Vp[:, h, :])], [khat, Vp], pu)
                            dv(lambda e: e.tensor_tensor(Cst[:, h, c, :], pu[:, 0:257], Cst[:, h, c, :], ALU.add), [pu, Cst], [Cst])
                            dv(lambda e: e.tensor_scalar_mul(Cst[:, h, c, :], Cst[:, h, c, :], bend[:, h:h + 1]), [Cst, bend], [Cst])
                            yield
                        dv(lambda e: e.tensor_copy(st[:, 0:1], nd[:, 256:257]), [nd], [st])
                        dv(lambda e: e.tensor_scalar_mul(st[:, 6:7], st[:, 0:1], -1.0), [st], [st])
                        dv(lambda e: e.tensor_tensor(st[:, 0:1], st[:, 0:1], st[:, 6:7], ALU.max), [st], [st])
                        dv(lambda e: e.tensor_tensor(st[:, 0:1], st[:, 0:1], gsm[:, 32 + h:33 + h], ALU.mult), [st, gsm], [st])
                        yield
                        dv(lambda e: e.tensor_tensor(st[:, 0:1], st[:, 0:1], gsm[:, 40 + h:41 + h], ALU.max), [st, gsm], [st])
                        dv(lambda e: e.reciprocal(st[:, 0:1], st[:, 0:1]), [st], [st])
                        dv(lambda e: e.tensor_tensor(st[:, 0:1], st[:, 0:1], gsm[:, 32 + h:33 + h], ALU.mult), [st, gsm], [st])
                        dv(lambda e: e.tensor_scalar_mul(hm[:], nd[:, 0:256], st[:, 0:1]), [nd, st], [hm])
                        yield
                        dv(lambda e: e.tensor_reduce(st[:, 1:2], hm[:], AX.X, ALU.add), [hm], [st])
                        dv(lambda e: e.tensor_scalar_mul(st[:, 1:2], st[:, 1:2], 1.0 / 256), [st], [st])
                        dv(lambda e: e.tensor_scalar_sub(hm[:], hm[:], st[:, 1:2]), [hm, st], [hm])
                        yield
                        dv(lambda e: e.tensor_tensor(hj[:], hm[:], hm[:], ALU.mult), [hm], [hj])
                        dv(lambda e: e.tensor_reduce(st[:, 2:3], hj[:], AX.X, ALU.add), [hj], [st])
                        rstd_col(st, 3, st[:, 2:3], 1.0 / 256, RMS_EPS)
                        yield
                        dv(lambda e: e.scalar_tensor_tensor(mixed[:, h * 256:(h + 1) * 256], hm[:], st[:, 3:4], MG[:, h * 256:(h + 1) * 256], ALU.mult, ALU.mult),
                           [hm, st, MG], [mixed])
                        yield

                def gen_g():
                    banks = [PS[2], PS[3]]; bi = [0]

                    def gps():
                        bi[0] += 1
                        return banks[bi[0] % 2]
                    for h in range(8):
                        hc = slice(h * 128, (h + 1) * 128)
                        p = gps()
                        mmg(p[0:64, 0:64], [(ktT[:, h, 0:64], qtT[:, h, 0:64])], [ktT, qtT], p)
                        mmg(p[:, 64:128], [(ktT[:, h, :], qtT[:, h, 64:128])], [ktT, qtT], p)
                        dv(lambda e: e.tensor_tensor(ATs[0:64, 0:64], p[0:64, 0:64], mle[s][0:64, 0:64], ALU.mult), [p, mle[s]], [ATs])
                        dv(lambda e: e.tensor_tensor(ATs[:, 64:128], p[:, 64:128], mle[s][:, 64:128], ALU.mult), [p, mle[s]], [ATs])
                        dv(lambda e: e.memset(ATs[64:128, 0:64], 0.0), [], [ATs])
                        yield
                        po = PS[7]
                        mmg(po[:, 0:128], [(ATs[:], vb[:, hc]), (qtT[:, h, :], Smidb[:, h, :])], [ATs, vb, qtT, Smidb], po)
                        pu = gps()
                        mmg(pu[:, 0:128], [(ktb[:, hc], vb[:, hc])], [ktb, vb], pu)
                        yield
                        dv(lambda e: e.tensor_tensor(Sst[:, h, :], pu[:, 0:128], Smid[:, h, :], ALU.add), [pu, Smid], [Sst])
                        dv(lambda e: e.tensor_scalar_mul(Sst[:, h, :], Sst[:, h, :], dec[:, h, 17:18]), [Sst, dec], [Sst])
                        yield
                        ac(lambda e: e.activation(hm2[:], po[:, 0:128], AF.Copy), [po], [hm2])
                        cx.op("pool", lambda e: e.tensor_tensor(hj2[:], hm2[:], hm2[:], ALU.mult), reads=[hm2], writes=[hj2])
                        yield
                        dv(lambda e: e.tensor_reduce(st2[:, 4:5], hj2[:], AX.X, ALU.add), [hj2], [st2])
                        rstd_col(st2, 5, st2[:, 4:5], 1.0 / 128, RMS_EPS)
                        yield
                        dv(lambda e: e.scalar_tensor_tensor(mixed[:, 1024 + hc.start:1024 + hc.stop], hm2[:], st2[:, 5:6], A[3][:, hc], ALU.mult, ALU.mult),
                           [hm2, st2, A[3]], [mixed])
                        yield
                gens = [gen_m(), gen_g()]
                while gens:
                    for g in list(gens):
                        try:
                            next(g)
                        except StopIteration:
                            gens.remove(g)
                dv(lambda e: e.tensor_copy(Cb[:], Cst[:]), [Cst], [Cb])
                if i == NTP - 1:
                    for h in range(4):
                        for c in range(2):
                            cx.dma("act", oC[0, h, c * 128:(c + 1) * 128, :], Cst[:, h, c, 0:256], reads=[Cst], writes=[oC], owner=Cst)
                            cx.dma("act", on[0, h, c * 128:(c + 1) * 128].rearrange("(p o) -> p o", o=1), Cst[:, h, c, 256:257], reads=[Cst], writes=[on], owner=Cst)
            for h in range(8 if s == 1 else 0):
                hc = slice(h * 128, (h + 1) * 128)
                p = nps()
                if s == 0:
                    mmg(p[0:64, 0:64], [(ktT[:, h, 0:64], qtT[:, h, 0:64])], [ktT, qtT], p)
                    mmg(p[:, 64:128], [(ktT[:, h, :], qtT[:, h, 64:128])], [ktT, qtT], p)
                    dv(lambda e, p=p: e.tensor_tensor(ATs[0:64, 0:64], p[0:64, 0:64], mle[s][0:64, 0:64], ALU.mult), [p, mle[s]], [ATs])
                    dv(lambda e, p=p: e.tensor_tensor(ATs[:, 64:128], p[:, 64:128], mle[s][:, 64:128], ALU.mult), [p, mle[s]], [ATs])
                    dv(lambda e: e.memset(ATs[64:128, 0:64], 0.0), [], [ATs])
                else:
                    mmg(p[:, 0:128], [(ktT[:, h, :], qtT[:, h, :])], [ktT, qtT], p)
                    dv(lambda e, p=p: e.tensor_tensor(ATs[:], p[:, 0:128], mle[s][:], ALU.mult), [p, mle[s]], [ATs])
                po = accps()
                if s == 0:
                    mmg(po[:, 0:128], [(ATs[:], vb[:, hc]), (qtT[:, h, :], Smidb[:, h, :])], [ATs, vb, qtT, Smidb], po)
                    pu = nps()
                    mmg(pu[:, 0:128], [(ktb[:, hc], vb[:, hc])], [ktb, vb], pu)
                    dv(lambda e, h=h, pu=pu: e.tensor_tensor(Sst[:, h, :], pu[:, 0:128], Smid[:, h, :], ALU.add), [pu, Smid], [Sst])
                    dv(lambda e, h=h: e.tensor_scalar_mul(Sst[:, h, :], Sst[:, h, :], dec[:, h, 17:18]), [Sst, dec], [Sst])
                else:
                    cx.op("pe", lambda e, hc=hc: e.matmul(po[:, 0:128], ATs[:], vb[:, hc], start=True, stop=False), reads=[ATs, vb], writes=[po])
                    for b in range(16):
                        Sq = Ss[ssi[0] % NSB]; Sqb = Ssb[ssi[0] % NSB]; ssi[0] += 1
                        cx.dma("sp", Sq[:], gS[b + 1, h], reads=[gS], writes=[Sq], owner=Sq)
                        cx.op("pool", lambda e, Sq=Sq, Sqb=Sqb: e.tensor_copy(Sqb[:], Sq[:]), reads=[Sq], writes=[Sqb])
                        dv(lambda e, b=b, h=h: e.tensor_tensor(qTm[:, 0, :], qtT[:, h, :], blkrow[:, b, :], ALU.mult), [qtT, blkrow], [qTm])
                        cx.op("pe", lambda e, b=b, Sqb=Sqb: e.matmul(po[:, 0:128], qTm[:, 0, :], Sqb[:], start=False, stop=(b == 15)), reads=[qTm, Sqb], writes=[po])
                        dv(lambda e, b=b, hc=hc: e.tensor_scalar_mul(khm[:, 0:128], ktb[:, hc], ref[1][:, 18 + b:19 + b]), [ktb, ref[1]], [khm])
                        pu = nps()
                        mmg(pu[:, 0:128], [(khm[:, 0:128], vb[:, hc])], [khm, vb], pu)
                        dv(lambda e, pu=pu, Sq=Sq: e.tensor_tensor(Sq[:], pu[:, 0:128], Sq[:], ALU.add), [pu, Sq], [Sq])
                        dv(lambda e, b=b, h=h, Sq=Sq: e.tensor_scalar_mul(Sq[:], Sq[:], dec[:, h, 17 + b + 1:17 + b + 2]), [Sq, dec], [Sq])
                        cx.dma("act", oS[b + 1, h], Sq[:], reads=[Sq], writes=[oS], owner=Sq)
                dv(lambda e, po=po: e.tensor_tensor(hj[:, 0:128], po[:, 0:128], po[:, 0:128], ALU.mult) if False else e.tensor_copy(hm[:, 0:128], po[:, 0:128]), [po], [hm])
                dv(lambda e: e.tensor_tensor(hj[:, 0:128], hm[:, 0:128], hm[:, 0:128], ALU.mult), [hm], [hj])
                dv(lambda e: e.tensor_reduce(st[:, 4:5], hj[:, 0:128], AX.X, ALU.add), [hj], [st])
                rstd_col(st, 5, st[:, 4:5], 1.0 / 128, RMS_EPS)
                dv(lambda e, hc=hc: e.scalar_tensor_tensor(mixed[:, 1024 + hc.start:1024 + hc.stop], hm[:, 0:128], st[:, 5:6], A[3][:, hc], ALU.mult, ALU.mult),
                   [hm, st, A[3]], [mixed])
            if s == 0 and i == NTP - 1:
                cx.dma("act", oS[0].rearrange("h k v -> k h v"), Sst[:], reads=[Sst], writes=[oS], owner=Sst)
            transpose_to(mixT, lambda c: mixT[:, c, r0:r0 + 128], mixed, lambda c: mixed[:, c * 128:(c + 1) * 128], 16)
        cx.dma("sp", oconv[0], ext_p[NP:NP + 3, :], reads=[ext_p], writes=[oconv], owner=oconv)
        cx.dma("sp", oconv[1:17], ext_s[:, 8:11, :], reads=[ext_s], writes=[oconv], owner=oconv)

    mix_d = dscr("mix_d", [6, 128, 16 * NTOK], BF16)

    @staged
    def stage_l1_mix(hT):
        shs = sb("shs", [NB, D]); shT = sb("shT", [128, 16, NB], BF16); muT = sb("muT", [128, 6, 16])
        xx = [sb("xx%d" % i, [128, NTOK], BF16) for i in range(2)]
        mo_ = [sb("mo%d" % i, [128, NTOK], BF16) for i in range(4)]
        cx.dma("sp", shs[:], rsh[:], writes=[shs], owner=shs)
        for c0 in range(0, 16, 4):
            p = nps()
            for j in range(4):
                c = c0 + j
                cx.op("pe", lambda e, j=j, c=c: e.transpose(p.v(j * 32, [[1, NB]]), shs[:, c * 128:(c + 1) * 128], ident[0:NB, 0:NB]),
                      reads=[shs, ident], writes=[p])
            for j in range(4):
                evac(shT[:, c0 + j, :], p.v(j * 32, [[1, NB]]), [p], [shT])
        for i in range(6):
            cx.dma("sp", muT[:, i, :], W["r_mu"][0, i].rearrange("(c p) -> p c", p=128), writes=[muT], owner=muT, allow_slow_non_contiguous=True)
        oi = 0
        for c in range(16):
            x = xx[c % 2]
            cx.op("dve", lambda e: e.tensor_tensor(x[:, 1:NTOK], hT[:, c, 0:NTOK - 1], hT[:, c, 1:NTOK], ALU.subtract), reads=[hT], writes=[x])
            cx.op("dve", lambda e: e.tensor_tensor(x[:, 0:1], shT[:, c, 0:1], hT[:, c, 0:1], ALU.subtract), reads=[hT, shT], writes=[x])
            cx.op("dve", lambda e: e.tensor_tensor(x.v(NP, [[8, 16]]), shT[:, c, 1:NB], hT.v(c * NTOK + NP, [[8, 16]]), ALU.subtract), reads=[hT, shT], writes=[x])
            for i in range(6):
                o = mo_[oi % 4]; oi += 1
                cx.op("dve", lambda e, i=i, o=o: e.scalar_tensor_tensor(o[:], x[:], muT[:, i, c:c + 1], hT[:, c, :], ALU.mult, ALU.add),
                      reads=[x, muT, hT], writes=[o])
                cx.dma("sp", mix_d[i, :, c * NTOK:(c + 1) * NTOK], o[:], reads=[o], writes=[mix_d], owner=o)

    def load_mix(i, hT):
        cx.dma("sp", hT[:].rearrange("p a b -> p (a b)"), mix_d[i], reads=[mix_d], writes=[hT], owner=hT)
        cx.barrier()

    @staged
    def lora1(aT, w1_ap, R, func, tT):
        w1 = sb("l1w", [128, 16, R], BF16)
        cx.dma("pool", w1[:], w1_ap.rearrange("(kc p) n -> p kc n", p=128), writes=[w1], owner=w1)
        for oc in range((R + 127) // 128):
            rc = min(128, R - oc * 128)
            for g0 in range(0, NTOK, 512):
                gn = min(512, NTOK - g0)
                p = nps()
                for kc in range(16):
                    cx.op("pe", lambda e, kc=kc: e.matmul(p[0:rc, 0:gn], w1[:, kc, oc * 128:oc * 128 + rc], aT[:, kc, g0:g0 + gn], start=(kc == 0), stop=(kc == 15)),
                          reads=[w1, aT], writes=[p])
                cx.op("act", lambda e: e.activation(tT[0:rc, oc, g0:g0 + gn], p[0:rc, 0:gn], func), reads=[p], writes=[tT])

    def stage_l1_proj(hT):
        def dst_rz(base):
            return lambda c0: (lambda i: [(rz_d, rz_d[i * 128:(i + 1) * 128, base + c0:base + c0 + 512], 0, 128)])
        def blocks_for(w, base):
            return [(w[:, c0:c0 + 512], dst_rz(base)(c0)) for c0 in range(0, D, 512)]
        stacks.append(contextlib.ExitStack()); stage_bufs.append([])
        tT = sb("tT", [128, 2, NTOK], BF16)
        load_mix(0, hT); proj_tok(hT, blocks_for(W["r_wr"][0], 0))
        load_mix(2, hT); proj_tok(hT, blocks_for(W["r_wk"][0], D))
        load_mix(3, hT); proj_tok(hT, blocks_for(W["r_wv"][0], 2 * D))
        load_mix(1, hT); lora1(hT, W["r_w1"][0], 96, AF.Tanh, tT)
        proj_tok(tT, blocks_for(W["r_w2"][0], 3 * D), K=1, kpart=96, bias_fn=lambda bi: W["r_w0"][0, bi * 512:(bi + 1) * 512], sigmoid=True)
        load_mix(4, hT); lora1(hT, W["r_a1"][0], 96, AF.Copy, tT)
        proj_tok(tT, blocks_for(W["r_a2"][0], 4 * D), K=1, kpart=96, bias_fn=lambda bi: W["r_a0"][0, bi * 512:(bi + 1) * 512], sigmoid=True)
        load_mix(5, hT); lora1(hT, W["r_g1"][0], 256, AF.Sigmoid, tT)
        proj_tok(tT, blocks_for(W["r_g2"][0], 5 * D), K=2, kpart=128)
        cx.barrier(); cx.release(stage_bufs.pop()); stacks.pop().close()

    @staged
    def stage_l1_rec(outT):
        npsmod[0] = 4
        dv = lambda fn, r, w: cx.op("dve", fn, reads=r, writes=w)
        ac = lambda fn, r, w: cx.op("act", fn, reads=r, writes=w)
        po_ = lambda fn, r, w: cx.op("pool", fn, reads=r, writes=w)
        mle = [sb("mle%d" % i, [128, 128]) for i in range(2)]; mlt = [sb("mlt%d" % i, [128, 128]) for i in range(2)]
        mgt = [sb("mgt%d" % i, [128, 128]) for i in range(2)]; refs = sb("refs", [128, 34])
        blkrow = sb("blkrow", [128, 16, 128], BF16)
        NBP = 4
        for i, (a_, b_, c_) in enumerate([("rle_p", "rlt_p", "rgt_p"), ("mle_s", "mlt_s", "mgt_s")]):
            cx.dma("sp", mle[i][:], CN[a_][:], writes=[mle[i]], owner=mle[i])
            cx.dma("sp", mlt[i][:], CN[b_][:], writes=[mlt[i]], owner=mlt[i])
            cx.dma("sp", mgt[i][:], CN[c_][:], writes=[mgt[i]], owner=mgt[i])
        cx.dma("sp", refs[:], CN["ref_s"][:], writes=[refs], owner=refs)
        rblk = sb("rblk", [128, NB]); rblkrow = sb("rblkrow", [128, NBP, 128], BF16)
        cx.dma("sp", rblk[:], CN["rblk_p"][:], writes=[rblk], owner=rblk)
        cx.dma("pool", rblkrow[:], CN["rblkrow_p"][:].rearrange("p (b t) -> p b t", b=NBP), writes=[rblkrow], owner=rblkrow)
        Hrd = [sb("Hrd%d" % i, [128, NBP, 64], BF16) for i in range(2)]
        rmk = sb("rmk", [128, NBP, 128], BF16)
        cx.dma("pool", blkrow[:], CN["blkrow"][:].rearrange("p (b t) -> p b t", b=16), writes=[blkrow], owner=blkrow)
        HW = 1024
        PRMH = [{k: sb("prm_%s%d" % (k, hf), [128, HW]) for k in ["r_kk", "r_ka", "r_rk"]} for hf in range(2)]
        for hf in range(2):
            for k_, b_ in PRMH[hf].items():
                cx.dma("sp", b_[:], W[k_][0, hf * HW:(hf + 1) * HW].partition_broadcast(128), writes=[b_], owner=b_)
        LNW = sb("prm_lnw", [128, HW]); LNB = sb("prm_lnb", [128, HW])
        Rb, Kb, Vb_, SW, Aa, Cc, KK, TMP, E1, E2 = [sb("rw%d" % i, [128, HW]) for i in range(10)]
        Gg = E2
        aTt, bTt, kTt, rTt = [sb("rt%d" % i, [128, 8, 128], BF16) for i in range(4)]
        btk, ktk, vtk = [sb("rk%d" % i, [128, HW], BF16) for i in range(3)]
        Hst = sb("Hst", [128, 16, 64]); Hbm = [sb("Hbm%d" % i, [128, 16, 64], BF16) for i in range(2)]
        bTm = [sb("bTm%d" % i, [128, 8, 128], BF16) for i in range(2)]; kTm = [sb("kTm%d" % i, [128, 8, 128], BF16) for i in range(2)]
        Hsbm = [[sb("Hsbm%d_%d" % (i, j), [128, 64], BF16) for j in range(2)] for i in range(3)]
        cx.op("dve", lambda e: e.memset(Hst[:], 0.0), writes=[Hst])
        for zb in Hbm + bTm + kTm + Hsbm[0] + Hsbm[1] + Hsbm[2] + Hrd:
            cx.op("dve", lambda e, zb=zb: e.memset(zb[:], 0.0), writes=[zb])
        Nb = [sb("Nb%d" % i, [128, 2, 128], BF16) for i in range(2)]; Mb = [sb("Mb%d" % i, [128, 2, 128], BF16) for i in range(2)]
        TtS = [sb("TtS%d" % i, [128, 2, 128], BF16) for i in range(3)]; AakS = [sb("AakS%d" % i, [128, 2, 128], BF16) for i in range(3)]
        ArbS = [sb("ArbS%d" % i, [128, 2, 128], BF16) for i in range(3)]; ArkS = [sb("ArkS%d" % i, [128, 2, 128], BF16) for i in range(3)]
        BSET = [dict(am=sb("am%d" % i, [128, 128], BF16), Xb=sb("Xbq%d" % i, [128, 128], BF16), Ubf=sb("Ubq%d" % i, [128, 128], BF16),
                     bm=sb("bmq%d" % i, [128, 128], BF16), km=sb("kmq%d" % i, [128, 128], BF16), rmk=sb("rmkq%d" % i, [128, 4, 128], BF16),
                     Hrd=[sb("Hrdq%d_%d" % (i, j), [128, 4, 64], BF16) for j in range(2)]) for i in range(2)]
        for i in range(2):
            for zb in BSET[i]["Hrd"]:
                cx.op("dve", lambda e, zb=zb: e.memset(zb[:], 0.0), writes=[zb])
        Tt, AakT, ArbT, ArkT = TtS[0], AakS[0], ArbS[0], ArkS[0]
        Xb = sb("Xb", [128, 128], BF16); Ub = sb("Ub", [128, 128], BF16); Ubf = Ub
        am = sb("am", [128, 128], BF16); rm = sb("rm", [128, 128], BF16); bm = sb("bm", [128, 128], BF16); km = sb("km", [128, 128], BF16)
        dL = sb("dL", [128, 8, NB]); stt_ = sb("stt", [128, 4, 16])
        NSB = 3
        Sin = [sb("Sin%d" % i, [64, 2, 64]) for i in range(NSB)]; Hs = [sb("Hs%d" % i, [128, 64]) for i in range(NSB)]
        Sout = [sb("Sout%d" % i, [64, 2, 64]) for i in range(NSB)]
        ssi = [0]
        h3 = lambda b_: b_[:].rearrange("p (h j) -> p h j", j=64)

        def prompt_pairs(half, ncp, Y):
            s = 0
            nit = 4

            def genA(cp, st):
                def pairmm(p, lT, rT_):
                    for hh in range(2):
                        lb = lT[hh] if isinstance(lT, list) else lT
                        rb = rT_[hh] if isinstance(rT_, list) else rT_
                        mmg(p[:, hh * 128:(hh + 1) * 128], [(lb[:, cp, :], rb[:, cp, :])], [lb, rb], p)

                def pairev(dst, p, mask):
                    dv(lambda e: e.tensor_tensor(dst[:], p[:, 0:256].rearrange("p (a b) -> p a b", a=2), mask.v(0, [[0, 2], [1, 128]]), ALU.mult), [p, mask], [dst])
                Tt_ = TtS[st]
                p = nps(); pairmm(p, aTt, bTm); pairev(Nb[0], p, mgt[s]); yield
                p = nps(); pairmm(p, bTm, aTt); pairev(Mb[0], p, mlt[s]); yield
                p = nps(); pairmm(p, kTm, aTt); pairev(AakS[st], p, mlt[s]); yield
                p = nps(); pairmm(p, bTm, rTt); pairev(ArbS[st], p, mle[s]); yield
                p = nps(); pairmm(p, kTm, rTt); pairev(ArkS[st], p, mle[s]); yield
                dv(lambda e: e.tensor_tensor(Tt_[:], Mb[0][:], identb.v(0, [[0, 2], [1, 128]]), ALU.add), [Mb[0], identb], [Tt_])
                cur = 0
                for it in range(nit):
                    nx = 1 - cur
                    p1 = nps(); p2 = nps()
                    for hh in range(2):
                        if it < nit - 1:
                            mmg(p1[:, hh * 128:(hh + 1) * 128], [(Nb[cur][:, hh, :], Mb[cur][:, hh, :])], [Nb[cur], Mb[cur]], p1)
                        mmg(p2[:, hh * 128:(hh + 1) * 128], [(Mb[cur][:, hh, :], Nb[cur][:, hh, :])], [Nb[cur], Mb[cur]], p2)
                    yield
                    if it < nit - 1:
                        ac(lambda e: e.activation(Mb[nx][:].rearrange("p a b -> p (a b)"), p1[:, 0:256], AF.Copy), [p1], [Mb[nx]])
                    dv(lambda e: e.tensor_copy(Nb[nx][:].rearrange("p a b -> p (a b)"), p2[:, 0:256]), [p2], [Nb[nx]])
                    p3 = nps()
                    for hh in range(2):
                        mmg(p3[:, hh * 128:(hh + 1) * 128], [(Nb[nx][:, hh, :], Tt_[:, hh, :])], [Nb[nx], Tt_], p3)
                    yield
                    dv(lambda e: e.tensor_tensor(Tt_[:].rearrange("p a b -> p (a b)"), p3[:, 0:256], Tt_[:].rearrange("p a b -> p (a b)"), ALU.add), [p3, Tt_], [Tt_])
                    cur = nx
                    yield

            def genB(cp, st, bs):
                gc = half * 8 + cp
                cs = slice(cp * 128, (cp + 1) * 128)
                Tt_, AakT_, ArbT_, ArkT_ = TtS[st], AakS[st], ArbS[st], ArkS[st]
                B_ = BSET[bs]
                am, Xb, Ubf, bm, km, rmk, Hrd = B_["am"], B_["Xb"], B_["Ubf"], B_["bm"], B_["km"], B_["rmk"], B_["Hrd"]
                px = pu = ph = PS[3 + 2 * bs]
                py = PS[4 + 2 * bs]
                for k in range(NBP):
                    dv(lambda e: e.tensor_tensor(am[:], aTt[:, cp, :], rblkrow[:, k, :], ALU.mult), [aTt, rblkrow], [am])
                    po_(lambda e: e.tensor_tensor(rmk[:, k, :], rTt[:, cp, :], rblkrow[:, k, :], ALU.mult), [rTt, rblkrow], [rmk])
                    Hsrc = [(Hbm[hh][:, gc, :], Hbm[hh]) if k == 0 else (Hrd[hh][:, k, :], Hrd[hh]) for hh in range(2)]
                    for hh in range(2):
                        mmg(px[:, hh * 64:(hh + 1) * 64], [(AakT_[:, hh, :], vtk[:, cp * 128 + hh * 64:cp * 128 + (hh + 1) * 64]),
                                                           (am[:], Hsrc[hh][0])], [AakT_, vtk, am, Hsrc[hh][1]], px)
                    yield
                    ac(lambda e: e.activation(Xb[:], px[:, 0:128], AF.Copy), [px], [Xb])
                    for hh in range(2):
                        mmg(pu[:, hh * 64:(hh + 1) * 64], [(Tt_[:, hh, :], Xb[:, hh * 64:(hh + 1) * 64])], [Tt_, Xb], pu)
                    yield
                    if k == 0:
                        dv(lambda e: e.tensor_scalar_mul(Ubf[:], pu[:, 0:128], rblk[:, k:k + 1]), [pu, rblk], [Ubf])
                    else:
                        dv(lambda e: e.scalar_tensor_tensor(Ubf[:], pu[:, 0:128], rblk[:, k:k + 1], Ubf[:], ALU.mult, ALU.add), [pu, rblk, Ubf], [Ubf])
                    ac(lambda e: e.activation(bm[:], btk[:, cs], AF.Copy, scale=rblk[:, k:k + 1]), [btk, rblk], [bm])
                    ac(lambda e: e.activation(km[:], ktk[:, cs], AF.Copy, scale=rblk[:, k:k + 1]), [ktk, rblk], [km])
                    mmg(ph[:, 0:128], [(bm[:], Ubf[:]), (km[:], vtk[:, cs])], [bm, Ubf, km, vtk], ph)
                    yield
                    for hh in range(2):
                        pb = 64 * hh
                        dv(lambda e: e.tensor_tensor(Hst[pb:pb + 64, gc, :], ph[pb:pb + 64, pb:pb + 64], Hst[pb:pb + 64, gc, :], ALU.add), [ph, Hst], [Hst])
                    ac(lambda e: e.activation(Hst[:, gc, :], Hst[:, gc, :], AF.Copy, scale=dL[:, cp, k:k + 1]), [Hst, dL], [Hst])
                    if k < NBP - 1:
                        for hh in range(2):
                            pb = 64 * hh
                            po_(lambda e: e.tensor_copy(Hrd[hh][pb:pb + 64, k + 1, :], Hst[pb:pb + 64, gc, :]), [Hst], [Hrd[hh]])
                    yield
                for hh in range(2):
                    vh = vtk[:, cp * 128 + hh * 64:cp * 128 + (hh + 1) * 64]
                    pairs = [(ArbT_[:, hh, :], Ubf[:, hh * 64:(hh + 1) * 64]), (ArkT_[:, hh, :], vh), (rmk[:, 0, :], Hbm[hh][:, gc, :])]
                    pairs += [(rmk[:, k, :], Hrd[hh][:, k, :]) for k in range(1, NBP)]
                    mmg(py[:, hh * 64:(hh + 1) * 64], pairs, [ArbT_, Ubf, ArkT_, vtk, rmk, Hbm[hh], Hrd[hh]], py)
                yield
                ac(lambda e: e.activation(Y[:, cs], py[:, 0:128], AF.Copy), [py], [Y])
                for hh in range(2):
                    pb = 64 * hh
                    po_(lambda e: e.tensor_copy(Hbm[hh][pb:pb + 64, gc, :], Hst[pb:pb + 64, gc, :]), [Hst], [Hbm[hh]])
                yield

            npsmod[0] = 3
            active = {}
            doneA = set(); doneB = set()
            nextA = 0; nextB = 0
            while len(doneB) < ncp:
                if nextA < ncp and "A" not in active and (nextA < 3 or (nextA - 3) in doneB):
                    active["A"] = (genA(nextA, nextA % 3), nextA); nextA += 1
                if nextB < ncp and nextB in doneA and ("B%d" % (nextB % 2)) not in active:
                    active["B%d" % (nextB % 2)] = (genB(nextB, nextB % 3, nextB % 2), nextB); nextB += 1
                for key in list(active):
                    g, idx = active[key]
                    try:
                        next(g)
                    except StopIteration:
                        del active[key]
                        (doneA if key == "A" else doneB).add(idx)
            npsmod[0] = 4

        def bc16(b_, col):
            return stt_.v(col * 16, [[1, 16], [0, 64]])

        import os
        for i in range(NT):
            s = 1 if i == NTP else 0
            if os.environ.get("K_REC") == "p" and s == 1:
                continue
            if os.environ.get("K_REC") == "s" and s == 0:
                continue
            r0 = i * 128
            nit = 2 if s else 4
            for half in range(2):
                f0 = half * HW
                PRM = dict(PRMH[half]); PRM["r_lnw"] = LNW; PRM["r_lnb"] = LNB
                for idx, b_ in enumerate([Rb, Kb, Vb_, SW, Aa]):
                    cx.dma("sp", b_[:], rz_d[r0:r0 + 128, idx * D + f0:idx * D + f0 + HW], reads=[rz_d], writes=[b_], owner=b_)
                cx.dma("sp", LNW[:], W["r_lnw"][0, f0:f0 + HW].partition_broadcast(128), writes=[LNW], owner=LNW)
                cx.dma("sp", LNB[:], W["r_lnb"][0, f0:f0 + HW].partition_broadcast(128), writes=[LNB], owner=LNB)
                dv(lambda e: e.tensor_scalar_mul(SW[:], SW[:], -0.6065306597126334), [SW], [SW])
                pc = [nps(), nps()]
                for hf in range(2):
                    mmg(pc[hf][:], [(mle[s][:], SW[:, hf * 512:(hf + 1) * 512])], [mle[s], SW], pc[hf])
                for hf in range(2):
                    evac(Cc[:, hf * 512:(hf + 1) * 512], pc[hf][:], [pc[hf]], [Cc])
                pd = nps()
                for cp in range(8):
                    mmg(pd[:, cp * NB:(cp + 1) * NB], [(SW[:, cp * 128:(cp + 1) * 128], refs[:, 17:34] if s else rblk[:])], [SW, refs, rblk], pd)
                ac(lambda e: e.activation(dL[:].rearrange("p a b -> p (a b)"), pd[:, 0:8 * NB], AF.Exp), [pd], [dL])
                dv(lambda e: e.tensor_tensor(KK[:], Kb[:], PRM["r_kk"][:], ALU.mult), [Kb, PRM["r_kk"]], [KK])
                ac(lambda e: e.activation(TMP[:], KK[:], AF.Square), [KK], [TMP])
                dv(lambda e: e.tensor_reduce(stt_[:, 0, :], h3(TMP), AX.X, ALU.add), [TMP], [stt_])
                dv(lambda e: e.tensor_scalar_max(stt_[:, 0, :], stt_[:, 0, :], 1e-24), [stt_], [stt_])
                ac(lambda e: e.activation(stt_[:, 0, :], stt_[:, 0, :], AF.Sqrt), [stt_], [stt_])
                dv(lambda e: e.reciprocal(stt_[:, 0, :], stt_[:, 0, :]), [stt_], [stt_])
                dv(lambda e: e.tensor_tensor(h3(KK), h3(KK), bc16(stt_, 0), ALU.mult), [KK, stt_], [KK])
                dv(lambda e: e.scalar_tensor_tensor(TMP[:], Aa[:], -1.0, PRM["r_ka"][:], ALU.add, ALU.mult), [Aa, PRM["r_ka"]], [TMP])
                dv(lambda e: e.scalar_tensor_tensor(Kb[:], TMP[:], 1.0, Kb[:], ALU.add, ALU.mult), [TMP, Kb], [Kb])
                po_(lambda e: e.tensor_tensor(TMP[:], Rb[:], Kb[:], ALU.mult), [Rb, Kb], [TMP])
                po_(lambda e: e.tensor_tensor(TMP[:], TMP[:], PRM["r_rk"][:], ALU.mult), [TMP, PRM["r_rk"]], [TMP])
                dv(lambda e: e.tensor_reduce(stt_[:, 1, :], h3(TMP), AX.X, ALU.add), [TMP], [stt_])
                dv(lambda e: e.tensor_tensor(Aa[:], KK[:], Aa[:], ALU.mult), [KK, Aa], [Aa])
                ac(lambda e: e.activation(E1[:], Cc[:], AF.Exp), [Cc], [E1])
                ac(lambda e: e.activation(E2[:], Cc[:], AF.Exp, scale=-1.0), [Cc], [E2])
                dv(lambda e: e.tensor_tensor(TMP[:], Cc[:], SW[:], ALU.subtract), [Cc, SW], [TMP])
                ac(lambda e: e.activation(TMP[:], TMP[:], AF.Exp), [TMP], [TMP])
                dv(lambda e: e.tensor_tensor(Rb[:], Rb[:], E1[:], ALU.mult), [Rb, E1], [Rb])
                dv(lambda e: e.scalar_tensor_tensor(KK[:], KK[:], -1.0, TMP[:], ALU.mult, ALU.mult), [KK, TMP], [KK])
                po_(lambda e: e.tensor_tensor(Aa[:], Aa[:], E2[:], ALU.mult), [Aa, E2], [Aa])
                dv(lambda e: e.tensor_tensor(Kb[:], Kb[:], E2[:], ALU.mult), [Kb, E2], [Kb])
                cx.dma("sp", Gg[:], rz_d[r0:r0 + 128, 5 * D + f0:5 * D + f0 + HW], reads=[rz_d], writes=[Gg], owner=Gg)
                ac(lambda e: e.activation(btk[:], Aa[:], AF.Copy), [Aa], [btk])
                ac(lambda e: e.activation(ktk[:], Kb[:], AF.Copy), [Kb], [ktk])
                ac(lambda e: e.activation(vtk[:], Vb_[:], AF.Copy), [Vb_], [vtk])
                transpose_to(aTt, lambda c: aTt[:, c, :], KK, lambda c: KK[:, c * 128:(c + 1) * 128], 8)
                transpose_to(bTt, lambda c: bTt[:, c, :], Aa, lambda c: Aa[:, c * 128:(c + 1) * 128], 8)
                transpose_to(kTt, lambda c: kTt[:, c, :], Kb, lambda c: Kb[:, c * 128:(c + 1) * 128], 8)
                transpose_to(rTt, lambda c: rTt[:, c, :], Rb, lambda c: Rb[:, c * 128:(c + 1) * 128], 8)
                for hh in range(2):
                    pb = 64 * hh
                    ac(lambda e, hh=hh, pb=pb: e.activation(bTm[hh][pb:pb + 64, :, :], bTt[pb:pb + 64, :, :], AF.Copy), [bTt], [bTm[hh]])
                    po_(lambda e, hh=hh, pb=pb: e.tensor_copy(kTm[hh][pb:pb + 64, :, :], kTt[pb:pb + 64, :, :]), [kTt], [kTm[hh]])
                Y = E1
                ncp = int(os.environ.get("K_NCP", "8"))
                if s == 0:
                    prompt_pairs(half, ncp, Y)
                for cp in range(ncp if s == 1 else 0):
                    gc = half * 8 + cp
                    cs = slice(cp * 128, (cp + 1) * 128)

                    def pairmm(p, lT, rT_):
                        for hh in range(2):
                            lb = lT[hh] if isinstance(lT, list) else lT
                            rb = rT_[hh] if isinstance(rT_, list) else rT_
                            mmg(p[:, hh * 128:(hh + 1) * 128], [(lb[:, cp, :], rb[:, cp, :])], [lb, rb], p)

                    def pairev(dst, p, mask):
                        dv(lambda e: e.tensor_tensor(dst[:], p[:, 0:256].rearrange("p (a b) -> p a b", a=2), mask.v(0, [[0, 2], [1, 128]]), ALU.mult), [p, mask], [dst])
                    p = nps(); pairmm(p, aTt, bTm); pairev(Nb[0], p, mgt[s])
                    p = nps(); pairmm(p, bTm, aTt); pairev(Mb[0], p, mlt[s])
                    p = nps(); pairmm(p, kTm, aTt); pairev(AakT, p, mlt[s])
                    p = nps(); pairmm(p, bTm, rTt); pairev(ArbT, p, mle[s])
                    p = nps(); pairmm(p, kTm, rTt); pairev(ArkT, p, mle[s])
                    dv(lambda e: e.tensor_tensor(Tt[:], Mb[0][:], identb.v(0, [[0, 2], [1, 128]]), ALU.add), [Mb[0], identb], [Tt])
                    cur = 0
                    for it in range(nit):
                        nx = 1 - cur
                        p1 = nps(); p2 = nps()
                        for hh in range(2):
                            if it < nit - 1:
                                mmg(p1[:, hh * 128:(hh + 1) * 128], [(Nb[cur][:, hh, :], Mb[cur][:, hh, :])], [Nb[cur], Mb[cur]], p1)
                            mmg(p2[:, hh * 128:(hh + 1) * 128], [(Mb[cur][:, hh, :], Nb[cur][:, hh, :])], [Nb[cur], Mb[cur]], p2)
                        if it < nit - 1:
                            ac(lambda e, p1=p1, nx=nx: e.activation(Mb[nx][:].rearrange("p a b -> p (a b)"), p1[:, 0:256], AF.Copy), [p1], [Mb[nx]])
                        dv(lambda e, p2=p2, nx=nx: e.tensor_copy(Nb[nx][:].rearrange("p a b -> p (a b)"), p2[:, 0:256]), [p2], [Nb[nx]])
                        p3 = nps()
                        for hh in range(2):
                            mmg(p3[:, hh * 128:(hh + 1) * 128], [(Nb[nx][:, hh, :], Tt[:, hh, :])], [Nb[nx], Tt], p3)
                        dv(lambda e, p3=p3: e.tensor_tensor(Tt[:].rearrange("p a b -> p (a b)"), p3[:, 0:256], Tt[:].rearrange("p a b -> p (a b)"), ALU.add), [p3, Tt], [Tt])
                        cur = nx
                    if s == 1:
                        pxs = [PS[4], PS[5]]; pys = [PS[6], PS[7]]
                        for hh in range(2):
                            cx.op("pe", lambda e, hh=hh: e.matmul(pxs[hh][:, 0:64], AakT[:, hh, :], vtk[:, cp * 128 + hh * 64:cp * 128 + (hh + 1) * 64], start=True, stop=False),
                                  reads=[AakT, vtk], writes=[pxs[hh]])
                    if s == 0:
                        for hh in range(2):
                            for k in range(1, NBP):
                                pass
                        for k in range(NBP):
                            dv(lambda e, k=k: e.tensor_tensor(am[:], aTt[:, cp, :], rblkrow[:, k, :], ALU.mult), [aTt, rblkrow], [am])
                            po_(lambda e, k=k: e.tensor_tensor(rmk[:, k, :], rTt[:, cp, :], rblkrow[:, k, :], ALU.mult), [rTt, rblkrow], [rmk])
                            Hsrc = [(Hbm[hh][:, gc, :], Hbm[hh]) if k == 0 else (Hrd[hh][:, k, :], Hrd[hh]) for hh in range(2)]
                            px = accps()
                            for hh in range(2):
                                mmg(px[:, hh * 64:(hh + 1) * 64], [(AakT[:, hh, :], vtk[:, cp * 128 + hh * 64:cp * 128 + (hh + 1) * 64]),
                                                                   (am[:], Hsrc[hh][0])], [AakT, vtk, am, Hsrc[hh][1]], px)
                            dv(lambda e, px=px: e.tensor_copy(Xb[:], px[:, 0:128]), [px], [Xb])
                            pu = nps()
                            for hh in range(2):
                                mmg(pu[:, hh * 64:(hh + 1) * 64], [(Tt[:, hh, :], Xb[:, hh * 64:(hh + 1) * 64])], [Tt, Xb], pu)
                            if k == 0:
                                dv(lambda e, pu=pu, k=k: e.tensor_scalar_mul(Ubf[:], pu[:, 0:128], rblk[:, k:k + 1]), [pu, rblk], [Ubf])
                            else:
                                dv(lambda e, pu=pu, k=k: e.scalar_tensor_tensor(Ubf[:], pu[:, 0:128], rblk[:, k:k + 1], Ubf[:], ALU.mult, ALU.add), [pu, rblk, Ubf], [Ubf])
                            dv(lambda e, k=k, cs=cs: e.tensor_scalar_mul(bm[:], btk[:, cs], rblk[:, k:k + 1]), [btk, rblk], [bm])
                            po_(lambda e, k=k, cs=cs: e.tensor_scalar_mul(km[:], ktk[:, cs], rblk[:, k:k + 1]), [ktk, rblk], [km])
                            ph = nps()
                            mmg(ph[:, 0:128], [(bm[:], Ubf[:]), (km[:], vtk[:, cs])], [bm, Ubf, km, vtk], ph)
                            for hh in range(2):
                                pb = 64 * hh
                                dv(lambda e, pb=pb, ph=ph: e.tensor_tensor(Hst[pb:pb + 64, gc, :], ph[pb:pb + 64, pb:pb + 64], Hst[pb:pb + 64, gc, :], ALU.add), [ph, Hst], [Hst])
                            dv(lambda e, k=k: e.tensor_scalar_mul(Hst[:, gc, :], Hst[:, gc, :], dL[:, cp, k:k + 1]), [Hst, dL], [Hst])
                            if k < NBP - 1:
                                for hh in range(2):
                                    pb = 64 * hh
                                    po_(lambda e, hh=hh, pb=pb, k=k: e.tensor_copy(Hrd[hh][pb:pb + 64, k + 1, :], Hst[pb:pb + 64, gc, :]), [Hst], [Hrd[hh]])
                        py = accps()
                        for hh in range(2):
                            vh = vtk[:, cp * 128 + hh * 64:cp * 128 + (hh + 1) * 64]
                            pairs = [(ArbT[:, hh, :], Ubf[:, hh * 64:(hh + 1) * 64]), (ArkT[:, hh, :], vh), (rmk[:, 0, :], Hbm[hh][:, gc, :])]
                            pairs += [(rmk[:, k, :], Hrd[hh][:, k, :]) for k in range(1, NBP)]
                            mmg(py[:, hh * 64:(hh + 1) * 64], pairs, [ArbT, Ubf, ArkT, vtk, rmk, Hbm[hh], Hrd[hh]], py)
                        dv(lambda e, py=py, cs=cs: e.tensor_copy(Y[:, cs], py[:, 0:128]), [py], [Y])
                        for hh in range(2):
                            pb = 64 * hh
                            po_(lambda e, hh=hh, pb=pb: e.tensor_copy(Hbm[hh][pb:pb + 64, gc, :], Hst[pb:pb + 64, gc, :]), [Hst], [Hbm[hh]])
                    else:
                        hs_list = []
                        for b in range(16):
                            k2 = ssi[0] % 3; ssi[0] += 1
                            cx.dma("sp", Sin[k2][:], rS[b + 1, 2 * gc:2 * gc + 2].rearrange("h i j -> i h j"), reads=[rS], writes=[Sin[k2]], owner=Sin[k2])
                            pt = nps()
                            cx.op("pe", lambda e, pt=pt, k2=k2: e.transpose(pt[:, 0:64], Sin[k2][:].rearrange("p a b -> p (a b)"), ident[0:64, 0:64]), reads=[Sin[k2], ident], writes=[pt])
                            for hh in range(2):
                                pb = 64 * hh
                                dv(lambda e, pt=pt, k2=k2, hh=hh, pb=pb: e.tensor_copy(Hsbm[k2][hh][pb:pb + 64, :], pt[pb:pb + 64, 0:64]), [pt], [Hsbm[k2][hh]])
                            dv(lambda e, b=b: e.tensor_tensor(am[:], aTt[:, cp, :], blkrow[:, b, :], ALU.mult), [aTt, blkrow], [am])
                            for hh in range(2):
                                cx.op("pe", lambda e, hh=hh, k2=k2, b=b: e.matmul(pxs[hh][:, 0:64], am[:], Hsbm[k2][hh][:], start=False, stop=(b == 15)),
                                      reads=[am, Hsbm[k2][hh]], writes=[pxs[hh]])
                        for hh in range(2):
                            dv(lambda e, hh=hh: e.tensor_copy(Xb[:, hh * 64:(hh + 1) * 64], pxs[hh][:, 0:64]), [pxs[hh]], [Xb])
                        pu = nps()
                        for hh in range(2):
                            mmg(pu[:, hh * 64:(hh + 1) * 64], [(Tt[:, hh, :], Xb[:, hh * 64:(hh + 1) * 64])], [Tt, Xb], pu)
                        ac(lambda e, pu=pu: e.activation(Ub[:], pu[:, 0:128], AF.Copy), [pu], [Ub])
                        for hh in range(2):
                            vh = vtk[:, cp * 128 + hh * 64:cp * 128 + (hh + 1) * 64]
                            cx.op("pe", lambda e, hh=hh: e.matmul(pys[hh][:, 0:64], ArbT[:, hh, :], Ub[:, hh * 64:(hh + 1) * 64], start=True, stop=False), reads=[ArbT, Ub], writes=[pys[hh]])
                            cx.op("pe", lambda e, hh=hh, vh=vh: e.matmul(pys[hh][:, 0:64], ArkT[:, hh, :], vh, start=False, stop=False), reads=[ArkT, vtk], writes=[pys[hh]])
                        for b in range(16):
                            k2 = ssi[0] % 3; ssi[0] += 1
                            cx.dma("sp", Sin[k2][:], rS[b + 1, 2 * gc:2 * gc + 2].rearrange("h i j -> i h j"), reads=[rS], writes=[Sin[k2]], owner=Sin[k2])
                            pt = nps()
                            cx.op("pe", lambda e, pt=pt, k2=k2: e.transpose(pt[:, 0:64], Sin[k2][:].rearrange("p a b -> p (a b)"), ident[0:64, 0:64]), reads=[Sin[k2], ident], writes=[pt])
                            for hh in range(2):
                                pb = 64 * hh
                                dv(lambda e, pt=pt, k2=k2, hh=hh, pb=pb: e.tensor_copy(Hsbm[k2][hh][pb:pb + 64, :], pt[pb:pb + 64, 0:64]), [pt], [Hsbm[k2][hh]])
                            ac(lambda e, pt=pt, k2=k2: e.activation(Hs[k2][:], pt[:, 0:64], AF.Copy), [pt], [Hs[k2]])
                            dv(lambda e, b=b: e.tensor_tensor(rm[:], rTt[:, cp, :], blkrow[:, b, :], ALU.mult), [rTt, blkrow], [rm])
                            for hh in range(2):
                                cx.op("pe", lambda e, hh=hh, k2=k2, b=b: e.matmul(pys[hh][:, 0:64], rm[:], Hsbm[k2][hh][:], start=False, stop=(b == 15)),
                                      reads=[rm, Hsbm[k2][hh]], writes=[pys[hh]])
                            dv(lambda e, b=b, cs=cs: e.tensor_scalar_mul(bm[:], btk[:, cs], refs[:, 18 + b:19 + b]), [btk, refs], [bm])
                            po_(lambda e, b=b, cs=cs: e.tensor_scalar_mul(km[:], ktk[:, cs], refs[:, 18 + b:19 + b]), [ktk, refs], [km])
                            ph = nps()
                            mmg(ph[:, 0:128], [(bm[:], Ub[:]), (km[:], vtk[:, cs])], [bm, Ub, km, vtk], ph)
                            for hh in range(2):
                                pb = 64 * hh
                                dv(lambda e, pb=pb, ph=ph, k2=k2: e.tensor_tensor(Hs[k2][pb:pb + 64, :], ph[pb:pb + 64, pb:pb + 64], Hs[k2][pb:pb + 64, :], ALU.add), [ph, Hs[k2]], [Hs[k2]])
                            dv(lambda e, k2=k2, b=b: e.tensor_scalar_mul(Hs[k2][:], Hs[k2][:], dL[:, cp, b + 1:b + 2]), [Hs[k2], dL], [Hs[k2]])
                            pt2 = nps()
                            cx.op("pe", lambda e, pt2=pt2, k2=k2: e.transpose(pt2[0:64, 0:128], Hs[k2][:], ident[:]), reads=[Hs[k2], ident], writes=[pt2])
                            ac(lambda e, pt2=pt2, k2=k2: e.activation(Sout[k2][:].rearrange("p a b -> p (a b)"), pt2[0:64, 0:128], AF.Copy), [pt2], [Sout[k2]])
                            cx.dma("act", orS[b + 1, 2 * gc:2 * gc + 2].rearrange("h i j -> i h j"), Sout[k2][:], reads=[Sout[k2]], writes=[orS], owner=Sout[k2])
                        for hh in range(2):
                            dv(lambda e, hh=hh, cp=cp: e.tensor_copy(Y[:, cp * 128 + hh * 64:cp * 128 + (hh + 1) * 64], pys[hh][:, 0:64]), [pys[hh]], [Y])
                dv(lambda e: e.tensor_reduce(stt_[:, 2, :], h3(Y), AX.X, ALU.add), [Y], [stt_])
                dv(lambda e: e.tensor_scalar_mul(stt_[:, 2, :], stt_[:, 2, :], 1.0 / 64), [stt_], [stt_])
                dv(lambda e: e.tensor_tensor(h3(Y), h3(Y), bc16(stt_, 2), ALU.subtract), [Y, stt_], [Y])
                ac(lambda e: e.activation(TMP[:], Y[:], AF.Square), [Y], [TMP])
                dv(lambda e: e.tensor_reduce(stt_[:, 3, :], h3(TMP), AX.X, ALU.add), [TMP], [stt_])
                dv(lambda e: e.tensor_scalar(stt_[:, 3, :], stt_[:, 3, :], 1.0 / 64, LN_X_EPS, ALU.mult, ALU.add), [stt_], [stt_])
                ac(lambda e: e.activation(stt_[:, 3, :], stt_[:, 3, :], AF.Sqrt), [stt_], [stt_])
                dv(lambda e: e.reciprocal(stt_[:, 3, :], stt_[:, 3, :]), [stt_], [stt_])
                dv(lambda e: e.tensor_tensor(h3(Y), h3(Y), bc16(stt_, 3), ALU.mult), [Y, stt_], [Y])
                dv(lambda e: e.tensor_tensor(Y[:], Y[:], PRM["r_lnw"][:], ALU.mult), [Y, PRM["r_lnw"]], [Y])
                po_(lambda e: e.tensor_tensor(Y[:], Y[:], PRM["r_lnb"][:], ALU.add), [Y, PRM["r_lnb"]], [Y])
                dv(lambda e: e.tensor_tensor(h3(TMP), h3(Vb_), bc16(stt_, 1), ALU.mult), [Vb_, stt_], [TMP])
                po_(lambda e: e.tensor_tensor(Y[:], Y[:], TMP[:], ALU.add), [Y, TMP], [Y])
                dv(lambda e: e.tensor_tensor(Y[:], Y[:], Gg[:], ALU.mult), [Y, Gg], [Y])
                transpose_to(outT, lambda c: outT[:, half * 8 + c, r0:r0 + 128], Y, lambda c: Y[:, c * 128:(c + 1) * 128], 8)
            if s == 0 and i == NTP - 1:
                for gc in range(16):
                    k2 = ssi[0] % 3; ssi[0] += 1
                    pt2 = nps()
                    cx.op("pe", lambda e, pt2=pt2, gc=gc: e.transpose(pt2[0:64, 0:128], Hst[:, gc, :], ident[:]), reads=[Hst, ident], writes=[pt2])
                    ac(lambda e, pt2=pt2, k2=k2: e.activation(Sout[k2][:].rearrange("p a b -> p (a b)"), pt2[0:64, 0:128], AF.Copy), [pt2], [Sout[k2]])
                    cx.dma("act", orS[0, 2 * gc:2 * gc + 2].rearrange("h i j -> i h j"), Sout[k2][:], reads=[Sout[k2]], writes=[orS], owner=Sout[k2])

    stage_mod()
    hT = sb("hT", [128, 16, NTOK], BF16)
    stage_norm(xin, 0, W["norm_mix"][0], 0, 1, hT)
    stage_l0_proj(hT)
    stage_l0_rec(hT)
    proj_resid(hT, lambda c0, n: W["ab_w_out"][0, :, c0:c0 + n], xin, x1_d, 0, 2)
    stage_norm(x1_d, 0, W["norm_ffn"][0], 3, 4, hT)
    stage_ffn(hT, 0, x1_d, x2_d)
    stage_norm(x2_d, 1, W["norm_mix"][1], 0, 1, hT, h_dram=h_d, shift_out=osh)
    import os
    stage_l1_mix(hT)
    if os.environ.get("K_SKIP") != "proj":
        stage_l1_proj(hT)
    if os.environ.get("K_SKIP") not in ("rec", "proj"):
        stage_l1_rec(hT)
    proj_resid(hT, lambda c0, n: W["r_wo"][0, :, c0:c0 + n], x2_d, x3_d, 1, 2)
    stage_norm(x3_d, 1, W["norm_ffn"][1], 3, 4, hT)
    stage_ffn(hT, 1, x3_d, x4_d)
    stage_final(x4_d)
    cx.barrier()
    return nc


def LAYER0(L):
    cx, xin, x1_d, NTOK = L["cx"], L["xin"], L["x1_d"], L["NTOK"]
    cx.dma("sp", x1_d[:], xin[:], reads=[xin], writes=[x1_d], owner=x1_d)
    cx.barrier()


def LAYER1(L):
    cx, x2_d, x3_d = L["cx"], L["x2_d"], L["x3_d"]
    cx.dma("sp", x3_d[:], x2_d[:], reads=[x2_d], writes=[x3_d], owner=x3_d)
    cx.barrier()


def make_consts(NTP):
    c = {}
    s = np.arange(128)
    blk = s // 8
    le_p = (s[:, None] <= s[None, :]).astype(np.float32)
    same = (blk[:, None] == blk[None, :])
    le_s = (le_p * same).astype(np.float32)
    lt_p = (s[:, None] < s[None, :]).astype(np.float32)
    lt_s = (lt_p * same).astype(np.float32)
    c["ident"] = np.eye(128, dtype=np.float32)
    c["ones"] = np.ones((128, 128), np.float32)
    c["mle_p"], c["mle_s"], c["mlt_p"], c["mlt_s"] = le_p, le_s, lt_p, lt_s
    c["mgt_p"], c["mgt_s"] = lt_p.T.copy(), lt_s.T.copy()
    c["neg_p"] = ((le_p.T - 1.0) * 1e30).astype(np.float32)
    c["neg_s"] = ((le_s.T - 1.0) * 1e30).astype(np.float32)
    c["trir_p"] = (le_p - (s[:, None] <= 63).astype(np.float32)).astype(np.float32)
    ref_p = np.zeros((128, 34), np.float32); ref_p[:, 0] = (s <= 63); ref_p[:, 17] = (s > 63)
    ref_s = np.zeros((128, 34), np.float32)
    for b in range(16):
        ref_s[8 * b:8 * b + 8, 17 + b + 1] = 1.0
    c["ref_p"], c["ref_s"] = ref_p, ref_s
    blkt_p = np.zeros((NB, 128), np.float32); blkt_p[0] = 1.0
    blkt_s = np.zeros((NB, 128), np.float32)
    sel_p = np.zeros((128, NB), np.float32); sel_p[127, 0] = 1.0
    sel_s = np.zeros((128, NB), np.float32)
    for b in range(16):
        blkt_s[b + 1, 8 * b:8 * b + 8] = 1.0
        sel_s[8 * b + 7, b + 1] = 1.0
    c["blkt_p"], c["blkt_s"], c["selend_p"], c["selend_s"] = blkt_p, blkt_s, sel_p, sel_s
    NBP = 4; LC = 128 // NBP
    pb_ = s // LC
    samep = (pb_[:, None] == pb_[None, :])
    c["rle_p"] = (le_p * samep).astype(np.float32); c["rlt_p"] = (lt_p * samep).astype(np.float32)
    c["rgt_p"] = c["rlt_p"].T.copy()
    rb = np.zeros((128, NB), np.float32)
    rbr = np.zeros((128, NBP, 128), np.float32)
    for k in range(NBP):
        rb[k * LC:(k + 1) * LC, k] = 1.0
        rbr[:, k, k * LC:(k + 1) * LC] = 1.0
    c["rblk_p"] = rb; c["rblkrow_p"] = rbr.reshape(128, NBP * 128)
    br = np.zeros((128, 16, 128), np.float32)
    for b in range(16):
        br[:, b, 8 * b:8 * b + 8] = 1.0
    c["blkrow"] = br.reshape(128, 16 * 128)
    return {"c_" + k: np.ascontiguousarray(v) for k, v in c.items()}


WNAMES = ["mod_w", "mod_b", "norm_mix", "norm_ffn", "ffn_w1", "ffn_w2", "final_norm", "ab_w_in", "ab_gate_b", "m_conv_w",
          "m_norm", "g_lb", "g_norm", "ab_w_out", "r_mu", "r_w0", "r_w1", "r_w2", "r_a0", "r_a1", "r_a2", "r_g1", "r_g2",
          "r_kk", "r_ka", "r_rk", "r_wr", "r_wk", "r_wv", "r_wo", "r_lnw", "r_lnb"]
_NC_CACHE = {}


def core_inputs(inp, core, NTP, consts):
    b = core // 2
    T = NTP * 128
    f = lambda a: np.ascontiguousarray(np.asarray(a, dtype=np.float32))
    sl = slice(16 * core, 16 * core + 16)
    m = {}
    m["xin"] = f(np.concatenate([inp["x_prompt"][b, :T], inp["x_sample"][sl].reshape(128, D)], axis=0))
    m["cin"] = f(np.concatenate([inp["c_prompt"][b:b + 1], inp["c_sample"][sl]], axis=0))

    def st(a):
        a = np.asarray(a)[0, sl]
        return f(np.concatenate([np.zeros((1,) + a.shape[1:], np.float32), a], axis=0))
    m["mC"] = st(inp["state_mlstm_C"]); m["mn"] = st(inp["state_mlstm_n"]); m["mm"] = st(inp["state_mlstm_m"])
    m["mconv"] = st(inp["state_mlstm_conv"]); m["gS"] = st(inp["state_hgrn_S"]); m["rS"] = st(inp["state_rwkv_S"])
    m["rsh"] = st(inp["state_rwkv_shift"])
    for n in WNAMES:
        a = f(inp[n])
        if n == "r_rk":
            a = a.reshape(1, D)
        m[n] = a
    m.update(consts)
    return m


def kernel(**inp):
    NTP = 16
    n = 8
    if NTP not in _NC_CACHE:
        _NC_CACHE[NTP] = build_nc(NTP)
    nc = _NC_CACHE[NTP]
    consts = make_consts(NTP)
    in_maps = [core_inputs(inp, c, NTP, consts) for c in range(n)]
    res = run_bass_kernel_spmd(nc, in_maps, core_ids=list(range(n)))
    R = res.results
    T = NTP * 128
    y_prompt = np.stack([R[2 * b]["y"][:T] for b in range(4)], axis=0)
    y_sample = np.concatenate([R[c]["y"][T:].reshape(16, 8, D) for c in range(n)], axis=0)

    def pst(k):
        return np.stack([R[2 * b][k][0] for b in range(4)], axis=0)[None]

    def sst(k):
        return np.concatenate([R[c][k][1:] for c in range(n)], axis=0)[None]
    keys = ["oC", "on", "om", "oconv", "oS", "orS", "osh"]
    outs = [y_prompt, y_sample] + [pst(k) for k in keys] + [sst(k) for k in keys]
    return tuple(np.ascontiguousarray(o, dtype=np.float32) for o in outs)
```
